# Optimizing a Trainium2 kernel written in Bass

```python
import math
import jax, jax.numpy as jnp
from jax import lax
import numpy as np

D_MODEL = 1024
BATCH = 8
SEQ = 2048
DEPTH = 1

CHUNK = 64
N_META = 16
Q_BLOCK = 128
D_MIX = D_MODEL
ATTN_WIDTH = D_MIX // 2
RWKV_WIDTH = D_MIX - ATTN_WIDTH
DA_HEAD_DIM = 64
DA_HEADS = ATTN_WIDTH // (2 * DA_HEAD_DIM)
RW_HEAD_DIM = 64
RW_HEADS = RWKV_WIDTH // RW_HEAD_DIM
W_LORA = 64
A_LORA = 64
G_LORA = 128
ATTN_COLS = 3 * ATTN_WIDTH
RW_COLS = 3 * RWKV_WIDTH + W_LORA + A_LORA + G_LORA
IN_COLS = ATTN_COLS + RW_COLS
D_FF = ((8 * D_MODEL // 3 + 255) // 256) * 256
RMS_EPS = 1e-6
GN_EPS = 64e-5

kernel_name = "hymba_diffattn_rwkv7_macaron_block"


def rms_norm(x, g):
    xf = x.astype(jnp.float32)
    y = xf * lax.rsqrt(jnp.mean(xf * xf, axis=-1, keepdims=True) + RMS_EPS)
    return (y * g.astype(jnp.float32)).astype(x.dtype)


def swiglu(x, w_gate, w_up, w_down):
    return (jax.nn.silu(x @ w_gate) * (x @ w_up)) @ w_down


def token_shift(x):
    return jnp.pad(x, ((0, 0), (1, 0), (0, 0)))[:, :-1]


def chunk_index(pos):
    return (pos - N_META) // CHUNK


def alibi_slopes(n_heads):
    return 2.0 ** (-8.0 * (jnp.arange(n_heads, dtype=jnp.float32) + 1.0) / n_heads)


def _diff_attn_block(q, qpos, k, v, kpos, slopes, lam):
    scale = DA_HEAD_DIM ** -0.5
    s = jnp.einsum('bqhcd,bkhcd->bhcqk', q, k).astype(jnp.float32) * scale
    dist = jnp.abs(qpos[:, None] - kpos[None, :]).astype(jnp.float32)
    s = s - slopes[None, :, None, None, None] * dist[None, None, None]
    visible = chunk_index(kpos)[None, :] <= chunk_index(qpos)[:, None]
    s = jnp.where(visible[None, None, None], s, -jnp.inf)
    p = jax.nn.softmax(s, axis=-1)
    attn = p[:, :, 0] - lam * p[:, :, 1]
    return jnp.einsum('bhqk,bkhe->bqhe', attn.astype(v.dtype), v)


def diff_attention_mixer(p, pos, slopes, q_gain, k_gain, lambda_vecs, out_gain, lam_init):
    B, L, _ = p.shape
    q, k, v = jnp.split(p, 3, axis=-1)
    q = rms_norm(q.reshape(B, L, DA_HEADS, 2, DA_HEAD_DIM), q_gain)
    k = rms_norm(k.reshape(B, L, DA_HEADS, 2, DA_HEAD_DIM), k_gain)
    v = v.reshape(B, L, DA_HEADS, 2 * DA_HEAD_DIM)
    lv = lambda_vecs.astype(jnp.float32)
    lam = jnp.exp(jnp.sum(lv[0] * lv[1])) - jnp.exp(jnp.sum(lv[2] * lv[3])) + lam_init

    def attend(qb, qpos):
        return _diff_attn_block(qb, qpos, k, v, pos, slopes, lam)

    y_meta = attend(q[:, :N_META], pos[:N_META])
    nb = (L - N_META) // Q_BLOCK
    qb = jnp.swapaxes(q[:, N_META:].reshape(B, nb, Q_BLOCK, DA_HEADS, 2, DA_HEAD_DIM), 0, 1)
    pb = pos[N_META:].reshape(nb, Q_BLOCK)
    y = lax.map(lambda args: attend(args[0], args[1]), (qb, pb))
    y = jnp.swapaxes(y, 0, 1).reshape(B, L - N_META, DA_HEADS, 2 * DA_HEAD_DIM)
    y = jnp.concatenate([y_meta, y], axis=1)
    y = rms_norm(y, out_gain) * (1.0 - lam_init)
    return y.reshape(B, L, ATTN_WIDTH)


def _heads(t):
    return t.reshape(t.shape[:-1] + (RW_HEADS, RW_HEAD_DIM))


def _rwkv7_scan(r, w, k, v, a, b):
    B, L, H, N = r.shape

    def step(S, inp):
        r_t, w_t, k_t, v_t, a_t, b_t = inp
        sa = jnp.einsum('bhij,bhj->bhi', S, a_t)
        S = (S * w_t[:, :, None, :] + sa[..., None] * b_t[:, :, None, :]
             + v_t[..., None] * k_t[:, :, None, :])
        return S, jnp.einsum('bhij,bhj->bhi', S, r_t)

    xs = tuple(jnp.swapaxes(t, 0, 1) for t in (r, w, k, v, a, b))
    _, y = lax.scan(step, jnp.zeros((B, H, N, N), jnp.float32), xs)
    return jnp.swapaxes(y, 0, 1)


def rwkv7_mixer(p, mu, w0, w_up, a0, a_up, g_up, k_k, k_a, r_k, ln_w, ln_b):
    B, L, _ = p.shape
    p = p + (token_shift(p) - p) * mu
    splits = [RWKV_WIDTH, 2 * RWKV_WIDTH, 3 * RWKV_WIDTH,
              3 * RWKV_WIDTH + W_LORA, 3 * RWKV_WIDTH + W_LORA + A_LORA]
    r, k, v, xw, xa, xg = jnp.split(p, splits, axis=-1)
    w_log = -jax.nn.softplus(-(w0 + jnp.tanh(xw) @ w_up).astype(jnp.float32)) - 0.5
    decay = jnp.exp(-jnp.exp(w_log))
    a = _heads(jax.nn.sigmoid((a0 + xa @ a_up).astype(jnp.float32)))
    g = jax.nn.sigmoid(xg) @ g_up
    kk = _heads((k * k_k).astype(jnp.float32))
    kk = kk / jnp.maximum(jnp.sqrt(jnp.sum(kk * kk, axis=-1, keepdims=True)), 1e-12)
    k_h = _heads(k.astype(jnp.float32)) * (1.0 + (a - 1.0) * _heads(k_a.astype(jnp.float32)))
    r_h = _heads(r.astype(jnp.float32))
    v_h = _heads(v.astype(jnp.float32))
    y = _rwkv7_scan(r_h, _heads(decay), k_h, v_h, -kk, kk * a)
    mean = jnp.mean(y, axis=-1, keepdims=True)
    var = jnp.mean(jnp.square(y - mean), axis=-1, keepdims=True)
    y = (y - mean) * lax.rsqrt(var + GN_EPS) * _heads(ln_w.astype(jnp.float32)) \
        + _heads(ln_b.astype(jnp.float32))
    y = y + jnp.sum(r_h * k_h * r_k.astype(jnp.float32), axis=-1, keepdims=True) * v_h
    return (y.reshape(B, L, RWKV_WIDTH) * g).astype(p.dtype)


def setup_inputs(seed: int = 0) -> dict:
    key = jax.random.key(seed)
    ks = jax.random.split(key, 28)
    f32 = jnp.float32

    def nrm(k, shape, scale):
        return jax.random.normal(k, shape, f32) * scale

    def gain(k, shape):
        return 1.0 + 0.02 * jax.random.normal(k, shape, f32)

    return {
        "x": nrm(ks[0], (BATCH, SEQ, D_MODEL), 1.0),
        "meta_tokens": nrm(ks[1], (N_META, D_MODEL), 1.0),
        "ffn1_norm": gain(ks[2], (DEPTH, D_MODEL)),
        "ffn1_gate": nrm(ks[3], (DEPTH, D_MODEL, D_FF), D_MODEL ** -0.5),
        "ffn1_up": nrm(ks[4], (DEPTH, D_MODEL, D_FF), D_MODEL ** -0.5),
        "ffn1_down": nrm(ks[5], (DEPTH, D_FF, D_MODEL), D_FF ** -0.5),
        "mix_norm": gain(ks[6], (DEPTH, D_MODEL)),
        "w_in": nrm(ks[7], (DEPTH, D_MODEL, IN_COLS), D_MODEL ** -0.5),
        "q_norm": gain(ks[8], (DEPTH, DA_HEAD_DIM)),
        "k_norm": gain(ks[9], (DEPTH, DA_HEAD_DIM)),
        "lambda_vecs": nrm(ks[10], (DEPTH, 4, DA_HEAD_DIM), 0.1),
        "attn_out_norm": gain(ks[11], (DEPTH, 2 * DA_HEAD_DIM)),
        "rw_mu": jax.random.uniform(ks[12], (DEPTH, RW_COLS), f32),
        "rw_w0": jax.random.uniform(ks[13], (DEPTH, RWKV_WIDTH), f32, minval=-5.0, maxval=-1.0),
        "rw_w_up": nrm(ks[14], (DEPTH, W_LORA, RWKV_WIDTH), 0.1),
        "rw_a0": nrm(ks[15], (DEPTH, RWKV_WIDTH), 0.1),
        "rw_a_up": nrm(ks[16], (DEPTH, A_LORA, RWKV_WIDTH), A_LORA ** -0.5),
        "rw_g_up": nrm(ks[17], (DEPTH, G_LORA, RWKV_WIDTH), G_LORA ** -0.5),
        "rw_k_k": 0.85 + nrm(ks[18], (DEPTH, RWKV_WIDTH), 0.05),
        "rw_k_a": 1.0 + nrm(ks[19], (DEPTH, RWKV_WIDTH), 0.05),
        "rw_r_k": nrm(ks[20], (DEPTH, RW_HEADS, RW_HEAD_DIM), 0.1),
        "rw_ln_w": gain(ks[21], (DEPTH, RWKV_WIDTH)),
        "rw_ln_b": nrm(ks[22], (DEPTH, RWKV_WIDTH), 0.02),
        "w_out": nrm(ks[23], (DEPTH, D_MIX, D_MODEL), D_MIX ** -0.5),
        "ffn2_norm": gain(ks[24], (DEPTH, D_MODEL)),
        "ffn2_gate": nrm(ks[25], (DEPTH, D_MODEL, D_FF), D_MODEL ** -0.5),
        "ffn2_up": nrm(ks[26], (DEPTH, D_MODEL, D_FF), D_MODEL ** -0.5),
        "ffn2_down": nrm(ks[27], (DEPTH, D_FF, D_MODEL), D_FF ** -0.5),
    }


def reference(x, meta_tokens, ffn1_norm, ffn1_gate, ffn1_up, ffn1_down, mix_norm, w_in,
              q_norm, k_norm, lambda_vecs, attn_out_norm, rw_mu, rw_w0, rw_w_up, rw_a0,
              rw_a_up, rw_g_up, rw_k_k, rw_k_a, rw_r_k, rw_ln_w, rw_ln_b, w_out,
              ffn2_norm, ffn2_gate, ffn2_up, ffn2_down):
    B = x.shape[0]
    meta = jnp.broadcast_to(meta_tokens[None].astype(x.dtype), (B, N_META, D_MODEL))
    h = jnp.concatenate([meta, x], axis=1)
    L = h.shape[1]
    pos = jnp.arange(L, dtype=jnp.int32)
    slopes = alibi_slopes(DA_HEADS)
    for l in range(DEPTH):
        lam_init = 0.8 - 0.6 * math.exp(-0.3 * l)
        h = h + 0.5 * swiglu(rms_norm(h, ffn1_norm[l]), ffn1_gate[l], ffn1_up[l], ffn1_down[l])
        u = rms_norm(h, mix_norm[l])
        proj = u @ w_in[l]
        y_attn = diff_attention_mixer(proj[..., :ATTN_COLS], pos, slopes, q_norm[l], k_norm[l],
                                      lambda_vecs[l], attn_out_norm[l], lam_init)
        y_rwkv = rwkv7_mixer(proj[..., ATTN_COLS:], rw_mu[l], rw_w0[l], rw_w_up[l], rw_a0[l],
                             rw_a_up[l], rw_g_up[l], rw_k_k[l], rw_k_a[l], rw_r_k[l],
                             rw_ln_w[l], rw_ln_b[l])
        h = h + jnp.concatenate([y_attn, y_rwkv], axis=-1) @ w_out[l]
        h = h + 0.5 * swiglu(rms_norm(h, ffn2_norm[l]), ffn2_gate[l], ffn2_up[l], ffn2_down[l])
    return h[:, N_META:]
```

```python
import contextlib
import math
import numpy as np
import concourse.bass as bass
import concourse.mybir as mybir
from concourse.bass_utils import run_bass_kernel_spmd

F32 = mybir.dt.float32
BF16 = mybir.dt.bfloat16
AF = mybir.ActivationFunctionType
ALU = mybir.AluOpType
AX = mybir.AxisListType

D = 1024
SEQ = 2048
NMETA = 16
DFF = 2816
NF = DFF // 128
NTC = 2112
MC0 = 48
FC0 = 64
RMS_EPS = 1e-6
GN_EPS = 64e-5
LAM_INIT = 0.8 - 0.6 * math.exp(-0.3 * 0)
C0 = math.exp(-0.5)
SLOPES = [2.0 ** (-8.0 * (h + 1) / 4) for h in range(4)]

_uid = [0]


def _nm(s):
    _uid[0] += 1
    return f"{s}_{_uid[0]}"


class Op:
    __slots__ = ("eng", "idx", "fn", "deps", "dma", "sig", "val", "dsem", "dval")


class _Rec:
    def __init__(self):
        self.call = None

    def __getattr__(self, name):
        def f(*a, **k):
            self.call = (name, a, k)
            return None
        return f


class Plan:
    ENGS = ("pe", "act", "dve", "pool", "sp")

    def __init__(self, nc, st, tag, ndma=6):
        self.nc = nc
        self.q = {e: [] for e in self.ENGS}
        self.sem = {e: st.enter_context(nc.semaphore(_nm(f"s{tag}{e}"))) for e in self.ENGS}
        self.dsems = {e: [st.enter_context(nc.semaphore(_nm(f"d{tag}{e}"))) for _ in range(ndma)] for e in ("sp", "pool")}
        self.ndma = {"sp": 0, "pool": 0}
        self.lastw = {}
        self.readers = {}
        self.dmas = []

    def emit(self, eng, fn, reads=(), writes=(), dma=False):
        op = Op()
        rec = _Rec()
        fn(rec)
        name_, a_, k_ = rec.call
        fn = (lambda e, name_=name_, a_=a_, k_=k_: getattr(e, name_)(*a_, **k_))
        op.eng, op.fn, op.dma, op.sig, op.val = eng, fn, dma, False, 0
        op.idx = len(self.q[eng])
        deps = []
        for k in reads:
            w = self.lastw.get(k)
            if w is not None:
                deps.append(w)
        for k in writes:
            w = self.lastw.get(k)
            if w is not None:
                deps.append(w)
            deps.extend(self.readers.get(k, {}).values())
        best = {}
        dl = []
        for d in deps:
            if d is op:
                continue
            if d.dma:
                if d not in dl:
                    dl.append(d)
            else:
                if d.eng == eng and eng == "pe":
                    continue
                b = best.get(d.eng)
                if b is None or d.idx > b.idx:
                    best[d.eng] = d
        op.deps = dl + list(best.values())
        for d in op.deps:
            d.sig = True
        for k in reads:
            self.readers.setdefault(k, {})[(eng, op.idx if dma else -1)] = op
        for k in writes:
            self.lastw[k] = op
            self.readers[k] = {}
        if dma:
            n = self.ndma[eng]
            self.ndma[eng] = n + 1
            sems = self.dsems[eng]
            op.dsem = sems[n % len(sems)]
            op.dval = 16 * (n // len(sems) + 1)
            self.dmas.append(op)
        self.q[eng].append(op)
        return op

    def finish(self):
        op = Op()
        op.eng, op.fn, op.dma, op.sig, op.val = "sp", (lambda e: e.nop()), False, False, 0
        op.idx = len(self.q["sp"])
        last = {}
        for d in self.dmas:
            last[id(d.dsem)] = d
        op.deps = list(last.values())
        self.q["sp"].append(op)
        for eng in self.ENGS:
            c = 0
            for o in self.q[eng]:
                if o.sig and not o.dma:
                    c += 1
                    o.val = c

    def _replay(self, eng):
        def run(e):
            seen = {}
            for op in self.q[eng]:
                waits = []
                for d in op.deps:
                    if d.dma:
                        waits.append((d.dsem, d.dval))
                    else:
                        waits.append((self.sem[d.eng], d.val))
                if op.dma and op.dval > 16:
                    waits.append((op.dsem, op.dval - 16))
                for s, v in waits:
                    if seen.get(id(s), 0) < v:
                        seen[id(s)] = v
                        e.wait_ge(s, v)
                ins = op.fn(e)
                if op.dma:
                    ins.then_inc(op.dsem, 16)
                elif op.sig:
                    ins.then_inc(self.sem[eng], 1)
        return run

    def run(self):
        self.finish()
        with self.nc.Block() as block:
            block.tensor(self._replay("pe"))
            block.scalar(self._replay("act"))
            block.vector(self._replay("dve"))
            block.gpsimd(self._replay("pool"))
            block.sync(self._replay("sp"))


def bk(b):
    return ("ps", b)


def build_nc():
    nc = bass.Bass("TRN2", target_bir_lowering=False)

    def din(name, shape):
        return nc.dram_tensor(name, list(shape), F32, kind="ExternalInput").ap()

    x = din("x", [SEQ, D])
    meta = din("meta_tokens", [NMETA, D])
    f1n = din("ffn1_norm", [1, D]); f1g = din("ffn1_gate", [D, DFF]); f1u = din("ffn1_up", [D, DFF]); f1d = din("ffn1_down", [DFF, D])
    f2n = din("ffn2_norm", [1, D]); f2g = din("ffn2_gate", [D, DFF]); f2u = din("ffn2_up", [D, DFF]); f2d = din("ffn2_down", [DFF, D])
    mixn = din("mix_norm", [1, D])
    w_in = din("w_in", [D, 3328])
    w_out = din("w_out", [D, D])
    qkg = din("qkg", [128, 2])
    lamv = din("lambda_vecs", [1, 256])
    aon = din("attn_out_norm", [1, 128])
    rwp = din("rwp", [128, 34])
    rw_wup = din("rw_w_up", [64, 512]); rw_aup = din("rw_a_up", [64, 512]); rw_gup = din("rw_g_up", [128, 512])
    lnwb = din("lnwb", [128, 512])
    y = nc.dram_tensor("y", [SEQ, D], F32, kind="ExternalOutput").ap()

    with contextlib.ExitStack() as top:
        def sbT(name, shape, dt):
            return top.enter_context(nc.sbuf_tensor(name, list(shape), dt))
        hres = sbT("hres", [128, 16 * D], F32)
        hmeta = sbT("hmeta", [16, D], F32)
        ident = sbT("ident", [128, 128], BF16)
        BD = sbT("BD", [128, 128], BF16)
        neghalf = sbT("neghalf", [128, 32], F32)
        PSF = top.enter_context(nc.psum_tensor("PSF", [128, 6 * 512], F32))
        PSB = top.enter_context(nc.psum_tensor("PSB", [128, 2 * 1024], BF16))

        def fb(b, n=512, p0=0, p1=128, off=0):
            return PSF[p0:p1, b * 512 + off: b * 512 + off + n]

        def htile(i):
            return hres[:, i * D:(i + 1) * D]

        def emit_norm_T(P, st, tag, srcs, gain_ap, dstT3, dkey, cache):
            n = len(srcs)
            if "gbc" not in cache:
                cache["gbc"] = st.enter_context(nc.sbuf_tensor(_nm(tag + "gbc"), [128, D], F32))
                cache["junk"] = st.enter_context(nc.sbuf_tensor(_nm(tag + "junk"), [128, D], BF16))
                cache["xn"] = [st.enter_context(nc.sbuf_tensor(_nm(tag + "xn"), [128, D], BF16)) for _ in range(2)]
                cache["ss"] = st.enter_context(nc.sbuf_tensor(_nm(tag + "ss"), [128, 32], F32))
                cache["ms"] = st.enter_context(nc.sbuf_tensor(_nm(tag + "ms"), [128, 32], F32))
                cache["rstd"] = st.enter_context(nc.sbuf_tensor(_nm(tag + "rstd"), [128, 32], F32))
                gbc0 = cache["gbc"]
                P.emit("sp", lambda e: e.dma_start(out=gbc0[:], in_=gain_ap[0:1, :].partition_broadcast(128)), writes=[("ngbc",)], dma=True)
            gbc, junk, xn, ss, ms, rstd = cache["gbc"], cache["junk"], cache["xn"], cache["ss"], cache["ms"], cache["rstd"]
            kg, kss, kms, krs = ("ngbc",), ("nss",), ("nms",), ("nrstd",)

            def stats(sub, base):
                P.emit("pool", lambda e: e.memset(ss[:, 0:len(sub)], 1.0), writes=[kss])
                for i, (src, np_, col, skey) in enumerate(sub):
                    P.emit("act", lambda e, src=src, np_=np_, i=i: e.activation(out=junk[:np_, :], in_=src, func=AF.Square,
                                                                                accum_out=ss[:np_, i:i + 1]),
                           reads=[skey, kss], writes=[("nssc", i), ("njunk",)])
                m = len(sub)
                P.emit("dve", lambda e: e.tensor_scalar(out=ms[:, 0:m], in0=ss[:, 0:m], scalar1=1.0 / D, scalar2=RMS_EPS,
                                                        op0=ALU.mult, op1=ALU.add), reads=[kss] + [("nssc", i) for i in range(m)], writes=[kms])
                P.emit("pool", lambda e: e.tensor_tensor(out=rstd[:, 0:m], in0=ms[:, 0:m], in1=neghalf[:, 0:m], op=ALU.pow),
                       reads=[kms], writes=[krs])
                for i, (src, np_, col, skey) in enumerate(sub):
                    xb = xn[(base + i) % 2]
                    kx = ("nxn", (base + i) % 2)
                    pb = (base + i) % 2
                    P.emit("dve", lambda e, src=src, np_=np_, i=i, xb=xb: e.scalar_tensor_tensor(
                        out=xb[:np_, :], in0=src, scalar=rstd[:np_, i:i + 1], in1=gbc[:np_, :], op0=ALU.mult, op1=ALU.mult),
                        reads=[skey, krs, kg], writes=[kx])
                    for k in range(8):
                        P.emit("pe", lambda e, k=k, np_=np_, xb=xb, pb=pb: e.transpose(
                            PSB[:, pb * 1024 + k * 128: pb * 1024 + k * 128 + np_], xb[:np_, k * 128:(k + 1) * 128], ident[:np_, :np_]),
                            reads=[kx], writes=[("psb", pb)])
                    P.emit("act", lambda e, np_=np_, col=col, pb=pb: e.activation(
                        out=dstT3[:, :, col:col + np_],
                        in_=PSB[:, pb * 1024:(pb + 1) * 1024].rearrange("p (k t) -> p k t", k=8)[:, :, 0:np_], func=AF.Copy),
                        reads=[("psb", pb)], writes=[(dkey, col)])
            for b0 in range(0, n, 16):
                stats(srcs[b0:b0 + 16], b0)

        def ffn_phase(tag, gain_ap, wg_ap, wu_ap, wd_ap, with_meta, first, last):
            with contextlib.ExitStack() as st:
                P = Plan(nc, st, tag)

                def sb(name, shape, dt):
                    return st.enter_context(nc.sbuf_tensor(_nm(tag + name), list(shape), dt))
                W = 1040
                xnT = sb("xnT", [128, 8, W], BF16)
                h1T = sb("h1T", [128, NF, W], BF16)
                wd_sb = sb("wd", [128, NF, D], BF16)
                wg_sb = [sb("wg", [128, 8, 128], BF16) for _ in range(3)]
                wu_sb = [sb("wu", [128, 8, 128], BF16) for _ in range(3)]
                sg = [sb("sg", [128, 512], BF16) for _ in range(2)]
                wgv = wg_ap.rearrange("(kc p) f -> p kc f", p=128)
                wuv = wu_ap.rearrange("(kc p) f -> p kc f", p=128)
                wdv = wd_ap.rearrange("(fc p) d -> p fc d", p=128)

                if first:
                    for i in range(16):
                        P.emit("sp", lambda e, i=i: e.dma_start(out=htile(i), in_=x[i * 128:(i + 1) * 128, :]), writes=[("h", i)], dma=True)
                    P.emit("sp", lambda e: e.dma_start(out=hmeta[:], in_=meta[:, :]), writes=[("hm",)], dma=True)
                    P.emit("pool", lambda e: e.memset(ident[:], 0.0), writes=[("ident",)])
                    P.emit("pool", lambda e: e.affine_select(out=ident[:], in_=ident[:], pattern=[[-1, 128]], compare_op=ALU.not_equal,
                                                             fill=1.0, base=0, channel_multiplier=1), writes=[("ident",)])
                    P.emit("pool", lambda e: e.memset(BD[:], 0.0), writes=[("BD",)])
                    P.emit("pool", lambda e: e.memset(BD[0:64, 0:64], 1.0), writes=[("BD",)])
                    P.emit("pool", lambda e: e.memset(BD[64:128, 64:128], 1.0), writes=[("BD",)])
                    P.emit("pool", lambda e: e.memset(neghalf[:], -0.5), writes=[("neghalf",)])

                def wdma(f):
                    s = f % 3
                    P.emit("pool", lambda e: e.dma_start(out=wg_sb[s][:], in_=wgv[:, :, f * 128:(f + 1) * 128]), writes=[("wg", s)], dma=True)
                    P.emit("pool", lambda e: e.dma_start(out=wu_sb[s][:], in_=wuv[:, :, f * 128:(f + 1) * 128]), writes=[("wu", s)], dma=True)

                def wd_dma(j):
                    P.emit("pool", lambda e: e.dma_start(out=wd_sb[:, 2 * j:2 * j + 2, :], in_=wdv[:, 2 * j:2 * j + 2, :]),
                           writes=[("wd", 2 * j), ("wd", 2 * j + 1)], dma=True)

                unit = [0]
                dunit = [0]
                ncache = {}
                for p in range(2):
                    tiles = [(htile(8 * p + t), 128, t * 128, ("h", 8 * p + t)) for t in range(8)]
                    blocks = [(0, 512), (512, 512)]
                    if with_meta and p == 0:
                        tiles.append((hmeta[0:16, :], 16, 1024, ("hm",)))
                        blocks.append((1024, 16))
                    if p == 0:
                        for f in range(3):
                            wdma(f)
                    emit_norm_T(P, st, tag + "n", tiles, gain_ap, xnT, "xnT", ncache)
                    if p == 1:
                        for f in range(3):
                            wdma(f)
                    for f in range(NF):
                        s = f % 3
                        for (c0, N) in blocks:
                            u = unit[0] % 3
                            unit[0] += 1
                            sgi = unit[0] % 2
                            xk = [("xnT", c0 + t * 128) for t in range((N + 127) // 128)]
                            for k in range(8):
                                P.emit("pe", lambda e, k=k, u=u, c0=c0, N=N, s=s: e.matmul(
                                    fb(2 * u, N), wg_sb[s][:, k, :], xnT[:, k, c0:c0 + N], start=(k == 0), stop=(k == 7)),
                                    reads=[("wg", s)] + xk, writes=[bk(2 * u)])
                            for k in range(8):
                                P.emit("pe", lambda e, k=k, u=u, c0=c0, N=N, s=s: e.matmul(
                                    fb(2 * u + 1, N), wu_sb[s][:, k, :], xnT[:, k, c0:c0 + N], start=(k == 0), stop=(k == 7)),
                                    reads=[("wu", s)] + xk, writes=[bk(2 * u + 1)])
                            P.emit("act", lambda e, u=u, N=N, sgi=sgi: e.activation(out=sg[sgi][:, 0:N], in_=fb(2 * u, N), func=AF.Silu),
                                   reads=[bk(2 * u)], writes=[("sg", sgi)])
                            P.emit("dve", lambda e, u=u, N=N, sgi=sgi, f=f, c0=c0: e.tensor_tensor(
                                out=h1T[:, f, c0:c0 + N], in0=fb(2 * u + 1, N), in1=sg[sgi][:, 0:N], op=ALU.mult),
                                reads=[bk(2 * u + 1), ("sg", sgi)], writes=[("h1T", f, c0 + t * 128) for t in range((N + 127) // 128)])
                        if f + 3 < NF:
                            wdma(f + 3)
                        if p == 0 and f < 11:
                            wd_dma(f)
                    for (src, np_, col, skey) in tiles:
                        u = dunit[0] % 3
                        dunit[0] += 1
                        for dh in range(2):
                            for f in range(NF):
                                P.emit("pe", lambda e, u=u, dh=dh, f=f, np_=np_, col=col: e.matmul(
                                    fb(2 * u + dh, 512, 0, np_), h1T[:, f, col:col + np_], wd_sb[:, f, dh * 512:(dh + 1) * 512],
                                    start=(f == 0), stop=(f == NF - 1)),
                                    reads=[("h1T", f, col), ("wd", f)], writes=[bk(2 * u + dh)])
                        P.emit("dve", lambda e, u=u, np_=np_, src=src: e.scalar_tensor_tensor(
                            out=src, in0=PSF[0:np_, 2 * u * 512: 2 * u * 512 + 1024], scalar=0.5, in1=src, op0=ALU.mult, op1=ALU.add),
                            reads=[bk(2 * u), bk(2 * u + 1), skey], writes=[skey])
                        if last:
                            ti = skey[1]
                            P.emit("sp", lambda e, ti=ti: e.dma_start(out=y[ti * 128:(ti + 1) * 128, :], in_=htile(ti)),
                                   reads=[skey], dma=True)
                P.run()

        def mixer_phase():
            with contextlib.ExitStack() as mst:
                uT = mst.enter_context(nc.sbuf_tensor("uT", [128, 8, NTC], BF16))
                with contextlib.ExitStack() as st:
                    P = Plan(nc, st, "m0")
                    P.emit("pool", lambda e: e.memset(uT[:, :, 0:MC0], 0.0), writes=[("uT", 0)])
                    srcs = [(hmeta[0:16, :], 16, MC0, ("hm",))] + [(htile(i), 128, FC0 + 128 * i, ("h", i)) for i in range(16)]
                    emit_norm_T(P, st, "m0n", srcs, mixn, uT, "uT", {})
                    P.run()
                attention_phase(uT)
                rwkv_phase(uT)

        def attention_phase(uT):
            with contextlib.ExitStack() as st:
                P = Plan(nc, st, "at")

                def sb(name, shape, dt):
                    return st.enter_context(nc.sbuf_tensor(_nm("at" + name), list(shape), dt))
                yaT = sb("yaT", [128, 4, SEQ], BF16)
                wo_a = sb("woa", [128, 4, D], BF16)
                wq = [sb("wq", [128, 8, 128], BF16) for _ in range(2)]
                wk = [sb("wk", [128, 8, 128], BF16) for _ in range(2)]
                wv = [sb("wv", [128, 8, 128], BF16) for _ in range(2)]
                qh = [sb("qh", [128, SEQ], BF16) for _ in range(2)]
                kh = [sb("kh", [128, NTC], BF16) for _ in range(2)]
                vt = [sb("vt", [128, 17, 129], BF16) for _ in range(2)]
                EE = [sb("EE", [128, 2048], BF16) for _ in range(2)]
                basef = sb("basef", [128, 2048], F32)
                absb = sb("absb", [128, 128], F32)
                vis = sb("vis", [128, 128], BF16)
                edt = sb("edt", [128, 128], BF16)
                gm = [sb("gm", [16, 128], BF16) for _ in range(2)]
                pt = [sb("pt", [128, 512], BF16) for _ in range(4)]
                ptm = [sb("ptm", [16, 128], BF16) for _ in range(2)]
                sqb = [sb("sqb", [128, 512], BF16) for _ in range(2)]
                msb = [sb("msb", [128, 512], F32) for _ in range(2)]
                rsb = [sb("rsb", [128, 512], F32) for _ in range(2)]
                gqk = sb("gqk", [128, 2], F32)
                gmb = sb("gmb", [16, 64], F32)
                epsb = sb("epsb", [128, 1], F32)
                gq8 = sb("gq8", [128, 1], F32)
                lv = sb("lv", [128, 256], F32)
                lvt = sb("lvt", [128, 128], F32)
                dd = sb("dd", [128, 2], F32)
                ed = sb("ed", [128, 2], F32)
                neglam = sb("neglam", [128, 1], F32)
                ogb = sb("ogb", [128, 128], F32)
                rz = sb("rz", [128, 2], F32)
                s1 = sb("s1", [128, 1], F32)
                y0 = sb("y0", [128, 128], F32)
                yy = sb("yy", [128, 128], F32)
                junk = sb("junk", [128, 128], BF16)
                ssq = sb("ssq", [128, 1], F32)
                msq = sb("msq", [128, 1], F32)
                rsq = sb("rsq", [128, 1], F32)
                yn = [sb("yn", [128, 128], BF16) for _ in range(2)]
                winv = w_in.rearrange("(kc p) f -> p kc f", p=128)
                wov = w_out.rearrange("(kc p) d -> p kc d", p=128)

                P.emit("sp", lambda e: e.dma_start(out=gqk[:], in_=qkg[:, :]), writes=[("gqk",)], dma=True)
                P.emit("sp", lambda e: e.dma_start(out=lv[:], in_=lamv[0:1, :].partition_broadcast(128)), writes=[("lv",)], dma=True)
                P.emit("sp", lambda e: e.dma_start(out=ogb[:], in_=aon[0:1, :].partition_broadcast(128)), writes=[("ogb",)], dma=True)
                P.emit("pool", lambda e: e.dma_start(out=wo_a[:], in_=wov[:, 0:4, :]), writes=[("woa",)], dma=True)
                P.emit("dve", lambda e: e.tensor_scalar(out=gq8[:], in0=gqk[:, 0:1], scalar1=0.125, scalar2=None, op0=ALU.mult),
                       reads=[("gqk",)], writes=[("gq8",)])
                P.emit("dve", lambda e: e.tensor_scalar(out=ogb[:], in0=ogb[:], scalar1=1.0 - LAM_INIT, scalar2=None, op0=ALU.mult),
                       reads=[("ogb",)], writes=[("ogb",)])
                lv4 = lv[:].rearrange("p (a b d) -> p a b d", a=2, b=2)
                P.emit("dve", lambda e: e.tensor_tensor(out=lvt[:].rearrange("p (a d) -> p a d", a=2), in0=lv4[:, :, 0, :], in1=lv4[:, :, 1, :],
                                                        op=ALU.mult), reads=[("lv",)], writes=[("lvt",)])
                P.emit("dve", lambda e: e.reduce_sum(out=dd[:], in_=lvt[:].rearrange("p (a d) -> p a d", a=2), axis=AX.X),
                       reads=[("lvt",)], writes=[("dd",)])
                P.emit("act", lambda e: e.activation(out=ed[:], in_=dd[:], func=AF.Exp), reads=[("dd",)], writes=[("ed",)])
                P.emit("dve", lambda e: e.tensor_tensor(out=s1[:], in0=ed[:, 0:1], in1=ed[:, 1:2], op=ALU.subtract),
                       reads=[("ed",)], writes=[("s1",)])
                P.emit("dve", lambda e: e.tensor_scalar(out=neglam[:], in0=s1[:], scalar1=LAM_INIT, scalar2=-1.0, op0=ALU.add, op1=ALU.mult),
                       reads=[("s1",)], writes=[("neglam",)])
                P.emit("pool", lambda e: e.iota(basef[:], pattern=[[1, 2048]], base=0, channel_multiplier=-1,
                                                allow_small_or_imprecise_dtypes=True), writes=[("basef",)])
                P.emit("dve", lambda e: e.tensor_scalar(out=y0[:], in0=basef[:, 0:128], scalar1=-1.0, scalar2=None, op0=ALU.mult),
                       reads=[("basef",)], writes=[("y0",)])
                P.emit("dve", lambda e: e.tensor_tensor(out=absb[:], in0=basef[:, 0:128], in1=y0[:], op=ALU.max),
                       reads=[("basef",), ("y0",)], writes=[("absb",)])
                P.emit("pool", lambda e: e.memset(vis[:], 1.0), writes=[("vis",)])
                for hh_ in range(4):
                    P.emit("pool", lambda e: e.iota(gmb[:, hh_ * 16:(hh_ + 1) * 16], pattern=[[128, 16]], base=16, channel_multiplier=0,
                                                    allow_small_or_imprecise_dtypes=True), writes=[("gmb",)])
                    P.emit("pool", lambda e: e.tensor_scalar(out=gmb[:, hh_ * 16:(hh_ + 1) * 16], in0=gmb[:, hh_ * 16:(hh_ + 1) * 16],
                                                             scalar1=-SLOPES[hh_], scalar2=None, op0=ALU.mult), writes=[("gmb",)])
                P.emit("pool", lambda e: e.memset(epsb[:], RMS_EPS), writes=[("epsb",)])
                P.emit("pool", lambda e: e.memset(vis[64:128, 0:64], 0.0), writes=[("vis",)])
                for b in range(2):
                    P.emit("pool", lambda e, b=b: e.memset(vt[b][:, :, 128:129], 1.0), writes=[("vt", b)])

                def wdma_head(h):
                    s = h % 2
                    P.emit("pool", lambda e: e.dma_start(out=wq[s][:], in_=winv[:, :, h * 128:(h + 1) * 128]), writes=[("wq", s)], dma=True)
                    P.emit("pool", lambda e: e.dma_start(out=wk[s][:], in_=winv[:, :, 512 + h * 128:512 + (h + 1) * 128]), writes=[("wk", s)], dma=True)
                    P.emit("pool", lambda e: e.dma_start(out=wv[s][:], in_=winv[:, :, 1024 + h * 128:1024 + (h + 1) * 128]), writes=[("wv", s)], dma=True)

                ctr = {"ps": 0, "nb": 0, "o": 0, "pt": 0, "ptm": 0, "yn": 0, "pb": 0}

                def proj_norm(wsb, wkey, c0, N, gain, dst, dkey):
                    b = ctr["ps"] % 3
                    ctr["ps"] += 1
                    nb = ctr["nb"] % 2
                    ctr["nb"] += 1
                    uk = [("uT", c) for c in ([MC0] if c0 == 0 else [])] + [("uT", c0 + t * 128) for t in range(N // 128)] + [("uT", 0)]
                    for k in range(8):
                        P.emit("pe", lambda e, k=k: e.matmul(fb(b, N), wsb[:, k, :], uT[:, k, c0:c0 + N], start=(k == 0), stop=(k == 7)),
                               reads=[wkey] + uk, writes=[bk(b)])
                    P.emit("act", lambda e: e.activation(out=sqb[nb][:, 0:N], in_=fb(b, N), func=AF.Square), reads=[bk(b)], writes=[("sqb", nb)])
                    P.emit("pe", lambda e: e.matmul(fb(5, N), BD[:], sqb[nb][:, 0:N], start=True, stop=True), reads=[("sqb", nb)], writes=[bk(5)])
                    P.emit("act", lambda e: e.activation(out=msb[nb][:, 0:N], in_=fb(5, N), func=AF.Ln, scale=1.0 / 64, bias=epsb[:, 0:1]),
                           reads=[bk(5), ("epsb",)], writes=[("msb", nb)])
                    P.emit("act", lambda e: e.activation(out=rsb[nb][:, 0:N], in_=msb[nb][:, 0:N], func=AF.Exp, scale=-0.5),
                           reads=[("msb", nb)], writes=[("rsb", nb)])
                    P.emit("dve", lambda e: e.scalar_tensor_tensor(out=dst, in0=fb(b, N), scalar=gain, in1=rsb[nb][:, 0:N],
                                                                   op0=ALU.mult, op1=ALU.mult),
                           reads=[bk(b), ("rsb", nb), ("gq8",), ("gqk",)], writes=[dkey])

                wdma_head(0)
                def proj_head(h):
                    s = h % 2
                    slope = SLOPES[h]
                    P.emit("act", lambda e: e.activation(out=EE[s][:], in_=basef[:], func=AF.Exp, scale=-slope), reads=[("basef",)], writes=[("EE", s)])
                    P.emit("act", lambda e: e.activation(out=edt[:], in_=absb[:], func=AF.Exp, scale=-slope), reads=[("absb",)], writes=[("edt",)])
                    P.emit("dve", lambda e: e.tensor_tensor(out=EE[s][:, 0:128], in0=edt[:], in1=vis[:], op=ALU.mult),
                           reads=[("edt",), ("vis",)], writes=[("EE", s)])
                    for g in range(4):
                        proj_norm(wq[s], ("wq", s), FC0 + 512 * g, 512, gq8[:, 0:1], qh[s][:, g * 512:(g + 1) * 512], ("qh", s, g))
                        yield
                    proj_norm(wk[s], ("wk", s), 0, 64, gqk[:, 1:2], kh[s][:, 0:64], ("kh", s, 0))
                    yield
                    for g in range(4):
                        proj_norm(wk[s], ("wk", s), FC0 + 512 * g, 512, gqk[:, 1:2], kh[s][:, FC0 + 512 * g:FC0 + 512 * (g + 1)], ("kh", s, g + 1))
                        yield
                    for t0 in range(0, 16, 4):
                        b = ctr["ps"] % 3
                        ctr["ps"] += 1
                        for t in range(4):
                            col = FC0 + (t0 + t) * 128
                            for k in range(8):
                                P.emit("pe", lambda e, k=k, t=t, col=col: e.matmul(fb(b, 128, off=t * 128), uT[:, k, col:col + 128], wv[s][:, k, :],
                                                                                   start=(k == 0), stop=(k == 7)),
                                       reads=[("wv", s), ("uT", col)], writes=[bk(b)])
                        P.emit("act", lambda e, t0=t0, b=b: e.activation(out=vt[s][:, t0:t0 + 4, 0:128],
                                                                         in_=fb(b).rearrange("p (t c) -> p t c", t=4), func=AF.Copy),
                               reads=[bk(b)], writes=[("vt", s)])
                        yield
                    b = ctr["ps"] % 3
                    ctr["ps"] += 1
                    for k in range(8):
                        P.emit("pe", lambda e, k=k, b=b: e.matmul(fb(b, 128, 0, 16), uT[:, k, MC0:MC0 + 16], wv[s][:, k, :], start=(k == 0), stop=(k == 7)),
                               reads=[("wv", s), ("uT", MC0)], writes=[bk(b)])
                    P.emit("act", lambda e, b=b: e.activation(out=vt[s][0:16, 16, 0:128], in_=fb(b, 128, 0, 16), func=AF.Copy),
                           reads=[bk(b)], writes=[("vt", s)])
                    yield

                for _ in proj_head(0):
                    pass
                for h in range(4):
                    s = h % 2
                    slope = SLOPES[h]
                    nxt = None
                    if h + 1 < 4:
                        wdma_head(h + 1)
                        nxt = proj_head(h + 1)
                    pend = []
                    late = []

                    def flush(n=2):
                        while len(pend) > n:
                            pend.pop(0)()

                    for qi in range(16):
                        gi = ctr["ptm"] % 2
                        P.emit("act", lambda e: e.activation(out=gm[gi][:], in_=basef[0:16, 0:128], func=AF.Exp, scale=-slope,
                                                             bias=gmb[0:16, h * 16 + qi:h * 16 + qi + 1]), reads=[("basef",), ("gmb",)], writes=[("gm", gi)])
                        ob = 3 + (ctr["o"] % 2)
                        ctr["o"] += 1
                        qk_ = ("qh", s, qi // 4)
                        for c in range(2):
                            cp = slice(c * 64, (c + 1) * 64)
                            nkt = qi + 1
                            first = [True]
                            for b0 in range(0, nkt, 4):
                                jjs = list(range(b0, min(b0 + 4, nkt)))
                                nbk = len(jjs)
                                b = ctr["ps"] % 3
                                ctr["ps"] += 1
                                pi = ctr["pt"] % 4
                                ctr["pt"] += 1
                                for m, jj in enumerate(jjs):
                                    j = qi - jj
                                    P.emit("pe", lambda e: e.matmul(
                                        fb(b, 128, off=m * 128), kh[s][cp, FC0 + 128 * j:FC0 + 128 * (j + 1)], qh[s][cp, qi * 128:(qi + 1) * 128],
                                        start=True, stop=True),
                                        reads=[("kh", s, 1 + j // 4), qk_], writes=[bk(b)])
                                P.emit("act", lambda e: e.activation(out=pt[pi][:, 0:nbk * 128], in_=fb(b, nbk * 128), func=AF.Exp),
                                       reads=[bk(b)], writes=[("pt", pi)])
                                P.emit("dve", lambda e: e.tensor_tensor(
                                    out=pt[pi][:, 0:nbk * 128], in0=pt[pi][:, 0:nbk * 128], in1=EE[s][:, b0 * 128:(b0 + nbk) * 128], op=ALU.mult),
                                    reads=[("pt", pi), ("EE", s)], writes=[("pt", pi)])

                                def pv(jjs=jjs, pi=pi, first=first, ob=ob, c=c, qi=qi):
                                    for m, jj in enumerate(jjs):
                                        j = qi - jj
                                        P.emit("pe", lambda e: e.matmul(
                                            fb(ob, 129, off=c * 129), pt[pi][:, m * 128:(m + 1) * 128], vt[s][:, j, :], start=first[0], stop=False),
                                            reads=[("pt", pi), ("vt", s)], writes=[bk(ob)])
                                        first[0] = False
                                flush()
                                pend.append(pv)
                            b = ctr["ps"] % 3
                            ctr["ps"] += 1
                            mi = ctr["ptm"] % 2
                            ctr["ptm"] += 1
                            P.emit("pe", lambda e: e.matmul(fb(b, 128, 0, 16), kh[s][cp, MC0:MC0 + 16], qh[s][cp, qi * 128:(qi + 1) * 128],
                                                            start=True, stop=True), reads=[("kh", s, 0), qk_], writes=[bk(b)])
                            P.emit("act", lambda e: e.activation(out=ptm[mi][:], in_=fb(b, 128, 0, 16), func=AF.Exp),
                                   reads=[bk(b)], writes=[("ptm", mi)])
                            P.emit("dve", lambda e: e.tensor_tensor(out=ptm[mi][:], in0=ptm[mi][:], in1=gm[gi][:], op=ALU.mult),
                                   reads=[("ptm", mi), ("gm", gi)], writes=[("ptm", mi)])

                            def pvm(mi=mi, ob=ob, c=c):
                                P.emit("pe", lambda e: e.matmul(fb(ob, 129, off=c * 129), ptm[mi][0:16, :], vt[s][0:16, 16, :], start=False, stop=True),
                                       reads=[("ptm", mi), ("vt", s)], writes=[bk(ob)])
                            flush()
                            pend.append(pvm)

                        def finalize(ob=ob, qi=qi):
                            O3 = fb(ob, 258).rearrange("p (c e) -> p c e", c=2)
                            P.emit("dve", lambda e: e.reciprocal(out=rz[:].rearrange("p (c o) -> p c o", o=1), in_=O3[:, :, 128:129]),
                                   reads=[bk(ob)], writes=[("rz",)])
                            P.emit("dve", lambda e: e.tensor_tensor(out=s1[:], in0=rz[:, 1:2], in1=neglam[:], op=ALU.mult),
                                   reads=[("rz",), ("neglam",)], writes=[("s1",)])
                            P.emit("dve", lambda e: e.tensor_scalar(out=y0[:], in0=O3[:, 0, 0:128], scalar1=rz[:, 0:1], scalar2=None, op0=ALU.mult),
                                   reads=[bk(ob), ("rz",)], writes=[("y0",)])
                            P.emit("dve", lambda e: e.scalar_tensor_tensor(out=yy[:], in0=O3[:, 1, 0:128], scalar=s1[:, 0:1], in1=y0[:],
                                                                           op0=ALU.mult, op1=ALU.add),
                                   reads=[bk(ob), ("s1",), ("y0",)], writes=[("yy",)])
                            P.emit("act", lambda e: e.activation(out=junk[:], in_=yy[:], func=AF.Square, accum_out=ssq[:]),
                                   reads=[("yy",)], writes=[("junk",), ("ssq",)])
                            P.emit("dve", lambda e: e.tensor_scalar(out=msq[:], in0=ssq[:], scalar1=1.0 / 128, scalar2=RMS_EPS, op0=ALU.mult, op1=ALU.add),
                                   reads=[("ssq",)], writes=[("msq",)])
                            P.emit("pool", lambda e: e.tensor_tensor(out=rsq[:], in0=msq[:], in1=neghalf[:, 0:1], op=ALU.pow),
                                   reads=[("msq",)], writes=[("rsq",)])
                            yi = ctr["yn"] % 2
                            ctr["yn"] += 1
                            P.emit("dve", lambda e: e.scalar_tensor_tensor(out=yn[yi][:], in0=yy[:], scalar=rsq[:, 0:1], in1=ogb[:],
                                                                           op0=ALU.mult, op1=ALU.mult),
                                   reads=[("yy",), ("rsq",), ("ogb",)], writes=[("yn", yi)])
                            pb = ctr["pb"] % 2
                            ctr["pb"] += 1

                            def tr(yi=yi, pb=pb, qi=qi):
                                P.emit("pe", lambda e: e.transpose(PSB[:, pb * 1024:pb * 1024 + 128], yn[yi][:], ident[:]),
                                       reads=[("yn", yi)], writes=[("psb", pb)])
                                P.emit("act", lambda e: e.activation(out=yaT[:, h, qi * 128:(qi + 1) * 128], in_=PSB[:, pb * 1024:pb * 1024 + 128],
                                                                     func=AF.Copy), reads=[("psb", pb)], writes=[("yaT", qi)])
                            late.append(tr)
                        while late:
                            late.pop(0)()
                        pend.append(finalize)
                        if nxt is not None:
                            next(nxt, None)
                    flush(0)
                    while late:
                        late.pop(0)()
                    if nxt is not None:
                        for _ in nxt:
                            pass
                for i in range(16):
                    u = i % 2
                    for dh in range(2):
                        for kc in range(4):
                            P.emit("pe", lambda e, u=u, dh=dh, kc=kc, i=i: e.matmul(
                                fb(2 * u + dh), yaT[:, kc, i * 128:(i + 1) * 128], wo_a[:, kc, dh * 512:(dh + 1) * 512], start=(kc == 0), stop=(kc == 3)),
                                reads=[("yaT", i), ("woa",)], writes=[bk(2 * u + dh)])
                    P.emit("dve", lambda e, u=u, i=i: e.tensor_tensor(out=htile(i), in0=PSF[:, 2 * u * 512:2 * u * 512 + 1024], in1=htile(i), op=ALU.add),
                           reads=[bk(2 * u), bk(2 * u + 1), ("h", i)], writes=[("h", i)])
                P.run()

        def rwkv_phase(uT):
            with contextlib.ExitStack() as st:
                P = Plan(nc, st, "rw")

                def sb(name, shape, dt):
                    return st.enter_context(nc.sbuf_tensor(_nm("rw" + name), list(shape), dt))
                NM = 256
                NCH = 4
                wo_r = sb("wor", [128, 4, D], BF16)
                wrw = [sb("wrw", [128, 8, 128], BF16) for _ in range(4)]
                waup = sb("waup", [128, 512], BF16)
                gup = sb("gup", [128, 512], BF16)
                pp = sb("pp", [128, 34], F32)
                lnt = sb("lnt", [128, 512], F32)
                prevl = sb("prevl", [128, 14], F32)
                pbuf = [sb("pbuf", [128, NM + 1], F32) for _ in range(2)]
                dtmp = [sb("dtmp", [128, NM], F32) for _ in range(2)]
                lp = sb("lp", [128, NM], F32)
                lo24 = sb("lo24", [128, NM], BF16)
                sxg = sb("sxg", [128, NM], BF16)
                k32 = [sb("k32", [128, NM], F32) for _ in range(2)]
                r32 = [sb("r32", [128, NM], F32) for _ in range(2)]
                vbf = [sb("vbf", [128, NM], BF16) for _ in range(2)]
                asig = [sb("asig", [128, NM], F32) for _ in range(2)]
                kkr = [sb("kkr", [128, NM], F32) for _ in range(2)]
                sqk = [sb("sqk", [128, NM], BF16) for _ in range(2)]
                ssm = [sb("ssm", [128, NM], F32) for _ in range(2)]
                rn = [sb("rn", [128, NM], F32) for _ in range(2)]
                kmod = [sb("kmod", [128, NM], F32) for _ in range(2)]
                kb = [sb("kb", [128, NM], F32) for _ in range(2)]
                bp = [sb("bp", [128, NM], BF16) for _ in range(2)]
                mask = sb("mask", [128, NM], F32)
                lmask = sb("lmask", [128, 64], F32)
                umask = sb("umask", [128, 128], F32)
                I2 = sb("I2", [128, 64], BF16)
                ones = sb("ones", [128, 1], BF16)
                ln20 = sb("ln20", [128, 1], F32)
                AR = [sb("AR", [128, NCH * 128], BF16) for _ in range(4)]
                TMw = sb("TMw", [128, 4 * NCH * 192], BF16)
                TM = [TMw[:, hp_ * NCH * 192:(hp_ + 1) * NCH * 192] for hp_ in range(4)]
                WA = NCH * 128
                ABTp = [sb("ABTp", [128, 2 * WA], BF16) for _ in range(2)]
                AKTp = [sb("AKTp", [128, 2 * WA], BF16) for _ in range(2)]
                TTp = [sb("TTp", [128, 2 * NM], BF16) for _ in range(2)]
                ABT = [ABTp[hp_ // 2][:, (hp_ % 2) * WA:(hp_ % 2 + 1) * WA] for hp_ in range(4)]
                AKT = [AKTp[hp_ // 2][:, (hp_ % 2) * WA:(hp_ % 2 + 1) * WA] for hp_ in range(4)]
                TT = [TTp[hp_ // 2][:, (hp_ % 2) * NM:(hp_ % 2 + 1) * NM] for hp_ in range(4)]
                PPw = [sb("PPw", [128, 2 * NM], BF16) for _ in range(2)]
                QQw = [sb("QQw", [128, 2 * NM], BF16) for _ in range(2)]
                gbfw = sb("gbfw", [128, 4 * NM], BF16)
                gbf = [gbfw[:, hp_ * NM:(hp_ + 1) * NM] for hp_ in range(4)]
                bonw = sb("bonw", [128, 4 * NCH], F32)
                bon = [bonw[:, hp_ * NCH:(hp_ + 1) * NCH] for hp_ in range(4)]
                PC = [sb("PC", [128, 8], F32) for _ in range(4)]
                S32 = [sb("S32", [128, 64], F32) for _ in range(4)]
                Sbf = [sb("Sbf", [128, 64], BF16) for _ in range(4)]
                Xbf = [sb("Xbf", [128, 64], BF16) for _ in range(4)]
                Ubf = [sb("Ubf", [128, 64], BF16) for _ in range(4)]
                t1 = [sb("t1", [128, 64], F32) for _ in range(4)]
                Ysb = sb("Ysb", [128, 4 * NM], F32)
                ysq = sb("ysq", [128, 4 * NM], F32)
                yc = sb("yc", [128, 4 * NM], F32)
                sgw = [Ysb[:, 0 * NM:1 * NM], Ysb[:, 1 * NM:2 * NM]]
                cs = [Ysb[:, 2 * NM:3 * NM], Ysb[:, 3 * NM:4 * NM]]
                csm = [ysq[:, 0 * NM:1 * NM], ysq[:, 1 * NM:2 * NM]]
                epos = [ysq[:, 2 * NM:3 * NM], ysq[:, 3 * NM:4 * NM]]
                eneg = [yc[:, 0 * NM:1 * NM], yc[:, 1 * NM:2 * NM]]
                eprev = [yc[:, 2 * NM:3 * NM], yc[:, 3 * NM:4 * NM]]
                KYsb = [("Ysb",), ("sgw", 0), ("sgw", 1), ("cs", 0), ("cs", 1)]
                Kysq = [("ysq",), ("csm", 0), ("csm", 1), ("epos", 0), ("epos", 1)]
                Kyc = [("yc",), ("eneg", 0), ("eneg", 1), ("eprev", 0), ("eprev", 1)]
                Kytm = [("ytm",), ("BT", 0), ("BT", 1), ("KT", 0), ("KT", 1)]
                st1 = sb("st1", [128, 16], F32)
                st2 = sb("st2", [128, 16], F32)
                mean = sb("mean", [128, 16], F32)
                msq = sb("msq", [128, 16], F32)
                var = sb("var", [128, 16], F32)
                rstd = sb("rstd", [128, 16], F32)
                ytm = sb("ytm", [128, 4 * NM], BF16)
                BT = [ytm[:, 0 * NM:1 * NM], ytm[:, 1 * NM:2 * NM]]
                KT = [ytm[:, 2 * NM:3 * NM], ytm[:, 3 * NM:4 * NM]]
                yrT = sb("yrT", [128, 4, NM], BF16)
                winv = w_in.rearrange("(kc p) f -> p kc f", p=128)
                wov = w_out.rearrange("(kc p) d -> p kc d", p=128)
                print("rwkv sbuf bytes remaining", nc.sbuf_bytes_remaining)

                P.emit("sp", lambda e: e.dma_start(out=pp[:], in_=rwp[:, :]), writes=[("pp",)], dma=True)
                P.emit("sp", lambda e: e.dma_start(out=lnt[:], in_=lnwb[:, :]), writes=[("lnt",)], dma=True)
                P.emit("pool", lambda e: e.dma_start(out=waup[0:64, :], in_=rw_wup[:, :]), writes=[("waup", 0)], dma=True)
                P.emit("pool", lambda e: e.dma_start(out=waup[64:128, :], in_=rw_aup[:, :]), writes=[("waup", 1)], dma=True)
                P.emit("pool", lambda e: e.dma_start(out=gup[:], in_=rw_gup[:, :]), writes=[("gup",)], dma=True)
                P.emit("pool", lambda e: e.dma_start(out=wo_r[:], in_=wov[:, 4:8, :]), writes=[("wor",)], dma=True)
                P.emit("pool", lambda e: e.memset(prevl[:], 0.0), writes=[("prevl", i) for i in range(14)])
                P.emit("pool", lambda e: e.memset(mask[:], 1.0), writes=[("mask",)])
                P.emit("pool", lambda e: e.memset(mask[:].rearrange("p (c t) -> p c t", t=64)[:, :, 0:1], 0.0), writes=[("mask",)])
                P.emit("pool", lambda e: e.memset(ones[:], 1.0), writes=[("ones",)])
                P.emit("pool", lambda e: e.memset(ln20[:], 20.0 * math.log(2.0)), writes=[("ln20",)])
                P.emit("pool", lambda e: e.memset(lmask[:], 1.0), writes=[("lmask",)])
                P.emit("pool", lambda e: e.memset(umask[:], 1.0), writes=[("umask",)])
                P.emit("pool", lambda e: e.memset(I2[:], 0.0), writes=[("I2",)])
                for hh in range(2):
                    hp_ = slice(hh * 64, (hh + 1) * 64)
                    P.emit("pool", lambda e, hp_=hp_: e.affine_select(out=lmask[hp_, :], in_=lmask[hp_, :], pattern=[[-1, 64]], compare_op=ALU.is_gt,
                                                                      fill=0.0, base=0, channel_multiplier=1), writes=[("lmask",)])
                    P.emit("pool", lambda e, hp_=hp_: e.affine_select(out=umask[hp_, 0:64], in_=umask[hp_, 0:64], pattern=[[1, 64]], compare_op=ALU.is_gt,
                                                                      fill=0.0, base=0, channel_multiplier=-1), writes=[("umask",)])
                    P.emit("pool", lambda e, hp_=hp_: e.affine_select(out=umask[hp_, 64:128], in_=umask[hp_, 64:128], pattern=[[1, 64]], compare_op=ALU.is_ge,
                                                                      fill=0.0, base=0, channel_multiplier=-1), writes=[("umask",)])
                    P.emit("pool", lambda e, hp_=hp_: e.affine_select(out=I2[hp_, :], in_=I2[hp_, :], pattern=[[-1, 64]], compare_op=ALU.not_equal,
                                                                      fill=1.0, base=0, channel_multiplier=1), writes=[("I2",)])
                for hp in range(4):
                    P.emit("pool", lambda e, hp=hp: e.memset(S32[hp][:], 0.0), writes=[("S32", hp)])
                    P.emit("pool", lambda e, hp=hp: e.memset(Sbf[hp][:], 0.0), writes=[("Sbf", hp)])

                MU0, W00, A00, KK0, KA0, RK0 = 0, 14, 18, 22, 26, 30
                wctr = [0]
                pctr = [0]

                def h2p(h2):
                    return slice(h2 * 64, (h2 + 1) * 64)

                def v3(ap):
                    return ap.rearrange("p (c t) -> p c t", t=64)

                RING = 4
                rowseq = []
                for _g in range(1 + SEQ // NM):
                    rowseq += [12, 13, 4, 0, 5, 1, 8, 9, 6, 2, 7, 3, 10, 11]
                pfc = [0]

                def prefetch_upto(n):
                    while pfc[0] < min(n, len(rowseq)):
                        i_ = pfc[0]
                        s_ = i_ % RING
                        col_ = 1536 + rowseq[i_] * 128
                        P.emit("pool", lambda e: e.dma_start(out=wrw[s_][:], in_=winv[:, :, col_:col_ + 128]), writes=[("wrw", s_)], dma=True)
                        pfc[0] += 1

                def proj_row(wc, c0, N, out_ap, out_key):
                    i_ = wctr[0]
                    wctr[0] += 1
                    assert rowseq[i_] == wc, (i_, wc, rowseq[i_])
                    s = i_ % RING
                    prefetch_upto(i_ + RING)
                    b = pctr[0] % 2
                    pctr[0] += 1
                    for k in range(8):
                        P.emit("pe", lambda e, k=k: e.matmul(fb(b, N), wrw[s][:, k, :], uT[:, k, c0:c0 + N], start=(k == 0), stop=(k == 7)),
                               reads=[("wrw", s)], writes=[bk(b)])
                    pb_ = pbuf[b]
                    P.emit("act", lambda e: e.activation(out=pb_[:, 1:N + 1], in_=fb(b, N), func=AF.Copy), reads=[bk(b)], writes=[("pbuf", b)])
                    P.emit("pool", lambda e: e.tensor_copy(out=pb_[:, 0:1], in_=prevl[:, wc:wc + 1]), reads=[("prevl", wc)], writes=[("pbuf", b)])
                    P.emit("pool", lambda e: e.tensor_copy(out=prevl[:, wc:wc + 1], in_=pb_[:, N:N + 1]), reads=[("pbuf", b)], writes=[("prevl", wc)])
                    P.emit("dve", lambda e: e.tensor_tensor(out=dtmp[b][:, 0:N], in0=pb_[:, 0:N], in1=pb_[:, 1:N + 1], op=ALU.subtract),
                           reads=[("pbuf", b)], writes=[("dtmp", b)])
                    P.emit("dve", lambda e: e.scalar_tensor_tensor(
                        out=out_ap, in0=dtmp[b][:, 0:N], scalar=pp[:, MU0 + wc:MU0 + wc + 1], in1=pb_[:, 1:N + 1], op0=ALU.mult, op1=ALU.add),
                        reads=[("dtmp", b), ("pbuf", b), ("pp",)], writes=[out_key])

                groups = [(0, 64, 1, False, 0)] + [(FC0 + NM * g, NM, NCH, True, g) for g in range(SEQ // NM)]
                wout_pending = []
                for (c0, N, nch, has_out, gidx) in groups:
                    def bc(ap8):
                        return ap8.unsqueeze(2).broadcast_to([128, nch, 64])
                    proj_row(12, c0, N, lp[:, 0:N], ("lp",))
                    P.emit("act", lambda e: e.activation(out=lo24[0:64, 0:N], in_=lp[0:64, 0:N], func=AF.Tanh), reads=[("lp",)], writes=[("lo24", 0)])
                    P.emit("act", lambda e: e.activation(out=lo24[64:128, 0:N], in_=lp[64:128, 0:N], func=AF.Copy), reads=[("lp",)], writes=[("lo24", 1)])
                    proj_row(13, c0, N, lp[:, 0:N], ("lp",))
                    P.emit("act", lambda e: e.activation(out=sxg[:, 0:N], in_=lp[:, 0:N], func=AF.Sigmoid), reads=[("lp",)], writes=[("sxg",)])

                    def chain(hp):
                        par = hp % 2
                        B0, B1 = (2, 3) if par == 0 else (4, 5)
                        hc = slice(hp * 128, (hp + 1) * 128)
                        proj_row(4 + hp, c0, N, k32[par][:, 0:N], ("k32", par))
                        proj_row(0 + hp, c0, N, r32[par][:, 0:N], ("r32", par))
                        yield
                        proj_row(8 + hp, c0, N, vbf[par][:, 0:N], ("vbf", par))
                        P.emit("pe", lambda e: e.matmul(fb(B0, N), waup[0:64, hc], lo24[0:64, 0:N], start=True, stop=True),
                               reads=[("waup", 0), ("lo24", 0)], writes=[bk(B0)])
                        yield
                        P.emit("act", lambda e: e.activation(out=sgw[par][:, 0:N], in_=fb(B0, N), func=AF.Sigmoid, bias=pp[:, W00 + hp:W00 + hp + 1]),
                               reads=[bk(B0), ("pp",)], writes=[("sgw", par)])
                        P.emit("pe", lambda e: e.matmul(fb(B1, N), waup[64:128, hc], lo24[64:128, 0:N], start=True, stop=True),
                               reads=[("waup", 1), ("lo24", 1)], writes=[bk(B1)])
                        yield
                        P.emit("act", lambda e: e.activation(out=asig[par][:, 0:N], in_=fb(B1, N), func=AF.Sigmoid, bias=pp[:, A00 + hp:A00 + hp + 1]),
                               reads=[bk(B1), ("pp",)], writes=[("asig", par)])
                        P.emit("dve", lambda e: e.tensor_tensor_scan(out=cs[par][:, 0:N], data0=mask[:, 0:N], data1=sgw[par][:, 0:N], initial=0.0,
                                                                     op0=ALU.mult, op1=ALU.add), reads=[("mask",), ("sgw", par)], writes=[("cs", par)])
                        yield
                        P.emit("dve", lambda e: e.tensor_tensor(out=csm[par][:, 0:N], in0=cs[par][:, 0:N], in1=sgw[par][:, 0:N], op=ALU.subtract),
                               reads=[("cs", par), ("sgw", par)], writes=[("csm", par)])
                        P.emit("act", lambda e: e.activation(out=epos[par][:, 0:N], in_=cs[par][:, 0:N], func=AF.Exp, scale=-C0), reads=[("cs", par)], writes=[("epos", par)])
                        yield
                        P.emit("act", lambda e: e.activation(out=eneg[par][:, 0:N], in_=cs[par][:, 0:N], func=AF.Exp, scale=C0), reads=[("cs", par)], writes=[("eneg", par)])
                        P.emit("act", lambda e: e.activation(out=eprev[par][:, 0:N], in_=csm[par][:, 0:N], func=AF.Exp, scale=-C0), reads=[("csm", par)], writes=[("eprev", par)])
                        yield
                        P.emit("pool", lambda e, hp=hp: e.tensor_copy(out=PC[hp][:, 0:nch], in_=v3(epos[par][:, 0:N])[:, :, 63]),
                               reads=[("epos", par)], writes=[("PC", hp)])
                        P.emit("act", lambda e: e.mul(out=kkr[par][:, 0:N], in_=k32[par][:, 0:N], mul=pp[:, KK0 + hp:KK0 + hp + 1]),
                               reads=[("k32", par), ("pp",)], writes=[("kkr", par)])
                        yield
                        P.emit("act", lambda e: e.activation(out=sqk[par][:, 0:N], in_=kkr[par][:, 0:N], func=AF.Square), reads=[("kkr", par)], writes=[("sqk", par)])
                        P.emit("pe", lambda e: e.matmul(fb(B0, N), BD[:], sqk[par][:, 0:N], start=True, stop=True), reads=[("sqk", par)], writes=[bk(B0)])
                        yield
                        P.emit("dve", lambda e: e.tensor_scalar(out=ssm[par][:, 0:N], in0=fb(B0, N), scalar1=1e-24, scalar2=None, op0=ALU.max),
                               reads=[bk(B0)], writes=[("ssm", par)])
                        P.emit("act", lambda e: e.activation(out=ssm[par][:, 0:N], in_=ssm[par][:, 0:N], func=AF.Ln, scale=float(2.0 ** 40)),
                               reads=[("ssm", par)], writes=[("ssm", par)])
                        yield
                        P.emit("act", lambda e: e.activation(out=rn[par][:, 0:N], in_=ssm[par][:, 0:N], func=AF.Exp, scale=-0.5, bias=ln20[:, 0:1]),
                               reads=[("ssm", par), ("ln20",)], writes=[("rn", par)])
                        P.emit("dve", lambda e: e.tensor_tensor(out=kkr[par][:, 0:N], in0=kkr[par][:, 0:N], in1=rn[par][:, 0:N], op=ALU.mult),
                               reads=[("kkr", par), ("rn", par)], writes=[("kkr", par)])
                        yield
                        P.emit("dve", lambda e: e.tensor_scalar(out=kmod[par][:, 0:N], in0=asig[par][:, 0:N], scalar1=-1.0, scalar2=pp[:, KA0 + hp:KA0 + hp + 1],
                                                                op0=ALU.add, op1=ALU.mult), reads=[("asig", par), ("pp",)], writes=[("kmod", par)])
                        P.emit("dve", lambda e: e.scalar_tensor_tensor(out=kmod[par][:, 0:N], in0=kmod[par][:, 0:N], scalar=1.0, in1=k32[par][:, 0:N], op0=ALU.add, op1=ALU.mult),
                               reads=[("kmod", par), ("k32", par)], writes=[("kmod", par)])
                        yield
                        AR3 = AR[hp][:, 0:nch * 128].rearrange("p (c a t) -> p c a t", a=2, t=64)
                        P.emit("dve", lambda e, AR3=AR3: e.scalar_tensor_tensor(
                            out=AR3[:, :, 0, :], in0=v3(kkr[par][:, 0:N]), scalar=-1.0, in1=v3(eprev[par][:, 0:N]), op0=ALU.mult, op1=ALU.mult),
                            reads=[("kkr", par), ("eprev", par)], writes=[("AR", hp)])
                        P.emit("pool", lambda e, AR3=AR3: e.tensor_tensor(out=AR3[:, :, 1, :], in0=v3(r32[par][:, 0:N]), in1=v3(epos[par][:, 0:N]), op=ALU.mult),
                               reads=[("r32", par), ("epos", par), ("AR", hp)], writes=[("AR2", hp)])
                        yield
                        P.emit("pool", lambda e: e.tensor_tensor(out=kb[par][:, 0:N], in0=kkr[par][:, 0:N], in1=asig[par][:, 0:N], op=ALU.mult),
                               reads=[("kkr", par), ("asig", par)], writes=[("kb", par)])
                        P.emit("pool", lambda e: e.tensor_tensor(out=BT[par][:, 0:N], in0=kb[par][:, 0:N], in1=eneg[par][:, 0:N], op=ALU.mult),
                               reads=[("kb", par), ("eneg", par)], writes=[("BT", par)])
                        yield
                        P.emit("pool", lambda e: e.tensor_tensor(out=KT[par][:, 0:N], in0=kmod[par][:, 0:N], in1=eneg[par][:, 0:N], op=ALU.mult),
                               reads=[("kmod", par), ("eneg", par)], writes=[("KT", par)])
                        P.emit("dve", lambda e: e.scalar_tensor_tensor(out=bp[par][:, 0:N], in0=r32[par][:, 0:N], scalar=pp[:, RK0 + hp:RK0 + hp + 1], in1=kmod[par][:, 0:N],
                                                                       op0=ALU.mult, op1=ALU.mult), reads=[("r32", par), ("kmod", par), ("pp",)], writes=[("bp", par)])
                        yield
                        P.emit("pe", lambda e: e.matmul(fb(B1, N), gup[:, hc], sxg[:, 0:N], start=True, stop=True), reads=[("gup",), ("sxg",)], writes=[bk(B1)])
                        P.emit("act", lambda e, hp=hp: e.activation(out=gbf[hp][:, 0:N], in_=fb(B1, N), func=AF.Copy), reads=[bk(B1)], writes=[("gbf", hp)])
                        yield
                        ARK = [("AR", hp), ("AR2", hp)]
                        for c in range(nch):
                            for si, (srcT, skey) in enumerate(((BT[par], ("BT", par)), (KT[par], ("KT", par)), (vbf[par], ("vbf", par)))):
                                for h2 in range(2):
                                    P.emit("pe", lambda e, c=c, si=si, srcT=srcT, h2=h2: e.transpose(
                                        PSB[h2p(h2), par * 1024 + (c * 3 + si) * 64:par * 1024 + (c * 3 + si) * 64 + 64],
                                        srcT[h2p(h2), c * 64:(c + 1) * 64], ident[h2p(h2), h2p(h2)]),
                                        reads=[skey], writes=[("psb", par)])
                        P.emit("act", lambda e, hp=hp: e.activation(out=TM[hp][:, 0:nch * 192], in_=PSB[:, par * 1024:par * 1024 + nch * 192], func=AF.Copy),
                               reads=[("psb", par)], writes=[("TM", hp)])
                        for c in range(nch):
                            for h2 in range(2):
                                P.emit("pe", lambda e, c=c, h2=h2: e.matmul(fb(B1, 1, h2 * 64, h2 * 64 + 64, off=256 + c), bp[par][h2p(h2), c * 64:(c + 1) * 64],
                                                                            ones[h2p(h2), 0:1], start=True, stop=True),
                                       reads=[("bp", par), ("ones",)], writes=[bk(B1)])
                        P.emit("act", lambda e, hp=hp: e.activation(out=bon[hp][:, 0:nch], in_=fb(B1, nch, off=256), func=AF.Copy),
                               reads=[bk(B1)], writes=[("bon", hp)])
                        yield

                    def G3(t, W, w, sz):
                        if w == W:
                            return t[:, 0:2 * W].rearrange("p (g s) -> p g s", s=sz)
                        assert w == sz
                        return t[:, 0:2 * W].rearrange("p (h x) -> p h x", h=2)[:, :, 0:w]

                    def pair_stage(pr):
                        ng = 2 * nch
                        hps = (2 * pr, 2 * pr + 1)
                        ARKS = [[("AR", hp), ("AR2", hp)] for hp in hps]
                        for hl, hp in enumerate(hps):
                            for c in range(nch):
                                for h2 in range(2):
                                    q_ = h2p(h2)
                                    P.emit("pe", lambda e: e.matmul(fb(2, 64, q_.start, q_.stop, off=hl * NM + c * 64), AR[hp][q_, c * 128:c * 128 + 64],
                                                                    BT[hl][q_, c * 64:(c + 1) * 64], start=True, stop=True),
                                           reads=ARKS[hl] + [("BT", hl)], writes=[bk(2)])
                        for hl, hp in enumerate(hps):
                            for c in range(nch):
                                for h2 in range(2):
                                    q_ = h2p(h2)
                                    P.emit("pe", lambda e: e.matmul(fb(3 + hl, 128, q_.start, q_.stop, off=c * 128), BT[hl][q_, c * 64:(c + 1) * 64],
                                                                    AR[hp][q_, c * 128:(c + 1) * 128], start=True, stop=True),
                                           reads=ARKS[hl] + [("BT", hl)], writes=[bk(3 + hl)])
                        P.emit("dve", lambda e: e.tensor_tensor(out=G3(PPw[0], NM, N, 64), in0=G3(PSF[:, 2 * 512:3 * 512], NM, N, 64),
                                                                in1=lmask[:].unsqueeze(1).broadcast_to([128, ng, 64]), op=ALU.mult),
                               reads=[bk(2), ("lmask",)], writes=[("PPw", 0)])
                        for hl, hp in enumerate(hps):
                            P.emit("dve", lambda e: e.tensor_tensor(out=ABT[hp][:, 0:nch * 128].rearrange("p (c s) -> p c s", s=128),
                                                                    in0=fb(3 + hl, nch * 128).rearrange("p (c s) -> p c s", s=128),
                                                                    in1=umask[:].unsqueeze(1).broadcast_to([128, nch, 128]), op=ALU.mult),
                                   reads=[bk(3 + hl), ("umask",)], writes=[("ABT", hp)])
                        for hl, hp in enumerate(hps):
                            for c in range(nch):
                                for h2 in range(2):
                                    q_ = h2p(h2)
                                    P.emit("pe", lambda e: e.matmul(fb(4 + hl, 128, q_.start, q_.stop, off=c * 128), KT[hl][q_, c * 64:(c + 1) * 64],
                                                                    AR[hp][q_, c * 128:(c + 1) * 128], start=True, stop=True),
                                           reads=ARKS[hl] + [("KT", hl)], writes=[bk(4 + hl)])
                        for hl, hp in enumerate(hps):
                            P.emit("dve", lambda e: e.tensor_tensor(out=AKT[hp][:, 0:nch * 128].rearrange("p (c s) -> p c s", s=128),
                                                                    in0=fb(4 + hl, nch * 128).rearrange("p (c s) -> p c s", s=128),
                                                                    in1=umask[:].unsqueeze(1).broadcast_to([128, nch, 128]), op=ALU.mult),
                                   reads=[bk(4 + hl), ("umask",)], writes=[("AKT", hp)])
                        Q0v = G3(ABTp[pr], WA, nch * 128, 128)[:, :, 0:64]
                        TTk = [("TT", hps[0]), ("TT", hps[1])]
                        P.emit("dve", lambda e: e.tensor_tensor(out=G3(TTp[pr], NM, N, 64), in0=Q0v,
                                                                in1=I2[:].unsqueeze(1).broadcast_to([128, ng, 64]), op=ALU.add),
                               reads=[("ABT", hps[0]), ("ABT", hps[1]), ("I2",)], writes=TTk)
                        P.emit("pool", lambda e: e.tensor_copy(out=G3(QQw[0], NM, N, 64), in_=Q0v),
                               reads=[("ABT", hps[0]), ("ABT", hps[1])], writes=[("QQw", 0)])
                        for lev in range(1, 6):
                            pi_, po_ = (lev - 1) % 2, lev % 2
                            Pp, Qp, Pn, Qn = PPw[pi_], QQw[pi_], PPw[po_], QQw[po_]
                            for hl in range(2):
                                for c in range(nch):
                                    for h2 in range(2):
                                        q_ = h2p(h2)
                                        o_ = hl * NM + c * 64
                                        P.emit("pe", lambda e: e.matmul(fb(2, 64, q_.start, q_.stop, off=o_), Qp[q_, o_:o_ + 64], Pp[q_, o_:o_ + 64],
                                                                        start=True, stop=True),
                                               reads=[("PPw", pi_), ("QQw", pi_)], writes=[bk(2)])
                            P.emit("act", lambda e: e.activation(out=G3(Pn, NM, N, 64), in_=G3(PSF[:, 2 * 512:3 * 512], NM, N, 64), func=AF.Copy),
                                   reads=[bk(2)], writes=[("PPw", po_)])
                            if lev < 5:
                                for hl in range(2):
                                    for c in range(nch):
                                        for h2 in range(2):
                                            q_ = h2p(h2)
                                            o_ = hl * NM + c * 64
                                            P.emit("pe", lambda e: e.matmul(fb(3, 64, q_.start, q_.stop, off=o_), Pp[q_, o_:o_ + 64], Qp[q_, o_:o_ + 64],
                                                                            start=True, stop=True),
                                                   reads=[("PPw", pi_), ("QQw", pi_)], writes=[bk(3)])
                                P.emit("dve", lambda e: e.tensor_copy(out=G3(Qn, NM, N, 64), in_=G3(PSF[:, 3 * 512:4 * 512], NM, N, 64)),
                                       reads=[bk(3)], writes=[("QQw", po_)])
                            for hl in range(2):
                                for c in range(nch):
                                    for h2 in range(2):
                                        q_ = h2p(h2)
                                        o_ = hl * NM + c * 64
                                        P.emit("pe", lambda e: e.matmul(fb(4, 64, q_.start, q_.stop, off=o_), Pn[q_, o_:o_ + 64], TTp[pr][q_, o_:o_ + 64],
                                                                        start=True, stop=True),
                                               reads=[("PPw", po_)] + TTk, writes=[bk(4)])
                            P.emit("dve", lambda e: e.tensor_tensor(out=G3(TTp[pr], NM, N, 64), in0=G3(PSF[:, 4 * 512:5 * 512], NM, N, 64),
                                                                    in1=G3(TTp[pr], NM, N, 64), op=ALU.add),
                                   reads=[bk(4)] + TTk, writes=TTk)

                    def lockstep(gens, hook_round=None):
                        gens = list(gens)
                        rnd = 0
                        while gens:
                            for g_ in list(gens):
                                try:
                                    next(g_)
                                except StopIteration:
                                    gens.remove(g_)
                            rnd += 1
                            if hook_round is not None and rnd == hook_round:
                                while wout_pending:
                                    wout_pending.pop(0)()
                    lockstep([chain(0), chain(1)], hook_round=3)
                    while wout_pending:
                        wout_pending.pop(0)()
                    pair_stage(0)
                    lockstep([chain(2), chain(3)])
                    pair_stage(1)
                    for c in range(nch):
                        vsl = slice((c * 3 + 2) * 64, (c * 3 + 3) * 64)
                        ksl = slice((c * 3 + 1) * 64, (c * 3 + 2) * 64)
                        bsl = slice((c * 3 + 0) * 64, (c * 3 + 1) * 64)
                        for hp in range(4):
                            ARK = [("AR", hp), ("AR2", hp)]
                            for h2 in range(2):
                                q_ = h2p(h2)
                                P.emit("pe", lambda e: e.matmul(fb(hp, 64, q_.start, q_.stop, off=0), AKT[hp][q_, c * 128:c * 128 + 64],
                                                                TM[hp][q_, vsl], start=True, stop=False),
                                       reads=[("AKT", hp), ("TM", hp)], writes=[bk(hp)])
                                P.emit("pe", lambda e: e.matmul(fb(hp, 64, q_.start, q_.stop, off=0), AR[hp][q_, c * 128:c * 128 + 64],
                                                                Sbf[hp][q_, :], start=False, stop=True),
                                       reads=ARK + [("Sbf", hp)], writes=[bk(hp)])
                        for hp in range(4):
                            P.emit("act", lambda e: e.activation(out=Xbf[hp][:], in_=fb(hp, 64, off=0), func=AF.Copy),
                                   reads=[bk(hp)], writes=[("Xbf", hp)])
                        for hp in range(4):
                            for h2 in range(2):
                                q_ = h2p(h2)
                                P.emit("pe", lambda e: e.matmul(fb(hp, 64, q_.start, q_.stop, off=64), TT[hp][q_, c * 64:(c + 1) * 64],
                                                                Xbf[hp][q_, :], start=True, stop=True),
                                       reads=[("TT", hp), ("Xbf", hp)], writes=[bk(hp)])
                        for hp in range(4):
                            P.emit("dve", lambda e: e.tensor_copy(out=Ubf[hp][:], in_=fb(hp, 64, off=64)),
                                   reads=[bk(hp)], writes=[("Ubf", hp)])
                        for hp in range(4):
                            ARK = [("AR", hp), ("AR2", hp)]
                            for h2 in range(2):
                                q_ = h2p(h2)
                                yo = fb(4 + hp // 2, 64, q_.start, q_.stop, off=(hp % 2) * 256 + c * 64)
                                P.emit("pe", lambda e: e.matmul(yo, AR[hp][q_, c * 128 + 64:c * 128 + 128], Sbf[hp][q_, :], start=True, stop=False),
                                       reads=ARK + [("Sbf", hp)], writes=[bk(4 + hp // 2)])
                                P.emit("pe", lambda e: e.matmul(yo, ABT[hp][q_, c * 128 + 64:c * 128 + 128], Ubf[hp][q_, :], start=False, stop=False),
                                       reads=[("ABT", hp), ("Ubf", hp)], writes=[bk(4 + hp // 2)])
                                P.emit("pe", lambda e: e.matmul(yo, AKT[hp][q_, c * 128 + 64:c * 128 + 128], TM[hp][q_, vsl], start=False, stop=True),
                                       reads=[("AKT", hp), ("TM", hp)], writes=[bk(4 + hp // 2)])
                        for hp in range(4):
                            for h2 in range(2):
                                q_ = h2p(h2)
                                so = fb(hp, 64, q_.start, q_.stop, off=128)
                                P.emit("pe", lambda e: e.matmul(so, TM[hp][q_, ksl], TM[hp][q_, vsl], start=True, stop=False),
                                       reads=[("TM", hp)], writes=[bk(hp)])
                                P.emit("pe", lambda e: e.matmul(so, TM[hp][q_, bsl], Ubf[hp][q_, :], start=False, stop=True),
                                       reads=[("TM", hp), ("Ubf", hp)], writes=[bk(hp)])
                        for hp in range(4):
                            P.emit("dve", lambda e: e.tensor_tensor(out=t1[hp][:], in0=fb(hp, 64, off=128), in1=S32[hp][:], op=ALU.add),
                                   reads=[bk(hp), ("S32", hp)], writes=[("t1", hp)])
                        for hp in range(4):
                            P.emit("act", lambda e: e.mul(out=Sbf[hp][:], in_=t1[hp][:], mul=PC[hp][:, c:c + 1]),
                                   reads=[("t1", hp), ("PC", hp)], writes=[("Sbf", hp)])
                            P.emit("pool", lambda e: e.tensor_scalar(out=S32[hp][:], in0=t1[hp][:], scalar1=PC[hp][:, c:c + 1], scalar2=None, op0=ALU.mult),
                                   reads=[("t1", hp), ("PC", hp)], writes=[("S32", hp)])
                    if not has_out:
                        continue
                    NW = 4 * N

                    def w16(ap):
                        return ap.rearrange("p (g t) -> p g t", t=64)

                    def w44(ap):
                        return ap.rearrange("p (h c t) -> p h c t", h=4, t=64)

                    def b16(ap):
                        return ap.unsqueeze(2).broadcast_to([128, 4 * nch, 64])
                    ykeys = [bk(4), bk(5)]
                    P.emit("act", lambda e: e.activation(out=Ysb[:, 0:NW], in_=PSF[:, 4 * 512:4 * 512 + NW], func=AF.Copy),
                           reads=ykeys, writes=KYsb)
                    P.emit("dve", lambda e: e.reduce_sum(out=st1[:, 0:16], in_=w16(Ysb[:, 0:NW]), axis=AX.X), reads=KYsb, writes=[("st1",)])
                    P.emit("act", lambda e: e.activation(out=ysq[:, 0:NW], in_=Ysb[:, 0:NW], func=AF.Square), reads=KYsb, writes=Kysq)
                    P.emit("dve", lambda e: e.reduce_sum(out=st2[:, 0:16], in_=w16(ysq[:, 0:NW]), axis=AX.X), reads=Kysq, writes=[("st2",)])
                    P.emit("dve", lambda e: e.tensor_scalar(out=mean[:, 0:16], in0=st1[:, 0:16], scalar1=1.0 / 64, scalar2=None, op0=ALU.mult),
                           reads=[("st1",)], writes=[("mean",)])
                    P.emit("dve", lambda e: e.tensor_tensor(out=msq[:, 0:16], in0=mean[:, 0:16], in1=mean[:, 0:16], op=ALU.mult),
                           reads=[("mean",)], writes=[("msq",)])
                    P.emit("dve", lambda e: e.scalar_tensor_tensor(out=var[:, 0:16], in0=st2[:, 0:16], scalar=1.0 / 64, in1=msq[:, 0:16],
                                                                   op0=ALU.mult, op1=ALU.subtract), reads=[("st2",), ("msq",)], writes=[("var",)])
                    P.emit("dve", lambda e: e.tensor_scalar(out=var[:, 0:16], in0=var[:, 0:16], scalar1=GN_EPS, scalar2=None, op0=ALU.add),
                           reads=[("var",)], writes=[("var",)])
                    P.emit("pool", lambda e: e.tensor_tensor(out=rstd[:, 0:16], in0=var[:, 0:16], in1=neghalf[:, 0:16], op=ALU.pow),
                           reads=[("var",)], writes=[("rstd",)])
                    P.emit("dve", lambda e: e.tensor_tensor(out=w16(yc[:, 0:NW]), in0=w16(Ysb[:, 0:NW]), in1=b16(mean[:, 0:16]), op=ALU.subtract),
                           reads=KYsb + [("mean",)], writes=Kyc)
                    V3w = TMw[:, 0:4 * nch * 192].rearrange("p (g a t) -> p g a t", a=3, t=64)[:, :, 2, :]
                    P.emit("dve", lambda e: e.tensor_tensor(out=w16(ysq[:, 0:NW]), in0=V3w, in1=b16(bonw[:, 0:16]), op=ALU.mult),
                           reads=[("TM", 0), ("TM", 1), ("TM", 2), ("TM", 3), ("bon", 0), ("bon", 1), ("bon", 2), ("bon", 3)], writes=Kysq)
                    P.emit("dve", lambda e: e.tensor_tensor(out=w16(yc[:, 0:NW]), in0=w16(yc[:, 0:NW]), in1=b16(rstd[:, 0:16]), op=ALU.mult),
                           reads=Kyc + [("rstd",)], writes=Kyc)
                    lnw4 = lnt[:, 0:256].rearrange("p (h i) -> p h i", h=4).unsqueeze(2).broadcast_to([128, 4, nch, 64])
                    lnb4 = lnt[:, 256:512].rearrange("p (h i) -> p h i", h=4).unsqueeze(2).broadcast_to([128, 4, nch, 64])
                    P.emit("dve", lambda e: e.tensor_tensor(out=w44(yc[:, 0:NW]), in0=w44(yc[:, 0:NW]), in1=lnw4, op=ALU.mult),
                           reads=Kyc + [("lnt",)], writes=Kyc)
                    P.emit("dve", lambda e: e.tensor_tensor(out=w44(ysq[:, 0:NW]), in0=w44(ysq[:, 0:NW]), in1=lnb4, op=ALU.add),
                           reads=Kysq + [("lnt",)], writes=Kysq)
                    P.emit("dve", lambda e: e.tensor_tensor(out=ytm[:, 0:NW], in0=yc[:, 0:NW], in1=ysq[:, 0:NW], op=ALU.add),
                           reads=Kyc + Kysq, writes=Kytm)
                    for hp in range(4):
                        for c in range(nch):
                            for h2 in range(2):
                                o_ = hp * N + c * 64
                                P.emit("pe", lambda e: e.transpose(PSB[h2p(h2), 1024 + o_:1024 + o_ + 64], ytm[h2p(h2), o_:o_ + 64],
                                                                   ident[h2p(h2), h2p(h2)]), reads=Kytm, writes=[("psb", 1)])
                    P.emit("dve", lambda e: e.tensor_tensor(out=yrT[:, :, :].rearrange("p h x -> p (h x)")[:, 0:NW], in0=PSB[:, 1024:1024 + NW], in1=gbfw[:, 0:NW], op=ALU.mult),
                           reads=[("psb", 1), ("gbf", 0), ("gbf", 1), ("gbf", 2), ("gbf", 3)], writes=[("yrT", 0), ("yrT", 1), ("yrT", 2), ("yrT", 3)])
                    def wout(gidx=gidx, N=N):
                        for t in range(N // 128):
                            i = gidx * (N // 128) + t
                            for dh in range(2):
                                for hp in range(4):
                                    P.emit("pe", lambda e: e.matmul(fb(dh), yrT[:, hp, t * 128:(t + 1) * 128], wo_r[:, hp, dh * 512:(dh + 1) * 512],
                                                                    start=(hp == 0), stop=(hp == 3)),
                                           reads=[("yrT", hp), ("wor",)], writes=[bk(dh)])
                            P.emit("dve", lambda e: e.tensor_tensor(out=htile(i), in0=PSF[:, 0:1024], in1=htile(i), op=ALU.add),
                                   reads=[bk(0), bk(1), ("h", i)], writes=[("h", i)])
                    wout_pending.append(wout)
                while wout_pending:
                    wout_pending.pop(0)()
                P.run()

        ffn_phase("f1", f1n, f1g, f1u, f1d, with_meta=True, first=True, last=False)
        mixer_phase()
        ffn_phase("f2", f2n, f2g, f2u, f2d, with_meta=False, first=False, last=True)
    return nc


_NC_CACHE = {}


def _prep_shared(inp):
    f = lambda a: np.ascontiguousarray(np.asarray(a, dtype=np.float32))
    sh = {}
    sh["meta_tokens"] = f(inp["meta_tokens"])
    for nme in ("ffn1_norm", "ffn2_norm", "mix_norm"):
        sh[nme] = f(inp[nme]).reshape(1, D)
    for nme in ("ffn1_gate", "ffn1_up", "ffn2_gate", "ffn2_up"):
        sh[nme] = f(inp[nme]).reshape(D, DFF)
    for nme in ("ffn1_down", "ffn2_down"):
        sh[nme] = f(inp[nme]).reshape(DFF, D)
    sh["w_in"] = f(inp["w_in"]).reshape(D, 3328)
    sh["w_out"] = f(inp["w_out"]).reshape(D, D)
    qn = f(inp["q_norm"]).reshape(64)
    kn = f(inp["k_norm"]).reshape(64)
    sh["qkg"] = f(np.stack([np.tile(qn, 2), np.tile(kn, 2)], axis=1))
    sh["lambda_vecs"] = f(inp["lambda_vecs"]).reshape(1, 256)
    sh["attn_out_norm"] = f(inp["attn_out_norm"]).reshape(1, 128)
    cols = [f(inp["rw_mu"]).reshape(14, 128).T]
    for nme in ("rw_w0", "rw_a0", "rw_k_k", "rw_k_a", "rw_r_k"):
        cols.append(f(inp[nme]).reshape(4, 128).T)
    sh["rwp"] = f(np.concatenate(cols, axis=1))
    sh["rw_w_up"] = f(inp["rw_w_up"]).reshape(64, 512)
    sh["rw_a_up"] = f(inp["rw_a_up"]).reshape(64, 512)
    sh["rw_g_up"] = f(inp["rw_g_up"]).reshape(128, 512)

    def lt(v):
        a = f(v).reshape(4, 2, 64)
        a = np.transpose(a, (1, 0, 2))
        a = np.repeat(a[:, None, :, :], 64, axis=1)
        return a.reshape(128, 256)
    sh["lnwb"] = f(np.concatenate([lt(inp["rw_ln_w"]), lt(inp["rw_ln_b"])], axis=1))
    return sh


def kernel(**inputs):
    x = np.asarray(inputs["x"], dtype=np.float32)
    B = x.shape[0]
    if "nc" not in _NC_CACHE:
        _NC_CACHE["nc"] = build_nc()
    nc = _NC_CACHE["nc"]
    sh = _prep_shared(inputs)
    in_maps = []
    for b in range(B):
        m = dict(sh)
        m["x"] = np.ascontiguousarray(x[b])
        in_maps.append(m)
    res = run_bass_kernel_spmd(nc, in_maps, core_ids=list(range(B)))
    return np.stack([np.asarray(r["y"], dtype=np.float32) for r in res.results], axis=0)
```

```python
import contextlib
import math
import numpy as np
import concourse.bass as bass
import concourse.mybir as mybir
from concourse.bass_utils import run_bass_kernel_spmd

F32 = mybir.dt.float32
BF16 = mybir.dt.bfloat16
AF = mybir.ActivationFunctionType
ALU = mybir.AluOpType
AX = mybir.AxisListType

D = 1024
SEQ = 2048
NMETA = 16
DFF = 2816
NF = DFF // 128
NTC = 2112
MC0 = 48
FC0 = 64
RMS_EPS = 1e-6
GN_EPS = 64e-5
LAM_INIT = 0.8 - 0.6 * math.exp(-0.3 * 0)
C0 = math.exp(-0.5)
SLOPES = [2.0 ** (-8.0 * (h + 1) / 4) for h in range(4)]

_uid = [0]


def _nm(s):
    _uid[0] += 1
    return f"{s}_{_uid[0]}"


class Op:
    __slots__ = ("eng", "idx", "fn", "deps", "dma", "sig", "val", "dsem", "dval")


class _Rec:
    def __init__(self):
        self.call = None

    def __getattr__(self, name):
        def f(*a, **k):
            self.call = (name, a, k)
            return None
        return f


class Plan:
    ENGS = ("pe", "act", "dve", "pool", "sp")

    def __init__(self, nc, st, tag, ndma=6):
        self.nc = nc
        self.q = {e: [] for e in self.ENGS}
        self.sem = {e: st.enter_context(nc.semaphore(_nm(f"s{tag}{e}"))) for e in self.ENGS}
        self.dsems = {e: [st.enter_context(nc.semaphore(_nm(f"d{tag}{e}"))) for _ in range(ndma)] for e in ("sp", "pool")}
        self.ndma = {"sp": 0, "pool": 0}
        self.lastw = {}
        self.readers = {}
        self.dmas = []

    def emit(self, eng, fn, reads=(), writes=(), dma=False):
        op = Op()
        rec = _Rec()
        fn(rec)
        name_, a_, k_ = rec.call
        fn = (lambda e, name_=name_, a_=a_, k_=k_: getattr(e, name_)(*a_, **k_))
        op.eng, op.fn, op.dma, op.sig, op.val = eng, fn, dma, False, 0
        op.idx = len(self.q[eng])
        deps = []
        for k in reads:
            w = self.lastw.get(k)
            if w is not None:
                deps.append(w)
        for k in writes:
            w = self.lastw.get(k)
            if w is not None:
                deps.append(w)
            deps.extend(self.readers.get(k, {}).values())
        best = {}
        dl = []
        for d in deps:
            if d is op:
                continue
            if d.dma:
                if d not in dl:
                    dl.append(d)
            else:
                if d.eng == eng and eng == "pe":
                    continue
                b = best.get(d.eng)
                if b is None or d.idx > b.idx:
                    best[d.eng] = d
        op.deps = dl + list(best.values())
        for d in op.deps:
            d.sig = True
        for k in reads:
            self.readers.setdefault(k, {})[(eng, op.idx if dma else -1)] = op
        for k in writes:
            self.lastw[k] = op
            self.readers[k] = {}
        if dma:
            n = self.ndma[eng]
            self.ndma[eng] = n + 1
            sems = self.dsems[eng]
            op.dsem = sems[n % len(sems)]
            op.dval = 16 * (n // len(sems) + 1)
            self.dmas.append(op)
        self.q[eng].append(op)
        return op

    def finish(self):
        op = Op()
        op.eng, op.fn, op.dma, op.sig, op.val = "sp", (lambda e: e.nop()), False, False, 0
        op.idx = len(self.q["sp"])
        last = {}
        for d in self.dmas:
            last[id(d.dsem)] = d
        op.deps = list(last.values())
        self.q["sp"].append(op)
        for eng in self.ENGS:
            c = 0
            for o in self.q[eng]:
                if o.sig and not o.dma:
                    c += 1
                    o.val = c

    def _replay(self, eng):
        def run(e):
            seen = {}
            for op in self.q[eng]:
                waits = []
                for d in op.deps:
                    if d.dma:
                        waits.append((d.dsem, d.dval))
                    else:
                        waits.append((self.sem[d.eng], d.val))
                if op.dma and op.dval > 16:
                    waits.append((op.dsem, op.dval - 16))
                for s, v in waits:
                    if seen.get(id(s), 0) < v:
                        seen[id(s)] = v
                        e.wait_ge(s, v)
                ins = op.fn(e)
                if op.dma:
                    ins.then_inc(op.dsem, 16)
                elif op.sig:
                    ins.then_inc(self.sem[eng], 1)
        return run

    def run(self):
        self.finish()
        with self.nc.Block() as block:
            block.tensor(self._replay("pe"))
            block.scalar(self._replay("act"))
            block.vector(self._replay("dve"))
            block.gpsimd(self._replay("pool"))
            block.sync(self._replay("sp"))


def bk(b):
    return ("ps", b)


def build_nc():
    nc = bass.Bass("TRN2", target_bir_lowering=False)

    def din(name, shape):
        return nc.dram_tensor(name, list(shape), F32, kind="ExternalInput").ap()

    x = din("x", [SEQ, D])
    meta = din("meta_tokens", [NMETA, D])
    f1n = din("ffn1_norm", [1, D]); f1g = din("ffn1_gate", [D, DFF]); f1u = din("ffn1_up", [D, DFF]); f1d = din("ffn1_down", [DFF, D])
    f2n = din("ffn2_norm", [1, D]); f2g = din("ffn2_gate", [D, DFF]); f2u = din("ffn2_up", [D, DFF]); f2d = din("ffn2_down", [DFF, D])
    mixn = din("mix_norm", [1, D])
    w_in = din("w_in", [D, 3328])
    w_out = din("w_out", [D, D])
    qkg = din("qkg", [128, 2])
    lamv = din("lambda_vecs", [1, 256])
    aon = din("attn_out_norm", [1, 128])
    rwp = din("rwp", [128, 34])
    rw_wup = din("rw_w_up", [64, 512]); rw_aup = din("rw_a_up", [64, 512]); rw_gup = din("rw_g_up", [128, 512])
    lnwb = din("lnwb", [128, 512])
    y = nc.dram_tensor("y", [SEQ, D], F32, kind="ExternalOutput").ap()

    with contextlib.ExitStack() as top:
        def sbT(name, shape, dt):
            return top.enter_context(nc.sbuf_tensor(name, list(shape), dt))
        hres = sbT("hres", [128, 16 * D], F32)
        hmeta = sbT("hmeta", [16, D], F32)
        ident = sbT("ident", [128, 128], BF16)
        BD = sbT("BD", [128, 128], BF16)
        neghalf = sbT("neghalf", [128, 32], F32)
        PSF = top.enter_context(nc.psum_tensor("PSF", [128, 6 * 512], F32))
        PSB = top.enter_context(nc.psum_tensor("PSB", [128, 2 * 1024], BF16))

        def fb(b, n=512, p0=0, p1=128, off=0):
            return PSF[p0:p1, b * 512 + off: b * 512 + off + n]

        def htile(i):
            return hres[:, i * D:(i + 1) * D]

        def emit_norm_T(P, st, tag, srcs, gain_ap, dstT3, dkey, cache):
            n = len(srcs)
            if "gbc" not in cache:
                cache["gbc"] = st.enter_context(nc.sbuf_tensor(_nm(tag + "gbc"), [128, D], F32))
                cache["junk"] = st.enter_context(nc.sbuf_tensor(_nm(tag + "junk"), [128, D], BF16))
                cache["xn"] = [st.enter_context(nc.sbuf_tensor(_nm(tag + "xn"), [128, D], BF16)) for _ in range(2)]
                cache["ss"] = st.enter_context(nc.sbuf_tensor(_nm(tag + "ss"), [128, 32], F32))
                cache["ms"] = st.enter_context(nc.sbuf_tensor(_nm(tag + "ms"), [128, 32], F32))
                cache["rstd"] = st.enter_context(nc.sbuf_tensor(_nm(tag + "rstd"), [128, 32], F32))
                gbc0 = cache["gbc"]
                P.emit("sp", lambda e: e.dma_start(out=gbc0[:], in_=gain_ap[0:1, :].partition_broadcast(128)), writes=[("ngbc",)], dma=True)
            gbc, junk, xn, ss, ms, rstd = cache["gbc"], cache["junk"], cache["xn"], cache["ss"], cache["ms"], cache["rstd"]
            kg, kss, kms, krs = ("ngbc",), ("nss",), ("nms",), ("nrstd",)

            def stats(sub, base):
                P.emit("pool", lambda e: e.memset(ss[:, 0:len(sub)], 1.0), writes=[kss])
                for i, (src, np_, col, skey) in enumerate(sub):
                    P.emit("act", lambda e, src=src, np_=np_, i=i: e.activation(out=junk[:np_, :], in_=src, func=AF.Square,
                                                                                accum_out=ss[:np_, i:i + 1]),
                           reads=[skey, kss], writes=[("nssc", i), ("njunk",)])
                m = len(sub)
                P.emit("dve", lambda e: e.tensor_scalar(out=ms[:, 0:m], in0=ss[:, 0:m], scalar1=1.0 / D, scalar2=RMS_EPS,
                                                        op0=ALU.mult, op1=ALU.add), reads=[kss] + [("nssc", i) for i in range(m)], writes=[kms])
                P.emit("pool", lambda e: e.tensor_tensor(out=rstd[:, 0:m], in0=ms[:, 0:m], in1=neghalf[:, 0:m], op=ALU.pow),
                       reads=[kms], writes=[krs])
                for i, (src, np_, col, skey) in enumerate(sub):
                    xb = xn[(base + i) % 2]
                    kx = ("nxn", (base + i) % 2)
                    pb = (base + i) % 2
                    P.emit("dve", lambda e, src=src, np_=np_, i=i, xb=xb: e.scalar_tensor_tensor(
                        out=xb[:np_, :], in0=src, scalar=rstd[:np_, i:i + 1], in1=gbc[:np_, :], op0=ALU.mult, op1=ALU.mult),
                        reads=[skey, krs, kg], writes=[kx])
                    for k in range(8):
                        P.emit("pe", lambda e, k=k, np_=np_, xb=xb, pb=pb: e.transpose(
                            PSB[:, pb * 1024 + k * 128: pb * 1024 + k * 128 + np_], xb[:np_, k * 128:(k + 1) * 128], ident[:np_, :np_]),
                            reads=[kx], writes=[("psb", pb)])
                    P.emit("act", lambda e, np_=np_, col=col, pb=pb: e.activation(
                        out=dstT3[:, :, col:col + np_],
                        in_=PSB[:, pb * 1024:(pb + 1) * 1024].rearrange("p (k t) -> p k t", k=8)[:, :, 0:np_], func=AF.Copy),
                        reads=[("psb", pb)], writes=[(dkey, col)])
            for b0 in range(0, n, 16):
                stats(srcs[b0:b0 + 16], b0)

        def ffn_phase(tag, gain_ap, wg_ap, wu_ap, wd_ap, with_meta, first, last):
            with contextlib.ExitStack() as st:
                P = Plan(nc, st, tag)

                def sb(name, shape, dt):
                    return st.enter_context(nc.sbuf_tensor(_nm(tag + name), list(shape), dt))
                W = 1040
                xnT = sb("xnT", [128, 8, W], BF16)
                h1T = sb("h1T", [128, NF, W], BF16)
                wd_sb = sb("wd", [128, NF, D], BF16)
                wg_sb = [sb("wg", [128, 8, 128], BF16) for _ in range(3)]
                wu_sb = [sb("wu", [128, 8, 128], BF16) for _ in range(3)]
                sg = [sb("sg", [128, 512], BF16) for _ in range(2)]
                wgv = wg_ap.rearrange("(kc p) f -> p kc f", p=128)
                wuv = wu_ap.rearrange("(kc p) f -> p kc f", p=128)
                wdv = wd_ap.rearrange("(fc p) d -> p fc d", p=128)

                if first:
                    for i in range(16):
                        P.emit("sp", lambda e, i=i: e.dma_start(out=htile(i), in_=x[i * 128:(i + 1) * 128, :]), writes=[("h", i)], dma=True)
                    P.emit("sp", lambda e: e.dma_start(out=hmeta[:], in_=meta[:, :]), writes=[("hm",)], dma=True)
                    P.emit("pool", lambda e: e.memset(ident[:], 0.0), writes=[("ident",)])
                    P.emit("pool", lambda e: e.affine_select(out=ident[:], in_=ident[:], pattern=[[-1, 128]], compare_op=ALU.not_equal,
                                                             fill=1.0, base=0, channel_multiplier=1), writes=[("ident",)])
                    P.emit("pool", lambda e: e.memset(BD[:], 0.0), writes=[("BD",)])
                    P.emit("pool", lambda e: e.memset(BD[0:64, 0:64], 1.0), writes=[("BD",)])
                    P.emit("pool", lambda e: e.memset(BD[64:128, 64:128], 1.0), writes=[("BD",)])
                    P.emit("pool", lambda e: e.memset(neghalf[:], -0.5), writes=[("neghalf",)])

                def wdma(f):
                    s = f % 3
                    P.emit("pool", lambda e: e.dma_start(out=wg_sb[s][:], in_=wgv[:, :, f * 128:(f + 1) * 128]), writes=[("wg", s)], dma=True)
                    P.emit("pool", lambda e: e.dma_start(out=wu_sb[s][:], in_=wuv[:, :, f * 128:(f + 1) * 128]), writes=[("wu", s)], dma=True)

                def wd_dma(j):
                    P.emit("pool", lambda e: e.dma_start(out=wd_sb[:, 2 * j:2 * j + 2, :], in_=wdv[:, 2 * j:2 * j + 2, :]),
                           writes=[("wd", 2 * j), ("wd", 2 * j + 1)], dma=True)

                unit = [0]
                dunit = [0]
                ncache = {}
                for p in range(2):
                    tiles = [(htile(8 * p + t), 128, t * 128, ("h", 8 * p + t)) for t in range(8)]
                    blocks = [(0, 512), (512, 512)]
                    if with_meta and p == 0:
                        tiles.append((hmeta[0:16, :], 16, 1024, ("hm",)))
                        blocks.append((1024, 16))
                    if p == 0:
                        for f in range(3):
                            wdma(f)
                    emit_norm_T(P, st, tag + "n", tiles, gain_ap, xnT, "xnT", ncache)
                    if p == 1:
                        for f in range(3):
                            wdma(f)
                    for f in range(NF):
                        s = f % 3
                        for (c0, N) in blocks:
                            u = unit[0] % 3
                            unit[0] += 1
                            sgi = unit[0] % 2
                            xk = [("xnT", c0 + t * 128) for t in range((N + 127) // 128)]
                            for k in range(8):
                                P.emit("pe", lambda e, k=k, u=u, c0=c0, N=N, s=s: e.matmul(
                                    fb(2 * u, N), wg_sb[s][:, k, :], xnT[:, k, c0:c0 + N], start=(k == 0), stop=(k == 7)),
                                    reads=[("wg", s)] + xk, writes=[bk(2 * u)])
                            for k in range(8):
                                P.emit("pe", lambda e, k=k, u=u, c0=c0, N=N, s=s: e.matmul(
                                    fb(2 * u + 1, N), wu_sb[s][:, k, :], xnT[:, k, c0:c0 + N], start=(k == 0), stop=(k == 7)),
                                    reads=[("wu", s)] + xk, writes=[bk(2 * u + 1)])
                            P.emit("act", lambda e, u=u, N=N, sgi=sgi: e.activation(out=sg[sgi][:, 0:N], in_=fb(2 * u, N), func=AF.Silu),
                                   reads=[bk(2 * u)], writes=[("sg", sgi)])
                            P.emit("dve", lambda e, u=u, N=N, sgi=sgi, f=f, c0=c0: e.tensor_tensor(
                                out=h1T[:, f, c0:c0 + N], in0=fb(2 * u + 1, N), in1=sg[sgi][:, 0:N], op=ALU.mult),
                                reads=[bk(2 * u + 1), ("sg", sgi)], writes=[("h1T", f, c0 + t * 128) for t in range((N + 127) // 128)])
                        if f + 3 < NF:
                            wdma(f + 3)
                        if p == 0 and f < 11:
                            wd_dma(f)
                    for (src, np_, col, skey) in tiles:
                        u = dunit[0] % 3
                        dunit[0] += 1
                        for dh in range(2):
                            for f in range(NF):
                                P.emit("pe", lambda e, u=u, dh=dh, f=f, np_=np_, col=col: e.matmul(
                                    fb(2 * u + dh, 512, 0, np_), h1T[:, f, col:col + np_], wd_sb[:, f, dh * 512:(dh + 1) * 512],
                                    start=(f == 0), stop=(f == NF - 1)),
                                    reads=[("h1T", f, col), ("wd", f)], writes=[bk(2 * u + dh)])
                        P.emit("dve", lambda e, u=u, np_=np_, src=src: e.scalar_tensor_tensor(
                            out=src, in0=PSF[0:np_, 2 * u * 512: 2 * u * 512 + 1024], scalar=0.5, in1=src, op0=ALU.mult, op1=ALU.add),
                            reads=[bk(2 * u), bk(2 * u + 1), skey], writes=[skey])
                        if last:
                            ti = skey[1]
                            P.emit("sp", lambda e, ti=ti: e.dma_start(out=y[ti * 128:(ti + 1) * 128, :], in_=htile(ti)),
                                   reads=[skey], dma=True)
                P.run()

        def mixer_phase():
            with contextlib.ExitStack() as mst:
                uT = mst.enter_context(nc.sbuf_tensor("uT", [128, 8, NTC], BF16))
                with contextlib.ExitStack() as st:
                    P = Plan(nc, st, "m0")
                    P.emit("pool", lambda e: e.memset(uT[:, :, 0:MC0], 0.0), writes=[("uT", 0)])
                    srcs = [(hmeta[0:16, :], 16, MC0, ("hm",))] + [(htile(i), 128, FC0 + 128 * i, ("h", i)) for i in range(16)]
                    emit_norm_T(P, st, "m0n", srcs, mixn, uT, "uT", {})
                    P.run()
                attention_phase(uT)
                rwkv_phase(uT)

        def attention_phase(uT):
            with contextlib.ExitStack() as st:
                P = Plan(nc, st, "at")

                def sb(name, shape, dt):
                    return st.enter_context(nc.sbuf_tensor(_nm("at" + name), list(shape), dt))
                yaT = sb("yaT", [128, 4, SEQ], BF16)
                wo_a = sb("woa", [128, 4, D], BF16)
                wq = [sb("wq", [128, 8, 128], BF16) for _ in range(2)]
                wk = [sb("wk", [128, 8, 128], BF16) for _ in range(2)]
                wv = [sb("wv", [128, 8, 128], BF16) for _ in range(2)]
                qh = [sb("qh", [128, SEQ], BF16) for _ in range(2)]
                kh = [sb("kh", [128, NTC], BF16) for _ in range(2)]
                vt = [sb("vt", [128, 17, 129], BF16) for _ in range(2)]
                EE = [sb("EE", [128, 2048], BF16) for _ in range(2)]
                basef = sb("basef", [128, 2048], F32)
                absb = sb("absb", [128, 128], F32)
                vis = sb("vis", [128, 128], BF16)
                edt = sb("edt", [128, 128], BF16)
                gm = [sb("gm", [16, 128], BF16) for _ in range(2)]
                pt = [sb("pt", [128, 512], BF16) for _ in range(4)]
                ptm = [sb("ptm", [16, 128], BF16) for _ in range(2)]
                sqb = [sb("sqb", [128, 512], BF16) for _ in range(2)]
                msb = [sb("msb", [128, 512], F32) for _ in range(2)]
                rsb = [sb("rsb", [128, 512], F32) for _ in range(2)]
                gqk = sb("gqk", [128, 2], F32)
                gmb = sb("gmb", [16, 64], F32)
                epsb = sb("epsb", [128, 1], F32)
                gq8 = sb("gq8", [128, 1], F32)
                lv = sb("lv", [128, 256], F32)
                lvt = sb("lvt", [128, 128], F32)
                dd = sb("dd", [128, 2], F32)
                ed = sb("ed", [128, 2], F32)
                neglam = sb("neglam", [128, 1], F32)
                ogb = sb("ogb", [128, 128], F32)
                rz = sb("rz", [128, 2], F32)
                s1 = sb("s1", [128, 1], F32)
                y0 = sb("y0", [128, 128], F32)
                yy = sb("yy", [128, 128], F32)
                junk = sb("junk", [128, 128], BF16)
                ssq = sb("ssq", [128, 1], F32)
                msq = sb("msq", [128, 1], F32)
                rsq = sb("rsq", [128, 1], F32)
                yn = [sb("yn", [128, 128], BF16) for _ in range(2)]
                winv = w_in.rearrange("(kc p) f -> p kc f", p=128)
                wov = w_out.rearrange("(kc p) d -> p kc d", p=128)

                P.emit("sp", lambda e: e.dma_start(out=gqk[:], in_=qkg[:, :]), writes=[("gqk",)], dma=True)
                P.emit("sp", lambda e: e.dma_start(out=lv[:], in_=lamv[0:1, :].partition_broadcast(128)), writes=[("lv",)], dma=True)
                P.emit("sp", lambda e: e.dma_start(out=ogb[:], in_=aon[0:1, :].partition_broadcast(128)), writes=[("ogb",)], dma=True)
                P.emit("pool", lambda e: e.dma_start(out=wo_a[:], in_=wov[:, 0:4, :]), writes=[("woa",)], dma=True)
                P.emit("dve", lambda e: e.tensor_scalar(out=gq8[:], in0=gqk[:, 0:1], scalar1=0.125, scalar2=None, op0=ALU.mult),
                       reads=[("gqk",)], writes=[("gq8",)])
                P.emit("dve", lambda e: e.tensor_scalar(out=ogb[:], in0=ogb[:], scalar1=1.0 - LAM_INIT, scalar2=None, op0=ALU.mult),
                       reads=[("ogb",)], writes=[("ogb",)])
                lv4 = lv[:].rearrange("p (a b d) -> p a b d", a=2, b=2)
                P.emit("dve", lambda e: e.tensor_tensor(out=lvt[:].rearrange("p (a d) -> p a d", a=2), in0=lv4[:, :, 0, :], in1=lv4[:, :, 1, :],
                                                        op=ALU.mult), reads=[("lv",)], writes=[("lvt",)])
                P.emit("dve", lambda e: e.reduce_sum(out=dd[:], in_=lvt[:].rearrange("p (a d) -> p a d", a=2), axis=AX.X),
                       reads=[("lvt",)], writes=[("dd",)])
                P.emit("act", lambda e: e.activation(out=ed[:], in_=dd[:], func=AF.Exp), reads=[("dd",)], writes=[("ed",)])
                P.emit("dve", lambda e: e.tensor_tensor(out=s1[:], in0=ed[:, 0:1], in1=ed[:, 1:2], op=ALU.subtract),
                       reads=[("ed",)], writes=[("s1",)])
                P.emit("dve", lambda e: e.tensor_scalar(out=neglam[:], in0=s1[:], scalar1=LAM_INIT, scalar2=-1.0, op0=ALU.add, op1=ALU.mult),
                       reads=[("s1",)], writes=[("neglam",)])
                P.emit("pool", lambda e: e.iota(basef[:], pattern=[[1, 2048]], base=0, channel_multiplier=-1,
                                                allow_small_or_imprecise_dtypes=True), writes=[("basef",)])
                P.emit("dve", lambda e: e.tensor_scalar(out=y0[:], in0=basef[:, 0:128], scalar1=-1.0, scalar2=None, op0=ALU.mult),
                       reads=[("basef",)], writes=[("y0",)])
                P.emit("dve", lambda e: e.tensor_tensor(out=absb[:], in0=basef[:, 0:128], in1=y0[:], op=ALU.max),
                       reads=[("basef",), ("y0",)], writes=[("absb",)])
                P.emit("pool", lambda e: e.memset(vis[:], 1.0), writes=[("vis",)])
                for hh_ in range(4):
                    P.emit("pool", lambda e: e.iota(gmb[:, hh_ * 16:(hh_ + 1) * 16], pattern=[[128, 16]], base=16, channel_multiplier=0,
                                                    allow_small_or_imprecise_dtypes=True), writes=[("gmb",)])
                    P.emit("pool", lambda e: e.tensor_scalar(out=gmb[:, hh_ * 16:(hh_ + 1) * 16], in0=gmb[:, hh_ * 16:(hh_ + 1) * 16],
                                                             scalar1=-SLOPES[hh_], scalar2=None, op0=ALU.mult), writes=[("gmb",)])
                P.emit("pool", lambda e: e.memset(epsb[:], RMS_EPS), writes=[("epsb",)])
                P.emit("pool", lambda e: e.memset(vis[64:128, 0:64], 0.0), writes=[("vis",)])
                for b in range(2):
                    P.emit("pool", lambda e, b=b: e.memset(vt[b][:, :, 128:129], 1.0), writes=[("vt", b)])

                def wdma_head(h):
                    s = h % 2
                    P.emit("pool", lambda e: e.dma_start(out=wq[s][:], in_=winv[:, :, h * 128:(h + 1) * 128]), writes=[("wq", s)], dma=True)
                    P.emit("pool", lambda e: e.dma_start(out=wk[s][:], in_=winv[:, :, 512 + h * 128:512 + (h + 1) * 128]), writes=[("wk", s)], dma=True)
                    P.emit("pool", lambda e: e.dma_start(out=wv[s][:], in_=winv[:, :, 1024 + h * 128:1024 + (h + 1) * 128]), writes=[("wv", s)], dma=True)

                ctr = {"ps": 0, "nb": 0, "o": 0, "pt": 0, "ptm": 0, "yn": 0, "pb": 0}

                def proj_norm(wsb, wkey, c0, N, gain, dst, dkey):
                    b = ctr["ps"] % 3
                    ctr["ps"] += 1
                    nb = ctr["nb"] % 2
                    ctr["nb"] += 1
                    uk = [("uT", c) for c in ([MC0] if c0 == 0 else [])] + [("uT", c0 + t * 128) for t in range(N // 128)] + [("uT", 0)]
                    for k in range(8):
                        P.emit("pe", lambda e, k=k: e.matmul(fb(b, N), wsb[:, k, :], uT[:, k, c0:c0 + N], start=(k == 0), stop=(k == 7)),
                               reads=[wkey] + uk, writes=[bk(b)])
                    P.emit("act", lambda e: e.activation(out=sqb[nb][:, 0:N], in_=fb(b, N), func=AF.Square), reads=[bk(b)], writes=[("sqb", nb)])
                    P.emit("pe", lambda e: e.matmul(fb(5, N), BD[:], sqb[nb][:, 0:N], start=True, stop=True), reads=[("sqb", nb)], writes=[bk(5)])
                    P.emit("act", lambda e: e.activation(out=msb[nb][:, 0:N], in_=fb(5, N), func=AF.Ln, scale=1.0 / 64, bias=epsb[:, 0:1]),
                           reads=[bk(5), ("epsb",)], writes=[("msb", nb)])
                    P.emit("act", lambda e: e.activation(out=rsb[nb][:, 0:N], in_=msb[nb][:, 0:N], func=AF.Exp, scale=-0.5),
                           reads=[("msb", nb)], writes=[("rsb", nb)])
                    P.emit("dve", lambda e: e.scalar_tensor_tensor(out=dst, in0=fb(b, N), scalar=gain, in1=rsb[nb][:, 0:N],
                                                                   op0=ALU.mult, op1=ALU.mult),
                           reads=[bk(b), ("rsb", nb), ("gq8",), ("gqk",)], writes=[dkey])

                wdma_head(0)
                def proj_head(h):
                    s = h % 2
                    slope = SLOPES[h]
                    P.emit("act", lambda e: e.activation(out=EE[s][:], in_=basef[:], func=AF.Exp, scale=-slope), reads=[("basef",)], writes=[("EE", s)])
                    P.emit("act", lambda e: e.activation(out=edt[:], in_=absb[:], func=AF.Exp, scale=-slope), reads=[("absb",)], writes=[("edt",)])
                    P.emit("dve", lambda e: e.tensor_tensor(out=EE[s][:, 0:128], in0=edt[:], in1=vis[:], op=ALU.mult),
                           reads=[("edt",), ("vis",)], writes=[("EE", s)])
                    for g in range(4):
                        proj_norm(wq[s], ("wq", s), FC0 + 512 * g, 512, gq8[:, 0:1], qh[s][:, g * 512:(g + 1) * 512], ("qh", s, g))
                        yield
                    proj_norm(wk[s], ("wk", s), 0, 64, gqk[:, 1:2], kh[s][:, 0:64], ("kh", s, 0))
                    yield
                    for g in range(4):
                        proj_norm(wk[s], ("wk", s), FC0 + 512 * g, 512, gqk[:, 1:2], kh[s][:, FC0 + 512 * g:FC0 + 512 * (g + 1)], ("kh", s, g + 1))
                        yield
                    for t0 in range(0, 16, 4):
                        b = ctr["ps"] % 3
                        ctr["ps"] += 1
                        for t in range(4):
                            col = FC0 + (t0 + t) * 128
                            for k in range(8):
                                P.emit("pe", lambda e, k=k, t=t, col=col: e.matmul(fb(b, 128, off=t * 128), uT[:, k, col:col + 128], wv[s][:, k, :],
                                                                                   start=(k == 0), stop=(k == 7)),
                                       reads=[("wv", s), ("uT", col)], writes=[bk(b)])
                        P.emit("act", lambda e, t0=t0, b=b: e.activation(out=vt[s][:, t0:t0 + 4, 0:128],
                                                                         in_=fb(b).rearrange("p (t c) -> p t c", t=4), func=AF.Copy),
                               reads=[bk(b)], writes=[("vt", s)])
                        yield
                    b = ctr["ps"] % 3
                    ctr["ps"] += 1
                    for k in range(8):
                        P.emit("pe", lambda e, k=k, b=b: e.matmul(fb(b, 128, 0, 16), uT[:, k, MC0:MC0 + 16], wv[s][:, k, :], start=(k == 0), stop=(k == 7)),
                               reads=[("wv", s), ("uT", MC0)], writes=[bk(b)])
                    P.emit("act", lambda e, b=b: e.activation(out=vt[s][0:16, 16, 0:128], in_=fb(b, 128, 0, 16), func=AF.Copy),
                           reads=[bk(b)], writes=[("vt", s)])
                    yield

                for _ in proj_head(0):
                    pass
                for h in range(4):
                    s = h % 2
                    slope = SLOPES[h]
                    nxt = None
                    if h + 1 < 4:
                        wdma_head(h + 1)
                        nxt = proj_head(h + 1)
                    pend = []
                    late = []

                    def flush(n=2):
                        while len(pend) > n:
                            pend.pop(0)()

                    for qi in range(16):
                        gi = ctr["ptm"] % 2
                        P.emit("act", lambda e: e.activation(out=gm[gi][:], in_=basef[0:16, 0:128], func=AF.Exp, scale=-slope,
                                                             bias=gmb[0:16, h * 16 + qi:h * 16 + qi + 1]), reads=[("basef",), ("gmb",)], writes=[("gm", gi)])
                        ob = 3 + (ctr["o"] % 2)
                        ctr["o"] += 1
                        qk_ = ("qh", s, qi // 4)
                        for c in range(2):
                            cp = slice(c * 64, (c + 1) * 64)
                            nkt = qi + 1
                            first = [True]
                            for b0 in range(0, nkt, 4):
                                jjs = list(range(b0, min(b0 + 4, nkt)))
                                nbk = len(jjs)
                                b = ctr["ps"] % 3
                                ctr["ps"] += 1
                                pi = ctr["pt"] % 4
                                ctr["pt"] += 1
                                for m, jj in enumerate(jjs):
                                    j = qi - jj
                                    P.emit("pe", lambda e: e.matmul(
                                        fb(b, 128, off=m * 128), kh[s][cp, FC0 + 128 * j:FC0 + 128 * (j + 1)], qh[s][cp, qi * 128:(qi + 1) * 128],
                                        start=True, stop=True),
                                        reads=[("kh", s, 1 + j // 4), qk_], writes=[bk(b)])
                                P.emit("act", lambda e: e.activation(out=pt[pi][:, 0:nbk * 128], in_=fb(b, nbk * 128), func=AF.Exp),
                                       reads=[bk(b)], writes=[("pt", pi)])
                                P.emit("dve", lambda e: e.tensor_tensor(
                                    out=pt[pi][:, 0:nbk * 128], in0=pt[pi][:, 0:nbk * 128], in1=EE[s][:, b0 * 128:(b0 + nbk) * 128], op=ALU.mult),
                                    reads=[("pt", pi), ("EE", s)], writes=[("pt", pi)])

                                def pv(jjs=jjs, pi=pi, first=first, ob=ob, c=c, qi=qi):
                                    for m, jj in enumerate(jjs):
                                        j = qi - jj
                                        P.emit("pe", lambda e: e.matmul(
                                            fb(ob, 129, off=c * 129), pt[pi][:, m * 128:(m + 1) * 128], vt[s][:, j, :], start=first[0], stop=False),
                                            reads=[("pt", pi), ("vt", s)], writes=[bk(ob)])
                                        first[0] = False
                                flush()
                                pend.append(pv)
                            b = ctr["ps"] % 3
                            ctr["ps"] += 1
                            mi = ctr["ptm"] % 2
                            ctr["ptm"] += 1
                            P.emit("pe", lambda e: e.matmul(fb(b, 128, 0, 16), kh[s][cp, MC0:MC0 + 16], qh[s][cp, qi * 128:(qi + 1) * 128],
                                                            start=True, stop=True), reads=[("kh", s, 0), qk_], writes=[bk(b)])
                            P.emit("act", lambda e: e.activation(out=ptm[mi][:], in_=fb(b, 128, 0, 16), func=AF.Exp),
                                   reads=[bk(b)], writes=[("ptm", mi)])
                            P.emit("dve", lambda e: e.tensor_tensor(out=ptm[mi][:], in0=ptm[mi][:], in1=gm[gi][:], op=ALU.mult),
                                   reads=[("ptm", mi), ("gm", gi)], writes=[("ptm", mi)])

                            def pvm(mi=mi, ob=ob, c=c):
                                P.emit("pe", lambda e: e.matmul(fb(ob, 129, off=c * 129), ptm[mi][0:16, :], vt[s][0:16, 16, :], start=False, stop=True),
                                       reads=[("ptm", mi), ("vt", s)], writes=[bk(ob)])
                            flush()
                            pend.append(pvm)

                        def finalize(ob=ob, qi=qi):
                            O3 = fb(ob, 258).rearrange("p (c e) -> p c e", c=2)
                            P.emit("dve", lambda e: e.reciprocal(out=rz[:].rearrange("p (c o) -> p c o", o=1), in_=O3[:, :, 128:129]),
                                   reads=[bk(ob)], writes=[("rz",)])
                            P.emit("dve", lambda e: e.tensor_tensor(out=s1[:], in0=rz[:, 1:2], in1=neglam[:], op=ALU.mult),
                                   reads=[("rz",), ("neglam",)], writes=[("s1",)])
                            P.emit("dve", lambda e: e.tensor_scalar(out=y0[:], in0=O3[:, 0, 0:128], scalar1=rz[:, 0:1], scalar2=None, op0=ALU.mult),
                                   reads=[bk(ob), ("rz",)], writes=[("y0",)])
                            P.emit("dve", lambda e: e.scalar_tensor_tensor(out=yy[:], in0=O3[:, 1, 0:128], scalar=s1[:, 0:1], in1=y0[:],
                                                                           op0=ALU.mult, op1=ALU.add),
                                   reads=[bk(ob), ("s1",), ("y0",)], writes=[("yy",)])
                            P.emit("act", lambda e: e.activation(out=junk[:], in_=yy[:], func=AF.Square, accum_out=ssq[:]),
                                   reads=[("yy",)], writes=[("junk",), ("ssq",)])
                            P.emit("dve", lambda e: e.tensor_scalar(out=msq[:], in0=ssq[:], scalar1=1.0 / 128, scalar2=RMS_EPS, op0=ALU.mult, op1=ALU.add),
                                   reads=[("ssq",)], writes=[("msq",)])
                            P.emit("pool", lambda e: e.tensor_tensor(out=rsq[:], in0=msq[:], in1=neghalf[:, 0:1], op=ALU.pow),
                                   reads=[("msq",)], writes=[("rsq",)])
                            yi = ctr["yn"] % 2
                            ctr["yn"] += 1
                            P.emit("dve", lambda e: e.scalar_tensor_tensor(out=yn[yi][:], in0=yy[:], scalar=rsq[:, 0:1], in1=ogb[:],
                                                                           op0=ALU.mult, op1=ALU.mult),
                                   reads=[("yy",), ("rsq",), ("ogb",)], writes=[("yn", yi)])
                            pb = ctr["pb"] % 2
                            ctr["pb"] += 1

                            def tr(yi=yi, pb=pb, qi=qi):
                                P.emit("pe", lambda e: e.transpose(PSB[:, pb * 1024:pb * 1024 + 128], yn[yi][:], ident[:]),
                                       reads=[("yn", yi)], writes=[("psb", pb)])
                                P.emit("act", lambda e: e.activation(out=yaT[:, h, qi * 128:(qi + 1) * 128], in_=PSB[:, pb * 1024:pb * 1024 + 128],
                                                                     func=AF.Copy), reads=[("psb", pb)], writes=[("yaT", qi)])
                            late.append(tr)
                        while late:
                            late.pop(0)()
                        pend.append(finalize)
                        if nxt is not None:
                            next(nxt, None)
                    flush(0)
                    while late:
                        late.pop(0)()
                    if nxt is not None:
                        for _ in nxt:
                            pass
                for i in range(16):
                    u = i % 2
                    for dh in range(2):
                        for kc in range(4):
                            P.emit("pe", lambda e, u=u, dh=dh, kc=kc, i=i: e.matmul(
                                fb(2 * u + dh), yaT[:, kc, i * 128:(i + 1) * 128], wo_a[:, kc, dh * 512:(dh + 1) * 512], start=(kc == 0), stop=(kc == 3)),
                                reads=[("yaT", i), ("woa",)], writes=[bk(2 * u + dh)])
                    P.emit("dve", lambda e, u=u, i=i: e.tensor_tensor(out=htile(i), in0=PSF[:, 2 * u * 512:2 * u * 512 + 1024], in1=htile(i), op=ALU.add),
                           reads=[bk(2 * u), bk(2 * u + 1), ("h", i)], writes=[("h", i)])
                P.run()

        def rwkv_phase(uT):
            with contextlib.ExitStack() as st:
                P = Plan(nc, st, "rw")

                def sb(name, shape, dt):
                    return st.enter_context(nc.sbuf_tensor(_nm("rw" + name), list(shape), dt))
                NM = 256
                NCH = 4
                wo_r = sb("wor", [128, 4, D], BF16)
                wrw = [sb("wrw", [128, 8, 128], BF16) for _ in range(4)]
                waup = sb("waup", [128, 512], BF16)
                gup = sb("gup", [128, 512], BF16)
                pp = sb("pp", [128, 34], F32)
                lnt = sb("lnt", [128, 512], F32)
                prevl = sb("prevl", [128, 14], F32)
                pbuf = [sb("pbuf", [128, NM + 1], F32) for _ in range(2)]
                dtmp = [sb("dtmp", [128, NM], F32) for _ in range(2)]
                lp = sb("lp", [128, NM], F32)
                lo24 = sb("lo24", [128, NM], BF16)
                sxg = sb("sxg", [128, NM], BF16)
                k32 = [sb("k32", [128, NM], F32) for _ in range(2)]
                r32 = [sb("r32", [128, NM], F32) for _ in range(2)]
                vbf = [sb("vbf", [128, NM], BF16) for _ in range(2)]
                asig = [sb("asig", [128, NM], F32) for _ in range(2)]
                kkr = [sb("kkr", [128, NM], F32) for _ in range(2)]
                sqk = [sb("sqk", [128, NM], BF16) for _ in range(2)]
                ssm = [sb("ssm", [128, NM], F32) for _ in range(2)]
                rn = [sb("rn", [128, NM], F32) for _ in range(2)]
                kmod = [sb("kmod", [128, NM], F32) for _ in range(2)]
                kb = [sb("kb", [128, NM], F32) for _ in range(2)]
                bp = [sb("bp", [128, NM], BF16) for _ in range(2)]
                mask = sb("mask", [128, NM], F32)
                lmask = sb("lmask", [128, 64], F32)
                umask = sb("umask", [128, 128], F32)
                I2 = sb("I2", [128, 64], BF16)
                ones = sb("ones", [128, 1], BF16)
                ln20 = sb("ln20", [128, 1], F32)
                AR = [sb("AR", [128, NCH * 128], BF16) for _ in range(4)]
                TMw = sb("TMw", [128, 4 * NCH * 192], BF16)
                TM = [TMw[:, hp_ * NCH * 192:(hp_ + 1) * NCH * 192] for hp_ in range(4)]
                WA = NCH * 128
                ABTp = [sb("ABTp", [128, 2 * WA], BF16) for _ in range(2)]
                AKTp = [sb("AKTp", [128, 2 * WA], BF16) for _ in range(2)]
                TTp = [sb("TTp", [128, 2 * NM], BF16) for _ in range(2)]
                ABT = [ABTp[hp_ // 2][:, (hp_ % 2) * WA:(hp_ % 2 + 1) * WA] for hp_ in range(4)]
                AKT = [AKTp[hp_ // 2][:, (hp_ % 2) * WA:(hp_ % 2 + 1) * WA] for hp_ in range(4)]
                TT = [TTp[hp_ // 2][:, (hp_ % 2) * NM:(hp_ % 2 + 1) * NM] for hp_ in range(4)]
                PPw = [sb("PPw", [128, 2 * NM], BF16) for _ in range(2)]
                QQw = [sb("QQw", [128, 2 * NM], BF16) for _ in range(2)]
                gbfw = sb("gbfw", [128, 4 * NM], BF16)
                gbf = [gbfw[:, hp_ * NM:(hp_ + 1) * NM] for hp_ in range(4)]
                bonw = sb("bonw", [128, 4 * NCH], F32)
                bon = [bonw[:, hp_ * NCH:(hp_ + 1) * NCH] for hp_ in range(4)]
                PC = [sb("PC", [128, 8], F32) for _ in range(4)]
                S32 = [sb("S32", [128, 64], F32) for _ in range(4)]
                Sbf = [sb("Sbf", [128, 64], BF16) for _ in range(4)]
                Xbf = [sb("Xbf", [128, 64], BF16) for _ in range(4)]
                Ubf = [sb("Ubf", [128, 64], BF16) for _ in range(4)]
                t1 = [sb("t1", [128, 64], F32) for _ in range(4)]
                Ysb = sb("Ysb", [128, 4 * NM], F32)
                ysq = sb("ysq", [128, 4 * NM], F32)
                yc = sb("yc", [128, 4 * NM], F32)
                sgw = [Ysb[:, 0 * NM:1 * NM], Ysb[:, 1 * NM:2 * NM]]
                cs = [Ysb[:, 2 * NM:3 * NM], Ysb[:, 3 * NM:4 * NM]]
                csm = [ysq[:, 0 * NM:1 * NM], ysq[:, 1 * NM:2 * NM]]
                epos = [ysq[:, 2 * NM:3 * NM], ysq[:, 3 * NM:4 * NM]]
                eneg = [yc[:, 0 * NM:1 * NM], yc[:, 1 * NM:2 * NM]]
                eprev = [yc[:, 2 * NM:3 * NM], yc[:, 3 * NM:4 * NM]]
                KYsb = [("Ysb",), ("sgw", 0), ("sgw", 1), ("cs", 0), ("cs", 1)]
                Kysq = [("ysq",), ("csm", 0), ("csm", 1), ("epos", 0), ("epos", 1)]
                Kyc = [("yc",), ("eneg", 0), ("eneg", 1), ("eprev", 0), ("eprev", 1)]
                Kytm = [("ytm",), ("BT", 0), ("BT", 1), ("KT", 0), ("KT", 1)]
                st1 = sb("st1", [128, 16], F32)
                st2 = sb("st2", [128, 16], F32)
                mean = sb("mean", [128, 16], F32)
                msq = sb("msq", [128, 16], F32)
                var = sb("var", [128, 16], F32)
                rstd = sb("rstd", [128, 16], F32)
                ytm = sb("ytm", [128, 4 * NM], BF16)
                BT = [ytm[:, 0 * NM:1 * NM], ytm[:, 1 * NM:2 * NM]]
                KT = [ytm[:, 2 * NM:3 * NM], ytm[:, 3 * NM:4 * NM]]
                yrT = sb("yrT", [128, 4, NM], BF16)
                winv = w_in.rearrange("(kc p) f -> p kc f", p=128)
                wov = w_out.rearrange("(kc p) d -> p kc d", p=128)
                print("rwkv sbuf bytes remaining", nc.sbuf_bytes_remaining)

                P.emit("sp", lambda e: e.dma_start(out=pp[:], in_=rwp[:, :]), writes=[("pp",)], dma=True)
                P.emit("sp", lambda e: e.dma_start(out=lnt[:], in_=lnwb[:, :]), writes=[("lnt",)], dma=True)
                P.emit("pool", lambda e: e.dma_start(out=waup[0:64, :], in_=rw_wup[:, :]), writes=[("waup", 0)], dma=True)
                P.emit("pool", lambda e: e.dma_start(out=waup[64:128, :], in_=rw_aup[:, :]), writes=[("waup", 1)], dma=True)
                P.emit("pool", lambda e: e.dma_start(out=gup[:], in_=rw_gup[:, :]), writes=[("gup",)], dma=True)
                P.emit("pool", lambda e: e.dma_start(out=wo_r[:], in_=wov[:, 4:8, :]), writes=[("wor",)], dma=True)
                P.emit("pool", lambda e: e.memset(prevl[:], 0.0), writes=[("prevl", i) for i in range(14)])
                P.emit("pool", lambda e: e.memset(mask[:], 1.0), writes=[("mask",)])
                P.emit("pool", lambda e: e.memset(mask[:].rearrange("p (c t) -> p c t", t=64)[:, :, 0:1], 0.0), writes=[("mask",)])
                P.emit("pool", lambda e: e.memset(ones[:], 1.0), writes=[("ones",)])
                P.emit("pool", lambda e: e.memset(ln20[:], 20.0 * math.log(2.0)), writes=[("ln20",)])
                P.emit("pool", lambda e: e.memset(lmask[:], 1.0), writes=[("lmask",)])
                P.emit("pool", lambda e: e.memset(umask[:], 1.0), writes=[("umask",)])
                P.emit("pool", lambda e: e.memset(I2[:], 0.0), writes=[("I2",)])
                for hh in range(2):
                    hp_ = slice(hh * 64, (hh + 1) * 64)
                    P.emit("pool", lambda e, hp_=hp_: e.affine_select(out=lmask[hp_, :], in_=lmask[hp_, :], pattern=[[-1, 64]], compare_op=ALU.is_gt,
                                                                      fill=0.0, base=0, channel_multiplier=1), writes=[("lmask",)])
                    P.emit("pool", lambda e, hp_=hp_: e.affine_select(out=umask[hp_, 0:64], in_=umask[hp_, 0:64], pattern=[[1, 64]], compare_op=ALU.is_gt,
                                                                      fill=0.0, base=0, channel_multiplier=-1), writes=[("umask",)])
                    P.emit("pool", lambda e, hp_=hp_: e.affine_select(out=umask[hp_, 64:128], in_=umask[hp_, 64:128], pattern=[[1, 64]], compare_op=ALU.is_ge,
                                                                      fill=0.0, base=0, channel_multiplier=-1), writes=[("umask",)])
                    P.emit("pool", lambda e, hp_=hp_: e.affine_select(out=I2[hp_, :], in_=I2[hp_, :], pattern=[[-1, 64]], compare_op=ALU.not_equal,
                                                                      fill=1.0, base=0, channel_multiplier=1), writes=[("I2",)])
                for hp in range(4):
                    P.emit("pool", lambda e, hp=hp: e.memset(S32[hp][:], 0.0), writes=[("S32", hp)])
                    P.emit("pool", lambda e, hp=hp: e.memset(Sbf[hp][:], 0.0), writes=[("Sbf", hp)])

                MU0, W00, A00, KK0, KA0, RK0 = 0, 14, 18, 22, 26, 30
                wctr = [0]
                pctr = [0]

                def h2p(h2):
                    return slice(h2 * 64, (h2 + 1) * 64)

                def v3(ap):
                    return ap.rearrange("p (c t) -> p c t", t=64)

                RING = 4
                rowseq = []
                for _g in range(1 + SEQ // NM):
                    rowseq += [12, 13, 4, 0, 5, 1, 8, 9, 6, 2, 7, 3, 10, 11]
                pfc = [0]

                def prefetch_upto(n):
                    while pfc[0] < min(n, len(rowseq)):
                        i_ = pfc[0]
                        s_ = i_ % RING
                        col_ = 1536 + rowseq[i_] * 128
                        P.emit("pool", lambda e: e.dma_start(out=wrw[s_][:], in_=winv[:, :, col_:col_ + 128]), writes=[("wrw", s_)], dma=True)
                        pfc[0] += 1

                def proj_row(wc, c0, N, out_ap, out_key):
                    i_ = wctr[0]
                    wctr[0] += 1
                    assert rowseq[i_] == wc, (i_, wc, rowseq[i_])
                    s = i_ % RING
                    prefetch_upto(i_ + RING)
                    b = pctr[0] % 2
                    pctr[0] += 1
                    for k in range(8):
                        P.emit("pe", lambda e, k=k: e.matmul(fb(b, N), wrw[s][:, k, :], uT[:, k, c0:c0 + N], start=(k == 0), stop=(k == 7)),
                               reads=[("wrw", s)], writes=[bk(b)])
                    pb_ = pbuf[b]
                    P.emit("act", lambda e: e.activation(out=pb_[:, 1:N + 1], in_=fb(b, N), func=AF.Copy), reads=[bk(b)], writes=[("pbuf", b)])
                    P.emit("act", lambda e: e.activation(out=pb_[:, 0:1], in_=prevl[:, wc:wc + 1], func=AF.Copy), reads=[("prevl", wc)], writes=[("pbuf", b)])
                    P.emit("act", lambda e: e.activation(out=prevl[:, wc:wc + 1], in_=pb_[:, N:N + 1], func=AF.Copy), reads=[("pbuf", b)], writes=[("prevl", wc)])
                    P.emit("dve", lambda e: e.tensor_tensor(out=dtmp[b][:, 0:N], in0=pb_[:, 0:N], in1=pb_[:, 1:N + 1], op=ALU.subtract),
                           reads=[("pbuf", b)], writes=[("dtmp", b)])
                    P.emit("dve", lambda e: e.scalar_tensor_tensor(
                        out=out_ap, in0=dtmp[b][:, 0:N], scalar=pp[:, MU0 + wc:MU0 + wc + 1], in1=pb_[:, 1:N + 1], op0=ALU.mult, op1=ALU.add),
                        reads=[("dtmp", b), ("pbuf", b), ("pp",)], writes=[out_key])

                groups = [(0, 64, 1, False, 0)] + [(FC0 + NM * g, NM, NCH, True, g) for g in range(SEQ // NM)]
                wout_pending = []
                for (c0, N, nch, has_out, gidx) in groups:
                    def bc(ap8):
                        return ap8.unsqueeze(2).broadcast_to([128, nch, 64])
                    proj_row(12, c0, N, lp[:, 0:N], ("lp",))
                    P.emit("act", lambda e: e.activation(out=lo24[0:64, 0:N], in_=lp[0:64, 0:N], func=AF.Tanh), reads=[("lp",)], writes=[("lo24", 0)])
                    P.emit("act", lambda e: e.activation(out=lo24[64:128, 0:N], in_=lp[64:128, 0:N], func=AF.Copy), reads=[("lp",)], writes=[("lo24", 1)])
                    proj_row(13, c0, N, lp[:, 0:N], ("lp",))
                    P.emit("act", lambda e: e.activation(out=sxg[:, 0:N], in_=lp[:, 0:N], func=AF.Sigmoid), reads=[("lp",)], writes=[("sxg",)])

                    def chain(hp):
                        par = hp % 2
                        B0, B1 = (2, 3) if par == 0 else (4, 5)
                        hc = slice(hp * 128, (hp + 1) * 128)
                        proj_row(4 + hp, c0, N, k32[par][:, 0:N], ("k32", par))
                        proj_row(0 + hp, c0, N, r32[par][:, 0:N], ("r32", par))
                        yield
                        proj_row(8 + hp, c0, N, vbf[par][:, 0:N], ("vbf", par))
                        P.emit("pe", lambda e: e.matmul(fb(B0, N), waup[0:64, hc], lo24[0:64, 0:N], start=True, stop=True),
                               reads=[("waup", 0), ("lo24", 0)], writes=[bk(B0)])
                        yield
                        P.emit("act", lambda e: e.activation(out=sgw[par][:, 0:N], in_=fb(B0, N), func=AF.Sigmoid, bias=pp[:, W00 + hp:W00 + hp + 1]),
                               reads=[bk(B0), ("pp",)], writes=[("sgw", par)])
                        P.emit("pe", lambda e: e.matmul(fb(B1, N), waup[64:128, hc], lo24[64:128, 0:N], start=True, stop=True),
                               reads=[("waup", 1), ("lo24", 1)], writes=[bk(B1)])
                        yield
                        P.emit("act", lambda e: e.activation(out=asig[par][:, 0:N], in_=fb(B1, N), func=AF.Sigmoid, bias=pp[:, A00 + hp:A00 + hp + 1]),
                               reads=[bk(B1), ("pp",)], writes=[("asig", par)])
                        P.emit("dve", lambda e: e.tensor_tensor_scan(out=cs[par][:, 0:N], data0=mask[:, 0:N], data1=sgw[par][:, 0:N], initial=0.0,
                                                                     op0=ALU.mult, op1=ALU.add), reads=[("mask",), ("sgw", par)], writes=[("cs", par)])
                        yield
                        P.emit("dve", lambda e: e.tensor_tensor(out=csm[par][:, 0:N], in0=cs[par][:, 0:N], in1=sgw[par][:, 0:N], op=ALU.subtract),
                               reads=[("cs", par), ("sgw", par)], writes=[("csm", par)])
                        P.emit("act", lambda e: e.activation(out=epos[par][:, 0:N], in_=cs[par][:, 0:N], func=AF.Exp, scale=-C0), reads=[("cs", par)], writes=[("epos", par)])
                        yield
                        P.emit("act", lambda e: e.activation(out=eneg[par][:, 0:N], in_=cs[par][:, 0:N], func=AF.Exp, scale=C0), reads=[("cs", par)], writes=[("eneg", par)])
                        P.emit("act", lambda e: e.activation(out=eprev[par][:, 0:N], in_=csm[par][:, 0:N], func=AF.Exp, scale=-C0), reads=[("csm", par)], writes=[("eprev", par)])
                        yield
                        P.emit("pool", lambda e, hp=hp: e.tensor_copy(out=PC[hp][:, 0:nch], in_=v3(epos[par][:, 0:N])[:, :, 63]),
                               reads=[("epos", par)], writes=[("PC", hp)])
                        P.emit("dve", lambda e: e.tensor_scalar(out=kkr[par][:, 0:N], in0=k32[par][:, 0:N], scalar1=pp[:, KK0 + hp:KK0 + hp + 1], scalar2=None, op0=ALU.mult),
                               reads=[("k32", par), ("pp",)], writes=[("kkr", par)])
                        yield
                        P.emit("act", lambda e: e.activation(out=sqk[par][:, 0:N], in_=kkr[par][:, 0:N], func=AF.Square), reads=[("kkr", par)], writes=[("sqk", par)])
                        P.emit("pe", lambda e: e.matmul(fb(B0, N), BD[:], sqk[par][:, 0:N], start=True, stop=True), reads=[("sqk", par)], writes=[bk(B0)])
                        yield
                        P.emit("dve", lambda e: e.tensor_scalar(out=ssm[par][:, 0:N], in0=fb(B0, N), scalar1=1e-24, scalar2=None, op0=ALU.max),
                               reads=[bk(B0)], writes=[("ssm", par)])
                        P.emit("act", lambda e: e.activation(out=ssm[par][:, 0:N], in_=ssm[par][:, 0:N], func=AF.Ln, scale=float(2.0 ** 40)),
                               reads=[("ssm", par)], writes=[("ssm", par)])
                        yield
                        P.emit("act", lambda e: e.activation(out=rn[par][:, 0:N], in_=ssm[par][:, 0:N], func=AF.Exp, scale=-0.5, bias=ln20[:, 0:1]),
                               reads=[("ssm", par), ("ln20",)], writes=[("rn", par)])
                        P.emit("dve", lambda e: e.tensor_tensor(out=kkr[par][:, 0:N], in0=kkr[par][:, 0:N], in1=rn[par][:, 0:N], op=ALU.mult),
                               reads=[("kkr", par), ("rn", par)], writes=[("kkr", par)])
                        yield
                        P.emit("dve", lambda e: e.tensor_scalar(out=kmod[par][:, 0:N], in0=asig[par][:, 0:N], scalar1=-1.0, scalar2=pp[:, KA0 + hp:KA0 + hp + 1],
                                                                op0=ALU.add, op1=ALU.mult), reads=[("asig", par), ("pp",)], writes=[("kmod", par)])
                        P.emit("dve", lambda e: e.scalar_tensor_tensor(out=kmod[par][:, 0:N], in0=kmod[par][:, 0:N], scalar=1.0, in1=k32[par][:, 0:N], op0=ALU.add, op1=ALU.mult),
                               reads=[("kmod", par), ("k32", par)], writes=[("kmod", par)])
                        yield
                        AR3 = AR[hp][:, 0:nch * 128].rearrange("p (c a t) -> p c a t", a=2, t=64)
                        P.emit("dve", lambda e, AR3=AR3: e.scalar_tensor_tensor(
                            out=AR3[:, :, 0, :], in0=v3(kkr[par][:, 0:N]), scalar=-1.0, in1=v3(eprev[par][:, 0:N]), op0=ALU.mult, op1=ALU.mult),
                            reads=[("kkr", par), ("eprev", par)], writes=[("AR", hp)])
                        P.emit("pool", lambda e, AR3=AR3: e.tensor_tensor(out=AR3[:, :, 1, :], in0=v3(r32[par][:, 0:N]), in1=v3(epos[par][:, 0:N]), op=ALU.mult),
                               reads=[("r32", par), ("epos", par), ("AR", hp)], writes=[("AR2", hp)])
                        yield
                        P.emit("pool", lambda e: e.tensor_tensor(out=kb[par][:, 0:N], in0=kkr[par][:, 0:N], in1=asig[par][:, 0:N], op=ALU.mult),
                               reads=[("kkr", par), ("asig", par)], writes=[("kb", par)])
                        P.emit("pool", lambda e: e.tensor_tensor(out=BT[par][:, 0:N], in0=kb[par][:, 0:N], in1=eneg[par][:, 0:N], op=ALU.mult),
                               reads=[("kb", par), ("eneg", par)], writes=[("BT", par)])
                        yield
                        P.emit("pool", lambda e: e.tensor_tensor(out=KT[par][:, 0:N], in0=kmod[par][:, 0:N], in1=eneg[par][:, 0:N], op=ALU.mult),
                               reads=[("kmod", par), ("eneg", par)], writes=[("KT", par)])
                        P.emit("dve", lambda e: e.scalar_tensor_tensor(out=bp[par][:, 0:N], in0=r32[par][:, 0:N], scalar=pp[:, RK0 + hp:RK0 + hp + 1], in1=kmod[par][:, 0:N],
                                                                       op0=ALU.mult, op1=ALU.mult), reads=[("r32", par), ("kmod", par), ("pp",)], writes=[("bp", par)])
                        yield
                        P.emit("pe", lambda e: e.matmul(fb(B1, N), gup[:, hc], sxg[:, 0:N], start=True, stop=True), reads=[("gup",), ("sxg",)], writes=[bk(B1)])
                        P.emit("act", lambda e, hp=hp: e.activation(out=gbf[hp][:, 0:N], in_=fb(B1, N), func=AF.Copy), reads=[bk(B1)], writes=[("gbf", hp)])
                        yield
                        ARK = [("AR", hp), ("AR2", hp)]
                        for c in range(nch):
                            for si, (srcT, skey) in enumerate(((BT[par], ("BT", par)), (KT[par], ("KT", par)), (vbf[par], ("vbf", par)))):
                                for h2 in range(2):
                                    P.emit("pe", lambda e, c=c, si=si, srcT=srcT, h2=h2: e.transpose(
                                        PSB[h2p(h2), par * 1024 + (c * 3 + si) * 64:par * 1024 + (c * 3 + si) * 64 + 64],
                                        srcT[h2p(h2), c * 64:(c + 1) * 64], ident[h2p(h2), h2p(h2)]),
                                        reads=[skey], writes=[("psb", par)])
                        P.emit("act", lambda e, hp=hp: e.activation(out=TM[hp][:, 0:nch * 192], in_=PSB[:, par * 1024:par * 1024 + nch * 192], func=AF.Copy),
                               reads=[("psb", par)], writes=[("TM", hp)])
                        for c in range(nch):
                            for h2 in range(2):
                                P.emit("pe", lambda e, c=c, h2=h2: e.matmul(fb(B1, 1, h2 * 64, h2 * 64 + 64, off=256 + c), bp[par][h2p(h2), c * 64:(c + 1) * 64],
                                                                            ones[h2p(h2), 0:1], start=True, stop=True),
                                       reads=[("bp", par), ("ones",)], writes=[bk(B1)])
                        P.emit("act", lambda e, hp=hp: e.activation(out=bon[hp][:, 0:nch], in_=fb(B1, nch, off=256), func=AF.Copy),
                               reads=[bk(B1)], writes=[("bon", hp)])
                        yield

                    def G3(t, W, w, sz):
                        if w == W:
                            return t[:, 0:2 * W].rearrange("p (g s) -> p g s", s=sz)
                        assert w == sz
                        return t[:, 0:2 * W].rearrange("p (h x) -> p h x", h=2)[:, :, 0:w]

                    def pair_stage(pr):
                        ng = 2 * nch
                        hps = (2 * pr, 2 * pr + 1)
                        ARKS = [[("AR", hp), ("AR2", hp)] for hp in hps]
                        for hl, hp in enumerate(hps):
                            for c in range(nch):
                                for h2 in range(2):
                                    q_ = h2p(h2)
                                    P.emit("pe", lambda e: e.matmul(fb(2, 64, q_.start, q_.stop, off=hl * NM + c * 64), AR[hp][q_, c * 128:c * 128 + 64],
                                                                    BT[hl][q_, c * 64:(c + 1) * 64], start=True, stop=True),
                                           reads=ARKS[hl] + [("BT", hl)], writes=[bk(2)])
                        for hl, hp in enumerate(hps):
                            for c in range(nch):
                                for h2 in range(2):
                                    q_ = h2p(h2)
                                    P.emit("pe", lambda e: e.matmul(fb(3 + hl, 128, q_.start, q_.stop, off=c * 128), BT[hl][q_, c * 64:(c + 1) * 64],
                                                                    AR[hp][q_, c * 128:(c + 1) * 128], start=True, stop=True),
                                           reads=ARKS[hl] + [("BT", hl)], writes=[bk(3 + hl)])
                        P.emit("dve", lambda e: e.tensor_tensor(out=G3(PPw[0], NM, N, 64), in0=G3(PSF[:, 2 * 512:3 * 512], NM, N, 64),
                                                                in1=lmask[:].unsqueeze(1).broadcast_to([128, ng, 64]), op=ALU.mult),
                               reads=[bk(2), ("lmask",)], writes=[("PPw", 0)])
                        for hl, hp in enumerate(hps):
                            P.emit("dve", lambda e: e.tensor_tensor(out=ABT[hp][:, 0:nch * 128].rearrange("p (c s) -> p c s", s=128),
                                                                    in0=fb(3 + hl, nch * 128).rearrange("p (c s) -> p c s", s=128),
                                                                    in1=umask[:].unsqueeze(1).broadcast_to([128, nch, 128]), op=ALU.mult),
                                   reads=[bk(3 + hl), ("umask",)], writes=[("ABT", hp)])
                        for hl, hp in enumerate(hps):
                            for c in range(nch):
                                for h2 in range(2):
                                    q_ = h2p(h2)
                                    P.emit("pe", lambda e: e.matmul(fb(4 + hl, 128, q_.start, q_.stop, off=c * 128), KT[hl][q_, c * 64:(c + 1) * 64],
                                                                    AR[hp][q_, c * 128:(c + 1) * 128], start=True, stop=True),
                                           reads=ARKS[hl] + [("KT", hl)], writes=[bk(4 + hl)])
                        for hl, hp in enumerate(hps):
                            P.emit("dve", lambda e: e.tensor_tensor(out=AKT[hp][:, 0:nch * 128].rearrange("p (c s) -> p c s", s=128),
                                                                    in0=fb(4 + hl, nch * 128).rearrange("p (c s) -> p c s", s=128),
                                                                    in1=umask[:].unsqueeze(1).broadcast_to([128, nch, 128]), op=ALU.mult),
                                   reads=[bk(4 + hl), ("umask",)], writes=[("AKT", hp)])
                        Q0v = G3(ABTp[pr], WA, nch * 128, 128)[:, :, 0:64]
                        TTk = [("TT", hps[0]), ("TT", hps[1])]
                        P.emit("dve", lambda e: e.tensor_tensor(out=G3(TTp[pr], NM, N, 64), in0=Q0v,
                                                                in1=I2[:].unsqueeze(1).broadcast_to([128, ng, 64]), op=ALU.add),
                               reads=[("ABT", hps[0]), ("ABT", hps[1]), ("I2",)], writes=TTk)
                        P.emit("pool", lambda e: e.tensor_copy(out=G3(QQw[0], NM, N, 64), in_=Q0v),
                               reads=[("ABT", hps[0]), ("ABT", hps[1])], writes=[("QQw", 0)])
                        for lev in range(1, 6):
                            pi_, po_ = (lev - 1) % 2, lev % 2
                            Pp, Qp, Pn, Qn = PPw[pi_], QQw[pi_], PPw[po_], QQw[po_]
                            for hl in range(2):
                                for c in range(nch):
                                    for h2 in range(2):
                                        q_ = h2p(h2)
                                        o_ = hl * NM + c * 64
                                        P.emit("pe", lambda e: e.matmul(fb(2, 64, q_.start, q_.stop, off=o_), Qp[q_, o_:o_ + 64], Pp[q_, o_:o_ + 64],
                                                                        start=True, stop=True),
                                               reads=[("PPw", pi_), ("QQw", pi_)], writes=[bk(2)])
                            P.emit("act", lambda e: e.activation(out=G3(Pn, NM, N, 64), in_=G3(PSF[:, 2 * 512:3 * 512], NM, N, 64), func=AF.Copy),
                                   reads=[bk(2)], writes=[("PPw", po_)])
                            if lev < 5:
                                for hl in range(2):
                                    for c in range(nch):
                                        for h2 in range(2):
                                            q_ = h2p(h2)
                                            o_ = hl * NM + c * 64
                                            P.emit("pe", lambda e: e.matmul(fb(3, 64, q_.start, q_.stop, off=o_), Pp[q_, o_:o_ + 64], Qp[q_, o_:o_ + 64],
                                                                            start=True, stop=True),
                                                   reads=[("PPw", pi_), ("QQw", pi_)], writes=[bk(3)])
                                P.emit("dve", lambda e: e.tensor_copy(out=G3(Qn, NM, N, 64), in_=G3(PSF[:, 3 * 512:4 * 512], NM, N, 64)),
                                       reads=[bk(3)], writes=[("QQw", po_)])
                            for hl in range(2):
                                for c in range(nch):
                                    for h2 in range(2):
                                        q_ = h2p(h2)
                                        o_ = hl * NM + c * 64
                                        P.emit("pe", lambda e: e.matmul(fb(4, 64, q_.start, q_.stop, off=o_), Pn[q_, o_:o_ + 64], TTp[pr][q_, o_:o_ + 64],
                                                                        start=True, stop=True),
                                               reads=[("PPw", po_)] + TTk, writes=[bk(4)])
                            P.emit("dve", lambda e: e.tensor_tensor(out=G3(TTp[pr], NM, N, 64), in0=G3(PSF[:, 4 * 512:5 * 512], NM, N, 64),
                                                                    in1=G3(TTp[pr], NM, N, 64), op=ALU.add),
                                   reads=[bk(4)] + TTk, writes=TTk)

                    def lockstep(gens, hook_round=None):
                        gens = list(gens)
                        rnd = 0
                        while gens:
                            for g_ in list(gens):
                                try:
                                    next(g_)
                                except StopIteration:
                                    gens.remove(g_)
                            rnd += 1
                            if hook_round is not None and rnd == hook_round:
                                while wout_pending:
                                    wout_pending.pop(0)()
                    lockstep([chain(0), chain(1)], hook_round=3)
                    while wout_pending:
                        wout_pending.pop(0)()
                    pair_stage(0)
                    lockstep([chain(2), chain(3)])
                    pair_stage(1)
                    for c in range(nch):
                        vsl = slice((c * 3 + 2) * 64, (c * 3 + 3) * 64)
                        ksl = slice((c * 3 + 1) * 64, (c * 3 + 2) * 64)
                        bsl = slice((c * 3 + 0) * 64, (c * 3 + 1) * 64)
                        for hp in range(4):
                            ARK = [("AR", hp), ("AR2", hp)]
                            for h2 in range(2):
                                q_ = h2p(h2)
                                P.emit("pe", lambda e: e.matmul(fb(hp, 64, q_.start, q_.stop, off=0), AKT[hp][q_, c * 128:c * 128 + 64],
                                                                TM[hp][q_, vsl], start=True, stop=False),
                                       reads=[("AKT", hp), ("TM", hp)], writes=[bk(hp)])
                                P.emit("pe", lambda e: e.matmul(fb(hp, 64, q_.start, q_.stop, off=0), AR[hp][q_, c * 128:c * 128 + 64],
                                                                Sbf[hp][q_, :], start=False, stop=True),
                                       reads=ARK + [("Sbf", hp)], writes=[bk(hp)])
                        for hp in range(4):
                            P.emit("act", lambda e: e.activation(out=Xbf[hp][:], in_=fb(hp, 64, off=0), func=AF.Copy),
                                   reads=[bk(hp)], writes=[("Xbf", hp)])
                        for hp in range(4):
                            for h2 in range(2):
                                q_ = h2p(h2)
                                P.emit("pe", lambda e: e.matmul(fb(hp, 64, q_.start, q_.stop, off=64), TT[hp][q_, c * 64:(c + 1) * 64],
                                                                Xbf[hp][q_, :], start=True, stop=True),
                                       reads=[("TT", hp), ("Xbf", hp)], writes=[bk(hp)])
                        for hp in range(4):
                            P.emit("dve", lambda e: e.tensor_copy(out=Ubf[hp][:], in_=fb(hp, 64, off=64)),
                                   reads=[bk(hp)], writes=[("Ubf", hp)])
                        for hp in range(4):
                            ARK = [("AR", hp), ("AR2", hp)]
                            for h2 in range(2):
                                q_ = h2p(h2)
                                yo = fb(4 + hp // 2, 64, q_.start, q_.stop, off=(hp % 2) * 256 + c * 64)
                                P.emit("pe", lambda e: e.matmul(yo, AR[hp][q_, c * 128 + 64:c * 128 + 128], Sbf[hp][q_, :], start=True, stop=False),
                                       reads=ARK + [("Sbf", hp)], writes=[bk(4 + hp // 2)])
                                P.emit("pe", lambda e: e.matmul(yo, ABT[hp][q_, c * 128 + 64:c * 128 + 128], Ubf[hp][q_, :], start=False, stop=False),
                                       reads=[("ABT", hp), ("Ubf", hp)], writes=[bk(4 + hp // 2)])
                                P.emit("pe", lambda e: e.matmul(yo, AKT[hp][q_, c * 128 + 64:c * 128 + 128], TM[hp][q_, vsl], start=False, stop=True),
                                       reads=[("AKT", hp), ("TM", hp)], writes=[bk(4 + hp // 2)])
                        for hp in range(4):
                            for h2 in range(2):
                                q_ = h2p(h2)
                                so = fb(hp, 64, q_.start, q_.stop, off=128)
                                P.emit("pe", lambda e: e.matmul(so, TM[hp][q_, ksl], TM[hp][q_, vsl], start=True, stop=False),
                                       reads=[("TM", hp)], writes=[bk(hp)])
                                P.emit("pe", lambda e: e.matmul(so, TM[hp][q_, bsl], Ubf[hp][q_, :], start=False, stop=True),
                                       reads=[("TM", hp), ("Ubf", hp)], writes=[bk(hp)])
                        for hp in range(4):
                            P.emit("dve", lambda e: e.tensor_tensor(out=t1[hp][:], in0=fb(hp, 64, off=128), in1=S32[hp][:], op=ALU.add),
                                   reads=[bk(hp), ("S32", hp)], writes=[("t1", hp)])
                        for hp in range(4):
                            P.emit("act", lambda e: e.mul(out=Sbf[hp][:], in_=t1[hp][:], mul=PC[hp][:, c:c + 1]),
                                   reads=[("t1", hp), ("PC", hp)], writes=[("Sbf", hp)])
                            P.emit("dve", lambda e: e.tensor_scalar(out=S32[hp][:], in0=t1[hp][:], scalar1=PC[hp][:, c:c + 1], scalar2=None, op0=ALU.mult),
                                   reads=[("t1", hp), ("PC", hp)], writes=[("S32", hp)])
                    if not has_out:
                        continue
                    NW = 4 * N

                    def w16(ap):
                        return ap.rearrange("p (g t) -> p g t", t=64)

                    def w44(ap):
                        return ap.rearrange("p (h c t) -> p h c t", h=4, t=64)

                    def b16(ap):
                        return ap.unsqueeze(2).broadcast_to([128, 4 * nch, 64])
                    ykeys = [bk(4), bk(5)]
                    P.emit("act", lambda e: e.activation(out=Ysb[:, 0:NW], in_=PSF[:, 4 * 512:4 * 512 + NW], func=AF.Copy),
                           reads=ykeys, writes=KYsb)
                    P.emit("dve", lambda e: e.reduce_sum(out=st1[:, 0:16], in_=w16(Ysb[:, 0:NW]), axis=AX.X), reads=KYsb, writes=[("st1",)])
                    P.emit("act", lambda e: e.activation(out=ysq[:, 0:NW], in_=Ysb[:, 0:NW], func=AF.Square), reads=KYsb, writes=Kysq)
                    P.emit("dve", lambda e: e.reduce_sum(out=st2[:, 0:16], in_=w16(ysq[:, 0:NW]), axis=AX.X), reads=Kysq, writes=[("st2",)])
                    P.emit("dve", lambda e: e.tensor_scalar(out=mean[:, 0:16], in0=st1[:, 0:16], scalar1=1.0 / 64, scalar2=None, op0=ALU.mult),
                           reads=[("st1",)], writes=[("mean",)])
                    P.emit("dve", lambda e: e.tensor_tensor(out=msq[:, 0:16], in0=mean[:, 0:16], in1=mean[:, 0:16], op=ALU.mult),
                           reads=[("mean",)], writes=[("msq",)])
                    P.emit("dve", lambda e: e.scalar_tensor_tensor(out=var[:, 0:16], in0=st2[:, 0:16], scalar=1.0 / 64, in1=msq[:, 0:16],
                                                                   op0=ALU.mult, op1=ALU.subtract), reads=[("st2",), ("msq",)], writes=[("var",)])
                    P.emit("dve", lambda e: e.tensor_scalar(out=var[:, 0:16], in0=var[:, 0:16], scalar1=GN_EPS, scalar2=None, op0=ALU.add),
                           reads=[("var",)], writes=[("var",)])
                    P.emit("pool", lambda e: e.tensor_tensor(out=rstd[:, 0:16], in0=var[:, 0:16], in1=neghalf[:, 0:16], op=ALU.pow),
                           reads=[("var",)], writes=[("rstd",)])
                    P.emit("dve", lambda e: e.tensor_tensor(out=w16(yc[:, 0:NW]), in0=w16(Ysb[:, 0:NW]), in1=b16(mean[:, 0:16]), op=ALU.subtract),
                           reads=KYsb + [("mean",)], writes=Kyc)
                    V3w = TMw[:, 0:4 * nch * 192].rearrange("p (g a t) -> p g a t", a=3, t=64)[:, :, 2, :]
                    P.emit("dve", lambda e: e.tensor_tensor(out=w16(ysq[:, 0:NW]), in0=V3w, in1=b16(bonw[:, 0:16]), op=ALU.mult),
                           reads=[("TM", 0), ("TM", 1), ("TM", 2), ("TM", 3), ("bon", 0), ("bon", 1), ("bon", 2), ("bon", 3)], writes=Kysq)
                    P.emit("dve", lambda e: e.tensor_tensor(out=w16(yc[:, 0:NW]), in0=w16(yc[:, 0:NW]), in1=b16(rstd[:, 0:16]), op=ALU.mult),
                           reads=Kyc + [("rstd",)], writes=Kyc)
                    lnw4 = lnt[:, 0:256].rearrange("p (h i) -> p h i", h=4).unsqueeze(2).broadcast_to([128, 4, nch, 64])
                    lnb4 = lnt[:, 256:512].rearrange("p (h i) -> p h i", h=4).unsqueeze(2).broadcast_to([128, 4, nch, 64])
                    P.emit("dve", lambda e: e.tensor_tensor(out=w44(yc[:, 0:NW]), in0=w44(yc[:, 0:NW]), in1=lnw4, op=ALU.mult),
                           reads=Kyc + [("lnt",)], writes=Kyc)
                    P.emit("dve", lambda e: e.tensor_tensor(out=w44(ysq[:, 0:NW]), in0=w44(ysq[:, 0:NW]), in1=lnb4, op=ALU.add),
                           reads=Kysq + [("lnt",)], writes=Kysq)
                    P.emit("dve", lambda e: e.tensor_tensor(out=ytm[:, 0:NW], in0=yc[:, 0:NW], in1=ysq[:, 0:NW], op=ALU.add),
                           reads=Kyc + Kysq, writes=Kytm)
                    for hp in range(4):
                        for c in range(nch):
                            for h2 in range(2):
                                o_ = hp * N + c * 64
                                P.emit("pe", lambda e: e.transpose(PSB[h2p(h2), 1024 + o_:1024 + o_ + 64], ytm[h2p(h2), o_:o_ + 64],
                                                                   ident[h2p(h2), h2p(h2)]), reads=Kytm, writes=[("psb", 1)])
                    P.emit("dve", lambda e: e.tensor_tensor(out=yrT[:, :, :].rearrange("p h x -> p (h x)")[:, 0:NW], in0=PSB[:, 1024:1024 + NW], in1=gbfw[:, 0:NW], op=ALU.mult),
                           reads=[("psb", 1), ("gbf", 0), ("gbf", 1), ("gbf", 2), ("gbf", 3)], writes=[("yrT", 0), ("yrT", 1), ("yrT", 2), ("yrT", 3)])
                    def wout(gidx=gidx, N=N):
                        for t in range(N // 128):
                            i = gidx * (N // 128) + t
                            for dh in range(2):
                                for hp in range(4):
                                    P.emit("pe", lambda e: e.matmul(fb(dh), yrT[:, hp, t * 128:(t + 1) * 128], wo_r[:, hp, dh * 512:(dh + 1) * 512],
                                                                    start=(hp == 0), stop=(hp == 3)),
                                           reads=[("yrT", hp), ("wor",)], writes=[bk(dh)])
                            P.emit("dve", lambda e: e.tensor_tensor(out=htile(i), in0=PSF[:, 0:1024], in1=htile(i), op=ALU.add),
                                   reads=[bk(0), bk(1), ("h", i)], writes=[("h", i)])
                    wout_pending.append(wout)
                while wout_pending:
                    wout_pending.pop(0)()
                P.run()

        ffn_phase("f1", f1n, f1g, f1u, f1d, with_meta=True, first=True, last=False)
        mixer_phase()
        ffn_phase("f2", f2n, f2g, f2u, f2d, with_meta=False, first=False, last=True)
    return nc


_NC_CACHE = {}


def _prep_shared(inp):
    f = lambda a: np.ascontiguousarray(np.asarray(a, dtype=np.float32))
    sh = {}
    sh["meta_tokens"] = f(inp["meta_tokens"])
    for nme in ("ffn1_norm", "ffn2_norm", "mix_norm"):
        sh[nme] = f(inp[nme]).reshape(1, D)
    for nme in ("ffn1_gate", "ffn1_up", "ffn2_gate", "ffn2_up"):
        sh[nme] = f(inp[nme]).reshape(D, DFF)
    for nme in ("ffn1_down", "ffn2_down"):
        sh[nme] = f(inp[nme]).reshape(DFF, D)
    sh["w_in"] = f(inp["w_in"]).reshape(D, 3328)
    sh["w_out"] = f(inp["w_out"]).reshape(D, D)
    qn = f(inp["q_norm"]).reshape(64)
    kn = f(inp["k_norm"]).reshape(64)
    sh["qkg"] = f(np.stack([np.tile(qn, 2), np.tile(kn, 2)], axis=1))
    sh["lambda_vecs"] = f(inp["lambda_vecs"]).reshape(1, 256)
    sh["attn_out_norm"] = f(inp["attn_out_norm"]).reshape(1, 128)
    cols = [f(inp["rw_mu"]).reshape(14, 128).T]
    for nme in ("rw_w0", "rw_a0", "rw_k_k", "rw_k_a", "rw_r_k"):
        cols.append(f(inp[nme]).reshape(4, 128).T)
    sh["rwp"] = f(np.concatenate(cols, axis=1))
    sh["rw_w_up"] = f(inp["rw_w_up"]).reshape(64, 512)
    sh["rw_a_up"] = f(inp["rw_a_up"]).reshape(64, 512)
    sh["rw_g_up"] = f(inp["rw_g_up"]).reshape(128, 512)

    def lt(v):
        a = f(v).reshape(4, 2, 64)
        a = np.transpose(a, (1, 0, 2))
        a = np.repeat(a[:, None, :, :], 64, axis=1)
        return a.reshape(128, 256)
    sh["lnwb"] = f(np.concatenate([lt(inp["rw_ln_w"]), lt(inp["rw_ln_b"])], axis=1))
    return sh


def kernel(**inputs):
    x = np.asarray(inputs["x"], dtype=np.float32)
    B = x.shape[0]
    if "nc" not in _NC_CACHE:
        _NC_CACHE["nc"] = build_nc()
    nc = _NC_CACHE["nc"]
    sh = _prep_shared(inputs)
    in_maps = []
    for b in range(B):
        m = dict(sh)
        m["x"] = np.ascontiguousarray(x[b])
        in_maps.append(m)
    res = run_bass_kernel_spmd(nc, in_maps, core_ids=list(range(B)))
    return np.stack([np.asarray(r["y"], dtype=np.float32) for r in res.results], axis=0)
```

```python
import contextlib
import math
import numpy as np
import concourse.bass as bass
import concourse.mybir as mybir
from concourse.bass_utils import run_bass_kernel_spmd

F32 = mybir.dt.float32
BF16 = mybir.dt.bfloat16
AF = mybir.ActivationFunctionType
ALU = mybir.AluOpType
AX = mybir.AxisListType

D = 1024
SEQ = 2048
NMETA = 16
DFF = 2816
NF = DFF // 128
NTC = 2112
MC0 = 48
FC0 = 64
RMS_EPS = 1e-6
GN_EPS = 64e-5
LAM_INIT = 0.8 - 0.6 * math.exp(-0.3 * 0)
C0 = math.exp(-0.5)
SLOPES = [2.0 ** (-8.0 * (h + 1) / 4) for h in range(4)]

_uid = [0]


def _nm(s):
    _uid[0] += 1
    return f"{s}_{_uid[0]}"


class Op:
    __slots__ = ("eng", "idx", "fn", "deps", "dma", "sig", "val", "dsem", "dval")


class _Rec:
    def __init__(self):
        self.call = None

    def __getattr__(self, name):
        def f(*a, **k):
            self.call = (name, a, k)
            return None
        return f


class Plan:
    ENGS = ("pe", "act", "dve", "pool", "sp")

    def __init__(self, nc, st, tag, ndma=6):
        self.nc = nc
        self.q = {e: [] for e in self.ENGS}
        self.sem = {e: st.enter_context(nc.semaphore(_nm(f"s{tag}{e}"))) for e in self.ENGS}
        self.dsems = {e: [st.enter_context(nc.semaphore(_nm(f"d{tag}{e}"))) for _ in range(ndma)] for e in ("sp", "pool")}
        self.ndma = {"sp": 0, "pool": 0}
        self.lastw = {}
        self.readers = {}
        self.dmas = []

    def emit(self, eng, fn, reads=(), writes=(), dma=False):
        op = Op()
        rec = _Rec()
        fn(rec)
        name_, a_, k_ = rec.call
        fn = (lambda e, name_=name_, a_=a_, k_=k_: getattr(e, name_)(*a_, **k_))
        op.eng, op.fn, op.dma, op.sig, op.val = eng, fn, dma, False, 0
        op.idx = len(self.q[eng])
        deps = []
        for k in reads:
            w = self.lastw.get(k)
            if w is not None:
                deps.append(w)
        for k in writes:
            w = self.lastw.get(k)
            if w is not None:
                deps.append(w)
            deps.extend(self.readers.get(k, {}).values())
        best = {}
        dl = []
        for d in deps:
            if d is op:
                continue
            if d.dma:
                if d not in dl:
                    dl.append(d)
            else:
                if d.eng == eng and eng == "pe":
                    continue
                b = best.get(d.eng)
                if b is None or d.idx > b.idx:
                    best[d.eng] = d
        op.deps = dl + list(best.values())
        for d in op.deps:
            d.sig = True
        for k in reads:
            self.readers.setdefault(k, {})[(eng, op.idx if dma else -1)] = op
        for k in writes:
            self.lastw[k] = op
            self.readers[k] = {}
        if dma:
            n = self.ndma[eng]
            self.ndma[eng] = n + 1
            sems = self.dsems[eng]
            op.dsem = sems[n % len(sems)]
            op.dval = 16 * (n // len(sems) + 1)
            self.dmas.append(op)
        self.q[eng].append(op)
        return op

    def finish(self):
        op = Op()
        op.eng, op.fn, op.dma, op.sig, op.val = "sp", (lambda e: e.nop()), False, False, 0
        op.idx = len(self.q["sp"])
        last = {}
        for d in self.dmas:
            last[id(d.dsem)] = d
        op.deps = list(last.values())
        self.q["sp"].append(op)
        for eng in self.ENGS:
            c = 0
            for o in self.q[eng]:
                if o.sig and not o.dma:
                    c += 1
                    o.val = c

    def _replay(self, eng):
        def run(e):
            seen = {}
            for op in self.q[eng]:
                waits = []
                for d in op.deps:
                    if d.dma:
                        waits.append((d.dsem, d.dval))
                    else:
                        waits.append((self.sem[d.eng], d.val))
                if op.dma and op.dval > 16:
                    waits.append((op.dsem, op.dval - 16))
                for s, v in waits:
                    if seen.get(id(s), 0) < v:
                        seen[id(s)] = v
                        e.wait_ge(s, v)
                ins = op.fn(e)
                if op.dma:
                    ins.then_inc(op.dsem, 16)
                elif op.sig:
                    ins.then_inc(self.sem[eng], 1)
        return run

    def run(self):
        self.finish()
        with self.nc.Block() as block:
            block.tensor(self._replay("pe"))
            block.scalar(self._replay("act"))
            block.vector(self._replay("dve"))
            block.gpsimd(self._replay("pool"))
            block.sync(self._replay("sp"))


def bk(b):
    return ("ps", b)


def build_nc():
    nc = bass.Bass("TRN2", target_bir_lowering=False)

    def din(name, shape):
        return nc.dram_tensor(name, list(shape), F32, kind="ExternalInput").ap()

    x = din("x", [SEQ, D])
    meta = din("meta_tokens", [NMETA, D])
    f1n = din("ffn1_norm", [1, D]); f1g = din("ffn1_gate", [D, DFF]); f1u = din("ffn1_up", [D, DFF]); f1d = din("ffn1_down", [DFF, D])
    f2n = din("ffn2_norm", [1, D]); f2g = din("ffn2_gate", [D, DFF]); f2u = din("ffn2_up", [D, DFF]); f2d = din("ffn2_down", [DFF, D])
    mixn = din("mix_norm", [1, D])
    w_in = din("w_in", [D, 3328])
    w_out = din("w_out", [D, D])
    qkg = din("qkg", [128, 2])
    lamv = din("lambda_vecs", [1, 256])
    aon = din("attn_out_norm", [1, 128])
    rwp = din("rwp", [128, 34])
    rw_wup = din("rw_w_up", [64, 512]); rw_aup = din("rw_a_up", [64, 512]); rw_gup = din("rw_g_up", [128, 512])
    lnwb = din("lnwb", [128, 512])
    y = nc.dram_tensor("y", [SEQ, D], F32, kind="ExternalOutput").ap()

    with contextlib.ExitStack() as top:
        def sbT(name, shape, dt):
            return top.enter_context(nc.sbuf_tensor(name, list(shape), dt))
        hres = sbT("hres", [128, 16 * D], F32)
        hmeta = sbT("hmeta", [16, D], F32)
        ident = sbT("ident", [128, 128], BF16)
        BD = sbT("BD", [128, 128], BF16)
        neghalf = sbT("neghalf", [128, 32], F32)
        PSF = top.enter_context(nc.psum_tensor("PSF", [128, 6 * 512], F32))
        PSB = top.enter_context(nc.psum_tensor("PSB", [128, 2 * 1024], BF16))

        def fb(b, n=512, p0=0, p1=128, off=0):
            return PSF[p0:p1, b * 512 + off: b * 512 + off + n]

        def htile(i):
            return hres[:, i * D:(i + 1) * D]

        def emit_norm_T(P, st, tag, srcs, gain_ap, dstT3, dkey, cache):
            n = len(srcs)
            if "gbc" not in cache:
                cache["gbc"] = st.enter_context(nc.sbuf_tensor(_nm(tag + "gbc"), [128, D], F32))
                cache["junk"] = st.enter_context(nc.sbuf_tensor(_nm(tag + "junk"), [128, D], BF16))
                cache["xn"] = [st.enter_context(nc.sbuf_tensor(_nm(tag + "xn"), [128, D], BF16)) for _ in range(2)]
                cache["ss"] = st.enter_context(nc.sbuf_tensor(_nm(tag + "ss"), [128, 32], F32))
                cache["ms"] = st.enter_context(nc.sbuf_tensor(_nm(tag + "ms"), [128, 32], F32))
                cache["rstd"] = st.enter_context(nc.sbuf_tensor(_nm(tag + "rstd"), [128, 32], F32))
                gbc0 = cache["gbc"]
                P.emit("sp", lambda e: e.dma_start(out=gbc0[:], in_=gain_ap[0:1, :].partition_broadcast(128)), writes=[("ngbc",)], dma=True)
            gbc, junk, xn, ss, ms, rstd = cache["gbc"], cache["junk"], cache["xn"], cache["ss"], cache["ms"], cache["rstd"]
            kg, kss, kms, krs = ("ngbc",), ("nss",), ("nms",), ("nrstd",)

            def stats(sub, base):
                P.emit("pool", lambda e: e.memset(ss[:, 0:len(sub)], 1.0), writes=[kss])
                for i, (src, np_, col, skey) in enumerate(sub):
                    P.emit("act", lambda e, src=src, np_=np_, i=i: e.activation(out=junk[:np_, :], in_=src, func=AF.Square,
                                                                                accum_out=ss[:np_, i:i + 1]),
                           reads=[skey, kss], writes=[("nssc", i), ("njunk",)])
                m = len(sub)
                P.emit("dve", lambda e: e.tensor_scalar(out=ms[:, 0:m], in0=ss[:, 0:m], scalar1=1.0 / D, scalar2=RMS_EPS,
                                                        op0=ALU.mult, op1=ALU.add), reads=[kss] + [("nssc", i) for i in range(m)], writes=[kms])
                P.emit("pool", lambda e: e.tensor_tensor(out=rstd[:, 0:m], in0=ms[:, 0:m], in1=neghalf[:, 0:m], op=ALU.pow),
                       reads=[kms], writes=[krs])
                for i, (src, np_, col, skey) in enumerate(sub):
                    xb = xn[(base + i) % 2]
                    kx = ("nxn", (base + i) % 2)
                    pb = (base + i) % 2
                    P.emit("dve", lambda e, src=src, np_=np_, i=i, xb=xb: e.scalar_tensor_tensor(
                        out=xb[:np_, :], in0=src, scalar=rstd[:np_, i:i + 1], in1=gbc[:np_, :], op0=ALU.mult, op1=ALU.mult),
                        reads=[skey, krs, kg], writes=[kx])
                    for k in range(8):
                        P.emit("pe", lambda e, k=k, np_=np_, xb=xb, pb=pb: e.transpose(
                            PSB[:, pb * 1024 + k * 128: pb * 1024 + k * 128 + np_], xb[:np_, k * 128:(k + 1) * 128], ident[:np_, :np_]),
                            reads=[kx], writes=[("psb", pb)])
                    P.emit("act", lambda e, np_=np_, col=col, pb=pb: e.activation(
                        out=dstT3[:, :, col:col + np_],
                        in_=PSB[:, pb * 1024:(pb + 1) * 1024].rearrange("p (k t) -> p k t", k=8)[:, :, 0:np_], func=AF.Copy),
                        reads=[("psb", pb)], writes=[(dkey, col)])
            for b0 in range(0, n, 16):
                stats(srcs[b0:b0 + 16], b0)

        def ffn_phase(tag, gain_ap, wg_ap, wu_ap, wd_ap, with_meta, first, last):
            with contextlib.ExitStack() as st:
                P = Plan(nc, st, tag)

                def sb(name, shape, dt):
                    return st.enter_context(nc.sbuf_tensor(_nm(tag + name), list(shape), dt))
                W = 1040
                xnT = sb("xnT", [128, 8, W], BF16)
                h1T = sb("h1T", [128, NF, W], BF16)
                wd_sb = sb("wd", [128, NF, D], BF16)
                wg_sb = [sb("wg", [128, 8, 128], BF16) for _ in range(3)]
                wu_sb = [sb("wu", [128, 8, 128], BF16) for _ in range(3)]
                sg = [sb("sg", [128, 512], BF16) for _ in range(2)]
                wgv = wg_ap.rearrange("(kc p) f -> p kc f", p=128)
                wuv = wu_ap.rearrange("(kc p) f -> p kc f", p=128)
                wdv = wd_ap.rearrange("(fc p) d -> p fc d", p=128)

                if first:
                    for i in range(16):
                        P.emit("sp", lambda e, i=i: e.dma_start(out=htile(i), in_=x[i * 128:(i + 1) * 128, :]), writes=[("h", i)], dma=True)
                    P.emit("sp", lambda e: e.dma_start(out=hmeta[:], in_=meta[:, :]), writes=[("hm",)], dma=True)
                    P.emit("pool", lambda e: e.memset(ident[:], 0.0), writes=[("ident",)])
                    P.emit("pool", lambda e: e.affine_select(out=ident[:], in_=ident[:], pattern=[[-1, 128]], compare_op=ALU.not_equal,
                                                             fill=1.0, base=0, channel_multiplier=1), writes=[("ident",)])
                    P.emit("pool", lambda e: e.memset(BD[:], 0.0), writes=[("BD",)])
                    P.emit("pool", lambda e: e.memset(BD[0:64, 0:64], 1.0), writes=[("BD",)])
                    P.emit("pool", lambda e: e.memset(BD[64:128, 64:128], 1.0), writes=[("BD",)])
                    P.emit("pool", lambda e: e.memset(neghalf[:], -0.5), writes=[("neghalf",)])

                def wdma(f):
                    s = f % 3
                    P.emit("pool", lambda e: e.dma_start(out=wg_sb[s][:], in_=wgv[:, :, f * 128:(f + 1) * 128]), writes=[("wg", s)], dma=True)
                    P.emit("pool", lambda e: e.dma_start(out=wu_sb[s][:], in_=wuv[:, :, f * 128:(f + 1) * 128]), writes=[("wu", s)], dma=True)

                def wd_dma(j):
                    P.emit("pool", lambda e: e.dma_start(out=wd_sb[:, 2 * j:2 * j + 2, :], in_=wdv[:, 2 * j:2 * j + 2, :]),
                           writes=[("wd", 2 * j), ("wd", 2 * j + 1)], dma=True)

                unit = [0]
                dunit = [0]
                ncache = {}
                for p in range(2):
                    tiles = [(htile(8 * p + t), 128, t * 128, ("h", 8 * p + t)) for t in range(8)]
                    blocks = [(0, 512), (512, 512)]
                    if with_meta and p == 0:
                        tiles.append((hmeta[0:16, :], 16, 1024, ("hm",)))
                        blocks.append((1024, 16))
                    if p == 0:
                        for f in range(3):
                            wdma(f)
                    emit_norm_T(P, st, tag + "n", tiles, gain_ap, xnT, "xnT", ncache)
                    if p == 1:
                        for f in range(3):
                            wdma(f)
                    for f in range(NF):
                        s = f % 3
                        for (c0, N) in blocks:
                            u = unit[0] % 3
                            unit[0] += 1
                            sgi = unit[0] % 2
                            xk = [("xnT", c0 + t * 128) for t in range((N + 127) // 128)]
                            for k in range(8):
                                P.emit("pe", lambda e, k=k, u=u, c0=c0, N=N, s=s: e.matmul(
                                    fb(2 * u, N), wg_sb[s][:, k, :], xnT[:, k, c0:c0 + N], start=(k == 0), stop=(k == 7)),
                                    reads=[("wg", s)] + xk, writes=[bk(2 * u)])
                            for k in range(8):
                                P.emit("pe", lambda e, k=k, u=u, c0=c0, N=N, s=s: e.matmul(
                                    fb(2 * u + 1, N), wu_sb[s][:, k, :], xnT[:, k, c0:c0 + N], start=(k == 0), stop=(k == 7)),
                                    reads=[("wu", s)] + xk, writes=[bk(2 * u + 1)])
                            P.emit("act", lambda e, u=u, N=N, sgi=sgi: e.activation(out=sg[sgi][:, 0:N], in_=fb(2 * u, N), func=AF.Silu),
                                   reads=[bk(2 * u)], writes=[("sg", sgi)])
                            P.emit("dve", lambda e, u=u, N=N, sgi=sgi, f=f, c0=c0: e.tensor_tensor(
                                out=h1T[:, f, c0:c0 + N], in0=fb(2 * u + 1, N), in1=sg[sgi][:, 0:N], op=ALU.mult),
                                reads=[bk(2 * u + 1), ("sg", sgi)], writes=[("h1T", f, c0 + t * 128) for t in range((N + 127) // 128)])
                        if f + 3 < NF:
                            wdma(f + 3)
                        if p == 0 and f < 11:
                            wd_dma(f)
                    for (src, np_, col, skey) in tiles:
                        u = dunit[0] % 3
                        dunit[0] += 1
                        for dh in range(2):
                            for f in range(NF):
                                P.emit("pe", lambda e, u=u, dh=dh, f=f, np_=np_, col=col: e.matmul(
                                    fb(2 * u + dh, 512, 0, np_), h1T[:, f, col:col + np_], wd_sb[:, f, dh * 512:(dh + 1) * 512],
                                    start=(f == 0), stop=(f == NF - 1)),
                                    reads=[("h1T", f, col), ("wd", f)], writes=[bk(2 * u + dh)])
                        P.emit("dve", lambda e, u=u, np_=np_, src=src: e.scalar_tensor_tensor(
                            out=src, in0=PSF[0:np_, 2 * u * 512: 2 * u * 512 + 1024], scalar=0.5, in1=src, op0=ALU.mult, op1=ALU.add),
                            reads=[bk(2 * u), bk(2 * u + 1), skey], writes=[skey])
                        if last:
                            ti = skey[1]
                            P.emit("sp", lambda e, ti=ti: e.dma_start(out=y[ti * 128:(ti + 1) * 128, :], in_=htile(ti)),
                                   reads=[skey], dma=True)
                P.run()

        def mixer_phase():
            with contextlib.ExitStack() as mst:
                uT = mst.enter_context(nc.sbuf_tensor("uT", [128, 8, NTC], BF16))
                with contextlib.ExitStack() as st:
                    P = Plan(nc, st, "m0")
                    P.emit("pool", lambda e: e.memset(uT[:, :, 0:MC0], 0.0), writes=[("uT", 0)])
                    srcs = [(hmeta[0:16, :], 16, MC0, ("hm",))] + [(htile(i), 128, FC0 + 128 * i, ("h", i)) for i in range(16)]
                    emit_norm_T(P, st, "m0n", srcs, mixn, uT, "uT", {})
                    P.run()
                attention_phase(uT)
                rwkv_phase(uT)

        def attention_phase(uT):
            with contextlib.ExitStack() as st:
                P = Plan(nc, st, "at")

                def sb(name, shape, dt):
                    return st.enter_context(nc.sbuf_tensor(_nm("at" + name), list(shape), dt))
                yaT = sb("yaT", [128, 4, SEQ], BF16)
                wo_a = sb("woa", [128, 4, D], BF16)
                wq = [sb("wq", [128, 8, 128], BF16) for _ in range(2)]
                wk = [sb("wk", [128, 8, 128], BF16) for _ in range(2)]
                wv = [sb("wv", [128, 8, 128], BF16) for _ in range(2)]
                qh = [sb("qh", [128, SEQ], BF16) for _ in range(2)]
                kh = [sb("kh", [128, NTC], BF16) for _ in range(2)]
                vt = [sb("vt", [128, 17, 129], BF16) for _ in range(2)]
                EE = [sb("EE", [128, 2048], BF16) for _ in range(2)]
                basef = sb("basef", [128, 2048], F32)
                absb = sb("absb", [128, 128], F32)
                vis = sb("vis", [128, 128], BF16)
                edt = sb("edt", [128, 128], BF16)
                gm = [sb("gm", [16, 128], BF16) for _ in range(2)]
                pt = [sb("pt", [128, 512], BF16) for _ in range(4)]
                ptm = [sb("ptm", [16, 128], BF16) for _ in range(2)]
                sqb = [sb("sqb", [128, 512], BF16) for _ in range(2)]
                msb = [sb("msb", [128, 512], F32) for _ in range(2)]
                rsb = [sb("rsb", [128, 512], F32) for _ in range(2)]
                gqk = sb("gqk", [128, 2], F32)
                gmb = sb("gmb", [16, 64], F32)
                epsb = sb("epsb", [128, 1], F32)
                gq8 = sb("gq8", [128, 1], F32)
                lv = sb("lv", [128, 256], F32)
                lvt = sb("lvt", [128, 128], F32)
                dd = sb("dd", [128, 2], F32)
                ed = sb("ed", [128, 2], F32)
                neglam = sb("neglam", [128, 1], F32)
                ogb = sb("ogb", [128, 128], F32)
                rz = sb("rz", [128, 2], F32)
                s1 = sb("s1", [128, 1], F32)
                y0 = sb("y0", [128, 128], F32)
                yy = sb("yy", [128, 128], F32)
                junk = sb("junk", [128, 128], BF16)
                ssq = sb("ssq", [128, 1], F32)
                msq = sb("msq", [128, 1], F32)
                rsq = sb("rsq", [128, 1], F32)
                yn = [sb("yn", [128, 128], BF16) for _ in range(2)]
                winv = w_in.rearrange("(kc p) f -> p kc f", p=128)
                wov = w_out.rearrange("(kc p) d -> p kc d", p=128)

                P.emit("sp", lambda e: e.dma_start(out=gqk[:], in_=qkg[:, :]), writes=[("gqk",)], dma=True)
                P.emit("sp", lambda e: e.dma_start(out=lv[:], in_=lamv[0:1, :].partition_broadcast(128)), writes=[("lv",)], dma=True)
                P.emit("sp", lambda e: e.dma_start(out=ogb[:], in_=aon[0:1, :].partition_broadcast(128)), writes=[("ogb",)], dma=True)
                P.emit("pool", lambda e: e.dma_start(out=wo_a[:], in_=wov[:, 0:4, :]), writes=[("woa",)], dma=True)
                P.emit("dve", lambda e: e.tensor_scalar(out=gq8[:], in0=gqk[:, 0:1], scalar1=0.125, scalar2=None, op0=ALU.mult),
                       reads=[("gqk",)], writes=[("gq8",)])
                P.emit("dve", lambda e: e.tensor_scalar(out=ogb[:], in0=ogb[:], scalar1=1.0 - LAM_INIT, scalar2=None, op0=ALU.mult),
                       reads=[("ogb",)], writes=[("ogb",)])
                lv4 = lv[:].rearrange("p (a b d) -> p a b d", a=2, b=2)
                P.emit("dve", lambda e: e.tensor_tensor(out=lvt[:].rearrange("p (a d) -> p a d", a=2), in0=lv4[:, :, 0, :], in1=lv4[:, :, 1, :],
                                                        op=ALU.mult), reads=[("lv",)], writes=[("lvt",)])
                P.emit("dve", lambda e: e.reduce_sum(out=dd[:], in_=lvt[:].rearrange("p (a d) -> p a d", a=2), axis=AX.X),
                       reads=[("lvt",)], writes=[("dd",)])
                P.emit("act", lambda e: e.activation(out=ed[:], in_=dd[:], func=AF.Exp), reads=[("dd",)], writes=[("ed",)])
                P.emit("dve", lambda e: e.tensor_tensor(out=s1[:], in0=ed[:, 0:1], in1=ed[:, 1:2], op=ALU.subtract),
                       reads=[("ed",)], writes=[("s1",)])
                P.emit("dve", lambda e: e.tensor_scalar(out=neglam[:], in0=s1[:], scalar1=LAM_INIT, scalar2=-1.0, op0=ALU.add, op1=ALU.mult),
                       reads=[("s1",)], writes=[("neglam",)])
                P.emit("pool", lambda e: e.iota(basef[:], pattern=[[1, 2048]], base=0, channel_multiplier=-1,
                                                allow_small_or_imprecise_dtypes=True), writes=[("basef",)])
                P.emit("dve", lambda e: e.tensor_scalar(out=y0[:], in0=basef[:, 0:128], scalar1=-1.0, scalar2=None, op0=ALU.mult),
                       reads=[("basef",)], writes=[("y0",)])
                P.emit("dve", lambda e: e.tensor_tensor(out=absb[:], in0=basef[:, 0:128], in1=y0[:], op=ALU.max),
                       reads=[("basef",), ("y0",)], writes=[("absb",)])
                P.emit("pool", lambda e: e.memset(vis[:], 1.0), writes=[("vis",)])
                for hh_ in range(4):
                    P.emit("pool", lambda e: e.iota(gmb[:, hh_ * 16:(hh_ + 1) * 16], pattern=[[128, 16]], base=16, channel_multiplier=0,
                                                    allow_small_or_imprecise_dtypes=True), writes=[("gmb",)])
                    P.emit("pool", lambda e: e.tensor_scalar(out=gmb[:, hh_ * 16:(hh_ + 1) * 16], in0=gmb[:, hh_ * 16:(hh_ + 1) * 16],
                                                             scalar1=-SLOPES[hh_], scalar2=None, op0=ALU.mult), writes=[("gmb",)])
                P.emit("pool", lambda e: e.memset(epsb[:], RMS_EPS), writes=[("epsb",)])
                P.emit("pool", lambda e: e.memset(vis[64:128, 0:64], 0.0), writes=[("vis",)])
                for b in range(2):
                    P.emit("pool", lambda e, b=b: e.memset(vt[b][:, :, 128:129], 1.0), writes=[("vt", b)])

                def wdma_head(h):
                    s = h % 2
                    P.emit("pool", lambda e: e.dma_start(out=wq[s][:], in_=winv[:, :, h * 128:(h + 1) * 128]), writes=[("wq", s)], dma=True)
                    P.emit("pool", lambda e: e.dma_start(out=wk[s][:], in_=winv[:, :, 512 + h * 128:512 + (h + 1) * 128]), writes=[("wk", s)], dma=True)
                    P.emit("pool", lambda e: e.dma_start(out=wv[s][:], in_=winv[:, :, 1024 + h * 128:1024 + (h + 1) * 128]), writes=[("wv", s)], dma=True)

                ctr = {"ps": 0, "nb": 0, "o": 0, "pt": 0, "ptm": 0, "yn": 0, "pb": 0}

                def proj_norm(wsb, wkey, c0, N, gain, dst, dkey):
                    b = ctr["ps"] % 3
                    ctr["ps"] += 1
                    nb = ctr["nb"] % 2
                    ctr["nb"] += 1
                    uk = [("uT", c) for c in ([MC0] if c0 == 0 else [])] + [("uT", c0 + t * 128) for t in range(N // 128)] + [("uT", 0)]
                    for k in range(8):
                        P.emit("pe", lambda e, k=k: e.matmul(fb(b, N), wsb[:, k, :], uT[:, k, c0:c0 + N], start=(k == 0), stop=(k == 7)),
                               reads=[wkey] + uk, writes=[bk(b)])
                    P.emit("act", lambda e: e.activation(out=sqb[nb][:, 0:N], in_=fb(b, N), func=AF.Square), reads=[bk(b)], writes=[("sqb", nb)])
                    P.emit("pe", lambda e: e.matmul(fb(5, N), BD[:], sqb[nb][:, 0:N], start=True, stop=True), reads=[("sqb", nb)], writes=[bk(5)])
                    P.emit("act", lambda e: e.activation(out=msb[nb][:, 0:N], in_=fb(5, N), func=AF.Ln, scale=1.0 / 64, bias=epsb[:, 0:1]),
                           reads=[bk(5), ("epsb",)], writes=[("msb", nb)])
                    P.emit("act", lambda e: e.activation(out=rsb[nb][:, 0:N], in_=msb[nb][:, 0:N], func=AF.Exp, scale=-0.5),
                           reads=[("msb", nb)], writes=[("rsb", nb)])
                    P.emit("dve", lambda e: e.scalar_tensor_tensor(out=dst, in0=fb(b, N), scalar=gain, in1=rsb[nb][:, 0:N],
                                                                   op0=ALU.mult, op1=ALU.mult),
                           reads=[bk(b), ("rsb", nb), ("gq8",), ("gqk",)], writes=[dkey])

                wdma_head(0)
                def proj_head(h):
                    s = h % 2
                    slope = SLOPES[h]
                    P.emit("act", lambda e: e.activation(out=EE[s][:], in_=basef[:], func=AF.Exp, scale=-slope), reads=[("basef",)], writes=[("EE", s)])
                    P.emit("act", lambda e: e.activation(out=edt[:], in_=absb[:], func=AF.Exp, scale=-slope), reads=[("absb",)], writes=[("edt",)])
                    P.emit("dve", lambda e: e.tensor_tensor(out=EE[s][:, 0:128], in0=edt[:], in1=vis[:], op=ALU.mult),
                           reads=[("edt",), ("vis",)], writes=[("EE", s)])
                    for g in range(4):
                        proj_norm(wq[s], ("wq", s), FC0 + 512 * g, 512, gq8[:, 0:1], qh[s][:, g * 512:(g + 1) * 512], ("qh", s, g))
                        yield
                    proj_norm(wk[s], ("wk", s), 0, 64, gqk[:, 1:2], kh[s][:, 0:64], ("kh", s, 0))
                    yield
                    for g in range(4):
                        proj_norm(wk[s], ("wk", s), FC0 + 512 * g, 512, gqk[:, 1:2], kh[s][:, FC0 + 512 * g:FC0 + 512 * (g + 1)], ("kh", s, g + 1))
                        yield
                    for t0 in range(0, 16, 4):
                        b = ctr["ps"] % 3
                        ctr["ps"] += 1
                        for t in range(4):
                            col = FC0 + (t0 + t) * 128
                            for k in range(8):
                                P.emit("pe", lambda e, k=k, t=t, col=col: e.matmul(fb(b, 128, off=t * 128), uT[:, k, col:col + 128], wv[s][:, k, :],
                                                                                   start=(k == 0), stop=(k == 7)),
                                       reads=[("wv", s), ("uT", col)], writes=[bk(b)])
                        P.emit("act", lambda e, t0=t0, b=b: e.activation(out=vt[s][:, t0:t0 + 4, 0:128],
                                                                         in_=fb(b).rearrange("p (t c) -> p t c", t=4), func=AF.Copy),
                               reads=[bk(b)], writes=[("vt", s)])
                        yield
                    b = ctr["ps"] % 3
                    ctr["ps"] += 1
                    for k in range(8):
                        P.emit("pe", lambda e, k=k, b=b: e.matmul(fb(b, 128, 0, 16), uT[:, k, MC0:MC0 + 16], wv[s][:, k, :], start=(k == 0), stop=(k == 7)),
                               reads=[("wv", s), ("uT", MC0)], writes=[bk(b)])
                    P.emit("act", lambda e, b=b: e.activation(out=vt[s][0:16, 16, 0:128], in_=fb(b, 128, 0, 16), func=AF.Copy),
                           reads=[bk(b)], writes=[("vt", s)])
                    yield

                for _ in proj_head(0):
                    pass
                for h in range(4):
                    s = h % 2
                    slope = SLOPES[h]
                    nxt = None
                    if h + 1 < 4:
                        wdma_head(h + 1)
                        nxt = proj_head(h + 1)
                    pend = []
                    late = []

                    def flush(n=2):
                        while len(pend) > n:
                            pend.pop(0)()

                    for qi in range(16):
                        gi = ctr["ptm"] % 2
                        P.emit("act", lambda e: e.activation(out=gm[gi][:], in_=basef[0:16, 0:128], func=AF.Exp, scale=-slope,
                                                             bias=gmb[0:16, h * 16 + qi:h * 16 + qi + 1]), reads=[("basef",), ("gmb",)], writes=[("gm", gi)])
                        ob = 3 + (ctr["o"] % 2)
                        ctr["o"] += 1
                        qk_ = ("qh", s, qi // 4)
                        for c in range(2):
                            cp = slice(c * 64, (c + 1) * 64)
                            nkt = qi + 1
                            first = [True]
                            for b0 in range(0, nkt, 4):
                                jjs = list(range(b0, min(b0 + 4, nkt)))
                                nbk = len(jjs)
                                b = ctr["ps"] % 3
                                ctr["ps"] += 1
                                pi = ctr["pt"] % 4
                                ctr["pt"] += 1
                                for m, jj in enumerate(jjs):
                                    j = qi - jj
                                    P.emit("pe", lambda e: e.matmul(
                                        fb(b, 128, off=m * 128), kh[s][cp, FC0 + 128 * j:FC0 + 128 * (j + 1)], qh[s][cp, qi * 128:(qi + 1) * 128],
                                        start=True, stop=True),
                                        reads=[("kh", s, 1 + j // 4), qk_], writes=[bk(b)])
                                P.emit("act", lambda e: e.activation(out=pt[pi][:, 0:nbk * 128], in_=fb(b, nbk * 128), func=AF.Exp),
                                       reads=[bk(b)], writes=[("pt", pi)])
                                P.emit("dve", lambda e: e.tensor_tensor(
                                    out=pt[pi][:, 0:nbk * 128], in0=pt[pi][:, 0:nbk * 128], in1=EE[s][:, b0 * 128:(b0 + nbk) * 128], op=ALU.mult),
                                    reads=[("pt", pi), ("EE", s)], writes=[("pt", pi)])

                                def pv(jjs=jjs, pi=pi, first=first, ob=ob, c=c, qi=qi):
                                    for m, jj in enumerate(jjs):
                                        j = qi - jj
                                        P.emit("pe", lambda e: e.matmul(
                                            fb(ob, 129, off=c * 129), pt[pi][:, m * 128:(m + 1) * 128], vt[s][:, j, :], start=first[0], stop=False),
                                            reads=[("pt", pi), ("vt", s)], writes=[bk(ob)])
                                        first[0] = False
                                flush()
                                pend.append(pv)
                            b = ctr["ps"] % 3
                            ctr["ps"] += 1
                            mi = ctr["ptm"] % 2
                            ctr["ptm"] += 1
                            P.emit("pe", lambda e: e.matmul(fb(b, 128, 0, 16), kh[s][cp, MC0:MC0 + 16], qh[s][cp, qi * 128:(qi + 1) * 128],
                                                            start=True, stop=True), reads=[("kh", s, 0), qk_], writes=[bk(b)])
                            P.emit("act", lambda e: e.activation(out=ptm[mi][:], in_=fb(b, 128, 0, 16), func=AF.Exp),
                                   reads=[bk(b)], writes=[("ptm", mi)])
                            P.emit("dve", lambda e: e.tensor_tensor(out=ptm[mi][:], in0=ptm[mi][:], in1=gm[gi][:], op=ALU.mult),
                                   reads=[("ptm", mi), ("gm", gi)], writes=[("ptm", mi)])

                            def pvm(mi=mi, ob=ob, c=c):
                                P.emit("pe", lambda e: e.matmul(fb(ob, 129, off=c * 129), ptm[mi][0:16, :], vt[s][0:16, 16, :], start=False, stop=True),
                                       reads=[("ptm", mi), ("vt", s)], writes=[bk(ob)])
                            flush()
                            pend.append(pvm)

                        def finalize(ob=ob, qi=qi):
                            O3 = fb(ob, 258).rearrange("p (c e) -> p c e", c=2)
                            P.emit("dve", lambda e: e.reciprocal(out=rz[:].rearrange("p (c o) -> p c o", o=1), in_=O3[:, :, 128:129]),
                                   reads=[bk(ob)], writes=[("rz",)])
                            P.emit("dve", lambda e: e.tensor_tensor(out=s1[:], in0=rz[:, 1:2], in1=neglam[:], op=ALU.mult),
                                   reads=[("rz",), ("neglam",)], writes=[("s1",)])
                            P.emit("dve", lambda e: e.tensor_scalar(out=y0[:], in0=O3[:, 0, 0:128], scalar1=rz[:, 0:1], scalar2=None, op0=ALU.mult),
                                   reads=[bk(ob), ("rz",)], writes=[("y0",)])
                            P.emit("dve", lambda e: e.scalar_tensor_tensor(out=yy[:], in0=O3[:, 1, 0:128], scalar=s1[:, 0:1], in1=y0[:],
                                                                           op0=ALU.mult, op1=ALU.add),
                                   reads=[bk(ob), ("s1",), ("y0",)], writes=[("yy",)])
                            P.emit("act", lambda e: e.activation(out=junk[:], in_=yy[:], func=AF.Square, accum_out=ssq[:]),
                                   reads=[("yy",)], writes=[("junk",), ("ssq",)])
                            P.emit("dve", lambda e: e.tensor_scalar(out=msq[:], in0=ssq[:], scalar1=1.0 / 128, scalar2=RMS_EPS, op0=ALU.mult, op1=ALU.add),
                                   reads=[("ssq",)], writes=[("msq",)])
                            P.emit("pool", lambda e: e.tensor_tensor(out=rsq[:], in0=msq[:], in1=neghalf[:, 0:1], op=ALU.pow),
                                   reads=[("msq",)], writes=[("rsq",)])
                            yi = ctr["yn"] % 2
                            ctr["yn"] += 1
                            P.emit("dve", lambda e: e.scalar_tensor_tensor(out=yn[yi][:], in0=yy[:], scalar=rsq[:, 0:1], in1=ogb[:],
                                                                           op0=ALU.mult, op1=ALU.mult),
                                   reads=[("yy",), ("rsq",), ("ogb",)], writes=[("yn", yi)])
                            pb = ctr["pb"] % 2
                            ctr["pb"] += 1

                            def tr(yi=yi, pb=pb, qi=qi):
                                P.emit("pe", lambda e: e.transpose(PSB[:, pb * 1024:pb * 1024 + 128], yn[yi][:], ident[:]),
                                       reads=[("yn", yi)], writes=[("psb", pb)])
                                P.emit("act", lambda e: e.activation(out=yaT[:, h, qi * 128:(qi + 1) * 128], in_=PSB[:, pb * 1024:pb * 1024 + 128],
                                                                     func=AF.Copy), reads=[("psb", pb)], writes=[("yaT", qi)])
                            late.append(tr)
                        while late:
                            late.pop(0)()
                        pend.append(finalize)
                        if nxt is not None:
                            next(nxt, None)
                    flush(0)
                    while late:
                        late.pop(0)()
                    if nxt is not None:
                        for _ in nxt:
                            pass
                for i in range(16):
                    u = i % 2
                    for dh in range(2):
                        for kc in range(4):
                            P.emit("pe", lambda e, u=u, dh=dh, kc=kc, i=i: e.matmul(
                                fb(2 * u + dh), yaT[:, kc, i * 128:(i + 1) * 128], wo_a[:, kc, dh * 512:(dh + 1) * 512], start=(kc == 0), stop=(kc == 3)),
                                reads=[("yaT", i), ("woa",)], writes=[bk(2 * u + dh)])
                    P.emit("dve", lambda e, u=u, i=i: e.tensor_tensor(out=htile(i), in0=PSF[:, 2 * u * 512:2 * u * 512 + 1024], in1=htile(i), op=ALU.add),
                           reads=[bk(2 * u), bk(2 * u + 1), ("h", i)], writes=[("h", i)])
                P.run()

        def rwkv_phase(uT):
            with contextlib.ExitStack() as st:
                P = Plan(nc, st, "rw")

                def sb(name, shape, dt):
                    return st.enter_context(nc.sbuf_tensor(_nm("rw" + name), list(shape), dt))
                NM = 256
                NCH = 4
                wo_r = sb("wor", [128, 4, D], BF16)
                wrw = [sb("wrw", [128, 8, 128], BF16) for _ in range(4)]
                waup = sb("waup", [128, 512], BF16)
                gup = sb("gup", [128, 512], BF16)
                pp = sb("pp", [128, 34], F32)
                lnt = sb("lnt", [128, 512], F32)
                prevl = sb("prevl", [128, 14], F32)
                pbuf = [sb("pbuf", [128, NM + 1], F32) for _ in range(2)]
                dtmp = [sb("dtmp", [128, NM], F32) for _ in range(2)]
                lp = sb("lp", [128, NM], F32)
                lo24 = sb("lo24", [128, NM], BF16)
                sxg = sb("sxg", [128, NM], BF16)
                k32 = [sb("k32", [128, NM], F32) for _ in range(2)]
                r32 = [sb("r32", [128, NM], F32) for _ in range(2)]
                vbf = [sb("vbf", [128, NM], BF16) for _ in range(2)]
                asig = [sb("asig", [128, NM], F32) for _ in range(2)]
                kkr = [sb("kkr", [128, NM], F32) for _ in range(2)]
                sqk = [sb("sqk", [128, NM], BF16) for _ in range(2)]
                ssm = [sb("ssm", [128, NM], F32) for _ in range(2)]
                rn = [sb("rn", [128, NM], F32) for _ in range(2)]
                kmod = [sb("kmod", [128, NM], F32) for _ in range(2)]
                kb = [sb("kb", [128, NM], F32) for _ in range(2)]
                bp = [sb("bp", [128, NM], BF16) for _ in range(2)]
                mask = sb("mask", [128, NM], F32)
                lmask = sb("lmask", [128, 64], F32)
                umask = sb("umask", [128, 128], F32)
                I2 = sb("I2", [128, 64], BF16)
                ones = sb("ones", [128, 1], BF16)
                ln20 = sb("ln20", [128, 1], F32)
                AR = [sb("AR", [128, NCH * 128], BF16) for _ in range(4)]
                TMw = sb("TMw", [128, 4 * NCH * 192], BF16)
                TM = [TMw[:, hp_ * NCH * 192:(hp_ + 1) * NCH * 192] for hp_ in range(4)]
                WA = NCH * 128
                ABTp = [sb("ABTp", [128, 2 * WA], BF16) for _ in range(2)]
                AKTp = [sb("AKTp", [128, 2 * WA], BF16) for _ in range(2)]
                TTp = [sb("TTp", [128, 2 * NM], BF16) for _ in range(2)]
                ABT = [ABTp[hp_ // 2][:, (hp_ % 2) * WA:(hp_ % 2 + 1) * WA] for hp_ in range(4)]
                AKT = [AKTp[hp_ // 2][:, (hp_ % 2) * WA:(hp_ % 2 + 1) * WA] for hp_ in range(4)]
                TT = [TTp[hp_ // 2][:, (hp_ % 2) * NM:(hp_ % 2 + 1) * NM] for hp_ in range(4)]
                PPw = [sb("PPw", [128, 2 * NM], BF16) for _ in range(2)]
                QQw = [sb("QQw", [128, 2 * NM], BF16) for _ in range(2)]
                gbfw = sb("gbfw", [128, 4 * NM], BF16)
                gbf = [gbfw[:, hp_ * NM:(hp_ + 1) * NM] for hp_ in range(4)]
                bonw = sb("bonw", [128, 4 * NCH], F32)
                bon = [bonw[:, hp_ * NCH:(hp_ + 1) * NCH] for hp_ in range(4)]
                PC = [sb("PC", [128, 8], F32) for _ in range(4)]
                S32 = [sb("S32", [128, 64], F32) for _ in range(4)]
                Sbf = [sb("Sbf", [128, 64], BF16) for _ in range(4)]
                Xbf = [sb("Xbf", [128, 64], BF16) for _ in range(4)]
                Ubf = [sb("Ubf", [128, 64], BF16) for _ in range(4)]
                t1 = [sb("t1", [128, 64], F32) for _ in range(4)]
                Ysb = sb("Ysb", [128, 4 * NM], F32)
                ysq = sb("ysq", [128, 4 * NM], F32)
                yc = sb("yc", [128, 4 * NM], F32)
                sgw = [Ysb[:, 0 * NM:1 * NM], Ysb[:, 1 * NM:2 * NM]]
                cs = [Ysb[:, 2 * NM:3 * NM], Ysb[:, 3 * NM:4 * NM]]
                csm = [ysq[:, 0 * NM:1 * NM], ysq[:, 1 * NM:2 * NM]]
                epos = [ysq[:, 2 * NM:3 * NM], ysq[:, 3 * NM:4 * NM]]
                eneg = [yc[:, 0 * NM:1 * NM], yc[:, 1 * NM:2 * NM]]
                eprev = [yc[:, 2 * NM:3 * NM], yc[:, 3 * NM:4 * NM]]
                KYsb = [("Ysb",), ("sgw", 0), ("sgw", 1), ("cs", 0), ("cs", 1)]
                Kysq = [("ysq",), ("csm", 0), ("csm", 1), ("epos", 0), ("epos", 1)]
                Kyc = [("yc",), ("eneg", 0), ("eneg", 1), ("eprev", 0), ("eprev", 1)]
                Kytm = [("ytm",), ("BT", 0), ("BT", 1), ("KT", 0), ("KT", 1)]
                st1 = sb("st1", [128, 16], F32)
                st2 = sb("st2", [128, 16], F32)
                mean = sb("mean", [128, 16], F32)
                msq = sb("msq", [128, 16], F32)
                var = sb("var", [128, 16], F32)
                rstd = sb("rstd", [128, 16], F32)
                ytm = sb("ytm", [128, 4 * NM], BF16)
                BT = [ytm[:, 0 * NM:1 * NM], ytm[:, 1 * NM:2 * NM]]
                KT = [ytm[:, 2 * NM:3 * NM], ytm[:, 3 * NM:4 * NM]]
                yrT = sb("yrT", [128, 4, NM], BF16)
                winv = w_in.rearrange("(kc p) f -> p kc f", p=128)
                wov = w_out.rearrange("(kc p) d -> p kc d", p=128)
                print("rwkv sbuf bytes remaining", nc.sbuf_bytes_remaining)

                P.emit("sp", lambda e: e.dma_start(out=pp[:], in_=rwp[:, :]), writes=[("pp",)], dma=True)
                P.emit("sp", lambda e: e.dma_start(out=lnt[:], in_=lnwb[:, :]), writes=[("lnt",)], dma=True)
                P.emit("pool", lambda e: e.dma_start(out=waup[0:64, :], in_=rw_wup[:, :]), writes=[("waup", 0)], dma=True)
                P.emit("pool", lambda e: e.dma_start(out=waup[64:128, :], in_=rw_aup[:, :]), writes=[("waup", 1)], dma=True)
                P.emit("pool", lambda e: e.dma_start(out=gup[:], in_=rw_gup[:, :]), writes=[("gup",)], dma=True)
                P.emit("pool", lambda e: e.dma_start(out=wo_r[:], in_=wov[:, 4:8, :]), writes=[("wor",)], dma=True)
                P.emit("pool", lambda e: e.memset(prevl[:], 0.0), writes=[("prevl", i) for i in range(14)])
                P.emit("pool", lambda e: e.memset(mask[:], 1.0), writes=[("mask",)])
                P.emit("pool", lambda e: e.memset(mask[:].rearrange("p (c t) -> p c t", t=64)[:, :, 0:1], 0.0), writes=[("mask",)])
                P.emit("pool", lambda e: e.memset(ones[:], 1.0), writes=[("ones",)])
                P.emit("pool", lambda e: e.memset(ln20[:], 20.0 * math.log(2.0)), writes=[("ln20",)])
                P.emit("pool", lambda e: e.memset(lmask[:], 1.0), writes=[("lmask",)])
                P.emit("pool", lambda e: e.memset(umask[:], 1.0), writes=[("umask",)])
                P.emit("pool", lambda e: e.memset(I2[:], 0.0), writes=[("I2",)])
                for hh in range(2):
                    hp_ = slice(hh * 64, (hh + 1) * 64)
                    P.emit("pool", lambda e, hp_=hp_: e.affine_select(out=lmask[hp_, :], in_=lmask[hp_, :], pattern=[[-1, 64]], compare_op=ALU.is_gt,
                                                                      fill=0.0, base=0, channel_multiplier=1), writes=[("lmask",)])
                    P.emit("pool", lambda e, hp_=hp_: e.affine_select(out=umask[hp_, 0:64], in_=umask[hp_, 0:64], pattern=[[1, 64]], compare_op=ALU.is_gt,
                                                                      fill=0.0, base=0, channel_multiplier=-1), writes=[("umask",)])
                    P.emit("pool", lambda e, hp_=hp_: e.affine_select(out=umask[hp_, 64:128], in_=umask[hp_, 64:128], pattern=[[1, 64]], compare_op=ALU.is_ge,
                                                                      fill=0.0, base=0, channel_multiplier=-1), writes=[("umask",)])
                    P.emit("pool", lambda e, hp_=hp_: e.affine_select(out=I2[hp_, :], in_=I2[hp_, :], pattern=[[-1, 64]], compare_op=ALU.not_equal,
                                                                      fill=1.0, base=0, channel_multiplier=1), writes=[("I2",)])
                for hp in range(4):
                    P.emit("pool", lambda e, hp=hp: e.memset(S32[hp][:], 0.0), writes=[("S32", hp)])
                    P.emit("pool", lambda e, hp=hp: e.memset(Sbf[hp][:], 0.0), writes=[("Sbf", hp)])

                MU0, W00, A00, KK0, KA0, RK0 = 0, 14, 18, 22, 26, 30
                wctr = [0]
                pctr = [0]

                def h2p(h2):
                    return slice(h2 * 64, (h2 + 1) * 64)

                def v3(ap):
                    return ap.rearrange("p (c t) -> p c t", t=64)

                RING = 4
                rowseq = []
                for _g in range(1 + SEQ // NM):
                    rowseq += [12, 13, 4, 0, 5, 1, 8, 9, 6, 2, 7, 3, 10, 11]
                pfc = [0]

                def prefetch_upto(n):
                    while pfc[0] < min(n, len(rowseq)):
                        i_ = pfc[0]
                        s_ = i_ % RING
                        col_ = 1536 + rowseq[i_] * 128
                        P.emit("pool", lambda e: e.dma_start(out=wrw[s_][:], in_=winv[:, :, col_:col_ + 128]), writes=[("wrw", s_)], dma=True)
                        pfc[0] += 1

                def proj_row(wc, c0, N, out_ap, out_key):
                    i_ = wctr[0]
                    wctr[0] += 1
                    assert rowseq[i_] == wc, (i_, wc, rowseq[i_])
                    s = i_ % RING
                    prefetch_upto(i_ + RING)
                    b = pctr[0] % 2
                    pctr[0] += 1
                    for k in range(8):
                        P.emit("pe", lambda e, k=k: e.matmul(fb(b, N), wrw[s][:, k, :], uT[:, k, c0:c0 + N], start=(k == 0), stop=(k == 7)),
                               reads=[("wrw", s)], writes=[bk(b)])
                    pb_ = pbuf[b]
                    P.emit("act", lambda e: e.activation(out=pb_[:, 1:N + 1], in_=fb(b, N), func=AF.Copy), reads=[bk(b)], writes=[("pbuf", b)])
                    P.emit("act", lambda e: e.activation(out=pb_[:, 0:1], in_=prevl[:, wc:wc + 1], func=AF.Copy), reads=[("prevl", wc)], writes=[("pbuf", b)])
                    P.emit("act", lambda e: e.activation(out=prevl[:, wc:wc + 1], in_=pb_[:, N:N + 1], func=AF.Copy), reads=[("pbuf", b)], writes=[("prevl", wc)])
                    P.emit("dve", lambda e: e.tensor_tensor(out=dtmp[b][:, 0:N], in0=pb_[:, 0:N], in1=pb_[:, 1:N + 1], op=ALU.subtract),
                           reads=[("pbuf", b)], writes=[("dtmp", b)])
                    P.emit("dve", lambda e: e.scalar_tensor_tensor(
                        out=out_ap, in0=dtmp[b][:, 0:N], scalar=pp[:, MU0 + wc:MU0 + wc + 1], in1=pb_[:, 1:N + 1], op0=ALU.mult, op1=ALU.add),
                        reads=[("dtmp", b), ("pbuf", b), ("pp",)], writes=[out_key])

                groups = [(0, 64, 1, False, 0)] + [(FC0 + NM * g, NM, NCH, True, g) for g in range(SEQ // NM)]
                wout_pending = []
                for (c0, N, nch, has_out, gidx) in groups:
                    def bc(ap8):
                        return ap8.unsqueeze(2).broadcast_to([128, nch, 64])
                    proj_row(12, c0, N, lp[:, 0:N], ("lp",))
                    P.emit("act", lambda e: e.activation(out=lo24[0:64, 0:N], in_=lp[0:64, 0:N], func=AF.Tanh), reads=[("lp",)], writes=[("lo24", 0)])
                    P.emit("act", lambda e: e.activation(out=lo24[64:128, 0:N], in_=lp[64:128, 0:N], func=AF.Copy), reads=[("lp",)], writes=[("lo24", 1)])
                    proj_row(13, c0, N, lp[:, 0:N], ("lp",))
                    P.emit("act", lambda e: e.activation(out=sxg[:, 0:N], in_=lp[:, 0:N], func=AF.Sigmoid), reads=[("lp",)], writes=[("sxg",)])

                    def chain(hp):
                        par = hp % 2
                        B0, B1 = (2, 3) if par == 0 else (4, 5)
                        hc = slice(hp * 128, (hp + 1) * 128)
                        proj_row(4 + hp, c0, N, k32[par][:, 0:N], ("k32", par))
                        proj_row(0 + hp, c0, N, r32[par][:, 0:N], ("r32", par))
                        yield
                        proj_row(8 + hp, c0, N, vbf[par][:, 0:N], ("vbf", par))
                        P.emit("pe", lambda e: e.matmul(fb(B0, N), waup[0:64, hc], lo24[0:64, 0:N], start=True, stop=True),
                               reads=[("waup", 0), ("lo24", 0)], writes=[bk(B0)])
                        yield
                        P.emit("act", lambda e: e.activation(out=sgw[par][:, 0:N], in_=fb(B0, N), func=AF.Sigmoid, bias=pp[:, W00 + hp:W00 + hp + 1]),
                               reads=[bk(B0), ("pp",)], writes=[("sgw", par)])
                        P.emit("pe", lambda e: e.matmul(fb(B1, N), waup[64:128, hc], lo24[64:128, 0:N], start=True, stop=True),
                               reads=[("waup", 1), ("lo24", 1)], writes=[bk(B1)])
                        yield
                        P.emit("act", lambda e: e.activation(out=asig[par][:, 0:N], in_=fb(B1, N), func=AF.Sigmoid, bias=pp[:, A00 + hp:A00 + hp + 1]),
                               reads=[bk(B1), ("pp",)], writes=[("asig", par)])
                        P.emit("dve", lambda e: e.tensor_tensor_scan(out=cs[par][:, 0:N], data0=mask[:, 0:N], data1=sgw[par][:, 0:N], initial=0.0,
                                                                     op0=ALU.mult, op1=ALU.add), reads=[("mask",), ("sgw", par)], writes=[("cs", par)])
                        yield
                        P.emit("dve", lambda e: e.tensor_tensor(out=csm[par][:, 0:N], in0=cs[par][:, 0:N], in1=sgw[par][:, 0:N], op=ALU.subtract),
                               reads=[("cs", par), ("sgw", par)], writes=[("csm", par)])
                        P.emit("act", lambda e: e.activation(out=epos[par][:, 0:N], in_=cs[par][:, 0:N], func=AF.Exp, scale=-C0), reads=[("cs", par)], writes=[("epos", par)])
                        yield
                        P.emit("act", lambda e: e.activation(out=eneg[par][:, 0:N], in_=cs[par][:, 0:N], func=AF.Exp, scale=C0), reads=[("cs", par)], writes=[("eneg", par)])
                        P.emit("act", lambda e: e.activation(out=eprev[par][:, 0:N], in_=csm[par][:, 0:N], func=AF.Exp, scale=-C0), reads=[("csm", par)], writes=[("eprev", par)])
                        yield
                        P.emit("act", lambda e, hp=hp: e.activation(out=PC[hp][:, 0:nch], in_=v3(epos[par][:, 0:N])[:, :, 63], func=AF.Copy),
                               reads=[("epos", par)], writes=[("PC", hp)])
                        P.emit("dve", lambda e: e.tensor_scalar(out=kkr[par][:, 0:N], in0=k32[par][:, 0:N], scalar1=pp[:, KK0 + hp:KK0 + hp + 1], scalar2=None, op0=ALU.mult),
                               reads=[("k32", par), ("pp",)], writes=[("kkr", par)])
                        yield
                        P.emit("act", lambda e: e.activation(out=sqk[par][:, 0:N], in_=kkr[par][:, 0:N], func=AF.Square), reads=[("kkr", par)], writes=[("sqk", par)])
                        P.emit("pe", lambda e: e.matmul(fb(B0, N), BD[:], sqk[par][:, 0:N], start=True, stop=True), reads=[("sqk", par)], writes=[bk(B0)])
                        yield
                        P.emit("dve", lambda e: e.tensor_scalar(out=ssm[par][:, 0:N], in0=fb(B0, N), scalar1=1e-24, scalar2=None, op0=ALU.max),
                               reads=[bk(B0)], writes=[("ssm", par)])
                        P.emit("act", lambda e: e.activation(out=ssm[par][:, 0:N], in_=ssm[par][:, 0:N], func=AF.Ln, scale=float(2.0 ** 40)),
                               reads=[("ssm", par)], writes=[("ssm", par)])
                        yield
                        P.emit("act", lambda e: e.activation(out=rn[par][:, 0:N], in_=ssm[par][:, 0:N], func=AF.Exp, scale=-0.5, bias=ln20[:, 0:1]),
                               reads=[("ssm", par), ("ln20",)], writes=[("rn", par)])
                        P.emit("dve", lambda e: e.tensor_tensor(out=kkr[par][:, 0:N], in0=kkr[par][:, 0:N], in1=rn[par][:, 0:N], op=ALU.mult),
                               reads=[("kkr", par), ("rn", par)], writes=[("kkr", par)])
                        yield
                        P.emit("dve", lambda e: e.tensor_scalar(out=kmod[par][:, 0:N], in0=asig[par][:, 0:N], scalar1=-1.0, scalar2=pp[:, KA0 + hp:KA0 + hp + 1],
                                                                op0=ALU.add, op1=ALU.mult), reads=[("asig", par), ("pp",)], writes=[("kmod", par)])
                        P.emit("dve", lambda e: e.scalar_tensor_tensor(out=kmod[par][:, 0:N], in0=kmod[par][:, 0:N], scalar=1.0, in1=k32[par][:, 0:N], op0=ALU.add, op1=ALU.mult),
                               reads=[("kmod", par), ("k32", par)], writes=[("kmod", par)])
                        yield
                        AR3 = AR[hp][:, 0:nch * 128].rearrange("p (c a t) -> p c a t", a=2, t=64)
                        P.emit("dve", lambda e, AR3=AR3: e.scalar_tensor_tensor(
                            out=AR3[:, :, 0, :], in0=v3(kkr[par][:, 0:N]), scalar=-1.0, in1=v3(eprev[par][:, 0:N]), op0=ALU.mult, op1=ALU.mult),
                            reads=[("kkr", par), ("eprev", par)], writes=[("AR", hp)])
                        P.emit("pool", lambda e, AR3=AR3: e.tensor_tensor(out=AR3[:, :, 1, :], in0=v3(r32[par][:, 0:N]), in1=v3(epos[par][:, 0:N]), op=ALU.mult),
                               reads=[("r32", par), ("epos", par), ("AR", hp)], writes=[("AR2", hp)])
                        yield
                        P.emit("pool", lambda e: e.tensor_tensor(out=kb[par][:, 0:N], in0=kkr[par][:, 0:N], in1=asig[par][:, 0:N], op=ALU.mult),
                               reads=[("kkr", par), ("asig", par)], writes=[("kb", par)])
                        P.emit("pool", lambda e: e.tensor_tensor(out=BT[par][:, 0:N], in0=kb[par][:, 0:N], in1=eneg[par][:, 0:N], op=ALU.mult),
                               reads=[("kb", par), ("eneg", par)], writes=[("BT", par)])
                        yield
                        P.emit("pool", lambda e: e.tensor_tensor(out=KT[par][:, 0:N], in0=kmod[par][:, 0:N], in1=eneg[par][:, 0:N], op=ALU.mult),
                               reads=[("kmod", par), ("eneg", par)], writes=[("KT", par)])
                        P.emit("dve", lambda e: e.scalar_tensor_tensor(out=bp[par][:, 0:N], in0=r32[par][:, 0:N], scalar=pp[:, RK0 + hp:RK0 + hp + 1], in1=kmod[par][:, 0:N],
                                                                       op0=ALU.mult, op1=ALU.mult), reads=[("r32", par), ("kmod", par), ("pp",)], writes=[("bp", par)])
                        yield
                        P.emit("pe", lambda e: e.matmul(fb(B1, N), gup[:, hc], sxg[:, 0:N], start=True, stop=True), reads=[("gup",), ("sxg",)], writes=[bk(B1)])
                        P.emit("act", lambda e, hp=hp: e.activation(out=gbf[hp][:, 0:N], in_=fb(B1, N), func=AF.Copy), reads=[bk(B1)], writes=[("gbf", hp)])
                        yield
                        ARK = [("AR", hp), ("AR2", hp)]
                        for c in range(nch):
                            for si, (srcT, skey) in enumerate(((BT[par], ("BT", par)), (KT[par], ("KT", par)), (vbf[par], ("vbf", par)))):
                                for h2 in range(2):
                                    P.emit("pe", lambda e, c=c, si=si, srcT=srcT, h2=h2: e.transpose(
                                        PSB[h2p(h2), par * 1024 + (c * 3 + si) * 64:par * 1024 + (c * 3 + si) * 64 + 64],
                                        srcT[h2p(h2), c * 64:(c + 1) * 64], ident[h2p(h2), h2p(h2)]),
                                        reads=[skey], writes=[("psb", par)])
                        P.emit("act", lambda e, hp=hp: e.activation(out=TM[hp][:, 0:nch * 192], in_=PSB[:, par * 1024:par * 1024 + nch * 192], func=AF.Copy),
                               reads=[("psb", par)], writes=[("TM", hp)])
                        for c in range(nch):
                            for h2 in range(2):
                                P.emit("pe", lambda e, c=c, h2=h2: e.matmul(fb(B1, 1, h2 * 64, h2 * 64 + 64, off=256 + c), bp[par][h2p(h2), c * 64:(c + 1) * 64],
                                                                            ones[h2p(h2), 0:1], start=True, stop=True),
                                       reads=[("bp", par), ("ones",)], writes=[bk(B1)])
                        P.emit("act", lambda e, hp=hp: e.activation(out=bon[hp][:, 0:nch], in_=fb(B1, nch, off=256), func=AF.Copy),
                               reads=[bk(B1)], writes=[("bon", hp)])
                        yield

                    def G3(t, W, w, sz):
                        if w == W:
                            return t[:, 0:2 * W].rearrange("p (g s) -> p g s", s=sz)
                        assert w == sz
                        return t[:, 0:2 * W].rearrange("p (h x) -> p h x", h=2)[:, :, 0:w]

                    def pair_stage(pr):
                        ng = 2 * nch
                        hps = (2 * pr, 2 * pr + 1)
                        ARKS = [[("AR", hp), ("AR2", hp)] for hp in hps]
                        for hl, hp in enumerate(hps):
                            for c in range(nch):
                                for h2 in range(2):
                                    q_ = h2p(h2)
                                    P.emit("pe", lambda e: e.matmul(fb(2, 64, q_.start, q_.stop, off=hl * NM + c * 64), AR[hp][q_, c * 128:c * 128 + 64],
                                                                    BT[hl][q_, c * 64:(c + 1) * 64], start=True, stop=True),
                                           reads=ARKS[hl] + [("BT", hl)], writes=[bk(2)])
                        for hl, hp in enumerate(hps):
                            for c in range(nch):
                                for h2 in range(2):
                                    q_ = h2p(h2)
                                    P.emit("pe", lambda e: e.matmul(fb(3 + hl, 128, q_.start, q_.stop, off=c * 128), BT[hl][q_, c * 64:(c + 1) * 64],
                                                                    AR[hp][q_, c * 128:(c + 1) * 128], start=True, stop=True),
                                           reads=ARKS[hl] + [("BT", hl)], writes=[bk(3 + hl)])
                        P.emit("dve", lambda e: e.tensor_tensor(out=G3(PPw[0], NM, N, 64), in0=G3(PSF[:, 2 * 512:3 * 512], NM, N, 64),
                                                                in1=lmask[:].unsqueeze(1).broadcast_to([128, ng, 64]), op=ALU.mult),
                               reads=[bk(2), ("lmask",)], writes=[("PPw", 0)])
                        for hl, hp in enumerate(hps):
                            P.emit("dve", lambda e: e.tensor_tensor(out=ABT[hp][:, 0:nch * 128].rearrange("p (c s) -> p c s", s=128),
                                                                    in0=fb(3 + hl, nch * 128).rearrange("p (c s) -> p c s", s=128),
                                                                    in1=umask[:].unsqueeze(1).broadcast_to([128, nch, 128]), op=ALU.mult),
                                   reads=[bk(3 + hl), ("umask",)], writes=[("ABT", hp)])
                        for hl, hp in enumerate(hps):
                            for c in range(nch):
                                for h2 in range(2):
                                    q_ = h2p(h2)
                                    P.emit("pe", lambda e: e.matmul(fb(4 + hl, 128, q_.start, q_.stop, off=c * 128), KT[hl][q_, c * 64:(c + 1) * 64],
                                                                    AR[hp][q_, c * 128:(c + 1) * 128], start=True, stop=True),
                                           reads=ARKS[hl] + [("KT", hl)], writes=[bk(4 + hl)])
                        for hl, hp in enumerate(hps):
                            P.emit("dve", lambda e: e.tensor_tensor(out=AKT[hp][:, 0:nch * 128].rearrange("p (c s) -> p c s", s=128),
                                                                    in0=fb(4 + hl, nch * 128).rearrange("p (c s) -> p c s", s=128),
                                                                    in1=umask[:].unsqueeze(1).broadcast_to([128, nch, 128]), op=ALU.mult),
                                   reads=[bk(4 + hl), ("umask",)], writes=[("AKT", hp)])
                        Q0v = G3(ABTp[pr], WA, nch * 128, 128)[:, :, 0:64]
                        TTk = [("TT", hps[0]), ("TT", hps[1])]
                        P.emit("dve", lambda e: e.tensor_tensor(out=G3(TTp[pr], NM, N, 64), in0=Q0v,
                                                                in1=I2[:].unsqueeze(1).broadcast_to([128, ng, 64]), op=ALU.add),
                               reads=[("ABT", hps[0]), ("ABT", hps[1]), ("I2",)], writes=TTk)
                        P.emit("act", lambda e: e.activation(out=G3(QQw[0], NM, N, 64), in_=Q0v, func=AF.Copy),
                               reads=[("ABT", hps[0]), ("ABT", hps[1])], writes=[("QQw", 0)])
                        for lev in range(1, 6):
                            pi_, po_ = (lev - 1) % 2, lev % 2
                            Pp, Qp, Pn, Qn = PPw[pi_], QQw[pi_], PPw[po_], QQw[po_]
                            for hl in range(2):
                                for c in range(nch):
                                    for h2 in range(2):
                                        q_ = h2p(h2)
                                        o_ = hl * NM + c * 64
                                        P.emit("pe", lambda e: e.matmul(fb(2, 64, q_.start, q_.stop, off=o_), Qp[q_, o_:o_ + 64], Pp[q_, o_:o_ + 64],
                                                                        start=True, stop=True),
                                               reads=[("PPw", pi_), ("QQw", pi_)], writes=[bk(2)])
                            P.emit("act", lambda e: e.activation(out=G3(Pn, NM, N, 64), in_=G3(PSF[:, 2 * 512:3 * 512], NM, N, 64), func=AF.Copy),
                                   reads=[bk(2)], writes=[("PPw", po_)])
                            if lev < 5:
                                for hl in range(2):
                                    for c in range(nch):
                                        for h2 in range(2):
                                            q_ = h2p(h2)
                                            o_ = hl * NM + c * 64
                                            P.emit("pe", lambda e: e.matmul(fb(3, 64, q_.start, q_.stop, off=o_), Pp[q_, o_:o_ + 64], Qp[q_, o_:o_ + 64],
                                                                            start=True, stop=True),
                                                   reads=[("PPw", pi_), ("QQw", pi_)], writes=[bk(3)])
                                P.emit("dve", lambda e: e.tensor_copy(out=G3(Qn, NM, N, 64), in_=G3(PSF[:, 3 * 512:4 * 512], NM, N, 64)),
                                       reads=[bk(3)], writes=[("QQw", po_)])
                            for hl in range(2):
                                for c in range(nch):
                                    for h2 in range(2):
                                        q_ = h2p(h2)
                                        o_ = hl * NM + c * 64
                                        P.emit("pe", lambda e: e.matmul(fb(4, 64, q_.start, q_.stop, off=o_), Pn[q_, o_:o_ + 64], TTp[pr][q_, o_:o_ + 64],
                                                                        start=True, stop=True),
                                               reads=[("PPw", po_)] + TTk, writes=[bk(4)])
                            P.emit("dve", lambda e: e.tensor_tensor(out=G3(TTp[pr], NM, N, 64), in0=G3(PSF[:, 4 * 512:5 * 512], NM, N, 64),
                                                                    in1=G3(TTp[pr], NM, N, 64), op=ALU.add),
                                   reads=[bk(4)] + TTk, writes=TTk)

                    def lockstep(gens, hook_round=None):
                        gens = list(gens)
                        rnd = 0
                        while gens:
                            for g_ in list(gens):
                                try:
                                    next(g_)
                                except StopIteration:
                                    gens.remove(g_)
                            rnd += 1
                            if hook_round is not None and rnd == hook_round:
                                while wout_pending:
                                    wout_pending.pop(0)()
                    lockstep([chain(0), chain(1)], hook_round=3)
                    while wout_pending:
                        wout_pending.pop(0)()
                    pair_stage(0)
                    lockstep([chain(2), chain(3)])
                    pair_stage(1)
                    for c in range(nch):
                        vsl = slice((c * 3 + 2) * 64, (c * 3 + 3) * 64)
                        ksl = slice((c * 3 + 1) * 64, (c * 3 + 2) * 64)
                        bsl = slice((c * 3 + 0) * 64, (c * 3 + 1) * 64)
                        for hp in range(4):
                            ARK = [("AR", hp), ("AR2", hp)]
                            for h2 in range(2):
                                q_ = h2p(h2)
                                P.emit("pe", lambda e: e.matmul(fb(hp, 64, q_.start, q_.stop, off=0), AKT[hp][q_, c * 128:c * 128 + 64],
                                                                TM[hp][q_, vsl], start=True, stop=False),
                                       reads=[("AKT", hp), ("TM", hp)], writes=[bk(hp)])
                                P.emit("pe", lambda e: e.matmul(fb(hp, 64, q_.start, q_.stop, off=0), AR[hp][q_, c * 128:c * 128 + 64],
                                                                Sbf[hp][q_, :], start=False, stop=True),
                                       reads=ARK + [("Sbf", hp)], writes=[bk(hp)])
                        for hp in range(4):
                            P.emit("act", lambda e: e.activation(out=Xbf[hp][:], in_=fb(hp, 64, off=0), func=AF.Copy),
                                   reads=[bk(hp)], writes=[("Xbf", hp)])
                        for hp in range(4):
                            for h2 in range(2):
                                q_ = h2p(h2)
                                P.emit("pe", lambda e: e.matmul(fb(hp, 64, q_.start, q_.stop, off=64), TT[hp][q_, c * 64:(c + 1) * 64],
                                                                Xbf[hp][q_, :], start=True, stop=True),
                                       reads=[("TT", hp), ("Xbf", hp)], writes=[bk(hp)])
                        for hp in range(4):
                            P.emit("dve", lambda e: e.tensor_copy(out=Ubf[hp][:], in_=fb(hp, 64, off=64)),
                                   reads=[bk(hp)], writes=[("Ubf", hp)])
                        for hp in range(4):
                            ARK = [("AR", hp), ("AR2", hp)]
                            for h2 in range(2):
                                q_ = h2p(h2)
                                yo = fb(4 + hp // 2, 64, q_.start, q_.stop, off=(hp % 2) * 256 + c * 64)
                                P.emit("pe", lambda e: e.matmul(yo, AR[hp][q_, c * 128 + 64:c * 128 + 128], Sbf[hp][q_, :], start=True, stop=False),
                                       reads=ARK + [("Sbf", hp)], writes=[bk(4 + hp // 2)])
                                P.emit("pe", lambda e: e.matmul(yo, ABT[hp][q_, c * 128 + 64:c * 128 + 128], Ubf[hp][q_, :], start=False, stop=False),
                                       reads=[("ABT", hp), ("Ubf", hp)], writes=[bk(4 + hp // 2)])
                                P.emit("pe", lambda e: e.matmul(yo, AKT[hp][q_, c * 128 + 64:c * 128 + 128], TM[hp][q_, vsl], start=False, stop=True),
                                       reads=[("AKT", hp), ("TM", hp)], writes=[bk(4 + hp // 2)])
                        for hp in range(4):
                            for h2 in range(2):
                                q_ = h2p(h2)
                                so = fb(hp, 64, q_.start, q_.stop, off=128)
                                P.emit("pe", lambda e: e.matmul(so, TM[hp][q_, ksl], TM[hp][q_, vsl], start=True, stop=False),
                                       reads=[("TM", hp)], writes=[bk(hp)])
                                P.emit("pe", lambda e: e.matmul(so, TM[hp][q_, bsl], Ubf[hp][q_, :], start=False, stop=True),
                                       reads=[("TM", hp), ("Ubf", hp)], writes=[bk(hp)])
                        for hp in range(4):
                            P.emit("dve", lambda e: e.tensor_tensor(out=t1[hp][:], in0=fb(hp, 64, off=128), in1=S32[hp][:], op=ALU.add),
                                   reads=[bk(hp), ("S32", hp)], writes=[("t1", hp)])
                        for hp in range(4):
                            P.emit("act", lambda e: e.mul(out=Sbf[hp][:], in_=t1[hp][:], mul=PC[hp][:, c:c + 1]),
                                   reads=[("t1", hp), ("PC", hp)], writes=[("Sbf", hp)])
                            P.emit("dve", lambda e: e.tensor_scalar(out=S32[hp][:], in0=t1[hp][:], scalar1=PC[hp][:, c:c + 1], scalar2=None, op0=ALU.mult),
                                   reads=[("t1", hp), ("PC", hp)], writes=[("S32", hp)])
                    if not has_out:
                        continue
                    NW = 4 * N

                    def w16(ap):
                        return ap.rearrange("p (g t) -> p g t", t=64)

                    def w44(ap):
                        return ap.rearrange("p (h c t) -> p h c t", h=4, t=64)

                    def b16(ap):
                        return ap.unsqueeze(2).broadcast_to([128, 4 * nch, 64])
                    ykeys = [bk(4), bk(5)]
                    P.emit("act", lambda e: e.activation(out=Ysb[:, 0:NW], in_=PSF[:, 4 * 512:4 * 512 + NW], func=AF.Copy),
                           reads=ykeys, writes=KYsb)
                    P.emit("dve", lambda e: e.reduce_sum(out=st1[:, 0:16], in_=w16(Ysb[:, 0:NW]), axis=AX.X), reads=KYsb, writes=[("st1",)])
                    P.emit("act", lambda e: e.activation(out=ysq[:, 0:NW], in_=Ysb[:, 0:NW], func=AF.Square), reads=KYsb, writes=Kysq)
                    P.emit("dve", lambda e: e.reduce_sum(out=st2[:, 0:16], in_=w16(ysq[:, 0:NW]), axis=AX.X), reads=Kysq, writes=[("st2",)])
                    P.emit("dve", lambda e: e.tensor_scalar(out=mean[:, 0:16], in0=st1[:, 0:16], scalar1=1.0 / 64, scalar2=None, op0=ALU.mult),
                           reads=[("st1",)], writes=[("mean",)])
                    P.emit("dve", lambda e: e.tensor_tensor(out=msq[:, 0:16], in0=mean[:, 0:16], in1=mean[:, 0:16], op=ALU.mult),
                           reads=[("mean",)], writes=[("msq",)])
                    P.emit("dve", lambda e: e.scalar_tensor_tensor(out=var[:, 0:16], in0=st2[:, 0:16], scalar=1.0 / 64, in1=msq[:, 0:16],
                                                                   op0=ALU.mult, op1=ALU.subtract), reads=[("st2",), ("msq",)], writes=[("var",)])
                    P.emit("dve", lambda e: e.tensor_scalar(out=var[:, 0:16], in0=var[:, 0:16], scalar1=GN_EPS, scalar2=None, op0=ALU.add),
                           reads=[("var",)], writes=[("var",)])
                    P.emit("pool", lambda e: e.tensor_tensor(out=rstd[:, 0:16], in0=var[:, 0:16], in1=neghalf[:, 0:16], op=ALU.pow),
                           reads=[("var",)], writes=[("rstd",)])
                    P.emit("dve", lambda e: e.tensor_tensor(out=w16(yc[:, 0:NW]), in0=w16(Ysb[:, 0:NW]), in1=b16(mean[:, 0:16]), op=ALU.subtract),
                           reads=KYsb + [("mean",)], writes=Kyc)
                    V3w = TMw[:, 0:4 * nch * 192].rearrange("p (g a t) -> p g a t", a=3, t=64)[:, :, 2, :]
                    P.emit("dve", lambda e: e.tensor_tensor(out=w16(ysq[:, 0:NW]), in0=V3w, in1=b16(bonw[:, 0:16]), op=ALU.mult),
                           reads=[("TM", 0), ("TM", 1), ("TM", 2), ("TM", 3), ("bon", 0), ("bon", 1), ("bon", 2), ("bon", 3)], writes=Kysq)
                    P.emit("dve", lambda e: e.tensor_tensor(out=w16(yc[:, 0:NW]), in0=w16(yc[:, 0:NW]), in1=b16(rstd[:, 0:16]), op=ALU.mult),
                           reads=Kyc + [("rstd",)], writes=Kyc)
                    lnw4 = lnt[:, 0:256].rearrange("p (h i) -> p h i", h=4).unsqueeze(2).broadcast_to([128, 4, nch, 64])
                    lnb4 = lnt[:, 256:512].rearrange("p (h i) -> p h i", h=4).unsqueeze(2).broadcast_to([128, 4, nch, 64])
                    P.emit("dve", lambda e: e.tensor_tensor(out=w44(yc[:, 0:NW]), in0=w44(yc[:, 0:NW]), in1=lnw4, op=ALU.mult),
                           reads=Kyc + [("lnt",)], writes=Kyc)
                    P.emit("dve", lambda e: e.tensor_tensor(out=w44(ysq[:, 0:NW]), in0=w44(ysq[:, 0:NW]), in1=lnb4, op=ALU.add),
                           reads=Kysq + [("lnt",)], writes=Kysq)
                    P.emit("dve", lambda e: e.tensor_tensor(out=ytm[:, 0:NW], in0=yc[:, 0:NW], in1=ysq[:, 0:NW], op=ALU.add),
                           reads=Kyc + Kysq, writes=Kytm)
                    for hp in range(4):
                        for c in range(nch):
                            for h2 in range(2):
                                o_ = hp * N + c * 64
                                P.emit("pe", lambda e: e.transpose(PSB[h2p(h2), 1024 + o_:1024 + o_ + 64], ytm[h2p(h2), o_:o_ + 64],
                                                                   ident[h2p(h2), h2p(h2)]), reads=Kytm, writes=[("psb", 1)])
                    P.emit("dve", lambda e: e.tensor_tensor(out=yrT[:, :, :].rearrange("p h x -> p (h x)")[:, 0:NW], in0=PSB[:, 1024:1024 + NW], in1=gbfw[:, 0:NW], op=ALU.mult),
                           reads=[("psb", 1), ("gbf", 0), ("gbf", 1), ("gbf", 2), ("gbf", 3)], writes=[("yrT", 0), ("yrT", 1), ("yrT", 2), ("yrT", 3)])
                    def wout(gidx=gidx, N=N):
                        for t in range(N // 128):
                            i = gidx * (N // 128) + t
                            for dh in range(2):
                                for hp in range(4):
                                    P.emit("pe", lambda e: e.matmul(fb(dh), yrT[:, hp, t * 128:(t + 1) * 128], wo_r[:, hp, dh * 512:(dh + 1) * 512],
                                                                    start=(hp == 0), stop=(hp == 3)),
                                           reads=[("yrT", hp), ("wor",)], writes=[bk(dh)])
                            P.emit("dve", lambda e: e.tensor_tensor(out=htile(i), in0=PSF[:, 0:1024], in1=htile(i), op=ALU.add),
                                   reads=[bk(0), bk(1), ("h", i)], writes=[("h", i)])
                    wout_pending.append(wout)
                while wout_pending:
                    wout_pending.pop(0)()
                P.run()

        ffn_phase("f1", f1n, f1g, f1u, f1d, with_meta=True, first=True, last=False)
        mixer_phase()
        ffn_phase("f2", f2n, f2g, f2u, f2d, with_meta=False, first=False, last=True)
    return nc


_NC_CACHE = {}


def _prep_shared(inp):
    f = lambda a: np.ascontiguousarray(np.asarray(a, dtype=np.float32))
    sh = {}
    sh["meta_tokens"] = f(inp["meta_tokens"])
    for nme in ("ffn1_norm", "ffn2_norm", "mix_norm"):
        sh[nme] = f(inp[nme]).reshape(1, D)
    for nme in ("ffn1_gate", "ffn1_up", "ffn2_gate", "ffn2_up"):
        sh[nme] = f(inp[nme]).reshape(D, DFF)
    for nme in ("ffn1_down", "ffn2_down"):
        sh[nme] = f(inp[nme]).reshape(DFF, D)
    sh["w_in"] = f(inp["w_in"]).reshape(D, 3328)
    sh["w_out"] = f(inp["w_out"]).reshape(D, D)
    qn = f(inp["q_norm"]).reshape(64)
    kn = f(inp["k_norm"]).reshape(64)
    sh["qkg"] = f(np.stack([np.tile(qn, 2), np.tile(kn, 2)], axis=1))
    sh["lambda_vecs"] = f(inp["lambda_vecs"]).reshape(1, 256)
    sh["attn_out_norm"] = f(inp["attn_out_norm"]).reshape(1, 128)
    cols = [f(inp["rw_mu"]).reshape(14, 128).T]
    for nme in ("rw_w0", "rw_a0", "rw_k_k", "rw_k_a", "rw_r_k"):
        cols.append(f(inp[nme]).reshape(4, 128).T)
    sh["rwp"] = f(np.concatenate(cols, axis=1))
    sh["rw_w_up"] = f(inp["rw_w_up"]).reshape(64, 512)
    sh["rw_a_up"] = f(inp["rw_a_up"]).reshape(64, 512)
    sh["rw_g_up"] = f(inp["rw_g_up"]).reshape(128, 512)

    def lt(v):
        a = f(v).reshape(4, 2, 64)
        a = np.transpose(a, (1, 0, 2))
        a = np.repeat(a[:, None, :, :], 64, axis=1)
        return a.reshape(128, 256)
    sh["lnwb"] = f(np.concatenate([lt(inp["rw_ln_w"]), lt(inp["rw_ln_b"])], axis=1))
    return sh


def kernel(**inputs):
    x = np.asarray(inputs["x"], dtype=np.float32)
    B = x.shape[0]
    if "nc" not in _NC_CACHE:
        _NC_CACHE["nc"] = build_nc()
    nc = _NC_CACHE["nc"]
    sh = _prep_shared(inputs)
    in_maps = []
    for b in range(B):
        m = dict(sh)
        m["x"] = np.ascontiguousarray(x[b])
        in_maps.append(m)
    res = run_bass_kernel_spmd(nc, in_maps, core_ids=list(range(B)))
    return np.stack([np.asarray(r["y"], dtype=np.float32) for r in res.results], axis=0)
```

```python
import contextlib
import math
import numpy as np
import concourse.bass as bass
import concourse.mybir as mybir
from concourse.bass_utils import run_bass_kernel_spmd

F32 = mybir.dt.float32
BF16 = mybir.dt.bfloat16
AF = mybir.ActivationFunctionType
ALU = mybir.AluOpType
AX = mybir.AxisListType

D = 1024
SEQ = 2048
NMETA = 16
DFF = 2816
NF = DFF // 128
NTC = 2112
MC0 = 48
FC0 = 64
RMS_EPS = 1e-6
GN_EPS = 64e-5
LAM_INIT = 0.8 - 0.6 * math.exp(-0.3 * 0)
C0 = math.exp(-0.5)
SLOPES = [2.0 ** (-8.0 * (h + 1) / 4) for h in range(4)]

_uid = [0]


def _nm(s):
    _uid[0] += 1
    return f"{s}_{_uid[0]}"


class Op:
    __slots__ = ("eng", "idx", "fn", "deps", "dma", "sig", "val", "dsem", "dval")


class _Rec:
    def __init__(self):
        self.call = None

    def __getattr__(self, name):
        def f(*a, **k):
            self.call = (name, a, k)
            return None
        return f


class Plan:
    ENGS = ("pe", "act", "dve", "pool", "sp")

    def __init__(self, nc, st, tag, ndma=6):
        self.nc = nc
        self.q = {e: [] for e in self.ENGS}
        self.sem = {e: st.enter_context(nc.semaphore(_nm(f"s{tag}{e}"))) for e in self.ENGS}
        self.dsems = {e: [st.enter_context(nc.semaphore(_nm(f"d{tag}{e}"))) for _ in range(ndma)] for e in ("sp", "pool")}
        self.ndma = {"sp": 0, "pool": 0}
        self.lastw = {}
        self.readers = {}
        self.dmas = []

    def emit(self, eng, fn, reads=(), writes=(), dma=False):
        op = Op()
        rec = _Rec()
        fn(rec)
        name_, a_, k_ = rec.call
        fn = (lambda e, name_=name_, a_=a_, k_=k_: getattr(e, name_)(*a_, **k_))
        op.eng, op.fn, op.dma, op.sig, op.val = eng, fn, dma, False, 0
        op.idx = len(self.q[eng])
        deps = []
        for k in reads:
            w = self.lastw.get(k)
            if w is not None:
                deps.append(w)
        for k in writes:
            w = self.lastw.get(k)
            if w is not None:
                deps.append(w)
            deps.extend(self.readers.get(k, {}).values())
        best = {}
        dl = []
        for d in deps:
            if d is op:
                continue
            if d.dma:
                if d not in dl:
                    dl.append(d)
            else:
                if d.eng == eng and eng == "pe":
                    continue
                b = best.get(d.eng)
                if b is None or d.idx > b.idx:
                    best[d.eng] = d
        op.deps = dl + list(best.values())
        for d in op.deps:
            d.sig = True
        for k in reads:
            self.readers.setdefault(k, {})[(eng, op.idx if dma else -1)] = op
        for k in writes:
            self.lastw[k] = op
            self.readers[k] = {}
        if dma:
            n = self.ndma[eng]
            self.ndma[eng] = n + 1
            sems = self.dsems[eng]
            op.dsem = sems[n % len(sems)]
            op.dval = 16 * (n // len(sems) + 1)
            self.dmas.append(op)
        self.q[eng].append(op)
        return op

    def finish(self):
        op = Op()
        op.eng, op.fn, op.dma, op.sig, op.val = "sp", (lambda e: e.nop()), False, False, 0
        op.idx = len(self.q["sp"])
        last = {}
        for d in self.dmas:
            last[id(d.dsem)] = d
        op.deps = list(last.values())
        self.q["sp"].append(op)
        for eng in self.ENGS:
            c = 0
            for o in self.q[eng]:
                if o.sig and not o.dma:
                    c += 1
                    o.val = c

    def _replay(self, eng):
        def run(e):
            seen = {}
            for op in self.q[eng]:
                waits = []
                for d in op.deps:
                    if d.dma:
                        waits.append((d.dsem, d.dval))
                    else:
                        waits.append((self.sem[d.eng], d.val))
                if op.dma and op.dval > 16:
                    waits.append((op.dsem, op.dval - 16))
                for s, v in waits:
                    if seen.get(id(s), 0) < v:
                        seen[id(s)] = v
                        e.wait_ge(s, v)
                ins = op.fn(e)
                if op.dma:
                    ins.then_inc(op.dsem, 16)
                elif op.sig:
                    ins.then_inc(self.sem[eng], 1)
        return run

    def run(self):
        self.finish()
        with self.nc.Block() as block:
            block.tensor(self._replay("pe"))
            block.scalar(self._replay("act"))
            block.vector(self._replay("dve"))
            block.gpsimd(self._replay("pool"))
            block.sync(self._replay("sp"))


def bk(b):
    return ("ps", b)


def build_nc():
    nc = bass.Bass("TRN2", target_bir_lowering=False)

    def din(name, shape):
        return nc.dram_tensor(name, list(shape), F32, kind="ExternalInput").ap()

    x = din("x", [SEQ, D])
    meta = din("meta_tokens", [NMETA, D])
    f1n = din("ffn1_norm", [1, D]); f1g = din("ffn1_gate", [D, DFF]); f1u = din("ffn1_up", [D, DFF]); f1d = din("ffn1_down", [DFF, D])
    f2n = din("ffn2_norm", [1, D]); f2g = din("ffn2_gate", [D, DFF]); f2u = din("ffn2_up", [D, DFF]); f2d = din("ffn2_down", [DFF, D])
    mixn = din("mix_norm", [1, D])
    w_in = din("w_in", [D, 3328])
    w_out = din("w_out", [D, D])
    qkg = din("qkg", [128, 2])
    lamv = din("lambda_vecs", [1, 256])
    aon = din("attn_out_norm", [1, 128])
    rwp = din("rwp", [128, 34])
    rw_wup = din("rw_w_up", [64, 512]); rw_aup = din("rw_a_up", [64, 512]); rw_gup = din("rw_g_up", [128, 512])
    lnwb = din("lnwb", [128, 512])
    y = nc.dram_tensor("y", [SEQ, D], F32, kind="ExternalOutput").ap()

    with contextlib.ExitStack() as top:
        def sbT(name, shape, dt):
            return top.enter_context(nc.sbuf_tensor(name, list(shape), dt))
        hres = sbT("hres", [128, 16 * D], F32)
        hmeta = sbT("hmeta", [16, D], F32)
        ident = sbT("ident", [128, 128], BF16)
        BD = sbT("BD", [128, 128], BF16)
        neghalf = sbT("neghalf", [128, 32], F32)
        PSF = top.enter_context(nc.psum_tensor("PSF", [128, 6 * 512], F32))
        PSB = top.enter_context(nc.psum_tensor("PSB", [128, 2 * 1024], BF16))

        def fb(b, n=512, p0=0, p1=128, off=0):
            return PSF[p0:p1, b * 512 + off: b * 512 + off + n]

        def htile(i):
            return hres[:, i * D:(i + 1) * D]

        def emit_norm_T(P, st, tag, srcs, gain_ap, dstT3, dkey, cache):
            n = len(srcs)
            if "gbc" not in cache:
                cache["gbc"] = st.enter_context(nc.sbuf_tensor(_nm(tag + "gbc"), [128, D], F32))
                cache["junk"] = st.enter_context(nc.sbuf_tensor(_nm(tag + "junk"), [128, D], BF16))
                cache["xn"] = [st.enter_context(nc.sbuf_tensor(_nm(tag + "xn"), [128, D], BF16)) for _ in range(2)]
                cache["ss"] = st.enter_context(nc.sbuf_tensor(_nm(tag + "ss"), [128, 32], F32))
                cache["ms"] = st.enter_context(nc.sbuf_tensor(_nm(tag + "ms"), [128, 32], F32))
                cache["rstd"] = st.enter_context(nc.sbuf_tensor(_nm(tag + "rstd"), [128, 32], F32))
                gbc0 = cache["gbc"]
                P.emit("sp", lambda e: e.dma_start(out=gbc0[:], in_=gain_ap[0:1, :].partition_broadcast(128)), writes=[("ngbc",)], dma=True)
            gbc, junk, xn, ss, ms, rstd = cache["gbc"], cache["junk"], cache["xn"], cache["ss"], cache["ms"], cache["rstd"]
            kg, kss, kms, krs = ("ngbc",), ("nss",), ("nms",), ("nrstd",)

            def stats(sub, base):
                P.emit("pool", lambda e: e.memset(ss[:, 0:len(sub)], 1.0), writes=[kss])
                for i, (src, np_, col, skey) in enumerate(sub):
                    P.emit("act", lambda e, src=src, np_=np_, i=i: e.activation(out=junk[:np_, :], in_=src, func=AF.Square,
                                                                                accum_out=ss[:np_, i:i + 1]),
                           reads=[skey, kss], writes=[("nssc", i), ("njunk",)])
                m = len(sub)
                P.emit("dve", lambda e: e.tensor_scalar(out=ms[:, 0:m], in0=ss[:, 0:m], scalar1=1.0 / D, scalar2=RMS_EPS,
                                                        op0=ALU.mult, op1=ALU.add), reads=[kss] + [("nssc", i) for i in range(m)], writes=[kms])
                P.emit("pool", lambda e: e.tensor_tensor(out=rstd[:, 0:m], in0=ms[:, 0:m], in1=neghalf[:, 0:m], op=ALU.pow),
                       reads=[kms], writes=[krs])
                for i, (src, np_, col, skey) in enumerate(sub):
                    xb = xn[(base + i) % 2]
                    kx = ("nxn", (base + i) % 2)
                    pb = (base + i) % 2
                    P.emit("dve", lambda e, src=src, np_=np_, i=i, xb=xb: e.scalar_tensor_tensor(
                        out=xb[:np_, :], in0=src, scalar=rstd[:np_, i:i + 1], in1=gbc[:np_, :], op0=ALU.mult, op1=ALU.mult),
                        reads=[skey, krs, kg], writes=[kx])
                    for k in range(8):
                        P.emit("pe", lambda e, k=k, np_=np_, xb=xb, pb=pb: e.transpose(
                            PSB[:, pb * 1024 + k * 128: pb * 1024 + k * 128 + np_], xb[:np_, k * 128:(k + 1) * 128], ident[:np_, :np_]),
                            reads=[kx], writes=[("psb", pb)])
                    P.emit("act", lambda e, np_=np_, col=col, pb=pb: e.activation(
                        out=dstT3[:, :, col:col + np_],
                        in_=PSB[:, pb * 1024:(pb + 1) * 1024].rearrange("p (k t) -> p k t", k=8)[:, :, 0:np_], func=AF.Copy),
                        reads=[("psb", pb)], writes=[(dkey, col)])
            for b0 in range(0, n, 16):
                stats(srcs[b0:b0 + 16], b0)

        def ffn_phase(tag, gain_ap, wg_ap, wu_ap, wd_ap, with_meta, first, last):
            with contextlib.ExitStack() as st:
                P = Plan(nc, st, tag)

                def sb(name, shape, dt):
                    return st.enter_context(nc.sbuf_tensor(_nm(tag + name), list(shape), dt))
                W = 1040
                xnT = sb("xnT", [128, 8, W], BF16)
                h1T = sb("h1T", [128, NF, W], BF16)
                wd_sb = sb("wd", [128, NF, D], BF16)
                wg_sb = [sb("wg", [128, 8, 128], BF16) for _ in range(3)]
                wu_sb = [sb("wu", [128, 8, 128], BF16) for _ in range(3)]
                sg = [sb("sg", [128, 512], BF16) for _ in range(2)]
                wgv = wg_ap.rearrange("(kc p) f -> p kc f", p=128)
                wuv = wu_ap.rearrange("(kc p) f -> p kc f", p=128)
                wdv = wd_ap.rearrange("(fc p) d -> p fc d", p=128)

                if first:
                    for i in range(16):
                        P.emit("sp", lambda e, i=i: e.dma_start(out=htile(i), in_=x[i * 128:(i + 1) * 128, :]), writes=[("h", i)], dma=True)
                    P.emit("sp", lambda e: e.dma_start(out=hmeta[:], in_=meta[:, :]), writes=[("hm",)], dma=True)
                    P.emit("pool", lambda e: e.memset(ident[:], 0.0), writes=[("ident",)])
                    P.emit("pool", lambda e: e.affine_select(out=ident[:], in_=ident[:], pattern=[[-1, 128]], compare_op=ALU.not_equal,
                                                             fill=1.0, base=0, channel_multiplier=1), writes=[("ident",)])
                    P.emit("pool", lambda e: e.memset(BD[:], 0.0), writes=[("BD",)])
                    P.emit("pool", lambda e: e.memset(BD[0:64, 0:64], 1.0), writes=[("BD",)])
                    P.emit("pool", lambda e: e.memset(BD[64:128, 64:128], 1.0), writes=[("BD",)])
                    P.emit("pool", lambda e: e.memset(neghalf[:], -0.5), writes=[("neghalf",)])

                def wdma(f):
                    s = f % 3
                    P.emit("pool", lambda e: e.dma_start(out=wg_sb[s][:], in_=wgv[:, :, f * 128:(f + 1) * 128]), writes=[("wg", s)], dma=True)
                    P.emit("pool", lambda e: e.dma_start(out=wu_sb[s][:], in_=wuv[:, :, f * 128:(f + 1) * 128]), writes=[("wu", s)], dma=True)

                def wd_dma(j):
                    P.emit("pool", lambda e: e.dma_start(out=wd_sb[:, 2 * j:2 * j + 2, :], in_=wdv[:, 2 * j:2 * j + 2, :]),
                           writes=[("wd", 2 * j), ("wd", 2 * j + 1)], dma=True)

                unit = [0]
                dunit = [0]
                ncache = {}
                for p in range(2):
                    tiles = [(htile(8 * p + t), 128, t * 128, ("h", 8 * p + t)) for t in range(8)]
                    blocks = [(0, 512), (512, 512)]
                    if with_meta and p == 0:
                        tiles.append((hmeta[0:16, :], 16, 1024, ("hm",)))
                        blocks.append((1024, 16))
                    if p == 0:
                        for f in range(3):
                            wdma(f)
                    emit_norm_T(P, st, tag + "n", tiles, gain_ap, xnT, "xnT", ncache)
                    if p == 1:
                        for f in range(3):
                            wdma(f)
                    for f in range(NF):
                        s = f % 3
                        for (c0, N) in blocks:
                            u = unit[0] % 3
                            unit[0] += 1
                            sgi = unit[0] % 2
                            xk = [("xnT", c0 + t * 128) for t in range((N + 127) // 128)]
                            for k in range(8):
                                P.emit("pe", lambda e, k=k, u=u, c0=c0, N=N, s=s: e.matmul(
                                    fb(2 * u, N), wg_sb[s][:, k, :], xnT[:, k, c0:c0 + N], start=(k == 0), stop=(k == 7)),
                                    reads=[("wg", s)] + xk, writes=[bk(2 * u)])
                            for k in range(8):
                                P.emit("pe", lambda e, k=k, u=u, c0=c0, N=N, s=s: e.matmul(
                                    fb(2 * u + 1, N), wu_sb[s][:, k, :], xnT[:, k, c0:c0 + N], start=(k == 0), stop=(k == 7)),
                                    reads=[("wu", s)] + xk, writes=[bk(2 * u + 1)])
                            P.emit("act", lambda e, u=u, N=N, sgi=sgi: e.activation(out=sg[sgi][:, 0:N], in_=fb(2 * u, N), func=AF.Silu),
                                   reads=[bk(2 * u)], writes=[("sg", sgi)])
                            P.emit("dve", lambda e, u=u, N=N, sgi=sgi, f=f, c0=c0: e.tensor_tensor(
                                out=h1T[:, f, c0:c0 + N], in0=fb(2 * u + 1, N), in1=sg[sgi][:, 0:N], op=ALU.mult),
                                reads=[bk(2 * u + 1), ("sg", sgi)], writes=[("h1T", f, c0 + t * 128) for t in range((N + 127) // 128)])
                        if f + 3 < NF:
                            wdma(f + 3)
                        if p == 0 and f < 11:
                            wd_dma(f)
                    for (src, np_, col, skey) in tiles:
                        u = dunit[0] % 3
                        dunit[0] += 1
                        for dh in range(2):
                            for f in range(NF):
                                P.emit("pe", lambda e, u=u, dh=dh, f=f, np_=np_, col=col: e.matmul(
                                    fb(2 * u + dh, 512, 0, np_), h1T[:, f, col:col + np_], wd_sb[:, f, dh * 512:(dh + 1) * 512],
                                    start=(f == 0), stop=(f == NF - 1)),
                                    reads=[("h1T", f, col), ("wd", f)], writes=[bk(2 * u + dh)])
                        P.emit("dve", lambda e, u=u, np_=np_, src=src: e.scalar_tensor_tensor(
                            out=src, in0=PSF[0:np_, 2 * u * 512: 2 * u * 512 + 1024], scalar=0.5, in1=src, op0=ALU.mult, op1=ALU.add),
                            reads=[bk(2 * u), bk(2 * u + 1), skey], writes=[skey])
                        if last:
                            ti = skey[1]
                            P.emit("sp", lambda e, ti=ti: e.dma_start(out=y[ti * 128:(ti + 1) * 128, :], in_=htile(ti)),
                                   reads=[skey], dma=True)
                P.run()

        def mixer_phase():
            with contextlib.ExitStack() as mst:
                uT = mst.enter_context(nc.sbuf_tensor("uT", [128, 8, NTC], BF16))
                with contextlib.ExitStack() as st:
                    P = Plan(nc, st, "m0")
                    P.emit("pool", lambda e: e.memset(uT[:, :, 0:MC0], 0.0), writes=[("uT", 0)])
                    srcs = [(hmeta[0:16, :], 16, MC0, ("hm",))] + [(htile(i), 128, FC0 + 128 * i, ("h", i)) for i in range(16)]
                    emit_norm_T(P, st, "m0n", srcs, mixn, uT, "uT", {})
                    P.run()
                attention_phase(uT)
                rwkv_phase(uT)

        def attention_phase(uT):
            with contextlib.ExitStack() as st:
                P = Plan(nc, st, "at")

                def sb(name, shape, dt):
                    return st.enter_context(nc.sbuf_tensor(_nm("at" + name), list(shape), dt))
                yaT = sb("yaT", [128, 4, SEQ], BF16)
                wo_a = sb("woa", [128, 4, D], BF16)
                wq = [sb("wq", [128, 8, 128], BF16) for _ in range(2)]
                wk = [sb("wk", [128, 8, 128], BF16) for _ in range(2)]
                wv = [sb("wv", [128, 8, 128], BF16) for _ in range(2)]
                qh = [sb("qh", [128, SEQ], BF16) for _ in range(2)]
                kh = [sb("kh", [128, NTC], BF16) for _ in range(2)]
                vt = [sb("vt", [128, 17, 129], BF16) for _ in range(2)]
                EE = [sb("EE", [128, 2048], BF16) for _ in range(2)]
                basef = sb("basef", [128, 2048], F32)
                absb = sb("absb", [128, 128], F32)
                vis = sb("vis", [128, 128], BF16)
                edt = sb("edt", [128, 128], BF16)
                gm = [sb("gm", [16, 128], BF16) for _ in range(2)]
                pt = [sb("pt", [128, 512], BF16) for _ in range(4)]
                ptm = [sb("ptm", [16, 128], BF16) for _ in range(2)]
                sqb = [sb("sqb", [128, 512], BF16) for _ in range(2)]
                msb = [sb("msb", [128, 512], F32) for _ in range(2)]
                rsb = [sb("rsb", [128, 512], F32) for _ in range(2)]
                gqk = sb("gqk", [128, 2], F32)
                gmb = sb("gmb", [16, 64], F32)
                epsb = sb("epsb", [128, 1], F32)
                gq8 = sb("gq8", [128, 1], F32)
                lv = sb("lv", [128, 256], F32)
                lvt = sb("lvt", [128, 128], F32)
                dd = sb("dd", [128, 2], F32)
                ed = sb("ed", [128, 2], F32)
                neglam = sb("neglam", [128, 1], F32)
                ogb = sb("ogb", [128, 128], F32)
                rz = sb("rz", [128, 2], F32)
                s1 = sb("s1", [128, 1], F32)
                y0 = sb("y0", [128, 128], F32)
                yy = sb("yy", [128, 128], F32)
                junk = sb("junk", [128, 128], BF16)
                ssq = sb("ssq", [128, 1], F32)
                msq = sb("msq", [128, 1], F32)
                rsq = sb("rsq", [128, 1], F32)
                yn = [sb("yn", [128, 128], BF16) for _ in range(2)]
                winv = w_in.rearrange("(kc p) f -> p kc f", p=128)
                wov = w_out.rearrange("(kc p) d -> p kc d", p=128)

                P.emit("sp", lambda e: e.dma_start(out=gqk[:], in_=qkg[:, :]), writes=[("gqk",)], dma=True)
                P.emit("sp", lambda e: e.dma_start(out=lv[:], in_=lamv[0:1, :].partition_broadcast(128)), writes=[("lv",)], dma=True)
                P.emit("sp", lambda e: e.dma_start(out=ogb[:], in_=aon[0:1, :].partition_broadcast(128)), writes=[("ogb",)], dma=True)
                P.emit("pool", lambda e: e.dma_start(out=wo_a[:], in_=wov[:, 0:4, :]), writes=[("woa",)], dma=True)
                P.emit("dve", lambda e: e.tensor_scalar(out=gq8[:], in0=gqk[:, 0:1], scalar1=0.125, scalar2=None, op0=ALU.mult),
                       reads=[("gqk",)], writes=[("gq8",)])
                P.emit("dve", lambda e: e.tensor_scalar(out=ogb[:], in0=ogb[:], scalar1=1.0 - LAM_INIT, scalar2=None, op0=ALU.mult),
                       reads=[("ogb",)], writes=[("ogb",)])
                lv4 = lv[:].rearrange("p (a b d) -> p a b d", a=2, b=2)
                P.emit("dve", lambda e: e.tensor_tensor(out=lvt[:].rearrange("p (a d) -> p a d", a=2), in0=lv4[:, :, 0, :], in1=lv4[:, :, 1, :],
                                                        op=ALU.mult), reads=[("lv",)], writes=[("lvt",)])
                P.emit("dve", lambda e: e.reduce_sum(out=dd[:], in_=lvt[:].rearrange("p (a d) -> p a d", a=2), axis=AX.X),
                       reads=[("lvt",)], writes=[("dd",)])
                P.emit("act", lambda e: e.activation(out=ed[:], in_=dd[:], func=AF.Exp), reads=[("dd",)], writes=[("ed",)])
                P.emit("dve", lambda e: e.tensor_tensor(out=s1[:], in0=ed[:, 0:1], in1=ed[:, 1:2], op=ALU.subtract),
                       reads=[("ed",)], writes=[("s1",)])
                P.emit("dve", lambda e: e.tensor_scalar(out=neglam[:], in0=s1[:], scalar1=LAM_INIT, scalar2=-1.0, op0=ALU.add, op1=ALU.mult),
                       reads=[("s1",)], writes=[("neglam",)])
                P.emit("pool", lambda e: e.iota(basef[:], pattern=[[1, 2048]], base=0, channel_multiplier=-1,
                                                allow_small_or_imprecise_dtypes=True), writes=[("basef",)])
                P.emit("dve", lambda e: e.tensor_scalar(out=y0[:], in0=basef[:, 0:128], scalar1=-1.0, scalar2=None, op0=ALU.mult),
                       reads=[("basef",)], writes=[("y0",)])
                P.emit("dve", lambda e: e.tensor_tensor(out=absb[:], in0=basef[:, 0:128], in1=y0[:], op=ALU.max),
                       reads=[("basef",), ("y0",)], writes=[("absb",)])
                P.emit("pool", lambda e: e.memset(vis[:], 1.0), writes=[("vis",)])
                for hh_ in range(4):
                    P.emit("pool", lambda e: e.iota(gmb[:, hh_ * 16:(hh_ + 1) * 16], pattern=[[128, 16]], base=16, channel_multiplier=0,
                                                    allow_small_or_imprecise_dtypes=True), writes=[("gmb",)])
                    P.emit("pool", lambda e: e.tensor_scalar(out=gmb[:, hh_ * 16:(hh_ + 1) * 16], in0=gmb[:, hh_ * 16:(hh_ + 1) * 16],
                                                             scalar1=-SLOPES[hh_], scalar2=None, op0=ALU.mult), writes=[("gmb",)])
                P.emit("pool", lambda e: e.memset(epsb[:], RMS_EPS), writes=[("epsb",)])
                P.emit("pool", lambda e: e.memset(vis[64:128, 0:64], 0.0), writes=[("vis",)])
                for b in range(2):
                    P.emit("pool", lambda e, b=b: e.memset(vt[b][:, :, 128:129], 1.0), writes=[("vt", b)])

                def wdma_head(h):
                    s = h % 2
                    P.emit("pool", lambda e: e.dma_start(out=wq[s][:], in_=winv[:, :, h * 128:(h + 1) * 128]), writes=[("wq", s)], dma=True)
                    P.emit("pool", lambda e: e.dma_start(out=wk[s][:], in_=winv[:, :, 512 + h * 128:512 + (h + 1) * 128]), writes=[("wk", s)], dma=True)
                    P.emit("pool", lambda e: e.dma_start(out=wv[s][:], in_=winv[:, :, 1024 + h * 128:1024 + (h + 1) * 128]), writes=[("wv", s)], dma=True)

                ctr = {"ps": 0, "nb": 0, "o": 0, "pt": 0, "ptm": 0, "yn": 0, "pb": 0}

                def proj_norm(wsb, wkey, c0, N, gain, dst, dkey):
                    b = ctr["ps"] % 3
                    ctr["ps"] += 1
                    nb = ctr["nb"] % 2
                    ctr["nb"] += 1
                    uk = [("uT", c) for c in ([MC0] if c0 == 0 else [])] + [("uT", c0 + t * 128) for t in range(N // 128)] + [("uT", 0)]
                    for k in range(8):
                        P.emit("pe", lambda e, k=k: e.matmul(fb(b, N), wsb[:, k, :], uT[:, k, c0:c0 + N], start=(k == 0), stop=(k == 7)),
                               reads=[wkey] + uk, writes=[bk(b)])
                    P.emit("act", lambda e: e.activation(out=sqb[nb][:, 0:N], in_=fb(b, N), func=AF.Square), reads=[bk(b)], writes=[("sqb", nb)])
                    P.emit("pe", lambda e: e.matmul(fb(5, N), BD[:], sqb[nb][:, 0:N], start=True, stop=True), reads=[("sqb", nb)], writes=[bk(5)])
                    P.emit("act", lambda e: e.activation(out=msb[nb][:, 0:N], in_=fb(5, N), func=AF.Ln, scale=1.0 / 64, bias=epsb[:, 0:1]),
                           reads=[bk(5), ("epsb",)], writes=[("msb", nb)])
                    P.emit("act", lambda e: e.activation(out=rsb[nb][:, 0:N], in_=msb[nb][:, 0:N], func=AF.Exp, scale=-0.5),
                           reads=[("msb", nb)], writes=[("rsb", nb)])
                    P.emit("dve", lambda e: e.scalar_tensor_tensor(out=dst, in0=fb(b, N), scalar=gain, in1=rsb[nb][:, 0:N],
                                                                   op0=ALU.mult, op1=ALU.mult),
                           reads=[bk(b), ("rsb", nb), ("gq8",), ("gqk",)], writes=[dkey])

                wdma_head(0)
                def proj_head(h):
                    s = h % 2
                    slope = SLOPES[h]
                    P.emit("act", lambda e: e.activation(out=EE[s][:], in_=basef[:], func=AF.Exp, scale=-slope), reads=[("basef",)], writes=[("EE", s)])
                    P.emit("act", lambda e: e.activation(out=edt[:], in_=absb[:], func=AF.Exp, scale=-slope), reads=[("absb",)], writes=[("edt",)])
                    P.emit("dve", lambda e: e.tensor_tensor(out=EE[s][:, 0:128], in0=edt[:], in1=vis[:], op=ALU.mult),
                           reads=[("edt",), ("vis",)], writes=[("EE", s)])
                    for g in range(4):
                        proj_norm(wq[s], ("wq", s), FC0 + 512 * g, 512, gq8[:, 0:1], qh[s][:, g * 512:(g + 1) * 512], ("qh", s, g))
                        yield
                    proj_norm(wk[s], ("wk", s), 0, 64, gqk[:, 1:2], kh[s][:, 0:64], ("kh", s, 0))
                    yield
                    for g in range(4):
                        proj_norm(wk[s], ("wk", s), FC0 + 512 * g, 512, gqk[:, 1:2], kh[s][:, FC0 + 512 * g:FC0 + 512 * (g + 1)], ("kh", s, g + 1))
                        yield
                    for t0 in range(0, 16, 4):
                        b = ctr["ps"] % 3
                        ctr["ps"] += 1
                        for t in range(4):
                            col = FC0 + (t0 + t) * 128
                            for k in range(8):
                                P.emit("pe", lambda e, k=k, t=t, col=col: e.matmul(fb(b, 128, off=t * 128), uT[:, k, col:col + 128], wv[s][:, k, :],
                                                                                   start=(k == 0), stop=(k == 7)),
                                       reads=[("wv", s), ("uT", col)], writes=[bk(b)])
                        P.emit("act", lambda e, t0=t0, b=b: e.activation(out=vt[s][:, t0:t0 + 4, 0:128],
                                                                         in_=fb(b).rearrange("p (t c) -> p t c", t=4), func=AF.Copy),
                               reads=[bk(b)], writes=[("vt", s)])
                        yield
                    b = ctr["ps"] % 3
                    ctr["ps"] += 1
                    for k in range(8):
                        P.emit("pe", lambda e, k=k, b=b: e.matmul(fb(b, 128, 0, 16), uT[:, k, MC0:MC0 + 16], wv[s][:, k, :], start=(k == 0), stop=(k == 7)),
                               reads=[("wv", s), ("uT", MC0)], writes=[bk(b)])
                    P.emit("act", lambda e, b=b: e.activation(out=vt[s][0:16, 16, 0:128], in_=fb(b, 128, 0, 16), func=AF.Copy),
                           reads=[bk(b)], writes=[("vt", s)])
                    yield

                for _ in proj_head(0):
                    pass
                for h in range(4):
                    s = h % 2
                    slope = SLOPES[h]
                    nxt = None
                    if h + 1 < 4:
                        wdma_head(h + 1)
                        nxt = proj_head(h + 1)
                    pend = []
                    late = []

                    def flush(n=2):
                        while len(pend) > n:
                            pend.pop(0)()

                    for qi in range(16):
                        gi = ctr["ptm"] % 2
                        P.emit("act", lambda e: e.activation(out=gm[gi][:], in_=basef[0:16, 0:128], func=AF.Exp, scale=-slope,
                                                             bias=gmb[0:16, h * 16 + qi:h * 16 + qi + 1]), reads=[("basef",), ("gmb",)], writes=[("gm", gi)])
                        ob = 3 + (ctr["o"] % 2)
                        ctr["o"] += 1
                        qk_ = ("qh", s, qi // 4)
                        for c in range(2):
                            cp = slice(c * 64, (c + 1) * 64)
                            nkt = qi + 1
                            first = [True]
                            for b0 in range(0, nkt, 4):
                                jjs = list(range(b0, min(b0 + 4, nkt)))
                                nbk = len(jjs)
                                b = ctr["ps"] % 3
                                ctr["ps"] += 1
                                pi = ctr["pt"] % 4
                                ctr["pt"] += 1
                                for m, jj in enumerate(jjs):
                                    j = qi - jj
                                    P.emit("pe", lambda e: e.matmul(
                                        fb(b, 128, off=m * 128), kh[s][cp, FC0 + 128 * j:FC0 + 128 * (j + 1)], qh[s][cp, qi * 128:(qi + 1) * 128],
                                        start=True, stop=True),
                                        reads=[("kh", s, 1 + j // 4), qk_], writes=[bk(b)])
                                P.emit("act", lambda e: e.activation(out=pt[pi][:, 0:nbk * 128], in_=fb(b, nbk * 128), func=AF.Exp),
                                       reads=[bk(b)], writes=[("pt", pi)])
                                P.emit("dve", lambda e: e.tensor_tensor(
                                    out=pt[pi][:, 0:nbk * 128], in0=pt[pi][:, 0:nbk * 128], in1=EE[s][:, b0 * 128:(b0 + nbk) * 128], op=ALU.mult),
                                    reads=[("pt", pi), ("EE", s)], writes=[("pt", pi)])

                                def pv(jjs=jjs, pi=pi, first=first, ob=ob, c=c, qi=qi):
                                    for m, jj in enumerate(jjs):
                                        j = qi - jj
                                        P.emit("pe", lambda e: e.matmul(
                                            fb(ob, 129, off=c * 129), pt[pi][:, m * 128:(m + 1) * 128], vt[s][:, j, :], start=first[0], stop=False),
                                            reads=[("pt", pi), ("vt", s)], writes=[bk(ob)])
                                        first[0] = False
                                flush()
                                pend.append(pv)
                            b = ctr["ps"] % 3
                            ctr["ps"] += 1
                            mi = ctr["ptm"] % 2
                            ctr["ptm"] += 1
                            P.emit("pe", lambda e: e.matmul(fb(b, 128, 0, 16), kh[s][cp, MC0:MC0 + 16], qh[s][cp, qi * 128:(qi + 1) * 128],
                                                            start=True, stop=True), reads=[("kh", s, 0), qk_], writes=[bk(b)])
                            P.emit("act", lambda e: e.activation(out=ptm[mi][:], in_=fb(b, 128, 0, 16), func=AF.Exp),
                                   reads=[bk(b)], writes=[("ptm", mi)])
                            P.emit("dve", lambda e: e.tensor_tensor(out=ptm[mi][:], in0=ptm[mi][:], in1=gm[gi][:], op=ALU.mult),
                                   reads=[("ptm", mi), ("gm", gi)], writes=[("ptm", mi)])

                            def pvm(mi=mi, ob=ob, c=c):
                                P.emit("pe", lambda e: e.matmul(fb(ob, 129, off=c * 129), ptm[mi][0:16, :], vt[s][0:16, 16, :], start=False, stop=True),
                                       reads=[("ptm", mi), ("vt", s)], writes=[bk(ob)])
                            flush()
                            pend.append(pvm)

                        def finalize(ob=ob, qi=qi):
                            O3 = fb(ob, 258).rearrange("p (c e) -> p c e", c=2)
                            P.emit("dve", lambda e: e.reciprocal(out=rz[:].rearrange("p (c o) -> p c o", o=1), in_=O3[:, :, 128:129]),
                                   reads=[bk(ob)], writes=[("rz",)])
                            P.emit("dve", lambda e: e.tensor_tensor(out=s1[:], in0=rz[:, 1:2], in1=neglam[:], op=ALU.mult),
                                   reads=[("rz",), ("neglam",)], writes=[("s1",)])
                            P.emit("dve", lambda e: e.tensor_scalar(out=y0[:], in0=O3[:, 0, 0:128], scalar1=rz[:, 0:1], scalar2=None, op0=ALU.mult),
                                   reads=[bk(ob), ("rz",)], writes=[("y0",)])
                            P.emit("dve", lambda e: e.scalar_tensor_tensor(out=yy[:], in0=O3[:, 1, 0:128], scalar=s1[:, 0:1], in1=y0[:],
                                                                           op0=ALU.mult, op1=ALU.add),
                                   reads=[bk(ob), ("s1",), ("y0",)], writes=[("yy",)])
                            P.emit("act", lambda e: e.activation(out=junk[:], in_=yy[:], func=AF.Square, accum_out=ssq[:]),
                                   reads=[("yy",)], writes=[("junk",), ("ssq",)])
                            P.emit("dve", lambda e: e.tensor_scalar(out=msq[:], in0=ssq[:], scalar1=1.0 / 128, scalar2=RMS_EPS, op0=ALU.mult, op1=ALU.add),
                                   reads=[("ssq",)], writes=[("msq",)])
                            P.emit("pool", lambda e: e.tensor_tensor(out=rsq[:], in0=msq[:], in1=neghalf[:, 0:1], op=ALU.pow),
                                   reads=[("msq",)], writes=[("rsq",)])
                            yi = ctr["yn"] % 2
                            ctr["yn"] += 1
                            P.emit("dve", lambda e: e.scalar_tensor_tensor(out=yn[yi][:], in0=yy[:], scalar=rsq[:, 0:1], in1=ogb[:],
                                                                           op0=ALU.mult, op1=ALU.mult),
                                   reads=[("yy",), ("rsq",), ("ogb",)], writes=[("yn", yi)])
                            pb = ctr["pb"] % 2
                            ctr["pb"] += 1

                            def tr(yi=yi, pb=pb, qi=qi):
                                P.emit("pe", lambda e: e.transpose(PSB[:, pb * 1024:pb * 1024 + 128], yn[yi][:], ident[:]),
                                       reads=[("yn", yi)], writes=[("psb", pb)])
                                P.emit("act", lambda e: e.activation(out=yaT[:, h, qi * 128:(qi + 1) * 128], in_=PSB[:, pb * 1024:pb * 1024 + 128],
                                                                     func=AF.Copy), reads=[("psb", pb)], writes=[("yaT", qi)])
                            late.append(tr)
                        while late:
                            late.pop(0)()
                        pend.append(finalize)
                        if nxt is not None:
                            next(nxt, None)
                    flush(0)
                    while late:
                        late.pop(0)()
                    if nxt is not None:
                        for _ in nxt:
                            pass
                for i in range(16):
                    u = i % 2
                    for dh in range(2):
                        for kc in range(4):
                            P.emit("pe", lambda e, u=u, dh=dh, kc=kc, i=i: e.matmul(
                                fb(2 * u + dh), yaT[:, kc, i * 128:(i + 1) * 128], wo_a[:, kc, dh * 512:(dh + 1) * 512], start=(kc == 0), stop=(kc == 3)),
                                reads=[("yaT", i), ("woa",)], writes=[bk(2 * u + dh)])
                    P.emit("dve", lambda e, u=u, i=i: e.tensor_tensor(out=htile(i), in0=PSF[:, 2 * u * 512:2 * u * 512 + 1024], in1=htile(i), op=ALU.add),
                           reads=[bk(2 * u), bk(2 * u + 1), ("h", i)], writes=[("h", i)])
                P.run()

        def rwkv_phase(uT):
            with contextlib.ExitStack() as st:
                P = Plan(nc, st, "rw")

                def sb(name, shape, dt):
                    return st.enter_context(nc.sbuf_tensor(_nm("rw" + name), list(shape), dt))
                NM = 256
                NCH = 4
                wo_r = sb("wor", [128, 4, D], BF16)
                wrw = [sb("wrw", [128, 8, 128], BF16) for _ in range(4)]
                waup = sb("waup", [128, 512], BF16)
                gup = sb("gup", [128, 512], BF16)
                pp = sb("pp", [128, 34], F32)
                lnt = sb("lnt", [128, 512], F32)
                prevl = sb("prevl", [128, 14], F32)
                pbuf = [sb("pbuf", [128, NM + 1], F32) for _ in range(2)]
                dtmp = [sb("dtmp", [128, NM], F32) for _ in range(2)]
                lp = sb("lp", [128, NM], F32)
                lo24 = sb("lo24", [128, NM], BF16)
                sxg = sb("sxg", [128, NM], BF16)
                k32 = [sb("k32", [128, NM], F32) for _ in range(2)]
                r32 = [sb("r32", [128, NM], F32) for _ in range(2)]
                vbf = [sb("vbf", [128, NM], BF16) for _ in range(2)]
                asig = [sb("asig", [128, NM], F32) for _ in range(2)]
                kkr = [sb("kkr", [128, NM], F32) for _ in range(2)]
                sqk = [sb("sqk", [128, NM], BF16) for _ in range(2)]
                ssm = [sb("ssm", [128, NM], F32) for _ in range(2)]
                rn = [sb("rn", [128, NM], F32) for _ in range(2)]
                kmod = [sb("kmod", [128, NM], F32) for _ in range(2)]
                kb = [sb("kb", [128, NM], F32) for _ in range(2)]
                bp = [sb("bp", [128, NM], BF16) for _ in range(2)]
                mask = sb("mask", [128, NM], F32)
                lmask = sb("lmask", [128, 64], F32)
                umask = sb("umask", [128, 128], F32)
                I2 = sb("I2", [128, 64], BF16)
                ones = sb("ones", [128, 1], BF16)
                ln20 = sb("ln20", [128, 1], F32)
                AR = [sb("AR", [128, NCH * 128], BF16) for _ in range(4)]
                TMw = sb("TMw", [128, 4 * NCH * 192], BF16)
                TM = [TMw[:, hp_ * NCH * 192:(hp_ + 1) * NCH * 192] for hp_ in range(4)]
                WA = NCH * 128
                ABTp = [sb("ABTp", [128, 2 * WA], BF16) for _ in range(2)]
                AKTp = [sb("AKTp", [128, 2 * WA], BF16) for _ in range(2)]
                TTp = [sb("TTp", [128, 2 * NM], BF16) for _ in range(2)]
                ABT = [ABTp[hp_ // 2][:, (hp_ % 2) * WA:(hp_ % 2 + 1) * WA] for hp_ in range(4)]
                AKT = [AKTp[hp_ // 2][:, (hp_ % 2) * WA:(hp_ % 2 + 1) * WA] for hp_ in range(4)]
                TT = [TTp[hp_ // 2][:, (hp_ % 2) * NM:(hp_ % 2 + 1) * NM] for hp_ in range(4)]
                PPw = [sb("PPw", [128, 2 * NM], BF16) for _ in range(2)]
                QQw = [sb("QQw", [128, 2 * NM], BF16) for _ in range(2)]
                gbfw = sb("gbfw", [128, 4 * NM], BF16)
                gbf = [gbfw[:, hp_ * NM:(hp_ + 1) * NM] for hp_ in range(4)]
                bonw = sb("bonw", [128, 4 * NCH], F32)
                bon = [bonw[:, hp_ * NCH:(hp_ + 1) * NCH] for hp_ in range(4)]
                PC = [sb("PC", [128, 8], F32) for _ in range(4)]
                S32 = [sb("S32", [128, 64], F32) for _ in range(4)]
                Sbf = [sb("Sbf", [128, 64], BF16) for _ in range(4)]
                Xbf = [sb("Xbf", [128, 64], BF16) for _ in range(4)]
                Ubf = [sb("Ubf", [128, 64], BF16) for _ in range(4)]
                t1 = [sb("t1", [128, 64], F32) for _ in range(4)]
                Ysb = sb("Ysb", [128, 4 * NM], F32)
                ysq = sb("ysq", [128, 4 * NM], F32)
                yc = sb("yc", [128, 4 * NM], F32)
                sgw = [Ysb[:, 0 * NM:1 * NM], Ysb[:, 1 * NM:2 * NM]]
                cs = [Ysb[:, 2 * NM:3 * NM], Ysb[:, 3 * NM:4 * NM]]
                csm = [ysq[:, 0 * NM:1 * NM], ysq[:, 1 * NM:2 * NM]]
                epos = [ysq[:, 2 * NM:3 * NM], ysq[:, 3 * NM:4 * NM]]
                eneg = [yc[:, 0 * NM:1 * NM], yc[:, 1 * NM:2 * NM]]
                eprev = [yc[:, 2 * NM:3 * NM], yc[:, 3 * NM:4 * NM]]
                KYsb = [("Ysb",), ("sgw", 0), ("sgw", 1), ("cs", 0), ("cs", 1)]
                Kysq = [("ysq",), ("csm", 0), ("csm", 1), ("epos", 0), ("epos", 1)]
                Kyc = [("yc",), ("eneg", 0), ("eneg", 1), ("eprev", 0), ("eprev", 1)]
                Kytm = [("ytm",), ("BT", 0), ("BT", 1), ("KT", 0), ("KT", 1)]
                st1 = sb("st1", [128, 16], F32)
                st2 = sb("st2", [128, 16], F32)
                mean = sb("mean", [128, 16], F32)
                msq = sb("msq", [128, 16], F32)
                var = sb("var", [128, 16], F32)
                rstd = sb("rstd", [128, 16], F32)
                ytm = sb("ytm", [128, 4 * NM], BF16)
                BT = [ytm[:, 0 * NM:1 * NM], ytm[:, 1 * NM:2 * NM]]
                KT = [ytm[:, 2 * NM:3 * NM], ytm[:, 3 * NM:4 * NM]]
                yrT = sb("yrT", [128, 4, NM], BF16)
                winv = w_in.rearrange("(kc p) f -> p kc f", p=128)
                wov = w_out.rearrange("(kc p) d -> p kc d", p=128)
                print("rwkv sbuf bytes remaining", nc.sbuf_bytes_remaining)

                P.emit("sp", lambda e: e.dma_start(out=pp[:], in_=rwp[:, :]), writes=[("pp",)], dma=True)
                P.emit("sp", lambda e: e.dma_start(out=lnt[:], in_=lnwb[:, :]), writes=[("lnt",)], dma=True)
                P.emit("pool", lambda e: e.dma_start(out=waup[0:64, :], in_=rw_wup[:, :]), writes=[("waup", 0)], dma=True)
                P.emit("pool", lambda e: e.dma_start(out=waup[64:128, :], in_=rw_aup[:, :]), writes=[("waup", 1)], dma=True)
                P.emit("pool", lambda e: e.dma_start(out=gup[:], in_=rw_gup[:, :]), writes=[("gup",)], dma=True)
                P.emit("pool", lambda e: e.dma_start(out=wo_r[:], in_=wov[:, 4:8, :]), writes=[("wor",)], dma=True)
                P.emit("pool", lambda e: e.memset(prevl[:], 0.0), writes=[("prevl", i) for i in range(14)])
                P.emit("pool", lambda e: e.memset(mask[:], 1.0), writes=[("mask",)])
                P.emit("pool", lambda e: e.memset(mask[:].rearrange("p (c t) -> p c t", t=64)[:, :, 0:1], 0.0), writes=[("mask",)])
                P.emit("pool", lambda e: e.memset(ones[:], 1.0), writes=[("ones",)])
                P.emit("pool", lambda e: e.memset(ln20[:], 20.0 * math.log(2.0)), writes=[("ln20",)])
                P.emit("pool", lambda e: e.memset(lmask[:], 1.0), writes=[("lmask",)])
                P.emit("pool", lambda e: e.memset(umask[:], 1.0), writes=[("umask",)])
                P.emit("pool", lambda e: e.memset(I2[:], 0.0), writes=[("I2",)])
                for hh in range(2):
                    hp_ = slice(hh * 64, (hh + 1) * 64)
                    P.emit("pool", lambda e, hp_=hp_: e.affine_select(out=lmask[hp_, :], in_=lmask[hp_, :], pattern=[[-1, 64]], compare_op=ALU.is_gt,
                                                                      fill=0.0, base=0, channel_multiplier=1), writes=[("lmask",)])
                    P.emit("pool", lambda e, hp_=hp_: e.affine_select(out=umask[hp_, 0:64], in_=umask[hp_, 0:64], pattern=[[1, 64]], compare_op=ALU.is_gt,
                                                                      fill=0.0, base=0, channel_multiplier=-1), writes=[("umask",)])
                    P.emit("pool", lambda e, hp_=hp_: e.affine_select(out=umask[hp_, 64:128], in_=umask[hp_, 64:128], pattern=[[1, 64]], compare_op=ALU.is_ge,
                                                                      fill=0.0, base=0, channel_multiplier=-1), writes=[("umask",)])
                    P.emit("pool", lambda e, hp_=hp_: e.affine_select(out=I2[hp_, :], in_=I2[hp_, :], pattern=[[-1, 64]], compare_op=ALU.not_equal,
                                                                      fill=1.0, base=0, channel_multiplier=1), writes=[("I2",)])
                for hp in range(4):
                    P.emit("pool", lambda e, hp=hp: e.memset(S32[hp][:], 0.0), writes=[("S32", hp)])
                    P.emit("pool", lambda e, hp=hp: e.memset(Sbf[hp][:], 0.0), writes=[("Sbf", hp)])

                MU0, W00, A00, KK0, KA0, RK0 = 0, 14, 18, 22, 26, 30
                wctr = [0]
                pctr = [0]

                def h2p(h2):
                    return slice(h2 * 64, (h2 + 1) * 64)

                def v3(ap):
                    return ap.rearrange("p (c t) -> p c t", t=64)

                RING = 4
                rowseq = []
                for _g in range(1 + SEQ // NM):
                    rowseq += [12, 13, 4, 0, 5, 1, 8, 9, 6, 2, 7, 3, 10, 11]
                pfc = [0]

                def prefetch_upto(n):
                    while pfc[0] < min(n, len(rowseq)):
                        i_ = pfc[0]
                        s_ = i_ % RING
                        col_ = 1536 + rowseq[i_] * 128
                        P.emit("pool", lambda e: e.dma_start(out=wrw[s_][:], in_=winv[:, :, col_:col_ + 128]), writes=[("wrw", s_)], dma=True)
                        pfc[0] += 1

                def proj_row(wc, c0, N, out_ap, out_key):
                    i_ = wctr[0]
                    wctr[0] += 1
                    assert rowseq[i_] == wc, (i_, wc, rowseq[i_])
                    s = i_ % RING
                    prefetch_upto(i_ + RING)
                    b = pctr[0] % 2
                    pctr[0] += 1
                    for k in range(8):
                        P.emit("pe", lambda e, k=k: e.matmul(fb(b, N), wrw[s][:, k, :], uT[:, k, c0:c0 + N], start=(k == 0), stop=(k == 7)),
                               reads=[("wrw", s)], writes=[bk(b)])
                    pb_ = pbuf[b]
                    P.emit("act", lambda e: e.activation(out=pb_[:, 1:N + 1], in_=fb(b, N), func=AF.Copy), reads=[bk(b)], writes=[("pbuf", b)])
                    P.emit("act", lambda e: e.activation(out=pb_[:, 0:1], in_=prevl[:, wc:wc + 1], func=AF.Copy), reads=[("prevl", wc)], writes=[("pbuf", b)])
                    P.emit("act", lambda e: e.activation(out=prevl[:, wc:wc + 1], in_=pb_[:, N:N + 1], func=AF.Copy), reads=[("pbuf", b)], writes=[("prevl", wc)])
                    P.emit("dve", lambda e: e.tensor_tensor(out=dtmp[b][:, 0:N], in0=pb_[:, 0:N], in1=pb_[:, 1:N + 1], op=ALU.subtract),
                           reads=[("pbuf", b)], writes=[("dtmp", b)])
                    P.emit("dve", lambda e: e.scalar_tensor_tensor(
                        out=out_ap, in0=dtmp[b][:, 0:N], scalar=pp[:, MU0 + wc:MU0 + wc + 1], in1=pb_[:, 1:N + 1], op0=ALU.mult, op1=ALU.add),
                        reads=[("dtmp", b), ("pbuf", b), ("pp",)], writes=[out_key])

                groups = [(0, 64, 1, False, 0)] + [(FC0 + NM * g, NM, NCH, True, g) for g in range(SEQ // NM)]
                wout_pending = []
                for (c0, N, nch, has_out, gidx) in groups:
                    def bc(ap8):
                        return ap8.unsqueeze(2).broadcast_to([128, nch, 64])
                    proj_row(12, c0, N, lp[:, 0:N], ("lp",))
                    P.emit("act", lambda e: e.activation(out=lo24[0:64, 0:N], in_=lp[0:64, 0:N], func=AF.Tanh), reads=[("lp",)], writes=[("lo24", 0)])
                    P.emit("act", lambda e: e.activation(out=lo24[64:128, 0:N], in_=lp[64:128, 0:N], func=AF.Copy), reads=[("lp",)], writes=[("lo24", 1)])
                    proj_row(13, c0, N, lp[:, 0:N], ("lp",))
                    P.emit("act", lambda e: e.activation(out=sxg[:, 0:N], in_=lp[:, 0:N], func=AF.Sigmoid), reads=[("lp",)], writes=[("sxg",)])

                    def chain(hp):
                        par = hp % 2
                        B0, B1 = (2, 3) if par == 0 else (4, 5)
                        hc = slice(hp * 128, (hp + 1) * 128)
                        proj_row(4 + hp, c0, N, k32[par][:, 0:N], ("k32", par))
                        proj_row(0 + hp, c0, N, r32[par][:, 0:N], ("r32", par))
                        yield
                        proj_row(8 + hp, c0, N, vbf[par][:, 0:N], ("vbf", par))
                        P.emit("pe", lambda e: e.matmul(fb(B0, N), waup[0:64, hc], lo24[0:64, 0:N], start=True, stop=True),
                               reads=[("waup", 0), ("lo24", 0)], writes=[bk(B0)])
                        yield
                        P.emit("act", lambda e: e.activation(out=sgw[par][:, 0:N], in_=fb(B0, N), func=AF.Sigmoid, bias=pp[:, W00 + hp:W00 + hp + 1]),
                               reads=[bk(B0), ("pp",)], writes=[("sgw", par)])
                        P.emit("pe", lambda e: e.matmul(fb(B1, N), waup[64:128, hc], lo24[64:128, 0:N], start=True, stop=True),
                               reads=[("waup", 1), ("lo24", 1)], writes=[bk(B1)])
                        yield
                        P.emit("act", lambda e: e.activation(out=asig[par][:, 0:N], in_=fb(B1, N), func=AF.Sigmoid, bias=pp[:, A00 + hp:A00 + hp + 1]),
                               reads=[bk(B1), ("pp",)], writes=[("asig", par)])
                        P.emit("dve", lambda e: e.tensor_tensor_scan(out=cs[par][:, 0:N], data0=mask[:, 0:N], data1=sgw[par][:, 0:N], initial=0.0,
                                                                     op0=ALU.mult, op1=ALU.add), reads=[("mask",), ("sgw", par)], writes=[("cs", par)])
                        yield
                        P.emit("dve", lambda e: e.tensor_tensor(out=csm[par][:, 0:N], in0=cs[par][:, 0:N], in1=sgw[par][:, 0:N], op=ALU.subtract),
                               reads=[("cs", par), ("sgw", par)], writes=[("csm", par)])
                        P.emit("act", lambda e: e.activation(out=epos[par][:, 0:N], in_=cs[par][:, 0:N], func=AF.Exp, scale=-C0), reads=[("cs", par)], writes=[("epos", par)])
                        yield
                        P.emit("act", lambda e: e.activation(out=eneg[par][:, 0:N], in_=cs[par][:, 0:N], func=AF.Exp, scale=C0), reads=[("cs", par)], writes=[("eneg", par)])
                        P.emit("act", lambda e: e.activation(out=eprev[par][:, 0:N], in_=csm[par][:, 0:N], func=AF.Exp, scale=-C0), reads=[("csm", par)], writes=[("eprev", par)])
                        yield
                        P.emit("act", lambda e, hp=hp: e.activation(out=PC[hp][:, 0:nch], in_=v3(epos[par][:, 0:N])[:, :, 63], func=AF.Copy),
                               reads=[("epos", par)], writes=[("PC", hp)])
                        P.emit("dve", lambda e: e.tensor_scalar(out=kkr[par][:, 0:N], in0=k32[par][:, 0:N], scalar1=pp[:, KK0 + hp:KK0 + hp + 1], scalar2=None, op0=ALU.mult),
                               reads=[("k32", par), ("pp",)], writes=[("kkr", par)])
                        yield
                        P.emit("act", lambda e: e.activation(out=sqk[par][:, 0:N], in_=kkr[par][:, 0:N], func=AF.Square), reads=[("kkr", par)], writes=[("sqk", par)])
                        P.emit("pe", lambda e: e.matmul(fb(B0, N), BD[:], sqk[par][:, 0:N], start=True, stop=True), reads=[("sqk", par)], writes=[bk(B0)])
                        yield
                        P.emit("dve", lambda e: e.tensor_scalar(out=ssm[par][:, 0:N], in0=fb(B0, N), scalar1=1e-24, scalar2=None, op0=ALU.max),
                               reads=[bk(B0)], writes=[("ssm", par)])
                        P.emit("act", lambda e: e.activation(out=ssm[par][:, 0:N], in_=ssm[par][:, 0:N], func=AF.Ln, scale=float(2.0 ** 40)),
                               reads=[("ssm", par)], writes=[("ssm", par)])
                        yield
                        P.emit("act", lambda e: e.activation(out=rn[par][:, 0:N], in_=ssm[par][:, 0:N], func=AF.Exp, scale=-0.5, bias=ln20[:, 0:1]),
                               reads=[("ssm", par), ("ln20",)], writes=[("rn", par)])
                        P.emit("dve", lambda e: e.tensor_tensor(out=kkr[par][:, 0:N], in0=kkr[par][:, 0:N], in1=rn[par][:, 0:N], op=ALU.mult),
                               reads=[("kkr", par), ("rn", par)], writes=[("kkr", par)])
                        yield
                        P.emit("dve", lambda e: e.tensor_scalar(out=kmod[par][:, 0:N], in0=asig[par][:, 0:N], scalar1=-1.0, scalar2=pp[:, KA0 + hp:KA0 + hp + 1],
                                                                op0=ALU.add, op1=ALU.mult), reads=[("asig", par), ("pp",)], writes=[("kmod", par)])
                        P.emit("dve", lambda e: e.scalar_tensor_tensor(out=kmod[par][:, 0:N], in0=kmod[par][:, 0:N], scalar=1.0, in1=k32[par][:, 0:N], op0=ALU.add, op1=ALU.mult),
                               reads=[("kmod", par), ("k32", par)], writes=[("kmod", par)])
                        yield
                        AR3 = AR[hp][:, 0:nch * 128].rearrange("p (c a t) -> p c a t", a=2, t=64)
                        P.emit("dve", lambda e, AR3=AR3: e.scalar_tensor_tensor(
                            out=AR3[:, :, 0, :], in0=v3(kkr[par][:, 0:N]), scalar=-1.0, in1=v3(eprev[par][:, 0:N]), op0=ALU.mult, op1=ALU.mult),
                            reads=[("kkr", par), ("eprev", par)], writes=[("AR", hp)])
                        P.emit("pool", lambda e, AR3=AR3: e.tensor_tensor(out=AR3[:, :, 1, :], in0=v3(r32[par][:, 0:N]), in1=v3(epos[par][:, 0:N]), op=ALU.mult),
                               reads=[("r32", par), ("epos", par), ("AR", hp)], writes=[("AR2", hp)])
                        yield
                        P.emit("pool", lambda e: e.tensor_tensor(out=kb[par][:, 0:N], in0=kkr[par][:, 0:N], in1=asig[par][:, 0:N], op=ALU.mult),
                               reads=[("kkr", par), ("asig", par)], writes=[("kb", par)])
                        P.emit("dve", lambda e: e.tensor_tensor(out=BT[par][:, 0:N], in0=kb[par][:, 0:N], in1=eneg[par][:, 0:N], op=ALU.mult),
                               reads=[("kb", par), ("eneg", par)], writes=[("BT", par)])
                        yield
                        P.emit("dve", lambda e: e.tensor_tensor(out=KT[par][:, 0:N], in0=kmod[par][:, 0:N], in1=eneg[par][:, 0:N], op=ALU.mult),
                               reads=[("kmod", par), ("eneg", par)], writes=[("KT", par)])
                        P.emit("dve", lambda e: e.scalar_tensor_tensor(out=bp[par][:, 0:N], in0=r32[par][:, 0:N], scalar=pp[:, RK0 + hp:RK0 + hp + 1], in1=kmod[par][:, 0:N],
                                                                       op0=ALU.mult, op1=ALU.mult), reads=[("r32", par), ("kmod", par), ("pp",)], writes=[("bp", par)])
                        yield
                        P.emit("pe", lambda e: e.matmul(fb(B1, N), gup[:, hc], sxg[:, 0:N], start=True, stop=True), reads=[("gup",), ("sxg",)], writes=[bk(B1)])
                        P.emit("act", lambda e, hp=hp: e.activation(out=gbf[hp][:, 0:N], in_=fb(B1, N), func=AF.Copy), reads=[bk(B1)], writes=[("gbf", hp)])
                        yield
                        ARK = [("AR", hp), ("AR2", hp)]
                        for c in range(nch):
                            for si, (srcT, skey) in enumerate(((BT[par], ("BT", par)), (KT[par], ("KT", par)), (vbf[par], ("vbf", par)))):
                                for h2 in range(2):
                                    P.emit("pe", lambda e, c=c, si=si, srcT=srcT, h2=h2: e.transpose(
                                        PSB[h2p(h2), par * 1024 + (c * 3 + si) * 64:par * 1024 + (c * 3 + si) * 64 + 64],
                                        srcT[h2p(h2), c * 64:(c + 1) * 64], ident[h2p(h2), h2p(h2)]),
                                        reads=[skey], writes=[("psb", par)])
                        P.emit("act", lambda e, hp=hp: e.activation(out=TM[hp][:, 0:nch * 192], in_=PSB[:, par * 1024:par * 1024 + nch * 192], func=AF.Copy),
                               reads=[("psb", par)], writes=[("TM", hp)])
                        for c in range(nch):
                            for h2 in range(2):
                                P.emit("pe", lambda e, c=c, h2=h2: e.matmul(fb(B1, 1, h2 * 64, h2 * 64 + 64, off=256 + c), bp[par][h2p(h2), c * 64:(c + 1) * 64],
                                                                            ones[h2p(h2), 0:1], start=True, stop=True),
                                       reads=[("bp", par), ("ones",)], writes=[bk(B1)])
                        P.emit("act", lambda e, hp=hp: e.activation(out=bon[hp][:, 0:nch], in_=fb(B1, nch, off=256), func=AF.Copy),
                               reads=[bk(B1)], writes=[("bon", hp)])
                        yield

                    def G3(t, W, w, sz):
                        if w == W:
                            return t[:, 0:2 * W].rearrange("p (g s) -> p g s", s=sz)
                        assert w == sz
                        return t[:, 0:2 * W].rearrange("p (h x) -> p h x", h=2)[:, :, 0:w]

                    def pair_stage(pr):
                        ng = 2 * nch
                        hps = (2 * pr, 2 * pr + 1)
                        ARKS = [[("AR", hp), ("AR2", hp)] for hp in hps]
                        for hl, hp in enumerate(hps):
                            for c in range(nch):
                                for h2 in range(2):
                                    q_ = h2p(h2)
                                    P.emit("pe", lambda e: e.matmul(fb(2, 64, q_.start, q_.stop, off=hl * NM + c * 64), AR[hp][q_, c * 128:c * 128 + 64],
                                                                    BT[hl][q_, c * 64:(c + 1) * 64], start=True, stop=True),
                                           reads=ARKS[hl] + [("BT", hl)], writes=[bk(2)])
                        for hl, hp in enumerate(hps):
                            for c in range(nch):
                                for h2 in range(2):
                                    q_ = h2p(h2)
                                    P.emit("pe", lambda e: e.matmul(fb(3 + hl, 128, q_.start, q_.stop, off=c * 128), BT[hl][q_, c * 64:(c + 1) * 64],
                                                                    AR[hp][q_, c * 128:(c + 1) * 128], start=True, stop=True),
                                           reads=ARKS[hl] + [("BT", hl)], writes=[bk(3 + hl)])
                        P.emit("dve", lambda e: e.tensor_tensor(out=G3(PPw[0], NM, N, 64), in0=G3(PSF[:, 2 * 512:3 * 512], NM, N, 64),
                                                                in1=lmask[:].unsqueeze(1).broadcast_to([128, ng, 64]), op=ALU.mult),
                               reads=[bk(2), ("lmask",)], writes=[("PPw", 0)])
                        for hl, hp in enumerate(hps):
                            P.emit("dve", lambda e: e.tensor_tensor(out=ABT[hp][:, 0:nch * 128].rearrange("p (c s) -> p c s", s=128),
                                                                    in0=fb(3 + hl, nch * 128).rearrange("p (c s) -> p c s", s=128),
                                                                    in1=umask[:].unsqueeze(1).broadcast_to([128, nch, 128]), op=ALU.mult),
                                   reads=[bk(3 + hl), ("umask",)], writes=[("ABT", hp)])
                        for hl, hp in enumerate(hps):
                            for c in range(nch):
                                for h2 in range(2):
                                    q_ = h2p(h2)
                                    P.emit("pe", lambda e: e.matmul(fb(4 + hl, 128, q_.start, q_.stop, off=c * 128), KT[hl][q_, c * 64:(c + 1) * 64],
                                                                    AR[hp][q_, c * 128:(c + 1) * 128], start=True, stop=True),
                                           reads=ARKS[hl] + [("KT", hl)], writes=[bk(4 + hl)])
                        for hl, hp in enumerate(hps):
                            P.emit("dve", lambda e: e.tensor_tensor(out=AKT[hp][:, 0:nch * 128].rearrange("p (c s) -> p c s", s=128),
                                                                    in0=fb(4 + hl, nch * 128).rearrange("p (c s) -> p c s", s=128),
                                                                    in1=umask[:].unsqueeze(1).broadcast_to([128, nch, 128]), op=ALU.mult),
                                   reads=[bk(4 + hl), ("umask",)], writes=[("AKT", hp)])
                        Q0v = G3(ABTp[pr], WA, nch * 128, 128)[:, :, 0:64]
                        TTk = [("TT", hps[0]), ("TT", hps[1])]
                        P.emit("dve", lambda e: e.tensor_tensor(out=G3(TTp[pr], NM, N, 64), in0=Q0v,
                                                                in1=I2[:].unsqueeze(1).broadcast_to([128, ng, 64]), op=ALU.add),
                               reads=[("ABT", hps[0]), ("ABT", hps[1]), ("I2",)], writes=TTk)
                        P.emit("act", lambda e: e.activation(out=G3(QQw[0], NM, N, 64), in_=Q0v, func=AF.Copy),
                               reads=[("ABT", hps[0]), ("ABT", hps[1])], writes=[("QQw", 0)])
                        for lev in range(1, 6):
                            pi_, po_ = (lev - 1) % 2, lev % 2
                            Pp, Qp, Pn, Qn = PPw[pi_], QQw[pi_], PPw[po_], QQw[po_]
                            for hl in range(2):
                                for c in range(nch):
                                    for h2 in range(2):
                                        q_ = h2p(h2)
                                        o_ = hl * NM + c * 64
                                        P.emit("pe", lambda e: e.matmul(fb(2, 64, q_.start, q_.stop, off=o_), Qp[q_, o_:o_ + 64], Pp[q_, o_:o_ + 64],
                                                                        start=True, stop=True),
                                               reads=[("PPw", pi_), ("QQw", pi_)], writes=[bk(2)])
                            P.emit("act", lambda e: e.activation(out=G3(Pn, NM, N, 64), in_=G3(PSF[:, 2 * 512:3 * 512], NM, N, 64), func=AF.Copy),
                                   reads=[bk(2)], writes=[("PPw", po_)])
                            if lev < 5:
                                for hl in range(2):
                                    for c in range(nch):
                                        for h2 in range(2):
                                            q_ = h2p(h2)
                                            o_ = hl * NM + c * 64
                                            P.emit("pe", lambda e: e.matmul(fb(3, 64, q_.start, q_.stop, off=o_), Pp[q_, o_:o_ + 64], Qp[q_, o_:o_ + 64],
                                                                            start=True, stop=True),
                                                   reads=[("PPw", pi_), ("QQw", pi_)], writes=[bk(3)])
                                P.emit("dve", lambda e: e.tensor_copy(out=G3(Qn, NM, N, 64), in_=G3(PSF[:, 3 * 512:4 * 512], NM, N, 64)),
                                       reads=[bk(3)], writes=[("QQw", po_)])
                            for hl in range(2):
                                for c in range(nch):
                                    for h2 in range(2):
                                        q_ = h2p(h2)
                                        o_ = hl * NM + c * 64
                                        P.emit("pe", lambda e: e.matmul(fb(4, 64, q_.start, q_.stop, off=o_), Pn[q_, o_:o_ + 64], TTp[pr][q_, o_:o_ + 64],
                                                                        start=True, stop=True),
                                               reads=[("PPw", po_)] + TTk, writes=[bk(4)])
                            P.emit("dve", lambda e: e.tensor_tensor(out=G3(TTp[pr], NM, N, 64), in0=G3(PSF[:, 4 * 512:5 * 512], NM, N, 64),
                                                                    in1=G3(TTp[pr], NM, N, 64), op=ALU.add),
                                   reads=[bk(4)] + TTk, writes=TTk)

                    def lockstep(gens, hook_round=None):
                        gens = list(gens)
                        rnd = 0
                        while gens:
                            for g_ in list(gens):
                                try:
                                    next(g_)
                                except StopIteration:
                                    gens.remove(g_)
                            rnd += 1
                            if hook_round is not None and rnd == hook_round:
                                while wout_pending:
                                    wout_pending.pop(0)()
                    lockstep([chain(0), chain(1)], hook_round=3)
                    while wout_pending:
                        wout_pending.pop(0)()
                    pair_stage(0)
                    lockstep([chain(2), chain(3)])
                    pair_stage(1)
                    for c in range(nch):
                        vsl = slice((c * 3 + 2) * 64, (c * 3 + 3) * 64)
                        ksl = slice((c * 3 + 1) * 64, (c * 3 + 2) * 64)
                        bsl = slice((c * 3 + 0) * 64, (c * 3 + 1) * 64)
                        for hp in range(4):
                            ARK = [("AR", hp), ("AR2", hp)]
                            for h2 in range(2):
                                q_ = h2p(h2)
                                P.emit("pe", lambda e: e.matmul(fb(hp, 64, q_.start, q_.stop, off=0), AKT[hp][q_, c * 128:c * 128 + 64],
                                                                TM[hp][q_, vsl], start=True, stop=False),
                                       reads=[("AKT", hp), ("TM", hp)], writes=[bk(hp)])
                                P.emit("pe", lambda e: e.matmul(fb(hp, 64, q_.start, q_.stop, off=0), AR[hp][q_, c * 128:c * 128 + 64],
                                                                Sbf[hp][q_, :], start=False, stop=True),
                                       reads=ARK + [("Sbf", hp)], writes=[bk(hp)])
                        for hp in range(4):
                            P.emit("act", lambda e: e.activation(out=Xbf[hp][:], in_=fb(hp, 64, off=0), func=AF.Copy),
                                   reads=[bk(hp)], writes=[("Xbf", hp)])
                        for hp in range(4):
                            for h2 in range(2):
                                q_ = h2p(h2)
                                P.emit("pe", lambda e: e.matmul(fb(hp, 64, q_.start, q_.stop, off=64), TT[hp][q_, c * 64:(c + 1) * 64],
                                                                Xbf[hp][q_, :], start=True, stop=True),
                                       reads=[("TT", hp), ("Xbf", hp)], writes=[bk(hp)])
                        for hp in range(4):
                            P.emit("dve", lambda e: e.tensor_copy(out=Ubf[hp][:], in_=fb(hp, 64, off=64)),
                                   reads=[bk(hp)], writes=[("Ubf", hp)])
                        for hp in range(4):
                            ARK = [("AR", hp), ("AR2", hp)]
                            for h2 in range(2):
                                q_ = h2p(h2)
                                yo = fb(4 + hp // 2, 64, q_.start, q_.stop, off=(hp % 2) * 256 + c * 64)
                                P.emit("pe", lambda e: e.matmul(yo, AR[hp][q_, c * 128 + 64:c * 128 + 128], Sbf[hp][q_, :], start=True, stop=False),
                                       reads=ARK + [("Sbf", hp)], writes=[bk(4 + hp // 2)])
                                P.emit("pe", lambda e: e.matmul(yo, ABT[hp][q_, c * 128 + 64:c * 128 + 128], Ubf[hp][q_, :], start=False, stop=False),
                                       reads=[("ABT", hp), ("Ubf", hp)], writes=[bk(4 + hp // 2)])
                                P.emit("pe", lambda e: e.matmul(yo, AKT[hp][q_, c * 128 + 64:c * 128 + 128], TM[hp][q_, vsl], start=False, stop=True),
                                       reads=[("AKT", hp), ("TM", hp)], writes=[bk(4 + hp // 2)])
                        for hp in range(4):
                            for h2 in range(2):
                                q_ = h2p(h2)
                                so = fb(hp, 64, q_.start, q_.stop, off=128)
                                P.emit("pe", lambda e: e.matmul(so, TM[hp][q_, ksl], TM[hp][q_, vsl], start=True, stop=False),
                                       reads=[("TM", hp)], writes=[bk(hp)])
                                P.emit("pe", lambda e: e.matmul(so, TM[hp][q_, bsl], Ubf[hp][q_, :], start=False, stop=True),
                                       reads=[("TM", hp), ("Ubf", hp)], writes=[bk(hp)])
                        for hp in range(4):
                            P.emit("dve", lambda e: e.tensor_tensor(out=t1[hp][:], in0=fb(hp, 64, off=128), in1=S32[hp][:], op=ALU.add),
                                   reads=[bk(hp), ("S32", hp)], writes=[("t1", hp)])
                        for hp in range(4):
                            P.emit("act", lambda e: e.mul(out=Sbf[hp][:], in_=t1[hp][:], mul=PC[hp][:, c:c + 1]),
                                   reads=[("t1", hp), ("PC", hp)], writes=[("Sbf", hp)])
                            P.emit("dve", lambda e: e.tensor_scalar(out=S32[hp][:], in0=t1[hp][:], scalar1=PC[hp][:, c:c + 1], scalar2=None, op0=ALU.mult),
                                   reads=[("t1", hp), ("PC", hp)], writes=[("S32", hp)])
                    if not has_out:
                        continue
                    NW = 4 * N

                    def w16(ap):
                        return ap.rearrange("p (g t) -> p g t", t=64)

                    def w44(ap):
                        return ap.rearrange("p (h c t) -> p h c t", h=4, t=64)

                    def b16(ap):
                        return ap.unsqueeze(2).broadcast_to([128, 4 * nch, 64])
                    ykeys = [bk(4), bk(5)]
                    P.emit("act", lambda e: e.activation(out=Ysb[:, 0:NW], in_=PSF[:, 4 * 512:4 * 512 + NW], func=AF.Copy),
                           reads=ykeys, writes=KYsb)
                    P.emit("dve", lambda e: e.reduce_sum(out=st1[:, 0:16], in_=w16(Ysb[:, 0:NW]), axis=AX.X), reads=KYsb, writes=[("st1",)])
                    P.emit("act", lambda e: e.activation(out=ysq[:, 0:NW], in_=Ysb[:, 0:NW], func=AF.Square), reads=KYsb, writes=Kysq)
                    P.emit("dve", lambda e: e.reduce_sum(out=st2[:, 0:16], in_=w16(ysq[:, 0:NW]), axis=AX.X), reads=Kysq, writes=[("st2",)])
                    P.emit("dve", lambda e: e.tensor_scalar(out=mean[:, 0:16], in0=st1[:, 0:16], scalar1=1.0 / 64, scalar2=None, op0=ALU.mult),
                           reads=[("st1",)], writes=[("mean",)])
                    P.emit("dve", lambda e: e.tensor_tensor(out=msq[:, 0:16], in0=mean[:, 0:16], in1=mean[:, 0:16], op=ALU.mult),
                           reads=[("mean",)], writes=[("msq",)])
                    P.emit("dve", lambda e: e.scalar_tensor_tensor(out=var[:, 0:16], in0=st2[:, 0:16], scalar=1.0 / 64, in1=msq[:, 0:16],
                                                                   op0=ALU.mult, op1=ALU.subtract), reads=[("st2",), ("msq",)], writes=[("var",)])
                    P.emit("dve", lambda e: e.tensor_scalar(out=var[:, 0:16], in0=var[:, 0:16], scalar1=GN_EPS, scalar2=None, op0=ALU.add),
                           reads=[("var",)], writes=[("var",)])
                    P.emit("pool", lambda e: e.tensor_tensor(out=rstd[:, 0:16], in0=var[:, 0:16], in1=neghalf[:, 0:16], op=ALU.pow),
                           reads=[("var",)], writes=[("rstd",)])
                    P.emit("dve", lambda e: e.tensor_tensor(out=w16(yc[:, 0:NW]), in0=w16(Ysb[:, 0:NW]), in1=b16(mean[:, 0:16]), op=ALU.subtract),
                           reads=KYsb + [("mean",)], writes=Kyc)
                    V3w = TMw[:, 0:4 * nch * 192].rearrange("p (g a t) -> p g a t", a=3, t=64)[:, :, 2, :]
                    P.emit("dve", lambda e: e.tensor_tensor(out=w16(ysq[:, 0:NW]), in0=V3w, in1=b16(bonw[:, 0:16]), op=ALU.mult),
                           reads=[("TM", 0), ("TM", 1), ("TM", 2), ("TM", 3), ("bon", 0), ("bon", 1), ("bon", 2), ("bon", 3)], writes=Kysq)
                    P.emit("dve", lambda e: e.tensor_tensor(out=w16(yc[:, 0:NW]), in0=w16(yc[:, 0:NW]), in1=b16(rstd[:, 0:16]), op=ALU.mult),
                           reads=Kyc + [("rstd",)], writes=Kyc)
                    lnw4 = lnt[:, 0:256].rearrange("p (h i) -> p h i", h=4).unsqueeze(2).broadcast_to([128, 4, nch, 64])
                    lnb4 = lnt[:, 256:512].rearrange("p (h i) -> p h i", h=4).unsqueeze(2).broadcast_to([128, 4, nch, 64])
                    P.emit("dve", lambda e: e.tensor_tensor(out=w44(yc[:, 0:NW]), in0=w44(yc[:, 0:NW]), in1=lnw4, op=ALU.mult),
                           reads=Kyc + [("lnt",)], writes=Kyc)
                    P.emit("dve", lambda e: e.tensor_tensor(out=w44(ysq[:, 0:NW]), in0=w44(ysq[:, 0:NW]), in1=lnb4, op=ALU.add),
                           reads=Kysq + [("lnt",)], writes=Kysq)
                    P.emit("dve", lambda e: e.tensor_tensor(out=ytm[:, 0:NW], in0=yc[:, 0:NW], in1=ysq[:, 0:NW], op=ALU.add),
                           reads=Kyc + Kysq, writes=Kytm)
                    for hp in range(4):
                        for c in range(nch):
                            for h2 in range(2):
                                o_ = hp * N + c * 64
                                P.emit("pe", lambda e: e.transpose(PSB[h2p(h2), 1024 + o_:1024 + o_ + 64], ytm[h2p(h2), o_:o_ + 64],
                                                                   ident[h2p(h2), h2p(h2)]), reads=Kytm, writes=[("psb", 1)])
                    P.emit("dve", lambda e: e.tensor_tensor(out=yrT[:, :, :].rearrange("p h x -> p (h x)")[:, 0:NW], in0=PSB[:, 1024:1024 + NW], in1=gbfw[:, 0:NW], op=ALU.mult),
                           reads=[("psb", 1), ("gbf", 0), ("gbf", 1), ("gbf", 2), ("gbf", 3)], writes=[("yrT", 0), ("yrT", 1), ("yrT", 2), ("yrT", 3)])
                    def wout(gidx=gidx, N=N):
                        for t in range(N // 128):
                            i = gidx * (N // 128) + t
                            for dh in range(2):
                                for hp in range(4):
                                    P.emit("pe", lambda e: e.matmul(fb(dh), yrT[:, hp, t * 128:(t + 1) * 128], wo_r[:, hp, dh * 512:(dh + 1) * 512],
                                                                    start=(hp == 0), stop=(hp == 3)),
                                           reads=[("yrT", hp), ("wor",)], writes=[bk(dh)])
                            P.emit("dve", lambda e: e.tensor_tensor(out=htile(i), in0=PSF[:, 0:1024], in1=htile(i), op=ALU.add),
                                   reads=[bk(0), bk(1), ("h", i)], writes=[("h", i)])
                    wout_pending.append(wout)
                while wout_pending:
                    wout_pending.pop(0)()
                P.run()

        ffn_phase("f1", f1n, f1g, f1u, f1d, with_meta=True, first=True, last=False)
        mixer_phase()
        ffn_phase("f2", f2n, f2g, f2u, f2d, with_meta=False, first=False, last=True)
    return nc


_NC_CACHE = {}


def _prep_shared(inp):
    f = lambda a: np.ascontiguousarray(np.asarray(a, dtype=np.float32))
    sh = {}
    sh["meta_tokens"] = f(inp["meta_tokens"])
    for nme in ("ffn1_norm", "ffn2_norm", "mix_norm"):
        sh[nme] = f(inp[nme]).reshape(1, D)
    for nme in ("ffn1_gate", "ffn1_up", "ffn2_gate", "ffn2_up"):
        sh[nme] = f(inp[nme]).reshape(D, DFF)
    for nme in ("ffn1_down", "ffn2_down"):
        sh[nme] = f(inp[nme]).reshape(DFF, D)
    sh["w_in"] = f(inp["w_in"]).reshape(D, 3328)
    sh["w_out"] = f(inp["w_out"]).reshape(D, D)
    qn = f(inp["q_norm"]).reshape(64)
    kn = f(inp["k_norm"]).reshape(64)
    sh["qkg"] = f(np.stack([np.tile(qn, 2), np.tile(kn, 2)], axis=1))
    sh["lambda_vecs"] = f(inp["lambda_vecs"]).reshape(1, 256)
    sh["attn_out_norm"] = f(inp["attn_out_norm"]).reshape(1, 128)
    cols = [f(inp["rw_mu"]).reshape(14, 128).T]
    for nme in ("rw_w0", "rw_a0", "rw_k_k", "rw_k_a", "rw_r_k"):
        cols.append(f(inp[nme]).reshape(4, 128).T)
    sh["rwp"] = f(np.concatenate(cols, axis=1))
    sh["rw_w_up"] = f(inp["rw_w_up"]).reshape(64, 512)
    sh["rw_a_up"] = f(inp["rw_a_up"]).reshape(64, 512)
    sh["rw_g_up"] = f(inp["rw_g_up"]).reshape(128, 512)

    def lt(v):
        a = f(v).reshape(4, 2, 64)
        a = np.transpose(a, (1, 0, 2))
        a = np.repeat(a[:, None, :, :], 64, axis=1)
        return a.reshape(128, 256)
    sh["lnwb"] = f(np.concatenate([lt(inp["rw_ln_w"]), lt(inp["rw_ln_b"])], axis=1))
    return sh


def kernel(**inputs):
    x = np.asarray(inputs["x"], dtype=np.float32)
    B = x.shape[0]
    if "nc" not in _NC_CACHE:
        _NC_CACHE["nc"] = build_nc()
    nc = _NC_CACHE["nc"]
    sh = _prep_shared(inputs)
    in_maps = []
    for b in range(B):
        m = dict(sh)
        m["x"] = np.ascontiguousarray(x[b])
        in_maps.append(m)
    res = run_bass_kernel_spmd(nc, in_maps, core_ids=list(range(B)))
    return np.stack([np.asarray(r["y"], dtype=np.float32) for r in res.results], axis=0)
```

```python
import contextlib
import math
import numpy as np
import concourse.bass as bass
import concourse.mybir as mybir
from concourse.bass_utils import run_bass_kernel_spmd

F32 = mybir.dt.float32
BF16 = mybir.dt.bfloat16
AF = mybir.ActivationFunctionType
ALU = mybir.AluOpType
AX = mybir.AxisListType

D = 1024
SEQ = 2048
NMETA = 16
DFF = 2816
NF = DFF // 128
NTC = 2112
MC0 = 48
FC0 = 64
RMS_EPS = 1e-6
GN_EPS = 64e-5
LAM_INIT = 0.8 - 0.6 * math.exp(-0.3 * 0)
C0 = math.exp(-0.5)
SLOPES = [2.0 ** (-8.0 * (h + 1) / 4) for h in range(4)]

_uid = [0]


def _nm(s):
    _uid[0] += 1
    return f"{s}_{_uid[0]}"


class Op:
    __slots__ = ("eng", "idx", "fn", "deps", "dma", "sig", "val", "dsem", "dval")


class _Rec:
    def __init__(self):
        self.call = None

    def __getattr__(self, name):
        def f(*a, **k):
            self.call = (name, a, k)
            return None
        return f


class Plan:
    ENGS = ("pe", "act", "dve", "pool", "sp")

    def __init__(self, nc, st, tag, ndma=6):
        self.nc = nc
        self.q = {e: [] for e in self.ENGS}
        self.sem = {e: st.enter_context(nc.semaphore(_nm(f"s{tag}{e}"))) for e in self.ENGS}
        self.dsems = {e: [st.enter_context(nc.semaphore(_nm(f"d{tag}{e}"))) for _ in range(ndma)] for e in ("sp", "pool")}
        self.ndma = {"sp": 0, "pool": 0}
        self.lastw = {}
        self.readers = {}
        self.dmas = []

    def emit(self, eng, fn, reads=(), writes=(), dma=False):
        op = Op()
        rec = _Rec()
        fn(rec)
        name_, a_, k_ = rec.call
        fn = (lambda e, name_=name_, a_=a_, k_=k_: getattr(e, name_)(*a_, **k_))
        op.eng, op.fn, op.dma, op.sig, op.val = eng, fn, dma, False, 0
        op.idx = len(self.q[eng])
        deps = []
        for k in reads:
            w = self.lastw.get(k)
            if w is not None:
                deps.append(w)
        for k in writes:
            w = self.lastw.get(k)
            if w is not None:
                deps.append(w)
            deps.extend(self.readers.get(k, {}).values())
        best = {}
        dl = []
        for d in deps:
            if d is op:
                continue
            if d.dma:
                if d not in dl:
                    dl.append(d)
            else:
                if d.eng == eng and eng == "pe":
                    continue
                b = best.get(d.eng)
                if b is None or d.idx > b.idx:
                    best[d.eng] = d
        op.deps = dl + list(best.values())
        for d in op.deps:
            d.sig = True
        for k in reads:
            self.readers.setdefault(k, {})[(eng, op.idx if dma else -1)] = op
        for k in writes:
            self.lastw[k] = op
            self.readers[k] = {}
        if dma:
            n = self.ndma[eng]
            self.ndma[eng] = n + 1
            sems = self.dsems[eng]
            op.dsem = sems[n % len(sems)]
            op.dval = 16 * (n // len(sems) + 1)
            self.dmas.append(op)
        self.q[eng].append(op)
        return op

    def finish(self):
        op = Op()
        op.eng, op.fn, op.dma, op.sig, op.val = "sp", (lambda e: e.nop()), False, False, 0
        op.idx = len(self.q["sp"])
        last = {}
        for d in self.dmas:
            last[id(d.dsem)] = d
        op.deps = list(last.values())
        self.q["sp"].append(op)
        for eng in self.ENGS:
            c = 0
            for o in self.q[eng]:
                if o.sig and not o.dma:
                    c += 1
                    o.val = c

    def _replay(self, eng):
        def run(e):
            seen = {}
            for op in self.q[eng]:
                waits = []
                for d in op.deps:
                    if d.dma:
                        waits.append((d.dsem, d.dval))
                    else:
                        waits.append((self.sem[d.eng], d.val))
                if op.dma and op.dval > 16:
                    waits.append((op.dsem, op.dval - 16))
                for s, v in waits:
                    if seen.get(id(s), 0) < v:
                        seen[id(s)] = v
                        e.wait_ge(s, v)
                ins = op.fn(e)
                if op.dma:
                    ins.then_inc(op.dsem, 16)
                elif op.sig:
                    ins.then_inc(self.sem[eng], 1)
        return run

    def run(self):
        self.finish()
        with self.nc.Block() as block:
            block.tensor(self._replay("pe"))
            block.scalar(self._replay("act"))
            block.vector(self._replay("dve"))
            block.gpsimd(self._replay("pool"))
            block.sync(self._replay("sp"))


def bk(b):
    return ("ps", b)


def build_nc():
    nc = bass.Bass("TRN2", target_bir_lowering=False)

    def din(name, shape):
        return nc.dram_tensor(name, list(shape), F32, kind="ExternalInput").ap()

    x = din("x", [SEQ, D])
    meta = din("meta_tokens", [NMETA, D])
    f1n = din("ffn1_norm", [1, D]); f1g = din("ffn1_gate", [D, DFF]); f1u = din("ffn1_up", [D, DFF]); f1d = din("ffn1_down", [DFF, D])
    f2n = din("ffn2_norm", [1, D]); f2g = din("ffn2_gate", [D, DFF]); f2u = din("ffn2_up", [D, DFF]); f2d = din("ffn2_down", [DFF, D])
    mixn = din("mix_norm", [1, D])
    w_in = din("w_in", [D, 3328])
    w_out = din("w_out", [D, D])
    qkg = din("qkg", [128, 2])
    lamv = din("lambda_vecs", [1, 256])
    aon = din("attn_out_norm", [1, 128])
    rwp = din("rwp", [128, 34])
    rw_wup = din("rw_w_up", [64, 512]); rw_aup = din("rw_a_up", [64, 512]); rw_gup = din("rw_g_up", [128, 512])
    lnwb = din("lnwb", [128, 512])
    y = nc.dram_tensor("y", [SEQ, D], F32, kind="ExternalOutput").ap()

    with contextlib.ExitStack() as top:
        def sbT(name, shape, dt):
            return top.enter_context(nc.sbuf_tensor(name, list(shape), dt))
        hres = sbT("hres", [128, 16 * D], F32)
        hmeta = sbT("hmeta", [16, D], F32)
        ident = sbT("ident", [128, 128], BF16)
        BD = sbT("BD", [128, 128], BF16)
        neghalf = sbT("neghalf", [128, 32], F32)
        PSF = top.enter_context(nc.psum_tensor("PSF", [128, 6 * 512], F32))
        PSB = top.enter_context(nc.psum_tensor("PSB", [128, 2 * 1024], BF16))

        def fb(b, n=512, p0=0, p1=128, off=0):
            return PSF[p0:p1, b * 512 + off: b * 512 + off + n]

        def htile(i):
            return hres[:, i * D:(i + 1) * D]

        def emit_norm_T(P, st, tag, srcs, gain_ap, dstT3, dkey, cache):
            n = len(srcs)
            if "gbc" not in cache:
                cache["gbc"] = st.enter_context(nc.sbuf_tensor(_nm(tag + "gbc"), [128, D], F32))
                cache["junk"] = st.enter_context(nc.sbuf_tensor(_nm(tag + "junk"), [128, D], BF16))
                cache["xn"] = [st.enter_context(nc.sbuf_tensor(_nm(tag + "xn"), [128, D], BF16)) for _ in range(2)]
                cache["ss"] = st.enter_context(nc.sbuf_tensor(_nm(tag + "ss"), [128, 32], F32))
                cache["ms"] = st.enter_context(nc.sbuf_tensor(_nm(tag + "ms"), [128, 32], F32))
                cache["rstd"] = st.enter_context(nc.sbuf_tensor(_nm(tag + "rstd"), [128, 32], F32))
                gbc0 = cache["gbc"]
                P.emit("sp", lambda e: e.dma_start(out=gbc0[:], in_=gain_ap[0:1, :].partition_broadcast(128)), writes=[("ngbc",)], dma=True)
            gbc, junk, xn, ss, ms, rstd = cache["gbc"], cache["junk"], cache["xn"], cache["ss"], cache["ms"], cache["rstd"]
            kg, kss, kms, krs = ("ngbc",), ("nss",), ("nms",), ("nrstd",)

            def stats(sub, base):
                P.emit("pool", lambda e: e.memset(ss[:, 0:len(sub)], 1.0), writes=[kss])
                for i, (src, np_, col, skey) in enumerate(sub):
                    P.emit("act", lambda e, src=src, np_=np_, i=i: e.activation(out=junk[:np_, :], in_=src, func=AF.Square,
                                                                                accum_out=ss[:np_, i:i + 1]),
                           reads=[skey, kss], writes=[("nssc", i), ("njunk",)])
                m = len(sub)
                P.emit("dve", lambda e: e.tensor_scalar(out=ms[:, 0:m], in0=ss[:, 0:m], scalar1=1.0 / D, scalar2=RMS_EPS,
                                                        op0=ALU.mult, op1=ALU.add), reads=[kss] + [("nssc", i) for i in range(m)], writes=[kms])
                P.emit("pool", lambda e: e.tensor_tensor(out=rstd[:, 0:m], in0=ms[:, 0:m], in1=neghalf[:, 0:m], op=ALU.pow),
                       reads=[kms], writes=[krs])
                for i, (src, np_, col, skey) in enumerate(sub):
                    xb = xn[(base + i) % 2]
                    kx = ("nxn", (base + i) % 2)
                    pb = (base + i) % 2
                    P.emit("dve", lambda e, src=src, np_=np_, i=i, xb=xb: e.scalar_tensor_tensor(
                        out=xb[:np_, :], in0=src, scalar=rstd[:np_, i:i + 1], in1=gbc[:np_, :], op0=ALU.mult, op1=ALU.mult),
                        reads=[skey, krs, kg], writes=[kx])
                    for k in range(8):
                        P.emit("pe", lambda e, k=k, np_=np_, xb=xb, pb=pb: e.transpose(
                            PSB[:, pb * 1024 + k * 128: pb * 1024 + k * 128 + np_], xb[:np_, k * 128:(k + 1) * 128], ident[:np_, :np_]),
                            reads=[kx], writes=[("psb", pb)])
                    P.emit("act", lambda e, np_=np_, col=col, pb=pb: e.activation(
                        out=dstT3[:, :, col:col + np_],
                        in_=PSB[:, pb * 1024:(pb + 1) * 1024].rearrange("p (k t) -> p k t", k=8)[:, :, 0:np_], func=AF.Copy),
                        reads=[("psb", pb)], writes=[(dkey, col)])
            for b0 in range(0, n, 16):
                stats(srcs[b0:b0 + 16], b0)

        def ffn_phase(tag, gain_ap, wg_ap, wu_ap, wd_ap, with_meta, first, last):
            with contextlib.ExitStack() as st:
                P = Plan(nc, st, tag)

                def sb(name, shape, dt):
                    return st.enter_context(nc.sbuf_tensor(_nm(tag + name), list(shape), dt))
                W = 1040
                xnT = sb("xnT", [128, 8, W], BF16)
                h1T = sb("h1T", [128, NF, W], BF16)
                wd_sb = sb("wd", [128, NF, D], BF16)
                wg_sb = [sb("wg", [128, 8, 128], BF16) for _ in range(3)]
                wu_sb = [sb("wu", [128, 8, 128], BF16) for _ in range(3)]
                sg = [sb("sg", [128, 512], BF16) for _ in range(2)]
                wgv = wg_ap.rearrange("(kc p) f -> p kc f", p=128)
                wuv = wu_ap.rearrange("(kc p) f -> p kc f", p=128)
                wdv = wd_ap.rearrange("(fc p) d -> p fc d", p=128)

                if first:
                    for i in range(16):
                        P.emit("sp", lambda e, i=i: e.dma_start(out=htile(i), in_=x[i * 128:(i + 1) * 128, :]), writes=[("h", i)], dma=True)
                    P.emit("sp", lambda e: e.dma_start(out=hmeta[:], in_=meta[:, :]), writes=[("hm",)], dma=True)
                    P.emit("pool", lambda e: e.memset(ident[:], 0.0), writes=[("ident",)])
                    P.emit("pool", lambda e: e.affine_select(out=ident[:], in_=ident[:], pattern=[[-1, 128]], compare_op=ALU.not_equal,
                                                             fill=1.0, base=0, channel_multiplier=1), writes=[("ident",)])
                    P.emit("pool", lambda e: e.memset(BD[:], 0.0), writes=[("BD",)])
                    P.emit("pool", lambda e: e.memset(BD[0:64, 0:64], 1.0), writes=[("BD",)])
                    P.emit("pool", lambda e: e.memset(BD[64:128, 64:128], 1.0), writes=[("BD",)])
                    P.emit("pool", lambda e: e.memset(neghalf[:], -0.5), writes=[("neghalf",)])

                def wdma(f):
                    s = f % 3
                    P.emit("pool", lambda e: e.dma_start(out=wg_sb[s][:], in_=wgv[:, :, f * 128:(f + 1) * 128]), writes=[("wg", s)], dma=True)
                    P.emit("pool", lambda e: e.dma_start(out=wu_sb[s][:], in_=wuv[:, :, f * 128:(f + 1) * 128]), writes=[("wu", s)], dma=True)

                def wd_dma(j):
                    P.emit("pool", lambda e: e.dma_start(out=wd_sb[:, 2 * j:2 * j + 2, :], in_=wdv[:, 2 * j:2 * j + 2, :]),
                           writes=[("wd", 2 * j), ("wd", 2 * j + 1)], dma=True)

                unit = [0]
                dunit = [0]
                ncache = {}
                for p in range(2):
                    tiles = [(htile(8 * p + t), 128, t * 128, ("h", 8 * p + t)) for t in range(8)]
                    blocks = [(0, 512), (512, 512)]
                    if with_meta and p == 0:
                        tiles.append((hmeta[0:16, :], 16, 1024, ("hm",)))
                        blocks.append((1024, 16))
                    if p == 0:
                        for f in range(3):
                            wdma(f)
                    emit_norm_T(P, st, tag + "n", tiles, gain_ap, xnT, "xnT", ncache)
                    if p == 1:
                        for f in range(3):
                            wdma(f)
                    for f in range(NF):
                        s = f % 3
                        for (c0, N) in blocks:
                            u = unit[0] % 3
                            unit[0] += 1
                            sgi = unit[0] % 2
                            xk = [("xnT", c0 + t * 128) for t in range((N + 127) // 128)]
                            for k in range(8):
                                P.emit("pe", lambda e, k=k, u=u, c0=c0, N=N, s=s: e.matmul(
                                    fb(2 * u, N), wg_sb[s][:, k, :], xnT[:, k, c0:c0 + N], start=(k == 0), stop=(k == 7)),
                                    reads=[("wg", s)] + xk, writes=[bk(2 * u)])
                            for k in range(8):
                                P.emit("pe", lambda e, k=k, u=u, c0=c0, N=N, s=s: e.matmul(
                                    fb(2 * u + 1, N), wu_sb[s][:, k, :], xnT[:, k, c0:c0 + N], start=(k == 0), stop=(k == 7)),
                                    reads=[("wu", s)] + xk, writes=[bk(2 * u + 1)])
                            P.emit("act", lambda e, u=u, N=N, sgi=sgi: e.activation(out=sg[sgi][:, 0:N], in_=fb(2 * u, N), func=AF.Silu),
                                   reads=[bk(2 * u)], writes=[("sg", sgi)])
                            P.emit("dve", lambda e, u=u, N=N, sgi=sgi, f=f, c0=c0: e.tensor_tensor(
                                out=h1T[:, f, c0:c0 + N], in0=fb(2 * u + 1, N), in1=sg[sgi][:, 0:N], op=ALU.mult),
                                reads=[bk(2 * u + 1), ("sg", sgi)], writes=[("h1T", f, c0 + t * 128) for t in range((N + 127) // 128)])
                        if f + 3 < NF:
                            wdma(f + 3)
                        if p == 0 and f < 11:
                            wd_dma(f)
                    for (src, np_, col, skey) in tiles:
                        u = dunit[0] % 3
                        dunit[0] += 1
                        for dh in range(2):
                            for f in range(NF):
                                P.emit("pe", lambda e, u=u, dh=dh, f=f, np_=np_, col=col: e.matmul(
                                    fb(2 * u + dh, 512, 0, np_), h1T[:, f, col:col + np_], wd_sb[:, f, dh * 512:(dh + 1) * 512],
                                    start=(f == 0), stop=(f == NF - 1)),
                                    reads=[("h1T", f, col), ("wd", f)], writes=[bk(2 * u + dh)])
                        P.emit("dve", lambda e, u=u, np_=np_, src=src: e.scalar_tensor_tensor(
                            out=src, in0=PSF[0:np_, 2 * u * 512: 2 * u * 512 + 1024], scalar=0.5, in1=src, op0=ALU.mult, op1=ALU.add),
                            reads=[bk(2 * u), bk(2 * u + 1), skey], writes=[skey])
                        if last:
                            ti = skey[1]
                            P.emit("sp", lambda e, ti=ti: e.dma_start(out=y[ti * 128:(ti + 1) * 128, :], in_=htile(ti)),
                                   reads=[skey], dma=True)
                P.run()

        def mixer_phase():
            with contextlib.ExitStack() as mst:
                uT = mst.enter_context(nc.sbuf_tensor("uT", [128, 8, NTC], BF16))
                with contextlib.ExitStack() as st:
                    P = Plan(nc, st, "m0")
                    P.emit("pool", lambda e: e.memset(uT[:, :, 0:MC0], 0.0), writes=[("uT", 0)])
                    srcs = [(hmeta[0:16, :], 16, MC0, ("hm",))] + [(htile(i), 128, FC0 + 128 * i, ("h", i)) for i in range(16)]
                    emit_norm_T(P, st, "m0n", srcs, mixn, uT, "uT", {})
                    P.run()
                attention_phase(uT)
                rwkv_phase(uT)

        def attention_phase(uT):
            with contextlib.ExitStack() as st:
                P = Plan(nc, st, "at")

                def sb(name, shape, dt):
                    return st.enter_context(nc.sbuf_tensor(_nm("at" + name), list(shape), dt))
                yaT = sb("yaT", [128, 4, SEQ], BF16)
                wo_a = sb("woa", [128, 4, D], BF16)
                wq = [sb("wq", [128, 8, 128], BF16) for _ in range(2)]
                wk = [sb("wk", [128, 8, 128], BF16) for _ in range(2)]
                wv = [sb("wv", [128, 8, 128], BF16) for _ in range(2)]
                qh = [sb("qh", [128, SEQ], BF16) for _ in range(2)]
                kh = [sb("kh", [128, NTC], BF16) for _ in range(2)]
                vt = [sb("vt", [128, 17, 129], BF16) for _ in range(2)]
                EE = [sb("EE", [128, 2048], BF16) for _ in range(2)]
                basef = sb("basef", [128, 2048], F32)
                absb = sb("absb", [128, 128], F32)
                vis = sb("vis", [128, 128], BF16)
                edt = sb("edt", [128, 128], BF16)
                gm = [sb("gm", [16, 128], BF16) for _ in range(2)]
                pt = [sb("pt", [128, 512], BF16) for _ in range(4)]
                ptm = [sb("ptm", [16, 128], BF16) for _ in range(2)]
                sqb = [sb("sqb", [128, 512], BF16) for _ in range(2)]
                msb = [sb("msb", [128, 512], F32) for _ in range(2)]
                rsb = [sb("rsb", [128, 512], F32) for _ in range(2)]
                gqk = sb("gqk", [128, 2], F32)
                gmb = sb("gmb", [16, 64], F32)
                epsb = sb("epsb", [128, 1], F32)
                gq8 = sb("gq8", [128, 1], F32)
                lv = sb("lv", [128, 256], F32)
                lvt = sb("lvt", [128, 128], F32)
                dd = sb("dd", [128, 2], F32)
                ed = sb("ed", [128, 2], F32)
                neglam = sb("neglam", [128, 1], F32)
                ogb = sb("ogb", [128, 128], F32)
                rz = sb("rz", [128, 2], F32)
                s1 = sb("s1", [128, 1], F32)
                y0 = sb("y0", [128, 128], F32)
                yy = sb("yy", [128, 128], F32)
                junk = sb("junk", [128, 128], BF16)
                ssq = sb("ssq", [128, 1], F32)
                msq = sb("msq", [128, 1], F32)
                rsq = sb("rsq", [128, 1], F32)
                yn = [sb("yn", [128, 128], BF16) for _ in range(2)]
                winv = w_in.rearrange("(kc p) f -> p kc f", p=128)
                wov = w_out.rearrange("(kc p) d -> p kc d", p=128)

                P.emit("sp", lambda e: e.dma_start(out=gqk[:], in_=qkg[:, :]), writes=[("gqk",)], dma=True)
                P.emit("sp", lambda e: e.dma_start(out=lv[:], in_=lamv[0:1, :].partition_broadcast(128)), writes=[("lv",)], dma=True)
                P.emit("sp", lambda e: e.dma_start(out=ogb[:], in_=aon[0:1, :].partition_broadcast(128)), writes=[("ogb",)], dma=True)
                P.emit("pool", lambda e: e.dma_start(out=wo_a[:], in_=wov[:, 0:4, :]), writes=[("woa",)], dma=True)
                P.emit("dve", lambda e: e.tensor_scalar(out=gq8[:], in0=gqk[:, 0:1], scalar1=0.125, scalar2=None, op0=ALU.mult),
                       reads=[("gqk",)], writes=[("gq8",)])
                P.emit("dve", lambda e: e.tensor_scalar(out=ogb[:], in0=ogb[:], scalar1=1.0 - LAM_INIT, scalar2=None, op0=ALU.mult),
                       reads=[("ogb",)], writes=[("ogb",)])
                lv4 = lv[:].rearrange("p (a b d) -> p a b d", a=2, b=2)
                P.emit("dve", lambda e: e.tensor_tensor(out=lvt[:].rearrange("p (a d) -> p a d", a=2), in0=lv4[:, :, 0, :], in1=lv4[:, :, 1, :],
                                                        op=ALU.mult), reads=[("lv",)], writes=[("lvt",)])
                P.emit("dve", lambda e: e.reduce_sum(out=dd[:], in_=lvt[:].rearrange("p (a d) -> p a d", a=2), axis=AX.X),
                       reads=[("lvt",)], writes=[("dd",)])
                P.emit("act", lambda e: e.activation(out=ed[:], in_=dd[:], func=AF.Exp), reads=[("dd",)], writes=[("ed",)])
                P.emit("dve", lambda e: e.tensor_tensor(out=s1[:], in0=ed[:, 0:1], in1=ed[:, 1:2], op=ALU.subtract),
                       reads=[("ed",)], writes=[("s1",)])
                P.emit("dve", lambda e: e.tensor_scalar(out=neglam[:], in0=s1[:], scalar1=LAM_INIT, scalar2=-1.0, op0=ALU.add, op1=ALU.mult),
                       reads=[("s1",)], writes=[("neglam",)])
                P.emit("pool", lambda e: e.iota(basef[:], pattern=[[1, 2048]], base=0, channel_multiplier=-1,
                                                allow_small_or_imprecise_dtypes=True), writes=[("basef",)])
                P.emit("dve", lambda e: e.tensor_scalar(out=y0[:], in0=basef[:, 0:128], scalar1=-1.0, scalar2=None, op0=ALU.mult),
                       reads=[("basef",)], writes=[("y0",)])
                P.emit("dve", lambda e: e.tensor_tensor(out=absb[:], in0=basef[:, 0:128], in1=y0[:], op=ALU.max),
                       reads=[("basef",), ("y0",)], writes=[("absb",)])
                P.emit("pool", lambda e: e.memset(vis[:], 1.0), writes=[("vis",)])
                for hh_ in range(4):
                    P.emit("pool", lambda e: e.iota(gmb[:, hh_ * 16:(hh_ + 1) * 16], pattern=[[128, 16]], base=16, channel_multiplier=0,
                                                    allow_small_or_imprecise_dtypes=True), writes=[("gmb",)])
                    P.emit("pool", lambda e: e.tensor_scalar(out=gmb[:, hh_ * 16:(hh_ + 1) * 16], in0=gmb[:, hh_ * 16:(hh_ + 1) * 16],
                                                             scalar1=-SLOPES[hh_], scalar2=None, op0=ALU.mult), writes=[("gmb",)])
                P.emit("pool", lambda e: e.memset(epsb[:], RMS_EPS), writes=[("epsb",)])
                P.emit("pool", lambda e: e.memset(vis[64:128, 0:64], 0.0), writes=[("vis",)])
                for b in range(2):
                    P.emit("pool", lambda e, b=b: e.memset(vt[b][:, :, 128:129], 1.0), writes=[("vt", b)])

                def wdma_head(h):
                    s = h % 2
                    P.emit("pool", lambda e: e.dma_start(out=wq[s][:], in_=winv[:, :, h * 128:(h + 1) * 128]), writes=[("wq", s)], dma=True)
                    P.emit("pool", lambda e: e.dma_start(out=wk[s][:], in_=winv[:, :, 512 + h * 128:512 + (h + 1) * 128]), writes=[("wk", s)], dma=True)
                    P.emit("pool", lambda e: e.dma_start(out=wv[s][:], in_=winv[:, :, 1024 + h * 128:1024 + (h + 1) * 128]), writes=[("wv", s)], dma=True)

                ctr = {"ps": 0, "nb": 0, "o": 0, "pt": 0, "ptm": 0, "yn": 0, "pb": 0}

                def proj_norm(wsb, wkey, c0, N, gain, dst, dkey):
                    b = ctr["ps"] % 3
                    ctr["ps"] += 1
                    nb = ctr["nb"] % 2
                    ctr["nb"] += 1
                    uk = [("uT", c) for c in ([MC0] if c0 == 0 else [])] + [("uT", c0 + t * 128) for t in range(N // 128)] + [("uT", 0)]
                    for k in range(8):
                        P.emit("pe", lambda e, k=k: e.matmul(fb(b, N), wsb[:, k, :], uT[:, k, c0:c0 + N], start=(k == 0), stop=(k == 7)),
                               reads=[wkey] + uk, writes=[bk(b)])
                    P.emit("act", lambda e: e.activation(out=sqb[nb][:, 0:N], in_=fb(b, N), func=AF.Square), reads=[bk(b)], writes=[("sqb", nb)])
                    P.emit("pe", lambda e: e.matmul(fb(5, N), BD[:], sqb[nb][:, 0:N], start=True, stop=True), reads=[("sqb", nb)], writes=[bk(5)])
                    P.emit("act", lambda e: e.activation(out=msb[nb][:, 0:N], in_=fb(5, N), func=AF.Ln, scale=1.0 / 64, bias=epsb[:, 0:1]),
                           reads=[bk(5), ("epsb",)], writes=[("msb", nb)])
                    P.emit("act", lambda e: e.activation(out=rsb[nb][:, 0:N], in_=msb[nb][:, 0:N], func=AF.Exp, scale=-0.5),
                           reads=[("msb", nb)], writes=[("rsb", nb)])
                    P.emit("dve", lambda e: e.scalar_tensor_tensor(out=dst, in0=fb(b, N), scalar=gain, in1=rsb[nb][:, 0:N],
                                                                   op0=ALU.mult, op1=ALU.mult),
                           reads=[bk(b), ("rsb", nb), ("gq8",), ("gqk",)], writes=[dkey])

                wdma_head(0)
                def proj_head(h):
                    s = h % 2
                    slope = SLOPES[h]
                    P.emit("act", lambda e: e.activation(out=EE[s][:], in_=basef[:], func=AF.Exp, scale=-slope), reads=[("basef",)], writes=[("EE", s)])
                    P.emit("act", lambda e: e.activation(out=edt[:], in_=absb[:], func=AF.Exp, scale=-slope), reads=[("absb",)], writes=[("edt",)])
                    P.emit("dve", lambda e: e.tensor_tensor(out=EE[s][:, 0:128], in0=edt[:], in1=vis[:], op=ALU.mult),
                           reads=[("edt",), ("vis",)], writes=[("EE", s)])
                    for g in range(4):
                        proj_norm(wq[s], ("wq", s), FC0 + 512 * g, 512, gq8[:, 0:1], qh[s][:, g * 512:(g + 1) * 512], ("qh", s, g))
                        yield
                    proj_norm(wk[s], ("wk", s), 0, 64, gqk[:, 1:2], kh[s][:, 0:64], ("kh", s, 0))
                    yield
                    for g in range(4):
                        proj_norm(wk[s], ("wk", s), FC0 + 512 * g, 512, gqk[:, 1:2], kh[s][:, FC0 + 512 * g:FC0 + 512 * (g + 1)], ("kh", s, g + 1))
                        yield
                    for t0 in range(0, 16, 4):
                        b = ctr["ps"] % 3
                        ctr["ps"] += 1
                        for t in range(4):
                            col = FC0 + (t0 + t) * 128
                            for k in range(8):
                                P.emit("pe", lambda e, k=k, t=t, col=col: e.matmul(fb(b, 128, off=t * 128), uT[:, k, col:col + 128], wv[s][:, k, :],
                                                                                   start=(k == 0), stop=(k == 7)),
                                       reads=[("wv", s), ("uT", col)], writes=[bk(b)])
                        P.emit("act", lambda e, t0=t0, b=b: e.activation(out=vt[s][:, t0:t0 + 4, 0:128],
                                                                         in_=fb(b).rearrange("p (t c) -> p t c", t=4), func=AF.Copy),
                               reads=[bk(b)], writes=[("vt", s)])
                        yield
                    b = ctr["ps"] % 3
                    ctr["ps"] += 1
                    for k in range(8):
                        P.emit("pe", lambda e, k=k, b=b: e.matmul(fb(b, 128, 0, 16), uT[:, k, MC0:MC0 + 16], wv[s][:, k, :], start=(k == 0), stop=(k == 7)),
                               reads=[("wv", s), ("uT", MC0)], writes=[bk(b)])
                    P.emit("act", lambda e, b=b: e.activation(out=vt[s][0:16, 16, 0:128], in_=fb(b, 128, 0, 16), func=AF.Copy),
                           reads=[bk(b)], writes=[("vt", s)])
                    yield

                for _ in proj_head(0):
                    pass
                for h in range(4):
                    s = h % 2
                    slope = SLOPES[h]
                    nxt = None
                    if h + 1 < 4:
                        wdma_head(h + 1)
                        nxt = proj_head(h + 1)
                    pend = []
                    late = []

                    def flush(n=2):
                        while len(pend) > n:
                            pend.pop(0)()

                    for qi in range(16):
                        gi = ctr["ptm"] % 2
                        P.emit("act", lambda e: e.activation(out=gm[gi][:], in_=basef[0:16, 0:128], func=AF.Exp, scale=-slope,
                                                             bias=gmb[0:16, h * 16 + qi:h * 16 + qi + 1]), reads=[("basef",), ("gmb",)], writes=[("gm", gi)])
                        ob = 3 + (ctr["o"] % 2)
                        ctr["o"] += 1
                        qk_ = ("qh", s, qi // 4)
                        for c in range(2):
                            cp = slice(c * 64, (c + 1) * 64)
                            nkt = qi + 1
                            first = [True]
                            for b0 in range(0, nkt, 4):
                                jjs = list(range(b0, min(b0 + 4, nkt)))
                                nbk = len(jjs)
                                b = ctr["ps"] % 3
                                ctr["ps"] += 1
                                pi = ctr["pt"] % 4
                                ctr["pt"] += 1
                                for m, jj in enumerate(jjs):
                                    j = qi - jj
                                    P.emit("pe", lambda e: e.matmul(
                                        fb(b, 128, off=m * 128), kh[s][cp, FC0 + 128 * j:FC0 + 128 * (j + 1)], qh[s][cp, qi * 128:(qi + 1) * 128],
                                        start=True, stop=True),
                                        reads=[("kh", s, 1 + j // 4), qk_], writes=[bk(b)])
                                P.emit("act", lambda e: e.activation(out=pt[pi][:, 0:nbk * 128], in_=fb(b, nbk * 128), func=AF.Exp),
                                       reads=[bk(b)], writes=[("pt", pi)])
                                P.emit("dve", lambda e: e.tensor_tensor(
                                    out=pt[pi][:, 0:nbk * 128], in0=pt[pi][:, 0:nbk * 128], in1=EE[s][:, b0 * 128:(b0 + nbk) * 128], op=ALU.mult),
                                    reads=[("pt", pi), ("EE", s)], writes=[("pt", pi)])

                                def pv(jjs=jjs, pi=pi, first=first, ob=ob, c=c, qi=qi):
                                    for m, jj in enumerate(jjs):
                                        j = qi - jj
                                        P.emit("pe", lambda e: e.matmul(
                                            fb(ob, 129, off=c * 129), pt[pi][:, m * 128:(m + 1) * 128], vt[s][:, j, :], start=first[0], stop=False),
                                            reads=[("pt", pi), ("vt", s)], writes=[bk(ob)])
                                        first[0] = False
                                flush()
                                pend.append(pv)
                            b = ctr["ps"] % 3
                            ctr["ps"] += 1
                            mi = ctr["ptm"] % 2
                            ctr["ptm"] += 1
                            P.emit("pe", lambda e: e.matmul(fb(b, 128, 0, 16), kh[s][cp, MC0:MC0 + 16], qh[s][cp, qi * 128:(qi + 1) * 128],
                                                            start=True, stop=True), reads=[("kh", s, 0), qk_], writes=[bk(b)])
                            P.emit("act", lambda e: e.activation(out=ptm[mi][:], in_=fb(b, 128, 0, 16), func=AF.Exp),
                                   reads=[bk(b)], writes=[("ptm", mi)])
                            P.emit("dve", lambda e: e.tensor_tensor(out=ptm[mi][:], in0=ptm[mi][:], in1=gm[gi][:], op=ALU.mult),
                                   reads=[("ptm", mi), ("gm", gi)], writes=[("ptm", mi)])

                            def pvm(mi=mi, ob=ob, c=c):
                                P.emit("pe", lambda e: e.matmul(fb(ob, 129, off=c * 129), ptm[mi][0:16, :], vt[s][0:16, 16, :], start=False, stop=True),
                                       reads=[("ptm", mi), ("vt", s)], writes=[bk(ob)])
                            flush()
                            pend.append(pvm)

                        def finalize(ob=ob, qi=qi):
                            O3 = fb(ob, 258).rearrange("p (c e) -> p c e", c=2)
                            P.emit("dve", lambda e: e.reciprocal(out=rz[:].rearrange("p (c o) -> p c o", o=1), in_=O3[:, :, 128:129]),
                                   reads=[bk(ob)], writes=[("rz",)])
                            P.emit("dve", lambda e: e.tensor_tensor(out=s1[:], in0=rz[:, 1:2], in1=neglam[:], op=ALU.mult),
                                   reads=[("rz",), ("neglam",)], writes=[("s1",)])
                            P.emit("dve", lambda e: e.tensor_scalar(out=y0[:], in0=O3[:, 0, 0:128], scalar1=rz[:, 0:1], scalar2=None, op0=ALU.mult),
                                   reads=[bk(ob), ("rz",)], writes=[("y0",)])
                            P.emit("dve", lambda e: e.scalar_tensor_tensor(out=yy[:], in0=O3[:, 1, 0:128], scalar=s1[:, 0:1], in1=y0[:],
                                                                           op0=ALU.mult, op1=ALU.add),
                                   reads=[bk(ob), ("s1",), ("y0",)], writes=[("yy",)])
                            P.emit("act", lambda e: e.activation(out=junk[:], in_=yy[:], func=AF.Square, accum_out=ssq[:]),
                                   reads=[("yy",)], writes=[("junk",), ("ssq",)])
                            P.emit("dve", lambda e: e.tensor_scalar(out=msq[:], in0=ssq[:], scalar1=1.0 / 128, scalar2=RMS_EPS, op0=ALU.mult, op1=ALU.add),
                                   reads=[("ssq",)], writes=[("msq",)])
                            P.emit("pool", lambda e: e.tensor_tensor(out=rsq[:], in0=msq[:], in1=neghalf[:, 0:1], op=ALU.pow),
                                   reads=[("msq",)], writes=[("rsq",)])
                            yi = ctr["yn"] % 2
                            ctr["yn"] += 1
                            P.emit("dve", lambda e: e.scalar_tensor_tensor(out=yn[yi][:], in0=yy[:], scalar=rsq[:, 0:1], in1=ogb[:],
                                                                           op0=ALU.mult, op1=ALU.mult),
                                   reads=[("yy",), ("rsq",), ("ogb",)], writes=[("yn", yi)])
                            pb = ctr["pb"] % 2
                            ctr["pb"] += 1

                            def tr(yi=yi, pb=pb, qi=qi):
                                P.emit("pe", lambda e: e.transpose(PSB[:, pb * 1024:pb * 1024 + 128], yn[yi][:], ident[:]),
                                       reads=[("yn", yi)], writes=[("psb", pb)])
                                P.emit("act", lambda e: e.activation(out=yaT[:, h, qi * 128:(qi + 1) * 128], in_=PSB[:, pb * 1024:pb * 1024 + 128],
                                                                     func=AF.Copy), reads=[("psb", pb)], writes=[("yaT", qi)])
                            late.append(tr)
                        while late:
                            late.pop(0)()
                        pend.append(finalize)
                        if nxt is not None:
                            next(nxt, None)
                    flush(0)
                    while late:
                        late.pop(0)()
                    if nxt is not None:
                        for _ in nxt:
                            pass
                for i in range(16):
                    u = i % 2
                    for dh in range(2):
                        for kc in range(4):
                            P.emit("pe", lambda e, u=u, dh=dh, kc=kc, i=i: e.matmul(
                                fb(2 * u + dh), yaT[:, kc, i * 128:(i + 1) * 128], wo_a[:, kc, dh * 512:(dh + 1) * 512], start=(kc == 0), stop=(kc == 3)),
                                reads=[("yaT", i), ("woa",)], writes=[bk(2 * u + dh)])
                    P.emit("dve", lambda e, u=u, i=i: e.tensor_tensor(out=htile(i), in0=PSF[:, 2 * u * 512:2 * u * 512 + 1024], in1=htile(i), op=ALU.add),
                           reads=[bk(2 * u), bk(2 * u + 1), ("h", i)], writes=[("h", i)])
                P.run()

        def rwkv_phase(uT):
            with contextlib.ExitStack() as st:
                P = Plan(nc, st, "rw")

                def sb(name, shape, dt):
                    return st.enter_context(nc.sbuf_tensor(_nm("rw" + name), list(shape), dt))
                NM = 256
                NCH = 4
                wo_r = sb("wor", [128, 4, D], BF16)
                wrw = [sb("wrw", [128, 8, 128], BF16) for _ in range(4)]
                waup = sb("waup", [128, 512], BF16)
                gup = sb("gup", [128, 512], BF16)
                pp = sb("pp", [128, 34], F32)
                lnt = sb("lnt", [128, 512], F32)
                prevl = sb("prevl", [128, 14], F32)
                pbuf = [sb("pbuf", [128, NM + 1], F32) for _ in range(2)]
                dtmp = [sb("dtmp", [128, NM], F32) for _ in range(2)]
                lp = sb("lp", [128, NM], F32)
                lo24 = sb("lo24", [128, NM], BF16)
                sxg = sb("sxg", [128, NM], BF16)
                k32 = [sb("k32", [128, NM], F32) for _ in range(2)]
                r32 = [sb("r32", [128, NM], F32) for _ in range(2)]
                vbf = [sb("vbf", [128, NM], BF16) for _ in range(2)]
                asig = [sb("asig", [128, NM], F32) for _ in range(2)]
                kkr = [sb("kkr", [128, NM], F32) for _ in range(2)]
                sqk = [sb("sqk", [128, NM], BF16) for _ in range(2)]
                ssm = [sb("ssm", [128, NM], F32) for _ in range(2)]
                rn = [sb("rn", [128, NM], F32) for _ in range(2)]
                kmod = [sb("kmod", [128, NM], F32) for _ in range(2)]
                kb = [sb("kb", [128, NM], F32) for _ in range(2)]
                bp = [sb("bp", [128, NM], BF16) for _ in range(2)]
                mask = sb("mask", [128, NM], F32)
                lmask = sb("lmask", [128, 64], F32)
                umask = sb("umask", [128, 128], F32)
                I2 = sb("I2", [128, 64], BF16)
                ones = sb("ones", [128, 1], BF16)
                ln20 = sb("ln20", [128, 1], F32)
                AR = [sb("AR", [128, NCH * 128], BF16) for _ in range(4)]
                TMw = sb("TMw", [128, 4 * NCH * 192], BF16)
                TM = [TMw[:, hp_ * NCH * 192:(hp_ + 1) * NCH * 192] for hp_ in range(4)]
                WA = NCH * 128
                ABTp = [sb("ABTp", [128, 2 * WA], BF16) for _ in range(2)]
                AKTp = [sb("AKTp", [128, 2 * WA], BF16) for _ in range(2)]
                TTp = [sb("TTp", [128, 2 * NM], BF16) for _ in range(2)]
                ABT = [ABTp[hp_ // 2][:, (hp_ % 2) * WA:(hp_ % 2 + 1) * WA] for hp_ in range(4)]
                AKT = [AKTp[hp_ // 2][:, (hp_ % 2) * WA:(hp_ % 2 + 1) * WA] for hp_ in range(4)]
                TT = [TTp[hp_ // 2][:, (hp_ % 2) * NM:(hp_ % 2 + 1) * NM] for hp_ in range(4)]
                PPw = [sb("PPw", [128, 2 * NM], BF16) for _ in range(2)]
                QQw = [sb("QQw", [128, 2 * NM], BF16) for _ in range(2)]
                gbfw = sb("gbfw", [128, 4 * NM], BF16)
                gbf = [gbfw[:, hp_ * NM:(hp_ + 1) * NM] for hp_ in range(4)]
                bonw = sb("bonw", [128, 4 * NCH], F32)
                bon = [bonw[:, hp_ * NCH:(hp_ + 1) * NCH] for hp_ in range(4)]
                PC = [sb("PC", [128, 8], F32) for _ in range(4)]
                S32 = [sb("S32", [128, 64], F32) for _ in range(4)]
                Sbf = [sb("Sbf", [128, 64], BF16) for _ in range(4)]
                Xbf = [sb("Xbf", [128, 64], BF16) for _ in range(4)]
                Ubf = [sb("Ubf", [128, 64], BF16) for _ in range(4)]
                t1 = [sb("t1", [128, 64], F32) for _ in range(4)]
                Ysb = sb("Ysb", [128, 4 * NM], F32)
                ysq = sb("ysq", [128, 4 * NM], F32)
                yc = sb("yc", [128, 4 * NM], F32)
                sgw = [Ysb[:, 0 * NM:1 * NM], Ysb[:, 1 * NM:2 * NM]]
                cs = [Ysb[:, 2 * NM:3 * NM], Ysb[:, 3 * NM:4 * NM]]
                csm = [ysq[:, 0 * NM:1 * NM], ysq[:, 1 * NM:2 * NM]]
                epos = [ysq[:, 2 * NM:3 * NM], ysq[:, 3 * NM:4 * NM]]
                eneg = [yc[:, 0 * NM:1 * NM], yc[:, 1 * NM:2 * NM]]
                eprev = [yc[:, 2 * NM:3 * NM], yc[:, 3 * NM:4 * NM]]
                KYsb = [("Ysb",), ("sgw", 0), ("sgw", 1), ("cs", 0), ("cs", 1)]
                Kysq = [("ysq",), ("csm", 0), ("csm", 1), ("epos", 0), ("epos", 1)]
                Kyc = [("yc",), ("eneg", 0), ("eneg", 1), ("eprev", 0), ("eprev", 1)]
                Kytm = [("ytm",), ("BT", 0), ("BT", 1), ("KT", 0), ("KT", 1)]
                st1 = sb("st1", [128, 16], F32)
                st2 = sb("st2", [128, 16], F32)
                mean = sb("mean", [128, 16], F32)
                msq = sb("msq", [128, 16], F32)
                var = sb("var", [128, 16], F32)
                rstd = sb("rstd", [128, 16], F32)
                ytm = sb("ytm", [128, 4 * NM], BF16)
                BT = [ytm[:, 0 * NM:1 * NM], ytm[:, 1 * NM:2 * NM]]
                KT = [ytm[:, 2 * NM:3 * NM], ytm[:, 3 * NM:4 * NM]]
                yrT = sb("yrT", [128, 4, NM], BF16)
                winv = w_in.rearrange("(kc p) f -> p kc f", p=128)
                wov = w_out.rearrange("(kc p) d -> p kc d", p=128)
                print("rwkv sbuf bytes remaining", nc.sbuf_bytes_remaining)

                P.emit("sp", lambda e: e.dma_start(out=pp[:], in_=rwp[:, :]), writes=[("pp",)], dma=True)
                P.emit("sp", lambda e: e.dma_start(out=lnt[:], in_=lnwb[:, :]), writes=[("lnt",)], dma=True)
                P.emit("pool", lambda e: e.dma_start(out=waup[0:64, :], in_=rw_wup[:, :]), writes=[("waup", 0)], dma=True)
                P.emit("pool", lambda e: e.dma_start(out=waup[64:128, :], in_=rw_aup[:, :]), writes=[("waup", 1)], dma=True)
                P.emit("pool", lambda e: e.dma_start(out=gup[:], in_=rw_gup[:, :]), writes=[("gup",)], dma=True)
                P.emit("pool", lambda e: e.dma_start(out=wo_r[:], in_=wov[:, 4:8, :]), writes=[("wor",)], dma=True)
                P.emit("pool", lambda e: e.memset(prevl[:], 0.0), writes=[("prevl", i) for i in range(14)])
                P.emit("pool", lambda e: e.memset(mask[:], 1.0), writes=[("mask",)])
                P.emit("pool", lambda e: e.memset(mask[:].rearrange("p (c t) -> p c t", t=64)[:, :, 0:1], 0.0), writes=[("mask",)])
                P.emit("pool", lambda e: e.memset(ones[:], 1.0), writes=[("ones",)])
                P.emit("pool", lambda e: e.memset(ln20[:], 20.0 * math.log(2.0)), writes=[("ln20",)])
                P.emit("pool", lambda e: e.memset(lmask[:], 1.0), writes=[("lmask",)])
                P.emit("pool", lambda e: e.memset(umask[:], 1.0), writes=[("umask",)])
                P.emit("pool", lambda e: e.memset(I2[:], 0.0), writes=[("I2",)])
                for hh in range(2):
                    hp_ = slice(hh * 64, (hh + 1) * 64)
                    P.emit("pool", lambda e, hp_=hp_: e.affine_select(out=lmask[hp_, :], in_=lmask[hp_, :], pattern=[[-1, 64]], compare_op=ALU.is_gt,
                                                                      fill=0.0, base=0, channel_multiplier=1), writes=[("lmask",)])
                    P.emit("pool", lambda e, hp_=hp_: e.affine_select(out=umask[hp_, 0:64], in_=umask[hp_, 0:64], pattern=[[1, 64]], compare_op=ALU.is_gt,
                                                                      fill=0.0, base=0, channel_multiplier=-1), writes=[("umask",)])
                    P.emit("pool", lambda e, hp_=hp_: e.affine_select(out=umask[hp_, 64:128], in_=umask[hp_, 64:128], pattern=[[1, 64]], compare_op=ALU.is_ge,
                                                                      fill=0.0, base=0, channel_multiplier=-1), writes=[("umask",)])
                    P.emit("pool", lambda e, hp_=hp_: e.affine_select(out=I2[hp_, :], in_=I2[hp_, :], pattern=[[-1, 64]], compare_op=ALU.not_equal,
                                                                      fill=1.0, base=0, channel_multiplier=1), writes=[("I2",)])
                for hp in range(4):
                    P.emit("pool", lambda e, hp=hp: e.memset(S32[hp][:], 0.0), writes=[("S32", hp)])
                    P.emit("pool", lambda e, hp=hp: e.memset(Sbf[hp][:], 0.0), writes=[("Sbf", hp)])

                MU0, W00, A00, KK0, KA0, RK0 = 0, 14, 18, 22, 26, 30
                wctr = [0]
                pctr = [0]

                def h2p(h2):
                    return slice(h2 * 64, (h2 + 1) * 64)

                def v3(ap):
                    return ap.rearrange("p (c t) -> p c t", t=64)

                RING = 4
                rowseq = []
                for _g in range(1 + SEQ // NM):
                    rowseq += [12, 13, 4, 0, 5, 1, 8, 9, 6, 2, 7, 3, 10, 11]
                pfc = [0]

                def prefetch_upto(n):
                    while pfc[0] < min(n, len(rowseq)):
                        i_ = pfc[0]
                        s_ = i_ % RING
                        col_ = 1536 + rowseq[i_] * 128
                        P.emit("pool", lambda e: e.dma_start(out=wrw[s_][:], in_=winv[:, :, col_:col_ + 128]), writes=[("wrw", s_)], dma=True)
                        pfc[0] += 1

                def proj_row(wc, c0, N, out_ap, out_key):
                    i_ = wctr[0]
                    wctr[0] += 1
                    assert rowseq[i_] == wc, (i_, wc, rowseq[i_])
                    s = i_ % RING
                    prefetch_upto(i_ + RING)
                    b = pctr[0] % 2
                    pctr[0] += 1
                    for k in range(8):
                        P.emit("pe", lambda e, k=k: e.matmul(fb(b, N), wrw[s][:, k, :], uT[:, k, c0:c0 + N], start=(k == 0), stop=(k == 7)),
                               reads=[("wrw", s)], writes=[bk(b)])
                    pb_ = pbuf[b]
                    P.emit("act", lambda e: e.activation(out=pb_[:, 1:N + 1], in_=fb(b, N), func=AF.Copy), reads=[bk(b)], writes=[("pbuf", b)])
                    P.emit("act", lambda e: e.activation(out=pb_[:, 0:1], in_=prevl[:, wc:wc + 1], func=AF.Copy), reads=[("prevl", wc)], writes=[("pbuf", b)])
                    P.emit("act", lambda e: e.activation(out=prevl[:, wc:wc + 1], in_=pb_[:, N:N + 1], func=AF.Copy), reads=[("pbuf", b)], writes=[("prevl", wc)])
                    P.emit("dve", lambda e: e.tensor_tensor(out=dtmp[b][:, 0:N], in0=pb_[:, 0:N], in1=pb_[:, 1:N + 1], op=ALU.subtract),
                           reads=[("pbuf", b)], writes=[("dtmp", b)])
                    P.emit("dve", lambda e: e.scalar_tensor_tensor(
                        out=out_ap, in0=dtmp[b][:, 0:N], scalar=pp[:, MU0 + wc:MU0 + wc + 1], in1=pb_[:, 1:N + 1], op0=ALU.mult, op1=ALU.add),
                        reads=[("dtmp", b), ("pbuf", b), ("pp",)], writes=[out_key])

                groups = [(0, 64, 1, False, 0)] + [(FC0 + NM * g, NM, NCH, True, g) for g in range(SEQ // NM)]
                wout_pending = []
                for (c0, N, nch, has_out, gidx) in groups:
                    def bc(ap8):
                        return ap8.unsqueeze(2).broadcast_to([128, nch, 64])
                    proj_row(12, c0, N, lp[:, 0:N], ("lp",))
                    P.emit("act", lambda e: e.activation(out=lo24[0:64, 0:N], in_=lp[0:64, 0:N], func=AF.Tanh), reads=[("lp",)], writes=[("lo24", 0)])
                    P.emit("act", lambda e: e.activation(out=lo24[64:128, 0:N], in_=lp[64:128, 0:N], func=AF.Copy), reads=[("lp",)], writes=[("lo24", 1)])
                    proj_row(13, c0, N, lp[:, 0:N], ("lp",))
                    P.emit("act", lambda e: e.activation(out=sxg[:, 0:N], in_=lp[:, 0:N], func=AF.Sigmoid), reads=[("lp",)], writes=[("sxg",)])

                    def chain(hp):
                        par = hp % 2
                        B0, B1 = (2, 3) if par == 0 else (4, 5)
                        hc = slice(hp * 128, (hp + 1) * 128)
                        proj_row(4 + hp, c0, N, k32[par][:, 0:N], ("k32", par))
                        proj_row(0 + hp, c0, N, r32[par][:, 0:N], ("r32", par))
                        yield
                        proj_row(8 + hp, c0, N, vbf[par][:, 0:N], ("vbf", par))
                        P.emit("pe", lambda e: e.matmul(fb(B0, N), waup[0:64, hc], lo24[0:64, 0:N], start=True, stop=True),
                               reads=[("waup", 0), ("lo24", 0)], writes=[bk(B0)])
                        yield
                        P.emit("act", lambda e: e.activation(out=sgw[par][:, 0:N], in_=fb(B0, N), func=AF.Sigmoid, bias=pp[:, W00 + hp:W00 + hp + 1]),
                               reads=[bk(B0), ("pp",)], writes=[("sgw", par)])
                        P.emit("pe", lambda e: e.matmul(fb(B1, N), waup[64:128, hc], lo24[64:128, 0:N], start=True, stop=True),
                               reads=[("waup", 1), ("lo24", 1)], writes=[bk(B1)])
                        yield
                        P.emit("act", lambda e: e.activation(out=asig[par][:, 0:N], in_=fb(B1, N), func=AF.Sigmoid, bias=pp[:, A00 + hp:A00 + hp + 1]),
                               reads=[bk(B1), ("pp",)], writes=[("asig", par)])
                        P.emit("dve", lambda e: e.tensor_tensor_scan(out=cs[par][:, 0:N], data0=mask[:, 0:N], data1=sgw[par][:, 0:N], initial=0.0,
                                                                     op0=ALU.mult, op1=ALU.add), reads=[("mask",), ("sgw", par)], writes=[("cs", par)])
                        yield
                        P.emit("dve", lambda e: e.tensor_tensor(out=csm[par][:, 0:N], in0=cs[par][:, 0:N], in1=sgw[par][:, 0:N], op=ALU.subtract),
                               reads=[("cs", par), ("sgw", par)], writes=[("csm", par)])
                        P.emit("act", lambda e: e.activation(out=epos[par][:, 0:N], in_=cs[par][:, 0:N], func=AF.Exp, scale=-C0), reads=[("cs", par)], writes=[("epos", par)])
                        yield
                        P.emit("act", lambda e: e.activation(out=eneg[par][:, 0:N], in_=cs[par][:, 0:N], func=AF.Exp, scale=C0), reads=[("cs", par)], writes=[("eneg", par)])
                        P.emit("act", lambda e: e.activation(out=eprev[par][:, 0:N], in_=csm[par][:, 0:N], func=AF.Exp, scale=-C0), reads=[("csm", par)], writes=[("eprev", par)])
                        yield
                        P.emit("act", lambda e, hp=hp: e.activation(out=PC[hp][:, 0:nch], in_=v3(epos[par][:, 0:N])[:, :, 63], func=AF.Copy),
                               reads=[("epos", par)], writes=[("PC", hp)])
                        P.emit("dve", lambda e: e.tensor_scalar(out=kkr[par][:, 0:N], in0=k32[par][:, 0:N], scalar1=pp[:, KK0 + hp:KK0 + hp + 1], scalar2=None, op0=ALU.mult),
                               reads=[("k32", par), ("pp",)], writes=[("kkr", par)])
                        yield
                        P.emit("act", lambda e: e.activation(out=sqk[par][:, 0:N], in_=kkr[par][:, 0:N], func=AF.Square), reads=[("kkr", par)], writes=[("sqk", par)])
                        P.emit("pe", lambda e: e.matmul(fb(B0, N), BD[:], sqk[par][:, 0:N], start=True, stop=True), reads=[("sqk", par)], writes=[bk(B0)])
                        yield
                        P.emit("dve", lambda e: e.tensor_scalar(out=ssm[par][:, 0:N], in0=fb(B0, N), scalar1=1e-24, scalar2=None, op0=ALU.max),
                               reads=[bk(B0)], writes=[("ssm", par)])
                        P.emit("act", lambda e: e.activation(out=ssm[par][:, 0:N], in_=ssm[par][:, 0:N], func=AF.Ln, scale=float(2.0 ** 40)),
                               reads=[("ssm", par)], writes=[("ssm", par)])
                        yield
                        P.emit("act", lambda e: e.activation(out=rn[par][:, 0:N], in_=ssm[par][:, 0:N], func=AF.Exp, scale=-0.5, bias=ln20[:, 0:1]),
                               reads=[("ssm", par), ("ln20",)], writes=[("rn", par)])
                        P.emit("dve", lambda e: e.tensor_tensor(out=kkr[par][:, 0:N], in0=kkr[par][:, 0:N], in1=rn[par][:, 0:N], op=ALU.mult),
                               reads=[("kkr", par), ("rn", par)], writes=[("kkr", par)])
                        yield
                        P.emit("dve", lambda e: e.tensor_scalar(out=kmod[par][:, 0:N], in0=asig[par][:, 0:N], scalar1=-1.0, scalar2=pp[:, KA0 + hp:KA0 + hp + 1],
                                                                op0=ALU.add, op1=ALU.mult), reads=[("asig", par), ("pp",)], writes=[("kmod", par)])
                        P.emit("dve", lambda e: e.scalar_tensor_tensor(out=kmod[par][:, 0:N], in0=kmod[par][:, 0:N], scalar=1.0, in1=k32[par][:, 0:N], op0=ALU.add, op1=ALU.mult),
                               reads=[("kmod", par), ("k32", par)], writes=[("kmod", par)])
                        yield
                        AR3 = AR[hp][:, 0:nch * 128].rearrange("p (c a t) -> p c a t", a=2, t=64)
                        P.emit("dve", lambda e, AR3=AR3: e.scalar_tensor_tensor(
                            out=AR3[:, :, 0, :], in0=v3(kkr[par][:, 0:N]), scalar=-1.0, in1=v3(eprev[par][:, 0:N]), op0=ALU.mult, op1=ALU.mult),
                            reads=[("kkr", par), ("eprev", par)], writes=[("AR", hp)])
                        P.emit("dve", lambda e, AR3=AR3: e.tensor_tensor(out=AR3[:, :, 1, :], in0=v3(r32[par][:, 0:N]), in1=v3(epos[par][:, 0:N]), op=ALU.mult),
                               reads=[("r32", par), ("epos", par), ("AR", hp)], writes=[("AR2", hp)])
                        yield
                        P.emit("dve", lambda e: e.tensor_tensor(out=kb[par][:, 0:N], in0=kkr[par][:, 0:N], in1=asig[par][:, 0:N], op=ALU.mult),
                               reads=[("kkr", par), ("asig", par)], writes=[("kb", par)])
                        P.emit("dve", lambda e: e.tensor_tensor(out=BT[par][:, 0:N], in0=kb[par][:, 0:N], in1=eneg[par][:, 0:N], op=ALU.mult),
                               reads=[("kb", par), ("eneg", par)], writes=[("BT", par)])
                        yield
                        P.emit("dve", lambda e: e.tensor_tensor(out=KT[par][:, 0:N], in0=kmod[par][:, 0:N], in1=eneg[par][:, 0:N], op=ALU.mult),
                               reads=[("kmod", par), ("eneg", par)], writes=[("KT", par)])
                        P.emit("dve", lambda e: e.scalar_tensor_tensor(out=bp[par][:, 0:N], in0=r32[par][:, 0:N], scalar=pp[:, RK0 + hp:RK0 + hp + 1], in1=kmod[par][:, 0:N],
                                                                       op0=ALU.mult, op1=ALU.mult), reads=[("r32", par), ("kmod", par), ("pp",)], writes=[("bp", par)])
                        yield
                        P.emit("pe", lambda e: e.matmul(fb(B1, N), gup[:, hc], sxg[:, 0:N], start=True, stop=True), reads=[("gup",), ("sxg",)], writes=[bk(B1)])
                        P.emit("act", lambda e, hp=hp: e.activation(out=gbf[hp][:, 0:N], in_=fb(B1, N), func=AF.Copy), reads=[bk(B1)], writes=[("gbf", hp)])
                        yield
                        ARK = [("AR", hp), ("AR2", hp)]
                        for c in range(nch):
                            for si, (srcT, skey) in enumerate(((BT[par], ("BT", par)), (KT[par], ("KT", par)), (vbf[par], ("vbf", par)))):
                                for h2 in range(2):
                                    P.emit("pe", lambda e, c=c, si=si, srcT=srcT, h2=h2: e.transpose(
                                        PSB[h2p(h2), par * 1024 + (c * 3 + si) * 64:par * 1024 + (c * 3 + si) * 64 + 64],
                                        srcT[h2p(h2), c * 64:(c + 1) * 64], ident[h2p(h2), h2p(h2)]),
                                        reads=[skey], writes=[("psb", par)])
                        P.emit("act", lambda e, hp=hp: e.activation(out=TM[hp][:, 0:nch * 192], in_=PSB[:, par * 1024:par * 1024 + nch * 192], func=AF.Copy),
                               reads=[("psb", par)], writes=[("TM", hp)])
                        for c in range(nch):
                            for h2 in range(2):
                                P.emit("pe", lambda e, c=c, h2=h2: e.matmul(fb(B1, 1, h2 * 64, h2 * 64 + 64, off=256 + c), bp[par][h2p(h2), c * 64:(c + 1) * 64],
                                                                            ones[h2p(h2), 0:1], start=True, stop=True),
                                       reads=[("bp", par), ("ones",)], writes=[bk(B1)])
                        P.emit("act", lambda e, hp=hp: e.activation(out=bon[hp][:, 0:nch], in_=fb(B1, nch, off=256), func=AF.Copy),
                               reads=[bk(B1)], writes=[("bon", hp)])
                        yield

                    def G3(t, W, w, sz):
                        if w == W:
                            return t[:, 0:2 * W].rearrange("p (g s) -> p g s", s=sz)
                        assert w == sz
                        return t[:, 0:2 * W].rearrange("p (h x) -> p h x", h=2)[:, :, 0:w]

                    def pair_stage(pr):
                        ng = 2 * nch
                        hps = (2 * pr, 2 * pr + 1)
                        ARKS = [[("AR", hp), ("AR2", hp)] for hp in hps]
                        for hl, hp in enumerate(hps):
                            for c in range(nch):
                                for h2 in range(2):
                                    q_ = h2p(h2)
                                    P.emit("pe", lambda e: e.matmul(fb(2, 64, q_.start, q_.stop, off=hl * NM + c * 64), AR[hp][q_, c * 128:c * 128 + 64],
                                                                    BT[hl][q_, c * 64:(c + 1) * 64], start=True, stop=True),
                                           reads=ARKS[hl] + [("BT", hl)], writes=[bk(2)])
                        for hl, hp in enumerate(hps):
                            for c in range(nch):
                                for h2 in range(2):
                                    q_ = h2p(h2)
                                    P.emit("pe", lambda e: e.matmul(fb(3 + hl, 128, q_.start, q_.stop, off=c * 128), BT[hl][q_, c * 64:(c + 1) * 64],
                                                                    AR[hp][q_, c * 128:(c + 1) * 128], start=True, stop=True),
                                           reads=ARKS[hl] + [("BT", hl)], writes=[bk(3 + hl)])
                        P.emit("dve", lambda e: e.tensor_tensor(out=G3(PPw[0], NM, N, 64), in0=G3(PSF[:, 2 * 512:3 * 512], NM, N, 64),
                                                                in1=lmask[:].unsqueeze(1).broadcast_to([128, ng, 64]), op=ALU.mult),
                               reads=[bk(2), ("lmask",)], writes=[("PPw", 0)])
                        for hl, hp in enumerate(hps):
                            P.emit("dve", lambda e: e.tensor_tensor(out=ABT[hp][:, 0:nch * 128].rearrange("p (c s) -> p c s", s=128),
                                                                    in0=fb(3 + hl, nch * 128).rearrange("p (c s) -> p c s", s=128),
                                                                    in1=umask[:].unsqueeze(1).broadcast_to([128, nch, 128]), op=ALU.mult),
                                   reads=[bk(3 + hl), ("umask",)], writes=[("ABT", hp)])
                        for hl, hp in enumerate(hps):
                            for c in range(nch):
                                for h2 in range(2):
                                    q_ = h2p(h2)
                                    P.emit("pe", lambda e: e.matmul(fb(4 + hl, 128, q_.start, q_.stop, off=c * 128), KT[hl][q_, c * 64:(c + 1) * 64],
                                                                    AR[hp][q_, c * 128:(c + 1) * 128], start=True, stop=True),
                                           reads=ARKS[hl] + [("KT", hl)], writes=[bk(4 + hl)])
                        for hl, hp in enumerate(hps):
                            P.emit("dve", lambda e: e.tensor_tensor(out=AKT[hp][:, 0:nch * 128].rearrange("p (c s) -> p c s", s=128),
                                                                    in0=fb(4 + hl, nch * 128).rearrange("p (c s) -> p c s", s=128),
                                                                    in1=umask[:].unsqueeze(1).broadcast_to([128, nch, 128]), op=ALU.mult),
                                   reads=[bk(4 + hl), ("umask",)], writes=[("AKT", hp)])
                        Q0v = G3(ABTp[pr], WA, nch * 128, 128)[:, :, 0:64]
                        TTk = [("TT", hps[0]), ("TT", hps[1])]
                        P.emit("dve", lambda e: e.tensor_tensor(out=G3(TTp[pr], NM, N, 64), in0=Q0v,
                                                                in1=I2[:].unsqueeze(1).broadcast_to([128, ng, 64]), op=ALU.add),
                               reads=[("ABT", hps[0]), ("ABT", hps[1]), ("I2",)], writes=TTk)
                        P.emit("act", lambda e: e.activation(out=G3(QQw[0], NM, N, 64), in_=Q0v, func=AF.Copy),
                               reads=[("ABT", hps[0]), ("ABT", hps[1])], writes=[("QQw", 0)])
                        for lev in range(1, 6):
                            pi_, po_ = (lev - 1) % 2, lev % 2
                            Pp, Qp, Pn, Qn = PPw[pi_], QQw[pi_], PPw[po_], QQw[po_]
                            for hl in range(2):
                                for c in range(nch):
                                    for h2 in range(2):
                                        q_ = h2p(h2)
                                        o_ = hl * NM + c * 64
                                        P.emit("pe", lambda e: e.matmul(fb(2, 64, q_.start, q_.stop, off=o_), Qp[q_, o_:o_ + 64], Pp[q_, o_:o_ + 64],
                                                                        start=True, stop=True),
                                               reads=[("PPw", pi_), ("QQw", pi_)], writes=[bk(2)])
                            P.emit("act", lambda e: e.activation(out=G3(Pn, NM, N, 64), in_=G3(PSF[:, 2 * 512:3 * 512], NM, N, 64), func=AF.Copy),
                                   reads=[bk(2)], writes=[("PPw", po_)])
                            if lev < 5:
                                for hl in range(2):
                                    for c in range(nch):
                                        for h2 in range(2):
                                            q_ = h2p(h2)
                                            o_ = hl * NM + c * 64
                                            P.emit("pe", lambda e: e.matmul(fb(3, 64, q_.start, q_.stop, off=o_), Pp[q_, o_:o_ + 64], Qp[q_, o_:o_ + 64],
                                                                            start=True, stop=True),
                                                   reads=[("PPw", pi_), ("QQw", pi_)], writes=[bk(3)])
                                P.emit("dve", lambda e: e.tensor_copy(out=G3(Qn, NM, N, 64), in_=G3(PSF[:, 3 * 512:4 * 512], NM, N, 64)),
                                       reads=[bk(3)], writes=[("QQw", po_)])
                            for hl in range(2):
                                for c in range(nch):
                                    for h2 in range(2):
                                        q_ = h2p(h2)
                                        o_ = hl * NM + c * 64
                                        P.emit("pe", lambda e: e.matmul(fb(4, 64, q_.start, q_.stop, off=o_), Pn[q_, o_:o_ + 64], TTp[pr][q_, o_:o_ + 64],
                                                                        start=True, stop=True),
                                               reads=[("PPw", po_)] + TTk, writes=[bk(4)])
                            P.emit("dve", lambda e: e.tensor_tensor(out=G3(TTp[pr], NM, N, 64), in0=G3(PSF[:, 4 * 512:5 * 512], NM, N, 64),
                                                                    in1=G3(TTp[pr], NM, N, 64), op=ALU.add),
                                   reads=[bk(4)] + TTk, writes=TTk)

                    def lockstep(gens, hook_round=None):
                        gens = list(gens)
                        rnd = 0
                        while gens:
                            for g_ in list(gens):
                                try:
                                    next(g_)
                                except StopIteration:
                                    gens.remove(g_)
                            rnd += 1
                            if hook_round is not None and rnd == hook_round:
                                while wout_pending:
                                    wout_pending.pop(0)()
                    lockstep([chain(0), chain(1)], hook_round=3)
                    while wout_pending:
                        wout_pending.pop(0)()
                    pair_stage(0)
                    lockstep([chain(2), chain(3)])
                    pair_stage(1)
                    for c in range(nch):
                        vsl = slice((c * 3 + 2) * 64, (c * 3 + 3) * 64)
                        ksl = slice((c * 3 + 1) * 64, (c * 3 + 2) * 64)
                        bsl = slice((c * 3 + 0) * 64, (c * 3 + 1) * 64)
                        for hp in range(4):
                            ARK = [("AR", hp), ("AR2", hp)]
                            for h2 in range(2):
                                q_ = h2p(h2)
                                P.emit("pe", lambda e: e.matmul(fb(hp, 64, q_.start, q_.stop, off=0), AKT[hp][q_, c * 128:c * 128 + 64],
                                                                TM[hp][q_, vsl], start=True, stop=False),
                                       reads=[("AKT", hp), ("TM", hp)], writes=[bk(hp)])
                                P.emit("pe", lambda e: e.matmul(fb(hp, 64, q_.start, q_.stop, off=0), AR[hp][q_, c * 128:c * 128 + 64],
                                                                Sbf[hp][q_, :], start=False, stop=True),
                                       reads=ARK + [("Sbf", hp)], writes=[bk(hp)])
                        for hp in range(4):
                            P.emit("act", lambda e: e.activation(out=Xbf[hp][:], in_=fb(hp, 64, off=0), func=AF.Copy),
                                   reads=[bk(hp)], writes=[("Xbf", hp)])
                        for hp in range(4):
                            for h2 in range(2):
                                q_ = h2p(h2)
                                P.emit("pe", lambda e: e.matmul(fb(hp, 64, q_.start, q_.stop, off=64), TT[hp][q_, c * 64:(c + 1) * 64],
                                                                Xbf[hp][q_, :], start=True, stop=True),
                                       reads=[("TT", hp), ("Xbf", hp)], writes=[bk(hp)])
                        for hp in range(4):
                            P.emit("dve", lambda e: e.tensor_copy(out=Ubf[hp][:], in_=fb(hp, 64, off=64)),
                                   reads=[bk(hp)], writes=[("Ubf", hp)])
                        for hp in range(4):
                            ARK = [("AR", hp), ("AR2", hp)]
                            for h2 in range(2):
                                q_ = h2p(h2)
                                yo = fb(4 + hp // 2, 64, q_.start, q_.stop, off=(hp % 2) * 256 + c * 64)
                                P.emit("pe", lambda e: e.matmul(yo, AR[hp][q_, c * 128 + 64:c * 128 + 128], Sbf[hp][q_, :], start=True, stop=False),
                                       reads=ARK + [("Sbf", hp)], writes=[bk(4 + hp // 2)])
                                P.emit("pe", lambda e: e.matmul(yo, ABT[hp][q_, c * 128 + 64:c * 128 + 128], Ubf[hp][q_, :], start=False, stop=False),
                                       reads=[("ABT", hp), ("Ubf", hp)], writes=[bk(4 + hp // 2)])
                                P.emit("pe", lambda e: e.matmul(yo, AKT[hp][q_, c * 128 + 64:c * 128 + 128], TM[hp][q_, vsl], start=False, stop=True),
                                       reads=[("AKT", hp), ("TM", hp)], writes=[bk(4 + hp // 2)])
                        for hp in range(4):
                            for h2 in range(2):
                                q_ = h2p(h2)
                                so = fb(hp, 64, q_.start, q_.stop, off=128)
                                P.emit("pe", lambda e: e.matmul(so, TM[hp][q_, ksl], TM[hp][q_, vsl], start=True, stop=False),
                                       reads=[("TM", hp)], writes=[bk(hp)])
                                P.emit("pe", lambda e: e.matmul(so, TM[hp][q_, bsl], Ubf[hp][q_, :], start=False, stop=True),
                                       reads=[("TM", hp), ("Ubf", hp)], writes=[bk(hp)])
                        for hp in range(4):
                            P.emit("dve", lambda e: e.tensor_tensor(out=t1[hp][:], in0=fb(hp, 64, off=128), in1=S32[hp][:], op=ALU.add),
                                   reads=[bk(hp), ("S32", hp)], writes=[("t1", hp)])
                        for hp in range(4):
                            P.emit("act", lambda e: e.mul(out=Sbf[hp][:], in_=t1[hp][:], mul=PC[hp][:, c:c + 1]),
                                   reads=[("t1", hp), ("PC", hp)], writes=[("Sbf", hp)])
                            P.emit("dve", lambda e: e.tensor_scalar(out=S32[hp][:], in0=t1[hp][:], scalar1=PC[hp][:, c:c + 1], scalar2=None, op0=ALU.mult),
                                   reads=[("t1", hp), ("PC", hp)], writes=[("S32", hp)])
                    if not has_out:
                        continue
                    NW = 4 * N

                    def w16(ap):
                        return ap.rearrange("p (g t) -> p g t", t=64)

                    def w44(ap):
                        return ap.rearrange("p (h c t) -> p h c t", h=4, t=64)

                    def b16(ap):
                        return ap.unsqueeze(2).broadcast_to([128, 4 * nch, 64])
                    ykeys = [bk(4), bk(5)]
                    P.emit("act", lambda e: e.activation(out=Ysb[:, 0:NW], in_=PSF[:, 4 * 512:4 * 512 + NW], func=AF.Copy),
                           reads=ykeys, writes=KYsb)
                    P.emit("dve", lambda e: e.reduce_sum(out=st1[:, 0:16], in_=w16(Ysb[:, 0:NW]), axis=AX.X), reads=KYsb, writes=[("st1",)])
                    P.emit("act", lambda e: e.activation(out=ysq[:, 0:NW], in_=Ysb[:, 0:NW], func=AF.Square), reads=KYsb, writes=Kysq)
                    P.emit("dve", lambda e: e.reduce_sum(out=st2[:, 0:16], in_=w16(ysq[:, 0:NW]), axis=AX.X), reads=Kysq, writes=[("st2",)])
                    P.emit("dve", lambda e: e.tensor_scalar(out=mean[:, 0:16], in0=st1[:, 0:16], scalar1=1.0 / 64, scalar2=None, op0=ALU.mult),
                           reads=[("st1",)], writes=[("mean",)])
                    P.emit("dve", lambda e: e.tensor_tensor(out=msq[:, 0:16], in0=mean[:, 0:16], in1=mean[:, 0:16], op=ALU.mult),
                           reads=[("mean",)], writes=[("msq",)])
                    P.emit("dve", lambda e: e.scalar_tensor_tensor(out=var[:, 0:16], in0=st2[:, 0:16], scalar=1.0 / 64, in1=msq[:, 0:16],
                                                                   op0=ALU.mult, op1=ALU.subtract), reads=[("st2",), ("msq",)], writes=[("var",)])
                    P.emit("dve", lambda e: e.tensor_scalar(out=var[:, 0:16], in0=var[:, 0:16], scalar1=GN_EPS, scalar2=None, op0=ALU.add),
                           reads=[("var",)], writes=[("var",)])
                    P.emit("pool", lambda e: e.tensor_tensor(out=rstd[:, 0:16], in0=var[:, 0:16], in1=neghalf[:, 0:16], op=ALU.pow),
                           reads=[("var",)], writes=[("rstd",)])
                    P.emit("dve", lambda e: e.tensor_tensor(out=w16(yc[:, 0:NW]), in0=w16(Ysb[:, 0:NW]), in1=b16(mean[:, 0:16]), op=ALU.subtract),
                           reads=KYsb + [("mean",)], writes=Kyc)
                    V3w = TMw[:, 0:4 * nch * 192].rearrange("p (g a t) -> p g a t", a=3, t=64)[:, :, 2, :]
                    P.emit("dve", lambda e: e.tensor_tensor(out=w16(ysq[:, 0:NW]), in0=V3w, in1=b16(bonw[:, 0:16]), op=ALU.mult),
                           reads=[("TM", 0), ("TM", 1), ("TM", 2), ("TM", 3), ("bon", 0), ("bon", 1), ("bon", 2), ("bon", 3)], writes=Kysq)
                    P.emit("dve", lambda e: e.tensor_tensor(out=w16(yc[:, 0:NW]), in0=w16(yc[:, 0:NW]), in1=b16(rstd[:, 0:16]), op=ALU.mult),
                           reads=Kyc + [("rstd",)], writes=Kyc)
                    lnw4 = lnt[:, 0:256].rearrange("p (h i) -> p h i", h=4).unsqueeze(2).broadcast_to([128, 4, nch, 64])
                    lnb4 = lnt[:, 256:512].rearrange("p (h i) -> p h i", h=4).unsqueeze(2).broadcast_to([128, 4, nch, 64])
                    P.emit("dve", lambda e: e.tensor_tensor(out=w44(yc[:, 0:NW]), in0=w44(yc[:, 0:NW]), in1=lnw4, op=ALU.mult),
                           reads=Kyc + [("lnt",)], writes=Kyc)
                    P.emit("dve", lambda e: e.tensor_tensor(out=w44(ysq[:, 0:NW]), in0=w44(ysq[:, 0:NW]), in1=lnb4, op=ALU.add),
                           reads=Kysq + [("lnt",)], writes=Kysq)
                    P.emit("dve", lambda e: e.tensor_tensor(out=ytm[:, 0:NW], in0=yc[:, 0:NW], in1=ysq[:, 0:NW], op=ALU.add),
                           reads=Kyc + Kysq, writes=Kytm)
                    for hp in range(4):
                        for c in range(nch):
                            for h2 in range(2):
                                o_ = hp * N + c * 64
                                P.emit("pe", lambda e: e.transpose(PSB[h2p(h2), 1024 + o_:1024 + o_ + 64], ytm[h2p(h2), o_:o_ + 64],
                                                                   ident[h2p(h2), h2p(h2)]), reads=Kytm, writes=[("psb", 1)])
                    P.emit("dve", lambda e: e.tensor_tensor(out=yrT[:, :, :].rearrange("p h x -> p (h x)")[:, 0:NW], in0=PSB[:, 1024:1024 + NW], in1=gbfw[:, 0:NW], op=ALU.mult),
                           reads=[("psb", 1), ("gbf", 0), ("gbf", 1), ("gbf", 2), ("gbf", 3)], writes=[("yrT", 0), ("yrT", 1), ("yrT", 2), ("yrT", 3)])
                    def wout(gidx=gidx, N=N):
                        for t in range(N // 128):
                            i = gidx * (N // 128) + t
                            for dh in range(2):
                                for hp in range(4):
                                    P.emit("pe", lambda e: e.matmul(fb(dh), yrT[:, hp, t * 128:(t + 1) * 128], wo_r[:, hp, dh * 512:(dh + 1) * 512],
                                                                    start=(hp == 0), stop=(hp == 3)),
                                           reads=[("yrT", hp), ("wor",)], writes=[bk(dh)])
                            P.emit("dve", lambda e: e.tensor_tensor(out=htile(i), in0=PSF[:, 0:1024], in1=htile(i), op=ALU.add),
                                   reads=[bk(0), bk(1), ("h", i)], writes=[("h", i)])
                    wout_pending.append(wout)
                while wout_pending:
                    wout_pending.pop(0)()
                P.run()

        ffn_phase("f1", f1n, f1g, f1u, f1d, with_meta=True, first=True, last=False)
        mixer_phase()
        ffn_phase("f2", f2n, f2g, f2u, f2d, with_meta=False, first=False, last=True)
    return nc


_NC_CACHE = {}


def _prep_shared(inp):
    f = lambda a: np.ascontiguousarray(np.asarray(a, dtype=np.float32))
    sh = {}
    sh["meta_tokens"] = f(inp["meta_tokens"])
    for nme in ("ffn1_norm", "ffn2_norm", "mix_norm"):
        sh[nme] = f(inp[nme]).reshape(1, D)
    for nme in ("ffn1_gate", "ffn1_up", "ffn2_gate", "ffn2_up"):
        sh[nme] = f(inp[nme]).reshape(D, DFF)
    for nme in ("ffn1_down", "ffn2_down"):
        sh[nme] = f(inp[nme]).reshape(DFF, D)
    sh["w_in"] = f(inp["w_in"]).reshape(D, 3328)
    sh["w_out"] = f(inp["w_out"]).reshape(D, D)
    qn = f(inp["q_norm"]).reshape(64)
    kn = f(inp["k_norm"]).reshape(64)
    sh["qkg"] = f(np.stack([np.tile(qn, 2), np.tile(kn, 2)], axis=1))
    sh["lambda_vecs"] = f(inp["lambda_vecs"]).reshape(1, 256)
    sh["attn_out_norm"] = f(inp["attn_out_norm"]).reshape(1, 128)
    cols = [f(inp["rw_mu"]).reshape(14, 128).T]
    for nme in ("rw_w0", "rw_a0", "rw_k_k", "rw_k_a", "rw_r_k"):
        cols.append(f(inp[nme]).reshape(4, 128).T)
    sh["rwp"] = f(np.concatenate(cols, axis=1))
    sh["rw_w_up"] = f(inp["rw_w_up"]).reshape(64, 512)
    sh["rw_a_up"] = f(inp["rw_a_up"]).reshape(64, 512)
    sh["rw_g_up"] = f(inp["rw_g_up"]).reshape(128, 512)

    def lt(v):
        a = f(v).reshape(4, 2, 64)
        a = np.transpose(a, (1, 0, 2))
        a = np.repeat(a[:, None, :, :], 64, axis=1)
        return a.reshape(128, 256)
    sh["lnwb"] = f(np.concatenate([lt(inp["rw_ln_w"]), lt(inp["rw_ln_b"])], axis=1))
    return sh


def kernel(**inputs):
    x = np.asarray(inputs["x"], dtype=np.float32)
    B = x.shape[0]
    if "nc" not in _NC_CACHE:
        _NC_CACHE["nc"] = build_nc()
    nc = _NC_CACHE["nc"]
    sh = _prep_shared(inputs)
    in_maps = []
    for b in range(B):
        m = dict(sh)
        m["x"] = np.ascontiguousarray(x[b])
        in_maps.append(m)
    res = run_bass_kernel_spmd(nc, in_maps, core_ids=list(range(B)))
    return np.stack([np.asarray(r["y"], dtype=np.float32) for r in res.results], axis=0)
```
